# Optimizing a Trainium2 kernel written in Bass

```python
import math
import jax
import jax.numpy as jnp
from jax import lax
import numpy as np

D_MODEL = 2048
BATCH = 16
SEQ = 256
DEPTH = 2
DEC_BATCH = 2
DEC_SEQ = 2048
PAST_LEN = 512

GRID_W = 64
A_HEADS = 6
A_HD = 128
NA_ROWS = 8
NA_COLS = 16
B_HEADS = 5
B_QK = 64
B_HD = 128
C_HEADS = 5
C_DK = 128
C_DV = 128
MLSTM_CHUNK = 64
Q_BLOCK = 128
D_FF = 5504
CONV_W = 3
ROPE_BASE = 10000.0
LN_EPS = 1e-5
RMS_EPS = 1e-6
ALPHA = (2 * DEPTH) ** 0.25
BETA = (8 * DEPTH) ** -0.25
A_W = A_HEADS * A_HD
B_QKW = B_HEADS * 2 * B_QK
B_VW = B_HEADS * B_HD
C_QKW = C_HEADS * C_DK
C_VW = C_HEADS * C_DV
N_GATES = 4 * C_HEADS
IN_SPLITS = (A_W, A_W, A_W, B_QKW, B_QKW, B_VW, C_QKW, C_QKW, C_VW, C_VW, N_GATES)
IN_COLS = sum(IN_SPLITS)

kernel_name = 'hybrid_natten_diffattn_mlstm_dit_step'


def _split_cols(p):
    out, o = [], 0
    for w in IN_SPLITS:
        out.append(p[..., o:o + w])
        o += w
    return out


def _heads(t, n):
    b, s, _ = t.shape
    return t.reshape(b, s, n, -1).transpose(0, 2, 1, 3)


def _merge(t):
    b, h, s, d = t.shape
    return t.transpose(0, 2, 1, 3).reshape(b, s, h * d)


def _layernorm(x, g, b):
    xf = x.astype(jnp.float32)
    mu = xf.mean(-1, keepdims=True)
    var = jnp.square(xf - mu).mean(-1, keepdims=True)
    return (xf - mu) * lax.rsqrt(var + LN_EPS) * g.astype(jnp.float32) + b.astype(jnp.float32)


def _rmsnorm(x, g):
    xf = x.astype(jnp.float32)
    return xf * lax.rsqrt(jnp.mean(xf * xf, -1, keepdims=True) + RMS_EPS) * g.astype(jnp.float32)


def _axial_rope_tables(s, dim):
    t = jnp.arange(s)
    rows = (t // GRID_W).astype(jnp.float32)
    cols = (t % GRID_W).astype(jnp.float32)
    half = dim // 2
    freqs = ROPE_BASE ** (-jnp.arange(0, half, 2, dtype=jnp.float32) / half)
    ar = rows[:, None] * freqs
    ac = cols[:, None] * freqs
    return jnp.cos(ar), jnp.sin(ar), jnp.cos(ac), jnp.sin(ac)


def _rotate_half(x, cos, sin):
    x1, x2 = jnp.split(x, 2, axis=-1)
    return jnp.concatenate([x1 * cos - x2 * sin, x1 * sin + x2 * cos], axis=-1)


def _axial_rope(x, tabs):
    cr, sr, cc, sc = tabs
    xr, xc = jnp.split(x.astype(jnp.float32), 2, axis=-1)
    return jnp.concatenate([_rotate_half(xr, cr, sr), _rotate_half(xc, cc, sc)], axis=-1).astype(x.dtype)


def _blockwise_attention(q1_segs, k1_segs, v, scale, q2_segs=None, k2_segs=None, lam=None):
    bq, h, q_len, _ = q1_segs[0].shape
    nb = q_len // Q_BLOCK
    vf = v.astype(jnp.float32)

    def blocks(t):
        return jnp.moveaxis(t.reshape(bq, h, nb, Q_BLOCK, t.shape[-1]), 2, 0)

    def probs(qblk, ks):
        s = jnp.concatenate([jnp.einsum('bhqd,bhkd->bhqk', q, k) for q, k in zip(qblk, ks)], axis=-1)
        return jax.nn.softmax(s.astype(jnp.float32) * scale, axis=-1)

    q1b = [blocks(q) for q in q1_segs]
    if q2_segs is None:
        def one(qb):
            return jnp.einsum('bhqk,bhkd->bhqd', probs(qb, k1_segs), vf)
        out = lax.map(one, q1b)
    else:
        q2b = [blocks(q) for q in q2_segs]

        def one(qb):
            qa, qc = qb
            p = probs(qa, k1_segs) - lam * probs(qc, k2_segs)
            return jnp.einsum('bhqk,bhkd->bhqd', p, vf)
        out = lax.map(one, (q1b, q2b))
    return jnp.moveaxis(out, 0, 2).reshape(bq, h, q_len, -1)


def _neighbourhood_attention(q, k, v, k_ctx, v_ctx, rpb, scale):
    bq, h, s, dh = q.shape
    n_rows = s // GRID_W
    wr = min(NA_ROWS, n_rows)
    qg = q.reshape(bq, h, n_rows, GRID_W, dh)
    kg = k.reshape(bq, h, n_rows, GRID_W, dh)
    vg = v.reshape(bq, h, n_rows, GRID_W, dh)
    cols = np.arange(GRID_W)
    cs = np.clip(cols - NA_COLS // 2, 0, GRID_W - NA_COLS)
    col_mask = (cols[None, :] >= cs[:, None]) & (cols[None, :] < cs[:, None] + NA_COLS)
    col_idx = np.clip(cols[None, :] - cols[:, None] + NA_COLS - 1, 0, 2 * NA_COLS - 2)
    mask = jnp.asarray(np.tile(col_mask[:, None, :], (1, wr, 1)).reshape(GRID_W, wr * GRID_W))
    n_nb = wr * GRID_W
    vcf = v_ctx.astype(jnp.float32)

    def one(r):
        rs = jnp.clip(r - wr // 2, 0, n_rows - wr)
        kb = lax.dynamic_slice_in_dim(kg, rs, wr, axis=2).reshape(bq, h, n_nb, dh)
        vb = lax.dynamic_slice_in_dim(vg, rs, wr, axis=2).reshape(bq, h, n_nb, dh)
        qr = lax.dynamic_index_in_dim(qg, r, axis=2, keepdims=False)
        row_idx = NA_ROWS - 1 + rs + jnp.arange(wr) - r
        bias = rpb[:, row_idx[:, None, None], col_idx[None]]
        bias = bias.transpose(0, 2, 1, 3).reshape(h, GRID_W, n_nb).astype(jnp.float32)
        s_nb = jnp.einsum('bhqd,bhkd->bhqk', qr, kb).astype(jnp.float32) * scale + bias
        s_nb = jnp.where(mask, s_nb, -jnp.inf)
        s_ctx = jnp.einsum('bhqd,bhkd->bhqk', qr, k_ctx).astype(jnp.float32) * scale
        p = jax.nn.softmax(jnp.concatenate([s_nb, s_ctx], axis=-1), axis=-1)
        return (jnp.einsum('bhqk,bhkd->bhqd', p[..., :n_nb], vb.astype(jnp.float32))
                + jnp.einsum('bhqk,bhkd->bhqd', p[..., n_nb:], vcf))

    out = lax.map(one, jnp.arange(n_rows))
    return jnp.moveaxis(out, 0, 2).reshape(bq, h, s, dh)


def _mlstm_scan(q, k, v, i_pre, logf, c0, n0, m0):
    bq, h, s, _ = q.shape
    nc = s // MLSTM_CHUNK
    tril = jnp.asarray(np.tril(np.ones((MLSTM_CHUNK, MLSTM_CHUNK), dtype=bool)))

    def chunks(t):
        t = t.astype(jnp.float32)
        return jnp.moveaxis(t.reshape((bq, h, nc, MLSTM_CHUNK) + t.shape[3:]), 2, 0)

    def step(carry, xs):
        c_st, n_st, m_st = carry
        qc, kc, vc, ic, fc = xs
        b = jnp.cumsum(fc, axis=-1)
        d = jnp.where(tril, b[..., :, None] - b[..., None, :] + ic[..., None, :], -jnp.inf)
        inter = b + m_st[..., None]
        m_row = jnp.maximum(d.max(-1), inter)
        w_intra = jnp.exp(d - m_row[..., None])
        w_state = jnp.exp(inter - m_row)
        sc = jnp.einsum('bhtd,bhsd->bhts', qc, kc) * w_intra
        num = w_state[..., None] * jnp.einsum('bhtd,bhde->bhte', qc, c_st) + jnp.einsum('bhts,bhse->bhte', sc, vc)
        den = w_state * jnp.einsum('bhtd,bhd->bht', qc, n_st) + sc.sum(-1)
        h_out = num / jnp.maximum(jnp.abs(den), jnp.exp(-m_row))[..., None]
        b_last = b[..., -1]
        g = b_last[..., None] - b + ic
        m_new = jnp.maximum(b_last + m_st, g.max(-1))
        w_old = jnp.exp(b_last + m_st - m_new)
        w_s = jnp.exp(g - m_new[..., None])
        c_new = w_old[..., None, None] * c_st + jnp.einsum('bhs,bhsd,bhse->bhde', w_s, kc, vc)
        n_new = w_old[..., None] * n_st + jnp.einsum('bhs,bhsd->bhd', w_s, kc)
        return (c_new, n_new, m_new), h_out

    xs = tuple(chunks(t) for t in (q, k, v, i_pre, logf))
    init = (c0.astype(jnp.float32), n0.astype(jnp.float32), m0.astype(jnp.float32))
    (c_f, n_f, m_f), hs = lax.scan(step, init, xs)
    return jnp.moveaxis(hs, 0, 2).reshape(bq, h, s, -1), c_f, n_f, m_f


def _mlstm_bidir(q, k, v, gates, c0, n0, m0):
    def flip(t):
        return jnp.flip(t, axis=2)
    i_f, f_f = gates[0], jax.nn.log_sigmoid(gates[1])
    i_b, f_b = gates[2], jax.nn.log_sigmoid(gates[3])
    hf, cf, nf, mf = _mlstm_scan(q, k, v, i_f, f_f, c0[:, 0], n0[:, 0], m0[:, 0])
    hb, cb, nb, mb = _mlstm_scan(flip(q), flip(k), flip(v), flip(i_b), flip(f_b), c0[:, 1], n0[:, 1], m0[:, 1])
    return hf + flip(hb), jnp.stack([cf, cb], 1), jnp.stack([nf, nb], 1), jnp.stack([mf, mb], 1)


def _diff_lambda(b_lambda, lam_init):
    lf = b_lambda.astype(jnp.float32)
    return jnp.exp(jnp.sum(lf[0] * lf[1])) - jnp.exp(jnp.sum(lf[2] * lf[3])) + lam_init


def _mixer(h, w_in, c_gate_b, a_rpb, b_lambda, b_subln, c_norm, w_out, lam_init, ctx=None):
    bq, s, _ = h.shape
    qa, ka, va, qb, kb, vb, qc, kc, vc, oc, gc = _split_cols(h @ w_in)
    qa, ka, va = _heads(qa, A_HEADS), _heads(ka, A_HEADS), _heads(va, A_HEADS)
    qb, kb, vb = _heads(qb, B_HEADS), _heads(kb, B_HEADS), _heads(vb, B_HEADS)
    qc, kc, vc = (_heads(t, C_HEADS).astype(jnp.float32) for t in (qc, kc, vc))
    gates = (gc.reshape(bq, s, 4, C_HEADS).astype(jnp.float32) + c_gate_b.astype(jnp.float32)).transpose(2, 0, 3, 1)
    q1, q2 = jnp.split(qb, 2, axis=-1)
    k1, k2 = jnp.split(kb, 2, axis=-1)
    lam = _diff_lambda(b_lambda, lam_init)
    a_scale = A_HD ** -0.5
    b_scale = B_QK ** -0.5
    if ctx is None:
        ya = _blockwise_attention([qa], [ka], va, a_scale)
        yb = _blockwise_attention([q1], [k1], vb, b_scale, [q2], [k2], lam)
        c0 = jnp.zeros((bq, 2, C_HEADS, C_DK, C_DV), jnp.float32)
        n0 = jnp.zeros((bq, 2, C_HEADS, C_DK), jnp.float32)
        m0 = jnp.zeros((bq, 2, C_HEADS), jnp.float32)
    else:
        a_k, a_v, b_k, b_v, c0, n0, m0 = ctx
        ya = _neighbourhood_attention(qa, ka, va, a_k, a_v, a_rpb, a_scale)
        tabs = _axial_rope_tables(s, B_QK)
        bk1, bk2 = jnp.split(b_k, 2, axis=-1)
        v_all = jnp.concatenate([vb, b_v.astype(vb.dtype)], axis=2)
        yb = _blockwise_attention([_axial_rope(q1, tabs), q1], [_axial_rope(k1, tabs), bk1], v_all, b_scale,
                                  [_axial_rope(q2, tabs), q2], [_axial_rope(k2, tabs), bk2], lam)
    hc, c_fin, n_fin, m_fin = _mlstm_bidir(qc * C_DK ** -0.5, kc, vc, gates, c0, n0, m0)
    yb = _rmsnorm(yb, b_subln) * (1.0 - lam_init)
    yc = jax.nn.sigmoid(oc.astype(jnp.float32)) * _merge(_rmsnorm(hc, c_norm))
    y = jnp.concatenate([_merge(ya), _merge(yb), yc], axis=-1).astype(h.dtype) @ w_out
    if ctx is None:
        return y, (ka, va, kb, vb, c_fin, n_fin, m_fin)
    return y, None


def _conv_ffn(h, w_up, conv_w, conv_b, w_down):
    s = h.shape[1]
    u = h @ w_up
    pad = CONV_W // 2
    up = jnp.pad(u, ((0, 0), (pad, pad), (0, 0)))
    u = sum(up[:, j:j + s] * conv_w[j] for j in range(CONV_W)) + conv_b
    a, g = jnp.split(u, 2, axis=-1)
    return (jax.nn.silu(g) * a) @ w_down


def _block(x, cond, lp, lam_init, ctx=None):
    (w_mod, b_mod, w_in, c_gate_b, a_rpb, b_lambda, b_subln, c_norm, w_out,
     ln1_g, ln1_b, ln2_g, ln2_b, w_up, conv_w, conv_b, w_down) = lp
    mods = jax.nn.silu(cond.astype(jnp.float32)) @ w_mod + b_mod
    sh_a, sc_a, g_a, sh_f, sc_f, g_f = [m[:, None, :].astype(x.dtype) for m in jnp.split(mods, 6, axis=-1)]
    y, new_ctx = _mixer(x * (1 + sc_a) + sh_a, w_in, c_gate_b, a_rpb, b_lambda, b_subln, c_norm, w_out,
                        lam_init, ctx)
    x = _layernorm(ALPHA * x + g_a * y, ln1_g, ln1_b).astype(x.dtype)
    f = _conv_ffn(x * (1 + sc_f) + sh_f, w_up, conv_w, conv_b, w_down)
    x = _layernorm(ALPHA * x + g_f * f, ln2_g, ln2_b).astype(x.dtype)
    return x, new_ctx


def setup_inputs(seed: int = 0) -> dict:
    key = jax.random.key(seed)
    ks = jax.random.split(key, 32)

    def nrm(k, shape, s):
        return jax.random.normal(k, shape, jnp.float32) * s

    f_bias = jnp.linspace(3.0, 6.0, C_HEADS)
    gate_base = jnp.array([0.0, 1.0, 0.0, 1.0], jnp.float32)[:, None] * f_bias[None, :]
    return {
        'x_prompt': nrm(ks[0], (BATCH, SEQ, D_MODEL), 1.0),
        'x_sample': nrm(ks[1], (DEC_BATCH, DEC_SEQ, D_MODEL), 1.0),
        'cache_a_k': nrm(ks[2], (DEC_BATCH, DEPTH, A_HEADS, PAST_LEN, A_HD), 1.0),
        'cache_a_v': nrm(ks[3], (DEC_BATCH, DEPTH, A_HEADS, PAST_LEN, A_HD), 1.0),
        'cache_b_k': nrm(ks[4], (DEC_BATCH, DEPTH, B_HEADS, PAST_LEN, 2 * B_QK), 1.0),
        'cache_b_v': nrm(ks[5], (DEC_BATCH, DEPTH, B_HEADS, PAST_LEN, B_HD), 1.0),
        'state_c_C': nrm(ks[6], (DEC_BATCH, DEPTH, 2, C_HEADS, C_DK, C_DV), 1.0),
        'state_c_n': nrm(ks[7], (DEC_BATCH, DEPTH, 2, C_HEADS, C_DK), 1.0),
        'state_c_m': nrm(ks[8], (DEC_BATCH, DEPTH, 2, C_HEADS), 1.0),
        'c': nrm(ks[9], (DEC_BATCH, D_MODEL), 1.0),
        'c_ctx': nrm(ks[10], (D_MODEL,), 1.0),
        'w_mod': nrm(ks[11], (DEPTH, D_MODEL, 6 * D_MODEL), 0.5 * D_MODEL ** -0.5),
        'b_mod': nrm(ks[12], (DEPTH, 6 * D_MODEL), 0.02),
        'w_in': nrm(ks[13], (DEPTH, D_MODEL, IN_COLS), D_MODEL ** -0.5),
        'c_gate_b': nrm(ks[14], (DEPTH, 4, C_HEADS), 0.1) + gate_base,
        'a_rpb': nrm(ks[15], (DEPTH, A_HEADS, 2 * NA_ROWS - 1, 2 * NA_COLS - 1), 0.1),
        'b_lambda': nrm(ks[16], (DEPTH, 4, B_QK), 0.1),
        'b_subln': 1.0 + nrm(ks[17], (DEPTH, B_HD), 0.02),
        'c_norm': 1.0 + nrm(ks[18], (DEPTH, C_DV), 0.02),
        'w_out': nrm(ks[19], (DEPTH, D_MODEL, D_MODEL), BETA * D_MODEL ** -0.5),
        'ln1_g': 1.0 + nrm(ks[20], (DEPTH, D_MODEL), 0.02),
        'ln1_b': nrm(ks[21], (DEPTH, D_MODEL), 0.02),
        'ln2_g': 1.0 + nrm(ks[22], (DEPTH, D_MODEL), 0.02),
        'ln2_b': nrm(ks[23], (DEPTH, D_MODEL), 0.02),
        'w_up': nrm(ks[24], (DEPTH, D_MODEL, 2 * D_FF), D_MODEL ** -0.5),
        'conv_w': nrm(ks[25], (DEPTH, CONV_W, 2 * D_FF), CONV_W ** -0.5),
        'conv_b': nrm(ks[26], (DEPTH, 2 * D_FF), 0.02),
        'w_down': nrm(ks[27], (DEPTH, D_FF, D_MODEL), BETA * D_FF ** -0.5),
    }


def reference(x_prompt, x_sample, cache_a_k, cache_a_v, cache_b_k, cache_b_v, state_c_C, state_c_n, state_c_m,
              c, c_ctx, w_mod, b_mod, w_in, c_gate_b, a_rpb, b_lambda, b_subln, c_norm, w_out,
              ln1_g, ln1_b, ln2_g, ln2_b, w_up, conv_w, conv_b, w_down):
    xp = x_prompt
    xs = x_sample
    ctx_out = []
    for l in range(DEPTH):
        lam_init = 0.8 - 0.6 * math.exp(-0.3 * l)
        lp = (w_mod[l], b_mod[l], w_in[l], c_gate_b[l], a_rpb[l], b_lambda[l], b_subln[l], c_norm[l], w_out[l],
              ln1_g[l], ln1_b[l], ln2_g[l], ln2_b[l], w_up[l], conv_w[l], conv_b[l], w_down[l])
        xp, new_ctx = _block(xp, c_ctx[None, :], lp, lam_init)
        ctx_out.append(new_ctx)
        cached = (cache_a_k[:, l], cache_a_v[:, l], cache_b_k[:, l], cache_b_v[:, l],
                  state_c_C[:, l], state_c_n[:, l], state_c_m[:, l])
        xs, _ = _block(xs, c, lp, lam_init, cached)
    new_a_k = jnp.stack([t[0] for t in ctx_out], axis=1)
    new_a_v = jnp.stack([t[1] for t in ctx_out], axis=1)
    new_b_k = jnp.stack([t[2] for t in ctx_out], axis=1)
    new_b_v = jnp.stack([t[3] for t in ctx_out], axis=1)
    new_c_C = jnp.stack([t[4] for t in ctx_out], axis=1)
    new_c_n = jnp.stack([t[5] for t in ctx_out], axis=1)
    new_c_m = jnp.stack([t[6] for t in ctx_out], axis=1)
    return (xp, xs, new_a_k, new_a_v, new_b_k, new_b_v, new_c_C, new_c_n, new_c_m)
```

```python
import math
import os
KSTOP = os.environ.get('KSTOP', '')
KSKIP = os.environ.get('KSKIP', '')
SKIP_IN = set()
if KSTOP.startswith('c'):
    SKIP_IN = {"nbias", "cakT", "cav", "cbkT", "cbv", "cC", "cn", "cm", "xsT"}
    if not KSTOP[1].isdigit() or int(KSTOP[1]) < 8:
        SKIP_IN |= {"w_up", "w_down"}
    if not KSTOP[1].isdigit() or int(KSTOP[1]) < 7:
        SKIP_IN |= {"w_out"}


class StopBuild(Exception):
    pass


def kstop(tag):
    if KSTOP == tag:
        raise StopBuild(tag)
from contextlib import ExitStack
import numpy as np
import ml_dtypes
import concourse.bass as bass
import concourse.mybir as mybir
from concourse.bass_utils import run_bass_kernel_spmd

F32 = mybir.dt.float32
BF16 = mybir.dt.bfloat16
AF = mybir.ActivationFunctionType
ALU = mybir.AluOpType
AX = mybir.AxisListType

L = 2
D = 2048
T = 512
DFF = 5504
NEG = -30000.0
ALPHA = (2 * L) ** 0.25
LN_EPS = 1e-5
RMS_EPS = 1e-6
RG = [[0, 1, 2, 3], [4, 5, 6, 7]]
NCONST = 11
C_ID, C_ONE, C_TRIF, C_TRIB, C_E0, C_E63, C_E64, C_E127, C_PERM, C_HA, C_HB = range(NCONST)


class Buf:
    __slots__ = ("name", "w", "r")

    def __init__(self, name=""):
        self.name = name
        self.w = None
        self.r = []


class FW:
    NDMA = 48

    def __init__(self, nc):
        self.nc = nc
        self.eng = {"pe": nc.tensor, "act": nc.scalar, "dve": nc.vector, "pool": nc.gpsimd, "sp": nc.sync}
        self.sem, self.cnt, self._stack = {}, {}, []
        for e in self.eng:
            cm = nc.semaphore("s_" + e)
            self.sem[e] = cm.__enter__()
            self._stack.append(cm)
            self.cnt[e] = 0
        self.dsem, self.dcnt = [], []
        for i in range(self.NDMA):
            cm = nc.semaphore("d_%d" % i)
            self.dsem.append(cm.__enter__())
            self._stack.append(cm)
            self.dcnt.append(0)
        self.dnext = 0
        self.seen = {e: {} for e in self.eng}
        self.ninst = 0
        self.nwaits = 0

    def close(self):
        for cm in reversed(self._stack):
            cm.__exit__(None, None, None)

    def _semobj(self, key):
        return self.sem[key] if isinstance(key, str) else self.dsem[key]

    def _wait(self, e, tok):
        if tok is None:
            return
        key, val = tok
        if self.seen[e].get(key, 0) >= val:
            return
        self.eng[e].wait_ge(self._semobj(key), val)
        self.seen[e][key] = val
        self.nwaits += 1

    def _deps(self, e, reads, writes, is_dma=False):
        for b in reads:
            if b.w is not None:
                if b.w[0] == e and (e == "pe" or is_dma):
                    continue
                self._wait(e, b.w)
        skip_same = (e == "pe" or is_dma)
        for b in writes:
            if b.w is not None and not (skip_same and b.w[0] == e):
                self._wait(e, b.w)
            for t in b.r:
                if not (skip_same and t[0] == e):
                    self._wait(e, t)

    def _commit(self, tok, reads, writes):
        for b in reads:
            b.r.append(tok)
            if len(b.r) > 16:
                best = {}
                for k, v in b.r:
                    if best.get(k, 0) < v:
                        best[k] = v
                b.r = list(best.items())
        for b in writes:
            b.w = tok
            b.r = []

    def op(self, e, fn, reads=(), writes=(), inc=True):
        self._deps(e, reads, writes)
        ins = fn()
        self.ninst += 1
        if inc:
            self.cnt[e] += 1
            ins.then_inc(self.sem[e], 1)
            tok = (e, self.cnt[e])
        else:
            tok = (e, self.cnt[e] + 1)
        self._commit(tok, reads, writes)
        return tok

    def _next_dsem(self, q, kind=None):
        kind = kind or q
        lo, hi = {"sp": (0, 32), "pool": (32, 44), "cc": (44, 48)}[kind]
        if not hasattr(self, "dnx"):
            self.dnx = {}
        i = self.dnx.get(kind, lo)
        self.dnx[kind] = lo + (i + 1 - lo) % (hi - lo)
        if self.dcnt[i] > 0:
            self._wait(q, (i, self.dcnt[i]))
        return i

    def dma(self, q, out, in_, reads=(), writes=(), slow=False):
        self._deps(q, reads, writes, is_dma=True)
        i = self._next_dsem(q)
        if slow:
            ins = self.eng[q].dma_start(out=out, in_=in_, allow_slow_non_contiguous=True)
        else:
            ins = self.eng[q].dma_start(out=out, in_=in_)
        self.dcnt[i] += 16
        ins.then_inc(self.dsem[i], 16)
        self.ninst += 1
        tok = (i, self.dcnt[i])
        self._commit(tok, reads, writes)
        return tok

    def allgather(self, in_ap, out_ap, reads=(), writes=()):
        q = "pool"
        self._deps(q, reads, writes, is_dma=True)
        i = self._next_dsem(q, "cc")
        ins = self.nc.gpsimd.collective_compute("AllGather", ALU.bypass, replica_groups=RG, ins=[in_ap], outs=[out_ap])
        self.dcnt[i] += 1
        ins.then_inc(self.dsem[i], 1)
        self.ninst += 1
        tok = (i, self.dcnt[i])
        self._commit(tok, reads, writes)
        return tok

    def barrier(self):
        for e in self.eng:
            for f in self.eng:
                if f != e and self.cnt[f] > 0:
                    self._wait(e, (f, self.cnt[f]))
            for i in range(self.NDMA):
                if self.dcnt[i] > 0:
                    self._wait(e, (i, self.dcnt[i]))

    def finish(self):
        for i in range(self.NDMA):
            if self.dcnt[i] > 0:
                self._wait("sp", (i, self.dcnt[i]))


class Prog:
    def __init__(self):
        nc = bass.Bass("TRN2", target_bir_lowering=False)
        self.nc = nc
        self.fw = FW(nc)
        self.es = ExitStack()
        self.ring_idx = {}
        self.din = {}
        self.dout = {}

    def inp(self, name, shape, dt=F32):
        if name in SKIP_IN:
            return None
        t = self.nc.dram_tensor(name, list(shape), dt, kind="ExternalInput").ap()
        self.din[name] = t
        return t

    def outp(self, name, shape, dt=F32):
        t = self.nc.dram_tensor(name, list(shape), dt, kind="ExternalOutput").ap()
        self.dout[name] = t
        return t

    def scratch(self, name, shape, dt=F32):
        return self.nc.dram_tensor(name, list(shape), dt).ap()

    def sb(self, es, name, shape, dt=F32):
        self.uid = getattr(self, "uid", 0) + 1
        t = es.enter_context(self.nc.sbuf_tensor("%s_%d" % (name, self.uid), list(shape), dt))
        return t, Buf(name)

    def ring(self, es, name, n, shape, dt=F32):
        items = [self.sb(es, "%s%d" % (name, i), shape, dt) for i in range(n)]
        key = name
        self.ring_idx[key] = 0

        def nxt():
            i = self.ring_idx[key]
            self.ring_idx[key] = (i + 1) % n
            return items[i]
        return nxt

    def V(self, fn, reads=(), writes=()):
        return self.fw.op("dve", fn, reads, writes)

    def A(self, fn, reads=(), writes=()):
        return self.fw.op("act", fn, reads, writes)

    def G(self, fn, reads=(), writes=()):
        return self.fw.op("pool", fn, reads, writes)

    def PE(self, fn, reads=(), writes=(), inc=True):
        return self.fw.op("pe", fn, reads, writes, inc=inc)

    def mm(self, out, lhsT, rhs, start, stop, reads, writes, inc=None, sgc=False):
        nc = self.nc
        return self.fw.op("pe", lambda: nc.tensor.matmul(out, lhsT=lhsT, rhs=rhs, start=start, stop=stop, skip_group_check=sgc),
                          reads, writes, inc=(stop if inc is None else inc))


def build_program():
    P = Prog()
    nc, fw = P.nc, P.fw
    V, A, PE, mm, G = P.V, P.A, P.PE, P.mm, P.G

    xin = {"P": P.inp("xpT", [D, T]), "S": P.inp("xsT", [D, T])}
    w_in = P.inp("w_in", [L, D, 6804])
    w_out = P.inp("w_out", [L, D, D])
    w_up = P.inp("w_up", [L, D, 2 * DFF])
    w_down = P.inp("w_down", [L, DFF, D])
    wmod = P.inp("wmod", [L, D, 3072])
    bmod = P.inp("bmod", [128, L * 24])
    cond2 = P.inp("cond2", [128, 32])
    lnp_d = P.inp("lnp", [128, L * 4 * 16])
    convp_d = P.inp("convp", [128, L * 86 * 4])
    gateb_d = P.inp("gateb", [L * 20])
    blam_d = P.inp("blam", [L * 256])
    subln_d = P.inp("subln", [128, L])
    cnorm_d = P.inp("cnorm", [L * 128])
    consts_d = P.inp("consts", [128, NCONST * 128])
    negm_d = P.inp("negm", [128, 2 * 640])
    rope_d = P.inp("rope", [128, 2 * T])
    nbias_d = P.inp("nbias", [L, 6, 16, 128, T])
    cakT = P.inp("cakT", [L, 6, 128, 512])
    cav = P.inp("cav", [L, 6, 512, 128])
    cbkT = P.inp("cbkT", [L, 5, 128, 512])
    cbv = P.inp("cbv", [L, 5, 512, 128])
    cC_d = P.inp("cC", [L, 2, 5, 128, 128])
    cn_d = P.inp("cn", [L, 128, 10])
    cm_d = P.inp("cm", [L * 10])
    cftab_d = P.inp("cftab", [128, 2 * 5 * 4 * 5])
    vtab_d = P.inp("vtab", [128, 2 * 5 * 5])
    sel_d = P.inp("sel", [128, 8])

    yout = {"P": P.outp("ypT", [D, T]), "S": P.outp("ysT", [D, T])}
    o_ak = P.outp("o_ak", [2, L, 6, 128, 256])
    o_av = P.outp("o_av", [2, L, 6, 256, 128])
    o_bk = P.outp("o_bk", [2, L, 5, 128, 256])
    o_bv = P.outp("o_bv", [2, L, 5, 256, 128])
    o_cC = P.outp("o_cC", [2, L, 2, 5, 128, 128])
    o_cn = P.outp("o_cn", [2, L, 2, 128, 5])
    o_cm = P.outp("o_cm", [2, L, 2, 5])
    B_out = Buf("outputs")

    xspill = {"P": P.scratch("xspP", [D, T]), "S": P.scratch("xspS", [D, T])}
    B_spill = {"P": Buf(), "S": Buf()}
    mg_in = P.scratch("mg_in", [128, 96]); mg_out = P.scratch("mg_out", [512, 96])
    B_mgi, B_mgo = Buf(), Buf()
    CH_NH = [4, 4, 3]
    HH_CH = [0, 0, 0, 0, 1, 1, 1, 1, 2, 2, 2]
    HH_IX = [0, 1, 2, 3, 0, 1, 2, 3, 0, 1, 2]
    bnc_in = [P.scratch("bnc_in%d" % i, [2 * n * 128, T], BF16) for i, n in enumerate(CH_NH)]
    bnc_out = [P.scratch("bnc_out%d" % i, [4 * 2 * n * 128, T], BF16) for i, n in enumerate(CH_NH)]
    B_bi = [Buf() for _ in CH_NH]
    B_bo = [Buf() for _ in CH_NH]

    def bnc_k_rows(hh):
        i = HH_IX[hh]
        return bnc_in[HH_CH[hh]][i * 128:(i + 1) * 128, :], B_bi[HH_CH[hh]]

    def bnc_v_rows(hh):
        c = HH_CH[hh]
        i = CH_NH[c] + HH_IX[hh]
        return bnc_in[c][i * 128:(i + 1) * 128, :], B_bi[c]
    CSF = 1310
    cs_in = P.scratch("cs_in", [128, CSF]); cs_out = P.scratch("cs_out", [512, CSF])
    B_csi, B_cso = Buf(), Buf()
    hb_in = P.scratch("hb_in", [128, 32]); hb_out = P.scratch("hb_out", [512, 32])
    B_hbi, B_hbo = Buf(), Buf()

    es = P.es
    x_sb, B_x = P.sb(es, "x_sb", [128, 16, T], F32)
    h_sb, B_h = P.sb(es, "h_sb", [128, 16, T], BF16)
    hh_sb, B_hh = P.sb(es, "hh_sb", [128, 16, 2], BF16)
    WSL = 8704
    wring = P.ring(es, "wr", 2, [128, WSL], BF16)
    cst, B_c = P.sb(es, "cst", [128, NCONST, 128], F32)
    negm, _ = P.sb(es, "negm", [128, 2, 640], F32)
    rope, _ = P.sb(es, "rope", [128, 2, T], F32)
    ones_bf, _ = P.sb(es, "ones_bf", [128, 128], BF16)
    id_bf, _ = P.sb(es, "id_bf", [128, 128], BF16)
    tri_bf, _ = P.sb(es, "tri_bf", [128, 2, 128], BF16)
    lnp, _ = P.sb(es, "lnp", [128, L, 4, 16], F32)
    convp, _ = P.sb(es, "convp", [128, L, 86, 4], F32)
    gateb, _ = P.sb(es, "gateb", [128, L, 20], F32)
    blam, _ = P.sb(es, "blam", [128, L, 4, 64], F32)
    subln, _ = P.sb(es, "subln", [128, L], F32)
    cnorm, _ = P.sb(es, "cnorm", [128, L, 128], F32)
    cftab, _ = P.sb(es, "cftab", [128, 2, 5, 4, 5], F32)
    vtab, _ = P.sb(es, "vtab", [128, 2, 5, 5], F32)
    sel, _ = P.sb(es, "sel", [128, 8], F32)
    modv, B_modv = P.sb(es, "modv", [128, L, 96, 2], F32)
    nlam, B_nlam = P.sb(es, "nlam", [128, L], F32)
    sublns, _ = P.sb(es, "sublns", [128, L], F32)

    def CM(i):
        return cst[:, i, :]

    pbanks = [es.enter_context(nc.psum_tensor("ps%d" % i, [128, 512], F32)) for i in range(8)]
    pbufs = [Buf("ps%d" % i) for i in range(8)]
    pidx = {"s": 0, "l": 0}

    def ps_short():
        i = pidx["s"]
        pidx["s"] = (i + 1) % 5
        return pbanks[i], pbufs[i]

    def ps_long():
        i = pidx["l"]
        pidx["l"] = (i + 1) % 3
        return pbanks[5 + i], pbufs[5 + i]

    def bcast(ap1d, n):
        return bass.AP(ap1d.tensor, 0, [[0, 128], [1, n]])

    fw.dma("sp", cst[:], consts_d.rearrange("p (k n) -> p k n", k=NCONST), writes=[B_c])
    fw.dma("sp", negm[:], negm_d.rearrange("p (k n) -> p k n", k=2), writes=[B_c])
    fw.dma("sp", rope[:], rope_d.rearrange("p (k n) -> p k n", k=2), writes=[B_c])
    fw.dma("sp", lnp[:], lnp_d.rearrange("p (l k c) -> p l k c", l=L, k=4), writes=[B_c])
    fw.dma("sp", convp[:], convp_d.rearrange("p (l c k) -> p l c k", l=L, k=4), writes=[B_c])
    fw.dma("sp", gateb[:], bcast(gateb_d, L * 20).rearrange("p (l k) -> p l k", l=L), writes=[B_c])
    fw.dma("sp", blam[:], bcast(blam_d, L * 256).rearrange("p (l k c) -> p l k c", l=L, k=4), writes=[B_c])
    fw.dma("sp", subln[:], subln_d, writes=[B_c])
    fw.dma("sp", cnorm[:], bcast(cnorm_d, L * 128).rearrange("p (l k) -> p l k", l=L), writes=[B_c])
    fw.dma("sp", cftab[:], cftab_d.rearrange("p (d i r h) -> p d i r h", d=2, i=5, r=4), writes=[B_c])
    fw.dma("sp", vtab[:], vtab_d.rearrange("p (d i h) -> p d i h", d=2, i=5), writes=[B_c])
    fw.dma("sp", sel[:], sel_d, writes=[B_c])
    A(lambda: nc.scalar.copy(out=ones_bf[:], in_=CM(C_ONE)), [B_c], [B_c])
    A(lambda: nc.scalar.copy(out=id_bf[:], in_=CM(C_ID)), [B_c], [B_c])
    A(lambda: nc.scalar.copy(out=tri_bf[:, 0, :], in_=CM(C_TRIF)), [B_c], [B_c])
    A(lambda: nc.scalar.copy(out=tri_bf[:, 1, :], in_=CM(C_TRIB)), [B_c], [B_c])

    def wload(src2d, kc, ncols):
        t, b = wring()
        view = t[:, 0:kc * ncols].rearrange("p (c n) -> p c n", n=ncols)
        srcv = src2d.rearrange("(c p) n -> p c n", p=128)
        step = max(1, 2048 // 128 // 1 if ncols >= 256 else 8)
        step = 16 if ncols >= 256 else 22
        for c0 in range(0, kc, step):
            c1 = min(kc, c0 + step)
            fw.dma("pool", view[:, c0:c1, :], srcv[:, c0:c1, :], writes=[b])
        return view, b

    with ExitStack() as s0:
        c2, B_c2 = P.sb(s0, "c2", [128, 16, 2], F32)
        c2b, _ = P.sb(s0, "c2b", [128, 16, 2], BF16)
        bm, B_bm = P.sb(s0, "bm", [128, L, 24], F32)
        mloc, B_ml = P.sb(s0, "mloc", [128, L, 24, 2], F32)
        mall, B_ma = P.sb(s0, "mall", [128, 4, L, 24, 2], F32)
        fw.dma("sp", c2[:], cond2.rearrange("p (c r) -> p c r", r=2), writes=[B_c2])
        fw.dma("sp", bm[:], bmod.rearrange("p (l c) -> p l c", l=L), writes=[B_bm])
        A(lambda: nc.scalar.activation(out=c2b[:], in_=c2[:], func=AF.Silu), [B_c2], [B_c2])
        for l in range(L):
            for t4 in range(6):
                wv, wb = wload(wmod[l, :, t4 * 512:(t4 + 1) * 512], 16, 512)
                for q in range(4):
                    cc = t4 * 4 + q
                    ps, pb = ps_short()
                    for c in range(16):
                        mm(ps[:, 0:2], wv[:, c, q * 128:(q + 1) * 128], c2b[:, c, :], c == 0, c == 15, [wb, B_c2], [pb])
                    A(lambda: nc.scalar.activation(out=mloc[:, l, cc, :], in_=ps[:, 0:2], func=AF.Identity,
                                                   bias=bm[:, l, cc:cc + 1], scale=1.0), [pb, B_bm], [B_ml])
        fw.dma("sp", mg_in, mloc[:].rearrange("p l c r -> p (l c r)"), reads=[B_ml], writes=[B_mgi])
        fw.allgather(mg_in, mg_out, reads=[B_mgi], writes=[B_mgo])
        fw.dma("sp", mall[:].rearrange("p r l c w -> p r (l c w)"), mg_out.rearrange("(r p) f -> p r f", p=128),
               reads=[B_mgo], writes=[B_ma])
        for r in range(4):
            for l in range(L):
                A(lambda: nc.scalar.copy(out=modv[:, l, r * 24:(r + 1) * 24, :], in_=mall[:, r, l, :, :]), [B_ma], [B_modv])
        for l in range(L):
            for v0 in (16, 64):
                A(lambda: nc.scalar.activation(out=modv[:, l, v0:v0 + 16, :], in_=modv[:, l, v0:v0 + 16, :], func=AF.Identity, bias=1.0, scale=1.0),
                  [B_modv], [B_modv])
        lt, B_lt = P.sb(s0, "lt", [128, 64], F32)
        ld, B_ld = P.sb(s0, "ld", [128, 4], F32)
        for l in range(L):
            lam_init = 0.8 - 0.6 * math.exp(-0.3 * l)
            for k in range(2):
                V(lambda: nc.vector.tensor_tensor(out=lt[:], in0=blam[:, l, 2 * k, :], in1=blam[:, l, 2 * k + 1, :], op=ALU.mult), [B_c], [B_lt])
                V(lambda: nc.vector.reduce_sum(out=ld[:, k:k + 1], in_=lt[:], axis=AX.X), [B_lt], [B_ld])
            A(lambda: nc.scalar.activation(out=ld[:, 2:4], in_=ld[:, 0:2], func=AF.Exp), [B_ld], [B_ld])
            V(lambda: nc.vector.tensor_tensor(out=nlam[:, l:l + 1], in0=ld[:, 3:4], in1=ld[:, 2:3], op=ALU.subtract), [B_ld], [B_nlam])
            A(lambda: nc.scalar.activation(out=nlam[:, l:l + 1], in_=nlam[:, l:l + 1], func=AF.Identity, bias=-lam_init, scale=1.0), [B_nlam], [B_nlam])
            A(lambda: nc.scalar.activation(out=sublns[:, l:l + 1], in_=subln[:, l:l + 1], func=AF.Identity, scale=1.0 - lam_init), [B_c], [B_nlam])
        fw.barrier()

    def modp(l, v, fc, row):
        return modv[:, l, v * 16 + fc, row:row + 1]

    def modulate(l, vsh, vsc, row):
        for c in range(16):
            A(lambda: nc.scalar.activation(out=h_sb[:, c, :], in_=x_sb[:, c, :], func=AF.Identity,
                                           bias=modp(l, vsh, c, row), scale=modp(l, vsc, c, row)), [B_x, B_modv], [B_h])

    def layernorm(l, k, scope):
        sq_ring = P.ring(scope, "lnsq%d" % k, 2, [128, T], F32)
        st, B_st = P.sb(scope, "lnst%d" % k, [128, 2, T], F32)
        p1, b1 = ps_long()
        p2, b2 = ps_long()
        for c in range(16):
            sq, bq = sq_ring()
            A(lambda: nc.scalar.activation(out=sq[:], in_=x_sb[:, c, :], func=AF.Square), [B_x], [bq])
            mm(p1[:], CM(C_ONE), x_sb[:, c, :], c == 0, c == 15, [B_c, B_x], [b1], inc=True)
            mm(p2[:], CM(C_ONE), sq[:], c == 0, c == 15, [B_c, bq], [b2], inc=True)
        mean, var = st[:, 0, :], st[:, 1, :]
        A(lambda: nc.scalar.activation(out=mean, in_=p1[:], func=AF.Identity, scale=1.0 / D), [b1], [B_st])
        V(lambda: nc.vector.tensor_tensor(out=var, in0=mean, in1=mean, op=ALU.mult), [B_st], [B_st])
        V(lambda: nc.vector.scalar_tensor_tensor(out=var, in0=p2[:], scalar=1.0 / D, in1=var, op0=ALU.mult, op1=ALU.subtract), [b2, B_st], [B_st])
        A(lambda: nc.scalar.activation(out=var, in_=var, func=AF.Ln, bias=LN_EPS, scale=1.0), [B_st], [B_st])
        A(lambda: nc.scalar.activation(out=var, in_=var, func=AF.Exp, scale=-0.5), [B_st], [B_st])
        for c in range(16):
            V(lambda: nc.vector.tensor_tensor(out=x_sb[:, c, :], in0=x_sb[:, c, :], in1=mean, op=ALU.subtract), [B_x, B_st], [B_x])
            V(lambda: nc.vector.tensor_tensor(out=x_sb[:, c, :], in0=x_sb[:, c, :], in1=var, op=ALU.mult), [B_x, B_st], [B_x])
            A(lambda: nc.scalar.activation(out=x_sb[:, c, :], in_=x_sb[:, c, :], func=AF.Identity,
                                           bias=lnp[:, l, 2 * k + 1, c:c + 1], scale=lnp[:, l, 2 * k, c:c + 1]), [B_x, B_c], [B_x])

    def residual_proj(l, wsrc, kc, ncols_tile, rhs_fn, rhs_bufs, vgate, row, scope, tag):
        tmp_ring = P.ring(scope, "rp" + tag, 2, [128, T], F32)
        per = ncols_tile // 128
        for tcol in range(D // ncols_tile):
            wv, wb = wload(wsrc[:, tcol * ncols_tile:(tcol + 1) * ncols_tile], kc, ncols_tile)
            for q in range(per):
                fc = tcol * per + q
                ps, pb = ps_short()
                for c in range(kc):
                    mm(ps[:], wv[:, c, q * 128:(q + 1) * 128], rhs_fn(c), c == 0, c == kc - 1, [wb] + rhs_bufs, [pb])
                tmp, tb = tmp_ring()
                A(lambda: nc.scalar.activation(out=tmp[:], in_=ps[:], func=AF.Identity, scale=modp(l, vgate, fc, row)), [pb, B_modv], [tb])
                V(lambda: nc.vector.scalar_tensor_tensor(out=x_sb[:, fc, :], in0=x_sb[:, fc, :], scalar=ALPHA, in1=tmp[:],
                                                         op0=ALU.mult, op1=ALU.add), [B_x, tb], [B_x])

    def attention(groups, scale, et_ring, btmp_ring, finish):
        yps, yb = ps_long()
        dps, db = ps_long()
        ng = len(groups)
        for gi, grp in enumerate(groups):
            st, sb_ = ps_short()
            for si, s in enumerate(grp):
                mm(st[:, s["c0"]:s["c0"] + s["n"]], s["k"], s["q"], si == 0, True, [s["kb"], s["qb"]], [sb_], inc=(si == len(grp) - 1), sgc=True)
            et, eb = et_ring()
            bias = grp[0].get("bias")
            if bias is not None:
                bt, btb = btmp_ring()
                V(lambda: nc.vector.scalar_tensor_tensor(out=bt[:], in0=st[:], scalar=scale, in1=bias, op0=ALU.mult, op1=ALU.add),
                  [sb_, grp[0]["bb"]], [btb])
                A(lambda: nc.scalar.activation(out=et[:], in_=bt[:], func=AF.Exp), [btb], [eb])
            else:
                A(lambda: nc.scalar.activation(out=et[:], in_=st[:], func=AF.Exp, scale=scale), [sb_], [eb])
            for si, s in enumerate(grp):
                mm(yps[:, s["c0"]:s["c0"] + s["n"]], s["v"], et[:, s["c0"]:s["c0"] + s["n"]],
                   gi == 0 and si == 0, gi == ng - 1, [s["vb"], eb], [yb], inc=False, sgc=True)
            mm(dps[:], ones_bf[:], et[:], gi == 0, gi == ng - 1, [B_c, eb], [db], inc=True)
        finish(yps, yb, dps, db)

    def block(l, g):
        row = 0 if g == "P" else 1
        lam_init = 0.8 - 0.6 * math.exp(-0.3 * l)
        xsrc = xin[g] if l == 0 else xspill[g]
        fw.dma("sp", x_sb[:], xsrc.rearrange("(c p) t -> p c t", p=128), reads=[B_spill[g]], writes=[B_x])
        modulate(l, 0, 1, row)
        with ExitStack() as sm:
            ycat, B_y = P.sb(sm, "ycat", [128, 16, T], BF16)
            with ExitStack() as sc:
                qtc, B_qtc = P.sb(sc, "qtc", [128, 5, T], BF16)
                ktc, B_ktc = P.sb(sc, "ktc", [128, 5, T], BF16)
                kcA, B_kcA = P.sb(sc, "kcA", [128, 4, 640], BF16)
                kcB, B_kcB = P.sb(sc, "kcB", [128, 4, 640], BF16)
                vc, B_vc = P.sb(sc, "vc", [128, 4, 640], BF16)
                sigoc, B_so = P.sb(sc, "sigoc", [128, 4, 640], F32)
                gat, B_gat = P.sb(sc, "gat", [128, 4, 20], F32)
                G(lambda: nc.gpsimd.memset(kcA[:], 0.0), [], [B_kcA])
                G(lambda: nc.gpsimd.memset(kcB[:], 0.0), [], [B_kcB])
                ctiles = [(4224, 512), (4736, 512), (5248, 512), (5760, 512), (6272, 532)]
                for (c0, ncol) in ctiles:
                    wv, wb = wload(w_in[l, :, c0:c0 + ncol], 16, ncol)
                    for q in range(min(4, ncol // 128)):
                        col = c0 + q * 128
                        k = col // 128
                        if 33 <= k <= 42:
                            ps, pb = ps_short()
                            for c in range(16):
                                mm(ps[:], wv[:, c, q * 128:(q + 1) * 128], h_sb[:, c, :], c == 0, c == 15, [wb, B_h], [pb])
                            if k <= 37:
                                A(lambda: nc.scalar.activation(out=qtc[:, k - 33, :], in_=ps[:], func=AF.Copy, scale=128.0 ** -0.5), [pb], [B_qtc])
                            else:
                                A(lambda: nc.scalar.copy(out=ktc[:, k - 38, :], in_=ps[:]), [pb], [B_ktc])
                    segs = []
                    for (name, lo, hi) in (("kc", 4864, 5504), ("vc", 5504, 6144), ("oc", 6144, 6784), ("gc", 6784, 6804)):
                        a, b_ = max(lo, c0), min(hi, c0 + ncol)
                        if a < b_:
                            segs.append((name, a, b_, lo))
                    for (name, a, b_, lo) in segs:
                        for tt in range(4):
                            ps, pb = ps_short()
                            n = b_ - a
                            for c in range(16):
                                mm(ps[:, 0:n], h_sb[:, c, tt * 128:(tt + 1) * 128], wv[:, c, a - c0:b_ - c0], c == 0, c == 15, [wb, B_h], [pb])
                            o0 = a - lo
                            if name == "kc":
                                A(lambda: nc.scalar.copy(out=kcA[0:64, tt, o0:o0 + n], in_=ps[0:64, 0:n]), [pb], [B_kcA])
                                A(lambda: nc.scalar.copy(out=kcB[64:128, tt, o0:o0 + n], in_=ps[64:128, 0:n]), [pb], [B_kcB])
                            elif name == "vc":
                                A(lambda: nc.scalar.copy(out=vc[:, tt, o0:o0 + n], in_=ps[:, 0:n]), [pb], [B_vc])
                            elif name == "oc":
                                A(lambda: nc.scalar.activation(out=sigoc[:, tt, o0:o0 + n], in_=ps[:, 0:n], func=AF.Sigmoid), [pb], [B_so])
                            else:
                                V(lambda: nc.vector.tensor_tensor(out=gat[:, tt, :], in0=ps[:, 0:20], in1=gateb[:, l, :], op=ALU.add), [pb, B_c], [B_gat])
                kstop("c1")
                bc, B_bc = P.sb(sc, "bc", [128, 2, 4, 10], F32)
                ea, B_ea = P.sb(sc, "ea", [128, 2, 4, 5], F32)
                bl, B_bl = P.sb(sc, "bl", [128, 2, 8, 10], F32)
                t5, B_t5 = P.sb(sc, "t5", [128, 4, 5], F32)
                vp, B_vp = P.sb(sc, "vp", [128, 2, 4, 5, 130], BF16)
                sgp = ExitStack()
                dg, B_dg = P.sb(sgp, "dg", [128, 5, 128], F32)
                mk, B_mk = P.sb(sgp, "mk", [128, 5, 128], F32)
                for d in range(2):
                    tri = CM(C_TRIF if d == 0 else C_TRIB)
                    for tt in range(4):
                        ig = gat[:, tt, d * 10:d * 10 + 5]
                        fg = gat[:, tt, d * 10 + 5:d * 10 + 10]
                        A(lambda: nc.scalar.activation(out=t5[:, 0, :], in_=fg, func=AF.Exp, scale=-1.0), [B_gat], [B_t5])
                        A(lambda: nc.scalar.activation(out=t5[:, 1, :], in_=t5[:, 0, :], func=AF.Ln, bias=1.0, scale=1.0), [B_t5], [B_t5])
                        ps, pb = ps_short()
                        mm(ps[:, 0:5], tri, t5[:, 1, :], True, True, [B_c, B_t5], [pb])
                        A(lambda: nc.scalar.copy(out=bc[:, d, tt, 0:5], in_=ps[:, 0:5]), [pb], [B_bc])
                        V(lambda: nc.vector.tensor_tensor(out=t5[:, 2, :], in0=ig, in1=bc[:, d, tt, 0:5], op=ALU.add), [B_gat, B_bc], [B_t5])
                        A(lambda: nc.scalar.activation(out=ea[:, d, tt, :], in_=t5[:, 2, :], func=AF.Exp), [B_t5], [B_ea])
                        for h in range(5):
                            A(lambda: nc.scalar.activation(out=dg[:, h, :], in_=CM(C_ID), func=AF.Identity, scale=t5[:, 2, h:h + 1]), [B_c, B_t5], [B_dg])
                        ps1, pb1 = ps_short()
                        mm(ps1[:, 0:384], CM(C_ONE), dg[:, 0:3, :].rearrange("p h s -> p (h s)"), True, True, [B_c, B_dg], [pb1])
                        ps2, pb2 = ps_short()
                        mm(ps2[:, 0:256], CM(C_ONE), dg[:, 3:5, :].rearrange("p h s -> p (h s)"), True, True, [B_c, B_dg], [pb2])
                        V(lambda: nc.vector.tensor_tensor(out=mk[:, 0:3, :].rearrange("p h s -> p (h s)"), in0=ps1[:, 0:384],
                                                          in1=negm[:, d, 0:384], op=ALU.add), [pb1, B_c], [B_mk])
                        V(lambda: nc.vector.tensor_tensor(out=mk[:, 3:5, :].rearrange("p h s -> p (h s)"), in0=ps2[:, 0:256],
                                                          in1=negm[:, d, 384:640], op=ALU.add), [pb2, B_c], [B_mk])
                        V(lambda: nc.vector.tensor_reduce(out=bc[:, d, tt, 5:10], in_=mk[:], axis=AX.X, op=ALU.max), [B_mk], [B_bc])
                        for X in range(2):
                            ep = (C_E63, C_E127)[X] if d == 0 else (C_E0, C_E64)[X]
                            ps, pb = ps_short()
                            mm(ps[:, 0:10], CM(ep), bc[:, d, tt, :], True, True, [B_c, B_bc], [pb])
                            A(lambda: nc.scalar.copy(out=bl[:, d, 2 * tt + X, :], in_=ps[:, 0:10]), [pb], [B_bl])
                        for h in range(5):
                            A(lambda: nc.scalar.activation(out=vp[:, d, tt, h, 0:128], in_=vc[:, tt, h * 128:(h + 1) * 128], func=AF.Identity,
                                                           scale=ea[:, d, tt, h:h + 1]), [B_vc, B_ea], [B_vp])
                        A(lambda: nc.scalar.copy(out=vp[:, d, tt, :, 128], in_=ea[:, d, tt, :]), [B_ea], [B_vp])

                fw.barrier()
                sgp.close()
                kstop("c2")
                mc, B_mc = P.sb(sc, "mc", [128, 2, 8, 5], F32)
                wold, B_wo = P.sb(sc, "wold", [128, 2, 8, 5], F32)
                snew, B_sn = P.sb(sc, "snew", [128, 2, 8, 5], F32)
                mcur, B_mcur = P.sb(sc, "mcur", [128, 2, 5], F32)
                mt, B_mt = P.sb(sc, "mt", [128, 2, 5], F32)
                nfacc, B_nf = P.sb(sc, "nfacc", [128, 2, 5], F32)
                cn, B_cn = P.sb(sc, "cn", [128, 10, 129], F32)
                cnb, B_cnb = P.sb(sc, "cnb", [128, 10, 130], BF16)
                tmpu_ring = P.ring(sc, "tmpu", 2, [128, 129], F32)

                def chunk_order(d, runs):
                    out = []
                    rr = runs if d == 0 else [list(reversed(r)) for r in reversed(runs)]
                    for r in rr:
                        out.append(r)
                    return out

                def mchain(d, run, m_init_fn):
                    m_init_fn(mcur[:, d, :])
                    for c in run:
                        A(lambda: nc.scalar.copy(out=mc[:, d, c, :], in_=mcur[:, d, :]), [B_mcur], [B_mc])
                        V(lambda: nc.vector.tensor_tensor(out=mt[:, 0, :], in0=mcur[:, d, :], in1=bl[:, d, c, 5:10], op=ALU.max), [B_mcur, B_bl], [B_mt])
                        V(lambda: nc.vector.tensor_tensor(out=mt[:, 1, :], in0=mcur[:, d, :], in1=mt[:, 0, :], op=ALU.subtract), [B_mcur, B_mt], [B_mt])
                        A(lambda: nc.scalar.activation(out=wold[:, d, c, :], in_=mt[:, 1, :], func=AF.Exp), [B_mt], [B_wo])
                        A(lambda: nc.scalar.activation(out=snew[:, d, c, :], in_=mt[:, 0, :], func=AF.Exp, scale=-1.0), [B_mt], [B_sn])
                        V(lambda: nc.vector.tensor_tensor(out=mcur[:, d, :], in0=mt[:, 0, :], in1=bl[:, d, c, 0:5], op=ALU.subtract), [B_mt, B_bl], [B_mcur])
                        V(lambda: nc.vector.tensor_tensor(out=nfacc[:, d, :], in0=nfacc[:, d, :], in1=bl[:, d, c, 0:5], op=ALU.add), [B_nf, B_bl], [B_nf])

                def state_update(d, h, c):
                    tt, X = c // 2, c % 2
                    kk = kcA if X == 0 else kcB
                    kkb = B_kcA if X == 0 else B_kcB
                    ps, pb = ps_short()
                    mm(ps[:, 0:129], kk[:, tt, h * 128:(h + 1) * 128], vp[:, d, tt, h, 0:129], True, True, [kkb, B_vp], [pb])
                    tu, tub = tmpu_ring()
                    A(lambda: nc.scalar.activation(out=tu[:], in_=ps[:, 0:129], func=AF.Identity, scale=snew[:, d, c, h:h + 1]), [pb, B_sn], [tub])
                    V(lambda: nc.vector.scalar_tensor_tensor(out=cn[:, d * 5 + h, :], in0=cn[:, d * 5 + h, :], scalar=wold[:, d, c, h:h + 1],
                                                             in1=tu[:], op0=ALU.mult, op1=ALU.add), [B_cn, B_wo, tub], [B_cn])
                    A(lambda: nc.scalar.copy(out=cnb[:, d * 5 + h, 0:129], in_=cn[:, d * 5 + h, :]), [B_cn], [B_cnb])

                def zero_state(d):
                    G(lambda: nc.gpsimd.memset(cn[:, d * 5:(d + 1) * 5, :], 0.0), [], [B_cn])
                    G(lambda: nc.gpsimd.memset(cnb[:, d * 5:(d + 1) * 5, :], 0.0), [], [B_cnb])

                def alloc_scan_bufs():
                    a_ = P.sb(sc, "hc", [128, 4, 640], F32)
                    b_ = P.sb(sc, "tok", [128, 2, 4, 15], F32)
                    c_ = P.sb(sc, "mcol", [128, 5], F32)
                    return (a_[0], a_[1], b_[0], b_[1], c_[0], c_[1], P.ring(sc, "gm", 3, [128, 128], BF16),
                            P.ring(sc, "ti", 3, [128, 129], F32), P.ring(sc, "hn", 3, [128, 129], F32), P.ring(sc, "s3", 3, [128, 3], F32))

                def token_scalars(d, tt):
                    A(lambda: nc.scalar.copy(out=mcol[0:64, :], in_=mc[0:64, d, 2 * tt, :]), [B_mc], [B_mcol])
                    A(lambda: nc.scalar.copy(out=mcol[64:128, :], in_=mc[64:128, d, 2 * tt + 1, :]), [B_mc], [B_mcol])
                    V(lambda: nc.vector.tensor_tensor(out=t5[:, 3, :], in0=bc[:, d, tt, 5:10], in1=mcol[:], op=ALU.max), [B_bc, B_mcol], [B_t5])
                    A(lambda: nc.scalar.activation(out=tok[:, d, tt, 0:5], in_=t5[:, 3, :], func=AF.Exp, scale=-1.0), [B_t5], [B_tok])
                    V(lambda: nc.vector.tensor_tensor(out=t5[:, 0, :], in0=mcol[:], in1=t5[:, 3, :], op=ALU.subtract), [B_mcol, B_t5], [B_t5])
                    A(lambda: nc.scalar.activation(out=tok[:, d, tt, 5:10], in_=t5[:, 0, :], func=AF.Exp), [B_t5], [B_tok])
                    V(lambda: nc.vector.tensor_tensor(out=t5[:, 1, :], in0=bc[:, d, tt, 0:5], in1=t5[:, 3, :], op=ALU.subtract), [B_bc, B_t5], [B_t5])
                    A(lambda: nc.scalar.activation(out=tok[:, d, tt, 10:15], in_=t5[:, 1, :], func=AF.Exp), [B_t5], [B_tok])

                def scan_outputs(runs, on_run_end, on_run_start):
                    G(lambda: nc.gpsimd.memset(hc[:], 0.0), [], [B_hc])
                    for d in range(2):
                        for tt in range(4):
                            token_scalars(d, tt)
                    order = {d: chunk_order(d, runs) for d in range(2)}
                    nsteps = sum(len(r) for r in runs) // 2
                    flat = {d: [c for r in order[d] for c in r] for d in range(2)}
                    run_start = {d: {r[0]: ri for ri, r in enumerate(order[d])} for d in range(2)}
                    run_end = {d: {r[-1]: ri for ri, r in enumerate(order[d])} for d in range(2)}
                    for step in range(nsteps):
                        for d in range(2):
                            c_pair = flat[d][2 * step:2 * step + 2]
                            tt = c_pair[0] // 2
                            if c_pair[0] in run_start[d]:
                                on_run_start(d, run_start[d][c_pair[0]], order[d])
                            for h in range(5):
                                gps, gpb = ps_short()
                                mm(gps[:, 0:128], ktc[:, h, tt * 128:(tt + 1) * 128], qtc[:, h, tt * 128:(tt + 1) * 128], True, True, [B_ktc, B_qtc], [gpb])
                                gm, gmb = gm_ring()
                                V(lambda: nc.vector.tensor_tensor(out=gm[:], in0=gps[:, 0:128], in1=CM(C_TRIF if d == 0 else C_TRIB), op=ALU.mult), [gpb, B_c], [gmb])
                                ips, ipb = ps_short()
                                mm(ips[:, 0:129], gm[:], vp[:, d, tt, h, 0:129], True, True, [gmb, B_vp], [ipb])
                                ti, tib = ti_ring()
                                A(lambda: nc.scalar.activation(out=ti[:], in_=ips[:, 0:129], func=AF.Identity, scale=tok[:, d, tt, h:h + 1]), [ipb, B_tok], [tib])
                                hn, hnb = hn_ring()
                                for c in c_pair:
                                    X = c % 2
                                    rs = slice(0, 64) if X == 0 else slice(64, 128)
                                    xps, xpb = ps_short()
                                    mm(xps[:, 0:129], qtc[:, h, tt * 128:(tt + 1) * 128], cnb[:, d * 5 + h, 0:129], True, True, [B_qtc, B_cnb], [xpb])
                                    V(lambda: nc.vector.scalar_tensor_tensor(out=hn[rs, :], in0=xps[rs, 0:129], scalar=tok[rs, d, tt, 5 + h:6 + h],
                                                                             in1=ti[rs, :], op0=ALU.mult, op1=ALU.add), [xpb, B_tok, tib], [hnb])
                                    state_update(d, h, c)
                                s3, s3b = s3_ring()
                                V(lambda: nc.vector.scalar_tensor_tensor(out=s3[:, 0:1], in0=hn[:, 128:129], scalar=-1.0, in1=hn[:, 128:129],
                                                                         op0=ALU.mult, op1=ALU.max), [hnb], [s3b])
                                V(lambda: nc.vector.tensor_tensor(out=s3[:, 1:2], in0=s3[:, 0:1], in1=tok[:, d, tt, 10 + h:11 + h], op=ALU.max), [s3b, B_tok], [s3b])
                                A(lambda: nc.scalar.activation(out=s3[:, 2:3], in_=s3[:, 1:2], func=AF.Ln), [s3b], [s3b])
                                A(lambda: nc.scalar.activation(out=s3[:, 2:3], in_=s3[:, 2:3], func=AF.Exp, scale=-1.0), [s3b], [s3b])
                                hsl = hc[:, tt, h * 128:(h + 1) * 128]
                                V(lambda: nc.vector.scalar_tensor_tensor(out=hsl, in0=hn[:, 0:128], scalar=s3[:, 2:3], in1=hsl,
                                                                         op0=ALU.mult, op1=ALU.add), [hnb, s3b, B_hc], [B_hc])
                            if c_pair[1] in run_end[d]:
                                on_run_end(d, run_end[d][c_pair[1]], order[d])

                def set_const(val):
                    def f(ap):
                        G(lambda: nc.gpsimd.memset(ap, val), [], [B_mcur])
                    return f

                G(lambda: nc.gpsimd.memset(nfacc[:], 0.0), [], [B_nf])
                if g == "P":
                    runs = [[0, 1, 2, 3], [4, 5, 6, 7]]
                    mfin, B_mfin = P.sb(sc, "mfin", [128, 2, 2, 5], F32)
                    for d in range(2):
                        for ri, r in enumerate(chunk_order(d, runs)):
                            mchain(d, r, set_const(0.0))
                            seq = r[0] // 4
                            A(lambda: nc.scalar.copy(out=mfin[:, seq, d, :], in_=mcur[:, d, :]), [B_mcur], [B_mfin])
                    for seq in range(2):
                        fw.dma("sp", o_cm[seq, l].rearrange("(o d) h -> o (d h)", o=1), mfin[0:1, seq, :, :].rearrange("p d h -> p (d h)"),
                               reads=[B_mfin], writes=[B_out])

                    def on_start(d, ri, order):
                        zero_state(d)

                    def on_end(d, ri, order):
                        seq = order[ri][0] // 4
                        fw.dma("sp", o_cC[seq, l, d].rearrange("h k v -> k h v"), cn[:, d * 5:(d + 1) * 5, 0:128], reads=[B_cn], writes=[B_out])
                        fw.dma("sp", o_cn[seq, l, d], cn[:, d * 5:(d + 1) * 5, 128], reads=[B_cn], writes=[B_out], slow=True)
                    hc, B_hc, tok, B_tok, mcol, B_mcol, gm_ring, ti_ring, hn_ring, s3_ring = alloc_scan_bufs()
                    scan_outputs(runs, on_end, on_start)
                else:
                    runs = [[0, 1, 2, 3, 4, 5, 6, 7]]
                    for d in range(2):
                        zero_state(d)
                        r = chunk_order(d, runs)[0]
                        mchain(d, r, set_const(NEG))
                        for c in r:
                            for h in range(5):
                                state_update(d, h, c)
                    with ExitStack() as sg:
                        cst_t, B_cst = P.sb(sg, "cst_t", [128, 20], F32)
                        A(lambda: nc.scalar.copy(out=cst_t[:, 0:10], in_=mcur[:].rearrange("p d h -> p (d h)")), [B_mcur], [B_cst])
                        A(lambda: nc.scalar.copy(out=cst_t[:, 10:20], in_=nfacc[:].rearrange("p d h -> p (d h)")), [B_nf], [B_cst])
                        fw.dma("sp", cs_in[:, 0:1290], cn[:].rearrange("p a b -> p (a b)"), reads=[B_cn], writes=[B_csi])
                        fw.dma("sp", cs_in[:, 1290:1310], cst_t[:], reads=[B_cst], writes=[B_csi])
                        fw.allgather(cs_in, cs_out, reads=[B_csi], writes=[B_cso])
                        gs, B_gs = P.sb(sg, "gs", [128, 4, CSF], F32)
                        fw.dma("sp", gs[:], cs_out.rearrange("(r p) f -> p r f", p=128), reads=[B_cso], writes=[B_gs])
                        c0t, B_c0 = P.sb(sg, "c0t", [128, 10, 129], F32)
                        m0t, B_m0 = P.sb(sg, "m0t", [128, 10], F32)
                        fw.dma("sp", c0t[:, :, 0:128], cC_d[l].rearrange("d h k v -> k (d h) v"), writes=[B_c0])
                        fw.dma("sp", c0t[:, :, 128], cn_d[l], writes=[B_c0], slow=True)
                        fw.dma("sp", m0t[:], bass.AP(cm_d.tensor, l * 10, [[0, 128], [1, 10]]), writes=[B_m0])
                        av, B_av = P.sb(sg, "av", [128, 2, 6, 5], F32)
                        wv5, B_wv5 = P.sb(sg, "wv5", [128, 2, 5, 5], F32)
                        for d in range(2):
                            for i in range(5):
                                src = m0t[:, d * 5:(d + 1) * 5] if i == 0 else gs[:, i - 1, 1290 + d * 5:1290 + d * 5 + 5]
                                V(lambda: nc.vector.tensor_tensor(out=av[:, d, i, :], in0=src, in1=vtab[:, d, i, :], op=ALU.add), [B_m0, B_gs, B_c], [B_av])
                                for r in range(4):
                                    V(lambda: nc.vector.tensor_tensor(out=t5[:, 0, :], in0=cftab[:, d, i, r, :], in1=gs[:, r, 1300 + d * 5:1305 + d * 5], op=ALU.mult),
                                      [B_c, B_gs], [B_t5])
                                    V(lambda: nc.vector.tensor_tensor(out=av[:, d, i, :], in0=av[:, d, i, :], in1=t5[:, 0, :], op=ALU.subtract), [B_av, B_t5], [B_av])
                            V(lambda: nc.vector.tensor_tensor(out=av[:, d, 5, :], in0=av[:, d, 0, :], in1=av[:, d, 1, :], op=ALU.max), [B_av], [B_av])
                            for i in range(2, 5):
                                V(lambda: nc.vector.tensor_tensor(out=av[:, d, 5, :], in0=av[:, d, 5, :], in1=av[:, d, i, :], op=ALU.max), [B_av], [B_av])
                            for i in range(5):
                                V(lambda: nc.vector.tensor_tensor(out=t5[:, 1, :], in0=av[:, d, i, :], in1=av[:, d, 5, :], op=ALU.subtract), [B_av], [B_t5])
                                A(lambda: nc.scalar.activation(out=wv5[:, d, i, :], in_=t5[:, 1, :], func=AF.Exp), [B_t5], [B_wv5])
                            for h in range(5):
                                dh = d * 5 + h
                                A(lambda: nc.scalar.activation(out=cn[:, dh, :], in_=c0t[:, dh, :], func=AF.Identity, scale=wv5[:, d, 0, h:h + 1]), [B_c0, B_wv5], [B_cn])
                                for r in range(4):
                                    V(lambda: nc.vector.scalar_tensor_tensor(out=cn[:, dh, :], in0=gs[:, r, dh * 129:(dh + 1) * 129], scalar=wv5[:, d, 1 + r, h:h + 1],
                                                                             in1=cn[:, dh, :], op0=ALU.mult, op1=ALU.add), [B_gs, B_wv5, B_cn], [B_cn])
                                A(lambda: nc.scalar.copy(out=cnb[:, dh, 0:129], in_=cn[:, dh, :]), [B_cn], [B_cnb])

                        def m_from_av(d):
                            def f(ap):
                                A(lambda: nc.scalar.copy(out=ap, in_=av[:, d, 5, :]), [B_av], [B_mcur])
                            return f
                        for d in range(2):
                            mchain(d, chunk_order(d, runs)[0], m_from_av(d))
                        fw.barrier()
                    hc, B_hc, tok, B_tok, mcol, B_mcol, gm_ring, ti_ring, hn_ring, s3_ring = alloc_scan_bufs()
                    scan_outputs(runs, lambda *a: None, lambda *a: None)

                kstop("c3")
                ss, B_ss = P.sb(sc, "ss", [128, 20], F32)
                junk, B_junk = P.sb(sc, "junk", [128, 128], F32)
                yct_ring = P.ring(sc, "yct", 3, [128, 128], BF16)
                ytmp_ring = P.ring(sc, "ytmp", 2, [128, 128], F32)
                G(lambda: nc.gpsimd.memset(ss[:], 0.0), [], [B_ss])
                for tt in range(4):
                    for h in range(5):
                        A(lambda: nc.scalar.activation(out=junk[:], in_=hc[:, tt, h * 128:(h + 1) * 128], func=AF.Square,
                                                       accum_out=ss[:, tt * 5 + h:tt * 5 + h + 1]), [B_hc], [B_junk, B_ss])
                A(lambda: nc.scalar.activation(out=ss[:], in_=ss[:], func=AF.Ln, scale=1.0 / 128, bias=RMS_EPS), [B_ss], [B_ss])
                A(lambda: nc.scalar.activation(out=ss[:], in_=ss[:], func=AF.Exp, scale=-0.5), [B_ss], [B_ss])
                kstop("ca")
                for h in range(5):
                    for tt in range(4):
                        yt, ytb = ytmp_ring()
                        A(lambda: nc.scalar.activation(out=yt[:], in_=hc[:, tt, h * 128:(h + 1) * 128], func=AF.Identity, scale=ss[:, tt * 5 + h:tt * 5 + h + 1]),
                          [B_hc, B_ss], [ytb])
                        V(lambda: nc.vector.tensor_tensor(out=yt[:], in0=yt[:], in1=cnorm[:, l, :], op=ALU.mult), [ytb, B_c], [ytb])
                        yc, ycb = yct_ring()
                        V(lambda: nc.vector.tensor_tensor(out=yc[:], in0=yt[:], in1=sigoc[:, tt, h * 128:(h + 1) * 128], op=ALU.mult), [ytb, B_so], [ycb])
                        if KSTOP == "cb":
                            continue
                        ps, pb = ps_short()
                        mm(ps[:, 0:128], yc[:], id_bf[:], True, True, [ycb, B_c], [pb])
                        if KSTOP == "cc":
                            continue
                        A(lambda: nc.scalar.copy(out=ycat[:, 11 + h, tt * 128:(tt + 1) * 128], in_=ps[:, 0:128]), [pb], [B_y])
                fw.barrier()

            kstop("c4")
            kstop("cb")
            kstop("cc")
            with ExitStack() as sa:
                qta, B_qta = P.sb(sa, "qta", [128, 6, T], BF16)
                q1p, B_q1p = P.sb(sa, "q1p", [128, 5, T], BF16)
                q2p, B_q2p = P.sb(sa, "q2p", [128, 5, T], BF16)
                G(lambda: nc.gpsimd.memset(q1p[:], 0.0), [], [B_q1p])
                G(lambda: nc.gpsimd.memset(q2p[:], 0.0), [], [B_q2p])
                if g == "S":
                    q1r, B_q1r = P.sb(sa, "q1r", [128, 5, T], BF16)
                    q2r, B_q2r = P.sb(sa, "q2r", [128, 5, T], BF16)
                    G(lambda: nc.gpsimd.memset(q1r[:], 0.0), [], [B_q1r])
                    G(lambda: nc.gpsimd.memset(q2r[:], 0.0), [], [B_q2r])
                    sip = ExitStack()
                    kst_ring = P.ring(sip, "kst", 3, [128, T], BF16)
                    vst, B_vst = P.sb(sip, "vst", [128, 4, 1408], BF16)
                    rp_ring = P.ring(sip, "rpx", 2, [128, T], F32)
                    rp2_ring = P.ring(sip, "rpy", 2, [128, T], F32)
                else:
                    kta, B_kta = P.sb(sa, "kta", [128, 6, T], BF16)
                    ktb, B_ktb = P.sb(sa, "ktb", [128, 5, T], BF16)
                    vab, B_vab = P.sb(sa, "vab", [128, 4, 1408], BF16)
                    stg_ring = P.ring(sa, "stg", 3, [128, T], F32)

                def rope_apply(ps, pb, outs):
                    xf, xb = rp_ring()
                    A(lambda: nc.scalar.copy(out=xf[:], in_=ps[:]), [pb], [xb])
                    p2, pb2 = ps_short()
                    mm(p2[:], CM(C_PERM), xf[:], True, True, [B_c, xb], [pb2])
                    x2, x2b = rp2_ring()
                    V(lambda: nc.vector.tensor_tensor(out=x2[:], in0=p2[:], in1=rope[:, 1, :], op=ALU.mult), [pb2, B_c], [x2b])
                    V(lambda: nc.vector.tensor_tensor(out=xf[:], in0=xf[:], in1=rope[:, 0, :], op=ALU.mult), [xb, B_c], [xb])
                    for (rs, dst, db_) in outs:
                        V(lambda: nc.vector.tensor_tensor(out=dst[rs, :], in0=xf[rs, :], in1=x2[rs, :], op=ALU.add), [xb, x2b], [db_])

                abtiles = [(i * 512, 512) for i in range(8)] + [(4096, 128)]
                if "t" in KSKIP:
                    abtiles = abtiles[:int(KSKIP[KSKIP.index("t") + 1])]
                for (c0, ncol) in abtiles:
                    wv, wb = wload(w_in[l, :, c0:c0 + ncol], 16, ncol)
                    for q in range(ncol // 128):
                        k = (c0 + q * 128) // 128
                        fm = (k <= 11) or (18 <= k <= 27)
                        if not fm:
                            continue
                        ps, pb = ps_short()
                        for c in range(16):
                            mm(ps[:], wv[:, c, q * 128:(q + 1) * 128], h_sb[:, c, :], c == 0, c == 15, [wb, B_h], [pb])
                        if k <= 5:
                            A(lambda: nc.scalar.copy(out=qta[:, k, :], in_=ps[:]), [pb], [B_qta])
                        elif k <= 11:
                            hh = k - 6
                            if g == "P":
                                A(lambda: nc.scalar.copy(out=kta[:, hh, :], in_=ps[:]), [pb], [B_kta])
                                sg, sgb = stg_ring()
                                A(lambda: nc.scalar.copy(out=sg[:], in_=ps[:]), [pb], [sgb])
                                if "k" not in KSKIP:
                                    fw.dma("sp", o_ak[:, l, hh].rearrange("s d t -> d s t"), sg[:].rearrange("p (s t) -> p s t", s=2), reads=[sgb], writes=[B_out])
                            else:
                                ks, ksb = kst_ring()
                                A(lambda: nc.scalar.copy(out=ks[:], in_=ps[:]), [pb], [ksb])
                                kr, krb = bnc_k_rows(hh)
                                fw.dma("sp", kr, ks[:], reads=[ksb], writes=[krb])
                        elif k <= 22:
                            hh = k - 18
                            A(lambda: nc.scalar.copy(out=q1p[0:64, hh, :], in_=ps[0:64, :]), [pb], [B_q1p])
                            A(lambda: nc.scalar.copy(out=q2p[64:128, hh, :], in_=ps[64:128, :]), [pb], [B_q2p])
                            if g == "S":
                                rope_apply(ps, pb, [(slice(0, 64), q1r[:, hh, :], B_q1r), (slice(64, 128), q2r[:, hh, :], B_q2r)])
                        else:
                            hh = k - 23
                            if g == "P":
                                A(lambda: nc.scalar.copy(out=ktb[:, hh, :], in_=ps[:]), [pb], [B_ktb])
                                sg, sgb = stg_ring()
                                A(lambda: nc.scalar.copy(out=sg[:], in_=ps[:]), [pb], [sgb])
                                if "k" not in KSKIP:
                                    fw.dma("sp", o_bk[:, l, hh].rearrange("s d t -> d s t"), sg[:].rearrange("p (s t) -> p s t", s=2), reads=[sgb], writes=[B_out])
                            else:
                                ks, ksb = kst_ring()
                                rope_apply(ps, pb, [(slice(0, 128), ks, ksb)])
                                kr, krb = bnc_k_rows(6 + hh)
                                fw.dma("sp", kr, ks[:], reads=[ksb], writes=[krb])
                    for (name, lo, hi, o_base) in (("va", 1536, 2304, 0), ("vb", 3584, 4224, 768)):
                        a, b_ = max(lo, c0), min(hi, c0 + ncol)
                        if a >= b_:
                            continue
                        n = b_ - a
                        o0 = o_base + a - lo
                        for tt in range(4):
                            ps, pb = ps_short()
                            for c in range(16):
                                mm(ps[:, 0:n], h_sb[:, c, tt * 128:(tt + 1) * 128], wv[:, c, a - c0:b_ - c0], c == 0, c == 15, [wb, B_h], [pb])
                            if g == "P":
                                A(lambda: nc.scalar.copy(out=vab[:, tt, o0:o0 + n], in_=ps[:, 0:n]), [pb], [B_vab])
                                sg, sgb = stg_ring()
                                A(lambda: nc.scalar.copy(out=sg[:, 0:n], in_=ps[:, 0:n]), [pb], [sgb])
                                seq, s0_ = tt // 2, (tt % 2) * 128
                                h0 = (a - lo) // 128
                                nh = n // 128
                                dst = (o_av if name == "va" else o_bv)[seq, l, h0:h0 + nh, s0_:s0_ + 128, :].rearrange("h s d -> s h d")
                                if "v" not in KSKIP:
                                    fw.dma("sp", dst, sg[:, 0:n].rearrange("p (h d) -> p h d", d=128), reads=[sgb], writes=[B_out])
                            else:
                                A(lambda: nc.scalar.copy(out=vst[:, tt, o0:o0 + n], in_=ps[:, 0:n]), [pb], [B_vst])

                kstop("c5")

                def alloc_attn_rings():
                    return (P.ring(sa, "et", 3, [128, T], BF16), P.ring(sa, "bt", 2, [128, T], F32),
                            P.ring(sa, "rd", 2, [128, T], F32), P.ring(sa, "ybt", 3, [128, T], F32))

                def fin_A(h):
                    def f(yps, yb, dps, db):
                        rd, rdb = rd_ring()
                        A(lambda: nc.scalar.activation(out=rd[:], in_=dps[:], func=AF.Ln), [db], [rdb])
                        A(lambda: nc.scalar.activation(out=rd[:], in_=rd[:], func=AF.Exp, scale=-1.0), [rdb], [rdb])
                        V(lambda: nc.vector.tensor_tensor(out=ycat[:, h, :], in0=yps[:], in1=rd[:], op=ALU.mult), [yb, rdb], [B_y])
                    return f

                def fin_B(dst, dstb):
                    def f(yps, yb, dps, db):
                        rd, rdb = rd_ring()
                        A(lambda: nc.scalar.activation(out=rd[:], in_=dps[:], func=AF.Ln), [db], [rdb])
                        A(lambda: nc.scalar.activation(out=rd[:], in_=rd[:], func=AF.Exp, scale=-1.0), [rdb], [rdb])
                        V(lambda: nc.vector.tensor_tensor(out=dst[:], in0=yps[:], in1=rd[:], op=ALU.mult), [yb, rdb], [dstb])
                    return f

                def diff_finish(h, y1, y1b, y2, y2b):
                    V(lambda: nc.vector.scalar_tensor_tensor(out=y1[:], in0=y2[:], scalar=nlam[:, l:l + 1], in1=y1[:], op0=ALU.mult, op1=ALU.add),
                      [y2b, y1b, B_nlam], [y1b])
                    A(lambda: nc.scalar.activation(out=y2[:], in_=y1[:], func=AF.Square), [y1b], [y2b])
                    sp_, spb = ps_short()
                    mm(sp_[:], CM(C_ONE), y2[:], True, True, [B_c, y2b], [spb])
                    A(lambda: nc.scalar.activation(out=y2[:], in_=sp_[:], func=AF.Ln, scale=1.0 / 128, bias=RMS_EPS), [spb], [y2b])
                    A(lambda: nc.scalar.activation(out=y2[:], in_=y2[:], func=AF.Exp, scale=-0.5), [y2b], [y2b])
                    V(lambda: nc.vector.tensor_tensor(out=y1[:], in0=y1[:], in1=y2[:], op=ALU.mult), [y1b, y2b], [y1b])
                    A(lambda: nc.scalar.activation(out=ycat[:, 6 + h, :], in_=y1[:], func=AF.Identity, scale=sublns[:, l:l + 1]), [y1b, B_nlam], [B_y])

                if g == "P":
                    et_ring, bt_ring, rd_ring, yb_ring = alloc_attn_rings()
                    for h in range(6):
                        groups = []
                        for kb in range(2):
                            grp = []
                            for s in range(2):
                                t0 = s * 256 + kb * 128
                                grp.append(dict(k=kta[:, h, t0:t0 + 128], kb=B_kta, q=qta[:, h, s * 256:(s + 1) * 256], qb=B_qta,
                                                v=vab[:, 2 * s + kb, h * 128:(h + 1) * 128], vb=B_vab, c0=s * 256, n=256))
                            groups.append(grp)
                        attention(groups, 128.0 ** -0.5, et_ring, bt_ring, fin_A(h))
                    for h in range(5):
                        ys = []
                        for (qp, qpb) in ((q1p, B_q1p), (q2p, B_q2p)):
                            groups = []
                            for kb in range(2):
                                grp = []
                                for s in range(2):
                                    t0 = s * 256 + kb * 128
                                    grp.append(dict(k=ktb[:, h, t0:t0 + 128], kb=B_ktb, q=qp[:, h, s * 256:(s + 1) * 256], qb=qpb,
                                                    v=vab[:, 2 * s + kb, 768 + h * 128:768 + (h + 1) * 128], vb=B_vab, c0=s * 256, n=256))
                                groups.append(grp)
                            yt, ytb = yb_ring()
                            attention(groups, 64.0 ** -0.5, et_ring, bt_ring, fin_B(yt, ytb))
                            ys.append((yt, ytb))
                        diff_finish(h, ys[0][0], ys[0][1], ys[1][0], ys[1][1])
                else:
                    for hh in range(11):
                        vr, vrb = bnc_v_rows(hh)
                        fw.dma("sp", vr.rearrange("p (tt d) -> p tt d", d=128),
                               vst[:, :, hh * 128:(hh + 1) * 128], reads=[B_vst], writes=[vrb])
                    for ci in range(3):
                        fw.allgather(bnc_in[ci], bnc_out[ci], reads=[B_bi[ci]], writes=[B_bo[ci]])
                    fw.barrier()
                    sip.close()
                    et_ring, bt_ring, rd_ring, yb_ring = alloc_attn_rings()
                    kall_ring = P.ring(sa, "kall", 2, [128, 4, T], BF16)
                    vall_ring = P.ring(sa, "vall", 2, [128, 4, T], BF16)
                    kctx_ring = P.ring(sa, "kctx", 2, [128, 512], BF16)
                    vctx_ring = P.ring(sa, "vctx", 2, [128, 4, 128], BF16)
                    nb_ring = P.ring(sa, "nbias", 3, [128, T], F32)
                    gviews = [bo.rearrange("(r x) t -> x r t", r=4) for bo in bnc_out]
                    for hh in range(11):
                        isA = hh < 6
                        h = hh if isA else hh - 6
                        ka, kab = kall_ring()
                        va_, vab_ = vall_ring()
                        kc_, kcb_ = kctx_ring()
                        vc_, vcb_ = vctx_ring()
                        gch = HH_CH[hh]
                        krow = HH_IX[hh] * 128
                        vrow = (CH_NH[gch] + HH_IX[hh]) * 128
                        fw.dma("sp", ka[:], gviews[gch][krow:krow + 128], reads=[B_bo[gch]], writes=[kab])
                        fw.dma("sp", va_[:], gviews[gch][vrow:vrow + 128], reads=[B_bo[gch]], writes=[vab_])
                        if isA:
                            fw.dma("pool", kc_[:], cakT[l, h], writes=[kcb_])
                            fw.dma("pool", vc_[:], cav[l, h].rearrange("(b p) d -> p b d", p=128), writes=[vcb_])
                        else:
                            fw.dma("pool", kc_[:], cbkT[l, h], writes=[kcb_])
                            fw.dma("pool", vc_[:], cbv[l, h].rearrange("(b p) d -> p b d", p=128), writes=[vcb_])

                        def mkgroups(q_lat, q_latb, q_ctx, q_ctxb):
                            groups = []
                            for kb in range(16):
                                r, t4 = kb // 4, kb % 4
                                sub = dict(k=ka[:, r, t4 * 128:(t4 + 1) * 128], kb=kab, q=q_lat, qb=q_latb,
                                           v=va_[:, r, t4 * 128:(t4 + 1) * 128], vb=vab_, c0=0, n=T)
                                if isA:
                                    nbt, nbb = nb_ring()
                                    fw.dma("sp", nbt[:], nbias_d[l, h, kb], writes=[nbb])
                                    sub["bias"] = nbt[:]
                                    sub["bb"] = nbb
                                groups.append([sub])
                            for kb in range(4):
                                groups.append([dict(k=kc_[:, kb * 128:(kb + 1) * 128], kb=kcb_, q=q_ctx, qb=q_ctxb,
                                                    v=vc_[:, kb, :], vb=vcb_, c0=0, n=T)])
                            return groups
                        if isA:
                            attention(mkgroups(qta[:, h, :], B_qta, qta[:, h, :], B_qta), 128.0 ** -0.5, et_ring, bt_ring, fin_A(h))
                        else:
                            ys = []
                            for (qr, qrb, qp, qpb) in ((q1r, B_q1r, q1p, B_q1p), (q2r, B_q2r, q2p, B_q2p)):
                                yt, ytb = yb_ring()
                                attention(mkgroups(qr[:, h, :], qrb, qp[:, h, :], qpb), 64.0 ** -0.5, et_ring, bt_ring, fin_B(yt, ytb))
                                ys.append((yt, ytb))
                            diff_finish(h, ys[0][0], ys[0][1], ys[1][0], ys[1][1])
                fw.barrier()

            kstop("c6")
            with ExitStack() as so:
                residual_proj(l, w_out[l], 16, 512, lambda c: ycat[:, c, :], [B_y], 2, row, so, "o")
                layernorm(l, 0, so)
                fw.barrier()
        kstop("c7")
        modulate(l, 3, 4, row)
        with ExitStack() as sf:
            actb, B_act = P.sb(sf, "actb", [128, 43, T], BF16)
            u_ring = P.ring(sf, "u1", 4, [128, T], F32)
            hal, B_hal = P.sb(sf, "hal", [128, 2, 2], F32)
            if g == "S":
                hbs, B_hbs = P.sb(sf, "hbs", [128, 16, 2], F32)
                hba, B_hba = P.sb(sf, "hba", [128, 4, 32], F32)
                hbf, B_hbf = P.sb(sf, "hbf", [128, 16, 2], F32)
                A(lambda: nc.scalar.copy(out=hbs[:, :, 0], in_=h_sb[:, :, 0]), [B_h], [B_hbs])
                A(lambda: nc.scalar.copy(out=hbs[:, :, 1], in_=h_sb[:, :, T - 1]), [B_h], [B_hbs])
                fw.dma("sp", hb_in, hbs[:].rearrange("p c k -> p (c k)"), reads=[B_hbs], writes=[B_hbi])
                fw.allgather(hb_in, hb_out, reads=[B_hbi], writes=[B_hbo])
                fw.dma("sp", hba[:], hb_out.rearrange("(r p) f -> p r f", p=128), reads=[B_hbo], writes=[B_hba])
                hv = hba[:].rearrange("p r (c k) -> p r c k", k=2)
                for (side, kk, so_) in ((0, 1, 0), (1, 0, 4)):
                    A(lambda: nc.scalar.activation(out=hbf[:, :, side], in_=hv[:, 0, :, kk], func=AF.Identity, scale=sel[:, so_:so_ + 1]), [B_hba, B_c], [B_hbf])
                    for r in range(1, 4):
                        V(lambda: nc.vector.scalar_tensor_tensor(out=hbf[:, :, side], in0=hv[:, r, :, kk], scalar=sel[:, so_ + r:so_ + r + 1],
                                                                 in1=hbf[:, :, side], op0=ALU.mult, op1=ALU.add), [B_hba, B_c, B_hbf], [B_hbf])
                A(lambda: nc.scalar.copy(out=hh_sb[:], in_=hbf[:]), [B_hbf], [B_hh])
            segs = [(0, 256), (256, 256)] if g == "P" else [(0, 512)]

            def conv_chunk(ps, pb, hps, hpb, ch):
                u, ub = u_ring()
                cp = convp[:, l, ch, :]
                A(lambda: nc.scalar.activation(out=u[:], in_=ps[:], func=AF.Identity, scale=cp[:, 1:2], bias=cp[:, 3:4]), [pb, B_c], [ub])
                for (s0_, n) in segs:
                    V(lambda: nc.vector.scalar_tensor_tensor(out=u[:, s0_ + 1:s0_ + n], in0=ps[:, s0_:s0_ + n - 1], scalar=cp[:, 0:1],
                                                             in1=u[:, s0_ + 1:s0_ + n], op0=ALU.mult, op1=ALU.add), [pb, B_c, ub], [ub])
                    V(lambda: nc.vector.scalar_tensor_tensor(out=u[:, s0_:s0_ + n - 1], in0=ps[:, s0_ + 1:s0_ + n], scalar=cp[:, 2:3],
                                                             in1=u[:, s0_:s0_ + n - 1], op0=ALU.mult, op1=ALU.add), [pb, B_c, ub], [ub])
                if g == "S":
                    V(lambda: nc.vector.scalar_tensor_tensor(out=u[:, 0:1], in0=hps[:, 0:1], scalar=cp[:, 0:1], in1=u[:, 0:1],
                                                             op0=ALU.mult, op1=ALU.add), [hpb, B_c, ub], [ub])
                    V(lambda: nc.vector.scalar_tensor_tensor(out=u[:, T - 1:T], in0=hps[:, 1:2], scalar=cp[:, 2:3], in1=u[:, T - 1:T],
                                                             op0=ALU.mult, op1=ALU.add), [hpb, B_c, ub], [ub])
                return u, ub

            for ti in range(11):
                ncol = 512 if ti < 10 else 384
                wa, wab = wload(w_up[l, :, ti * 512:ti * 512 + ncol], 16, ncol)
                wg, wgb = wload(w_up[l, :, DFF + ti * 512:DFF + ti * 512 + ncol], 16, ncol)
                for q in range(ncol // 128):
                    j = ti * 4 + q
                    res = []
                    for (wv, wb, ch) in ((wa, wab, j), (wg, wgb, 43 + j)):
                        ps, pb = ps_short()
                        for c in range(16):
                            mm(ps[:], wv[:, c, q * 128:(q + 1) * 128], h_sb[:, c, :], c == 0, c == 15, [wb, B_h], [pb])
                        hps, hpb = None, None
                        if g == "S":
                            hps, hpb = ps_short()
                            for c in range(16):
                                mm(hps[:, 0:2], wv[:, c, q * 128:(q + 1) * 128], hh_sb[:, c, :], c == 0, c == 15, [wb, B_hh], [hpb])
                        res.append(conv_chunk(ps, pb, hps, hpb, ch))
                    (ua, uab), (ug, ugb) = res
                    A(lambda: nc.scalar.activation(out=ug[:], in_=ug[:], func=AF.Silu), [ugb], [ugb])
                    V(lambda: nc.vector.tensor_tensor(out=actb[:, j, :], in0=ug[:], in1=ua[:], op=ALU.mult), [ugb, uab], [B_act])
            residual_proj(l, w_down[l], 43, 128, lambda c: actb[:, c, :], [B_act], 5, row, sf, "d")
            layernorm(l, 1, sf)
            fw.barrier()
        if l == L - 1:
            fw.dma("sp", yout[g].rearrange("(c p) t -> p c t", p=128), x_sb[:], reads=[B_x], writes=[B_out])
        else:
            fw.dma("sp", xspill[g].rearrange("(c p) t -> p c t", p=128), x_sb[:], reads=[B_x], writes=[B_spill[g]])

    nblk = 0
    try:
        for l in range(L):
            for g in ("P", "S"):
                if KSTOP.startswith("b") and nblk >= int(KSTOP[1]):
                    break
                if KSTOP.startswith("c") and nblk >= 1:
                    break
                block(l, g)
                nblk += 1
    except StopBuild:
        fw.barrier()
        fw.finish()
        return P
    fw.finish()
    P.es.close()
    fw.close()
    return P


def _consts():
    c = np.zeros((NCONST, 128, 128), np.float32)
    idx = np.arange(128)
    c[C_ID] = np.eye(128)
    c[C_ONE] = 1.0
    same = (idx[:, None] // 64) == (idx[None, :] // 64)
    c[C_TRIF] = (same & (idx[:, None] <= idx[None, :]))
    c[C_TRIB] = (same & (idx[:, None] >= idx[None, :]))
    for k, p in ((C_E0, 0), (C_E63, 63), (C_E64, 64), (C_E127, 127)):
        c[k][p, :] = 1.0
    dd = idx % 32
    partner = np.where(dd < 16, idx + 16, idx - 16)
    c[C_PERM][partner, idx] = 1.0
    negm = np.zeros((2, 128, 5, 128), np.float32)
    negm[0] = np.where(c[C_TRIF].T[:, None, :] > 0, 0.0, NEG)
    negm[1] = np.where(c[C_TRIB].T[:, None, :] > 0, 0.0, NEG)
    return (np.ascontiguousarray(c.transpose(1, 0, 2)).reshape(128, NCONST * 128),
            np.ascontiguousarray(negm.transpose(1, 0, 2, 3)).reshape(128, 2 * 640))


def _rope_tables(j):
    t = np.arange(T) + j * T
    rows = (t // 64).astype(np.float32)
    cols = (t % 64).astype(np.float32)
    freqs = (np.float32(10000.0) ** (-np.arange(0, 32, 2, dtype=np.float32) / np.float32(32))).astype(np.float32)
    p = np.arange(128)
    dd = p % 64
    idx = dd % 32
    f = idx % 16
    first = idx < 16
    pos = np.where((dd < 32)[:, None], rows[None, :], cols[None, :]).astype(np.float32)
    ang = (pos * freqs[f][:, None]).astype(np.float32)
    cos = np.cos(ang).astype(np.float32)
    sin = np.sin(ang).astype(np.float32)
    sins = np.where(first[:, None], -sin, sin).astype(np.float32)
    return np.concatenate([cos, sins], axis=1)


def _natten_bias(a_rpb, j):
    kt = np.arange(2048)
    krow, kcol = kt // 64, kt % 64
    qt = np.arange(T) + j * T
    qrow, qcol = qt // 64, qt % 64
    rs = np.clip(qrow - 4, 0, 24)
    cs = np.clip(qcol - 8, 0, 48)
    vr = (krow[:, None] >= rs[None, :]) & (krow[:, None] < rs[None, :] + 8)
    vcm = (kcol[:, None] >= cs[None, :]) & (kcol[:, None] < cs[None, :] + 16)
    valid = vr & vcm
    ri = np.clip(7 + krow[:, None] - qrow[None, :], 0, 14)
    ci = np.clip(kcol[:, None] - qcol[None, :] + 15, 0, 30)
    out = np.empty((L, 6, 2048, T), np.float32)
    for l in range(L):
        for h in range(6):
            out[l, h] = np.where(valid, a_rpb[l, h][ri, ci], np.float32(NEG))
    return out.reshape(L, 6, 16, 128, T)


def _combine_tables(j):
    cf = np.zeros((2, 5, 4), np.float32)
    vt = np.zeros((2, 5), np.float32)
    for r2 in range(4):
        if r2 < j:
            cf[0, 0, r2] = 1
        if r2 > j:
            cf[1, 0, r2] = 1
    for r in range(4):
        vt[0, 1 + r] = 0.0 if r < j else NEG
        vt[1, 1 + r] = 0.0 if r > j else NEG
        for r2 in range(4):
            if r < r2 < j:
                cf[0, 1 + r, r2] = 1
            if j < r2 < r:
                cf[1, 1 + r, r2] = 1
    cft = np.broadcast_to(cf[None, :, :, :, None], (128, 2, 5, 4, 5)).reshape(128, -1)
    vtt = np.broadcast_to(vt[None, :, :, None], (128, 2, 5, 5)).reshape(128, -1)
    sel = np.zeros((8,), np.float32)
    if j > 0:
        sel[j - 1] = 1
    if j < 3:
        sel[4 + j + 1] = 1
    return np.ascontiguousarray(cft), np.ascontiguousarray(vtt), np.ascontiguousarray(np.broadcast_to(sel[None], (128, 8)))


_PROG = None


def kernel(x_prompt, x_sample, cache_a_k, cache_a_v, cache_b_k, cache_b_v, state_c_C, state_c_n, state_c_m,
           c, c_ctx, w_mod, b_mod, w_in, c_gate_b, a_rpb, b_lambda, b_subln, c_norm, w_out,
           ln1_g, ln1_b, ln2_g, ln2_b, w_up, conv_w, conv_b, w_down):
    global _PROG
    f = lambda a: np.ascontiguousarray(np.asarray(a, dtype=np.float32))
    x_prompt, x_sample = f(x_prompt), f(x_sample)
    if _PROG is None:
        _PROG = build_program()
    P = _PROG
    consts, negm = _consts()
    lnp = np.stack([f(ln1_g), f(ln1_b), f(ln2_g), f(ln2_b)], 1).reshape(L, 4, 16, 128).transpose(3, 0, 1, 2).reshape(128, -1)
    cw = np.concatenate([f(conv_w), f(conv_b)[:, None, :]], 1)
    convp = cw.reshape(L, 4, 86, 128).transpose(3, 0, 2, 1).reshape(128, -1)
    shared = {
        "w_in": f(w_in), "w_out": f(w_out), "w_up": f(w_up), "w_down": f(w_down),
        "lnp": np.ascontiguousarray(lnp), "convp": np.ascontiguousarray(convp),
        "gateb": f(c_gate_b).reshape(-1), "blam": f(b_lambda).reshape(-1),
        "subln": np.ascontiguousarray(f(b_subln).T), "cnorm": f(c_norm).reshape(-1),
        "consts": consts, "negm": negm,
    }
    w_mod, b_mod = f(w_mod), f(b_mod)
    in_maps = []
    for i in range(8):
        b, j = i // 4, i % 4
        m = dict(shared)
        m["xpT"] = np.ascontiguousarray(x_prompt[2 * i:2 * i + 2].reshape(T, D).T)
        m["xsT"] = np.ascontiguousarray(x_sample[b, j * T:(j + 1) * T].T)
        m["wmod"] = np.ascontiguousarray(w_mod[:, :, j * 3072:(j + 1) * 3072])
        m["bmod"] = np.ascontiguousarray(b_mod[:, j * 3072:(j + 1) * 3072].reshape(L, 24, 128).transpose(2, 0, 1).reshape(128, -1))
        cond = np.stack([f(c_ctx), f(c)[b]], 1)
        m["cond2"] = np.ascontiguousarray(cond.reshape(16, 128, 2).transpose(1, 0, 2).reshape(128, 32))
        m["rope"] = _rope_tables(j)
        m["nbias"] = _natten_bias(f(a_rpb), j)
        m["cakT"] = np.ascontiguousarray(f(cache_a_k)[b].transpose(0, 1, 3, 2))
        m["cav"] = f(cache_a_v)[b]
        m["cbkT"] = np.ascontiguousarray(f(cache_b_k)[b].transpose(0, 1, 3, 2))
        m["cbv"] = f(cache_b_v)[b]
        m["cC"] = f(state_c_C)[b]
        m["cn"] = np.ascontiguousarray(f(state_c_n)[b].reshape(L, 10, 128).transpose(0, 2, 1))
        m["cm"] = f(state_c_m)[b].reshape(-1)
        m["cftab"], m["vtab"], m["sel"] = _combine_tables(j)
        in_maps.append(m)
    in_maps = [{k: v for k, v in m.items() if k in P.din} for m in in_maps]
    res = run_bass_kernel_spmd(P.nc, in_maps, core_ids=list(range(8))).results
    yp = np.stack([r["ypT"].T.reshape(2, 256, D) for r in res], 0).reshape(16, 256, D)
    ys = np.stack([r["ysT"].T for r in res], 0).reshape(2, 4 * T, D)
    cat = lambda k: np.concatenate([r[k] for r in res], 0)
    n_ak = np.ascontiguousarray(cat("o_ak").transpose(0, 1, 2, 4, 3))
    n_av = cat("o_av")
    n_bk = np.ascontiguousarray(cat("o_bk").transpose(0, 1, 2, 4, 3))
    n_bv = cat("o_bv")
    return (np.ascontiguousarray(yp, dtype=np.float32), np.ascontiguousarray(ys, dtype=np.float32), n_ak, n_av, n_bk, n_bv,
            cat("o_cC"), np.ascontiguousarray(cat("o_cn").transpose(0, 1, 2, 4, 3)), cat("o_cm"))
```

```python
import math
import os
KSTOP = os.environ.get('KSTOP', '')
KSKIP = os.environ.get('KSKIP', '')
SKIP_IN = set()
if KSTOP.startswith('c'):
    SKIP_IN = {"nbias", "cakT", "cav", "cbkT", "cbv", "cC", "cn", "cm", "xsT"}
    if not KSTOP[1].isdigit() or int(KSTOP[1]) < 8:
        SKIP_IN |= {"w_up", "w_down"}
    if not KSTOP[1].isdigit() or int(KSTOP[1]) < 7:
        SKIP_IN |= {"w_out"}


class StopBuild(Exception):
    pass


def kstop(tag):
    if KSTOP == tag:
        raise StopBuild(tag)
from contextlib import ExitStack
import numpy as np
import ml_dtypes
import concourse.bass as bass
import concourse.mybir as mybir
from concourse.bass_utils import run_bass_kernel_spmd

F32 = mybir.dt.float32
BF16 = mybir.dt.bfloat16
AF = mybir.ActivationFunctionType
ALU = mybir.AluOpType
AX = mybir.AxisListType

L = 2
D = 2048
T = 512
DFF = 5504
NEG = -30000.0
ALPHA = (2 * L) ** 0.25
LN_EPS = 1e-5
RMS_EPS = 1e-6
RG = [[0, 1, 2, 3], [4, 5, 6, 7]]
NCONST = 11
C_ID, C_ONE, C_TRIF, C_TRIB, C_E0, C_E63, C_E64, C_E127, C_PERM, C_HA, C_HB = range(NCONST)


class Buf:
    __slots__ = ("name", "w", "r")

    def __init__(self, name=""):
        self.name = name
        self.w = None
        self.r = []


class FW:
    NDMA = 48

    def __init__(self, nc):
        self.nc = nc
        self.eng = {"pe": nc.tensor, "act": nc.scalar, "dve": nc.vector, "pool": nc.gpsimd, "sp": nc.sync}
        self.sem, self.cnt, self._stack = {}, {}, []
        for e in self.eng:
            cm = nc.semaphore("s_" + e)
            self.sem[e] = cm.__enter__()
            self._stack.append(cm)
            self.cnt[e] = 0
        self.dsem, self.dcnt = [], []
        for i in range(self.NDMA):
            cm = nc.semaphore("d_%d" % i)
            self.dsem.append(cm.__enter__())
            self._stack.append(cm)
            self.dcnt.append(0)
        self.dnext = 0
        self.seen = {e: {} for e in self.eng}
        self.ninst = 0
        self.nwaits = 0

    def close(self):
        for cm in reversed(self._stack):
            cm.__exit__(None, None, None)

    def _semobj(self, key):
        return self.sem[key] if isinstance(key, str) else self.dsem[key]

    def _wait(self, e, tok):
        if tok is None:
            return
        key, val = tok
        if self.seen[e].get(key, 0) >= val:
            return
        self.eng[e].wait_ge(self._semobj(key), val)
        self.seen[e][key] = val
        self.nwaits += 1

    def _deps(self, e, reads, writes, is_dma=False):
        for b in reads:
            if b.w is not None:
                if b.w[0] == e and (e == "pe" or is_dma):
                    continue
                self._wait(e, b.w)
        skip_same = (e == "pe" or is_dma)
        for b in writes:
            if b.w is not None and not (skip_same and b.w[0] == e):
                self._wait(e, b.w)
            for t in b.r:
                if not (skip_same and t[0] == e):
                    self._wait(e, t)

    def _commit(self, tok, reads, writes):
        for b in reads:
            b.r.append(tok)
            if len(b.r) > 16:
                best = {}
                for k, v in b.r:
                    if best.get(k, 0) < v:
                        best[k] = v
                b.r = list(best.items())
        for b in writes:
            b.w = tok
            b.r = []

    def op(self, e, fn, reads=(), writes=(), inc=True):
        self._deps(e, reads, writes)
        ins = fn()
        self.ninst += 1
        if inc:
            self.cnt[e] += 1
            ins.then_inc(self.sem[e], 1)
            tok = (e, self.cnt[e])
        else:
            tok = (e, self.cnt[e] + 1)
        self._commit(tok, reads, writes)
        return tok

    def _next_dsem(self, q, kind=None):
        kind = kind or q
        lo, hi = {"sp": (0, 32), "pool": (32, 44), "cc": (44, 48)}[kind]
        if not hasattr(self, "dnx"):
            self.dnx = {}
        i = self.dnx.get(kind, lo)
        self.dnx[kind] = lo + (i + 1 - lo) % (hi - lo)
        if self.dcnt[i] > 0:
            self._wait(q, (i, self.dcnt[i]))
        return i

    def dma(self, q, out, in_, reads=(), writes=(), slow=False):
        self._deps(q, reads, writes, is_dma=True)
        i = self._next_dsem(q)
        if slow:
            ins = self.eng[q].dma_start(out=out, in_=in_, allow_slow_non_contiguous=True)
        else:
            ins = self.eng[q].dma_start(out=out, in_=in_)
        self.dcnt[i] += 16
        ins.then_inc(self.dsem[i], 16)
        self.ninst += 1
        tok = (i, self.dcnt[i])
        self._commit(tok, reads, writes)
        return tok

    def allgather(self, in_ap, out_ap, reads=(), writes=()):
        q = "pool"
        self._deps(q, reads, writes, is_dma=True)
        i = self._next_dsem(q, "cc")
        ins = self.nc.gpsimd.collective_compute("AllGather", ALU.bypass, replica_groups=RG, ins=[in_ap], outs=[out_ap])
        self.dcnt[i] += 1
        ins.then_inc(self.dsem[i], 1)
        self.ninst += 1
        tok = (i, self.dcnt[i])
        self._commit(tok, reads, writes)
        return tok

    def barrier(self):
        for e in self.eng:
            for f in self.eng:
                if f != e and self.cnt[f] > 0:
                    self._wait(e, (f, self.cnt[f]))
            for i in range(self.NDMA):
                if self.dcnt[i] > 0:
                    self._wait(e, (i, self.dcnt[i]))

    def finish(self):
        for i in range(self.NDMA):
            if self.dcnt[i] > 0:
                self._wait("sp", (i, self.dcnt[i]))


class Prog:
    def __init__(self):
        nc = bass.Bass("TRN2", target_bir_lowering=False)
        self.nc = nc
        self.fw = FW(nc)
        self.es = ExitStack()
        self.ring_idx = {}
        self.din = {}
        self.dout = {}

    def inp(self, name, shape, dt=F32):
        if name in SKIP_IN:
            return None
        t = self.nc.dram_tensor(name, list(shape), dt, kind="ExternalInput").ap()
        self.din[name] = t
        return t

    def outp(self, name, shape, dt=F32):
        t = self.nc.dram_tensor(name, list(shape), dt, kind="ExternalOutput").ap()
        self.dout[name] = t
        return t

    def scratch(self, name, shape, dt=F32):
        return self.nc.dram_tensor(name, list(shape), dt).ap()

    def sb(self, es, name, shape, dt=F32):
        self.uid = getattr(self, "uid", 0) + 1
        t = es.enter_context(self.nc.sbuf_tensor("%s_%d" % (name, self.uid), list(shape), dt))
        return t, Buf(name)

    def ring(self, es, name, n, shape, dt=F32):
        items = [self.sb(es, "%s%d" % (name, i), shape, dt) for i in range(n)]
        key = name
        self.ring_idx[key] = 0

        def nxt():
            i = self.ring_idx[key]
            self.ring_idx[key] = (i + 1) % n
            return items[i]
        return nxt

    def V(self, fn, reads=(), writes=()):
        return self.fw.op("dve", fn, reads, writes)

    def A(self, fn, reads=(), writes=()):
        return self.fw.op("act", fn, reads, writes)

    def G(self, fn, reads=(), writes=()):
        return self.fw.op("pool", fn, reads, writes)

    def PE(self, fn, reads=(), writes=(), inc=True):
        return self.fw.op("pe", fn, reads, writes, inc=inc)

    def mm(self, out, lhsT, rhs, start, stop, reads, writes, inc=None, sgc=False):
        nc = self.nc
        return self.fw.op("pe", lambda: nc.tensor.matmul(out, lhsT=lhsT, rhs=rhs, start=start, stop=stop, skip_group_check=sgc),
                          reads, writes, inc=(stop if inc is None else inc))


def build_program():
    P = Prog()
    nc, fw = P.nc, P.fw
    V, A, PE, mm, G = P.V, P.A, P.PE, P.mm, P.G

    xin = {"P": P.inp("xpT", [D, T]), "S": P.inp("xsT", [D, T])}
    w_in = P.inp("w_in", [L, D, 6804])
    w_out = P.inp("w_out", [L, D, D])
    w_up = P.inp("w_up", [L, D, 2 * DFF])
    w_down = P.inp("w_down", [L, DFF, D])
    wmod = P.inp("wmod", [L, D, 3072])
    bmod = P.inp("bmod", [128, L * 24])
    cond2 = P.inp("cond2", [128, 32])
    lnp_d = P.inp("lnp", [128, L * 4 * 16])
    convp_d = P.inp("convp", [128, L * 86 * 4])
    gateb_d = P.inp("gateb", [L * 20])
    blam_d = P.inp("blam", [L * 256])
    subln_d = P.inp("subln", [128, L])
    cnorm_d = P.inp("cnorm", [L * 128])
    consts_d = P.inp("consts", [128, NCONST * 128])
    negm_d = P.inp("negm", [128, 2 * 640])
    rope_d = P.inp("rope", [128, 2 * T])
    nbias_d = P.inp("nbias", [L, 6, 16, 128, T])
    cakT = P.inp("cakT", [L, 6, 128, 512])
    cav = P.inp("cav", [L, 6, 512, 128])
    cbkT = P.inp("cbkT", [L, 5, 128, 512])
    cbv = P.inp("cbv", [L, 5, 512, 128])
    cC_d = P.inp("cC", [L, 2, 5, 128, 128])
    cn_d = P.inp("cn", [L, 128, 10])
    cm_d = P.inp("cm", [L * 10])
    cftab_d = P.inp("cftab", [128, 2 * 5 * 4 * 5])
    vtab_d = P.inp("vtab", [128, 2 * 5 * 5])
    sel_d = P.inp("sel", [128, 8])

    yout = {"P": P.outp("ypT", [D, T]), "S": P.outp("ysT", [D, T])}
    o_ak = P.outp("o_ak", [2, L, 6, 128, 256])
    o_av = P.outp("o_av", [2, L, 6, 256, 128])
    o_bk = P.outp("o_bk", [2, L, 5, 128, 256])
    o_bv = P.outp("o_bv", [2, L, 5, 256, 128])
    o_cC = P.outp("o_cC", [2, L, 2, 5, 128, 128])
    o_cn = P.outp("o_cn", [2, L, 2, 128, 5])
    o_cm = P.outp("o_cm", [2, L, 2, 5])
    B_out = Buf("outputs")

    xspill = {"P": P.scratch("xspP", [D, T]), "S": P.scratch("xspS", [D, T])}
    B_spill = {"P": Buf(), "S": Buf()}
    mg_in = P.scratch("mg_in", [128, 96]); mg_out = P.scratch("mg_out", [512, 96])
    B_mgi, B_mgo = Buf(), Buf()
    CH_NH = [4, 4, 3]
    HH_CH = [0, 0, 0, 0, 1, 1, 1, 1, 2, 2, 2]
    HH_IX = [0, 1, 2, 3, 0, 1, 2, 3, 0, 1, 2]
    bnc_in = [P.scratch("bnc_in%d" % i, [2 * n * 128, T], BF16) for i, n in enumerate(CH_NH)]
    bnc_out = [P.scratch("bnc_out%d" % i, [4 * 2 * n * 128, T], BF16) for i, n in enumerate(CH_NH)]
    B_bi = [Buf() for _ in CH_NH]
    B_bo = [Buf() for _ in CH_NH]

    def bnc_k_rows(hh):
        i = HH_IX[hh]
        return bnc_in[HH_CH[hh]][i * 128:(i + 1) * 128, :], B_bi[HH_CH[hh]]

    def bnc_v_rows(hh):
        c = HH_CH[hh]
        i = CH_NH[c] + HH_IX[hh]
        return bnc_in[c][i * 128:(i + 1) * 128, :], B_bi[c]
    CSF = 1310
    cs_in = P.scratch("cs_in", [128, CSF]); cs_out = P.scratch("cs_out", [512, CSF])
    B_csi, B_cso = Buf(), Buf()
    hb_in = P.scratch("hb_in", [128, 32]); hb_out = P.scratch("hb_out", [512, 32])
    B_hbi, B_hbo = Buf(), Buf()

    es = P.es
    x_sb, B_x = P.sb(es, "x_sb", [128, 16, T], F32)
    h_sb, B_h = P.sb(es, "h_sb", [128, 16, T], BF16)
    hh_sb, B_hh = P.sb(es, "hh_sb", [128, 16, 2], BF16)
    WSL = 8704
    wslots = [P.sb(es, "wr%d" % i, [128, WSL], BF16) for i in range(2)]
    wstate = {"i": 0}

    def wring():
        i = wstate["i"] % len(wslots)
        wstate["i"] += 1
        return wslots[i]

    class extra_slots:
        def __init__(self, want, reserve=2048):
            self.want, self.reserve = want, reserve

        def __enter__(self):
            self.sx = ExitStack()
            self.n = 0
            while self.n < self.want and nc.sbuf_bytes_remaining >= WSL * 2 + self.reserve + 256:
                wslots.append(P.sb(self.sx, "wx", [128, WSL], BF16))
                self.n += 1
            return self

        def __exit__(self, *a):
            if a[0] is None:
                fw.barrier()
                for _ in range(self.n):
                    wslots.pop()
                self.sx.close()
            return False
    cst, B_c = P.sb(es, "cst", [128, NCONST, 128], F32)
    negm, _ = P.sb(es, "negm", [128, 2, 640], F32)
    rope, _ = P.sb(es, "rope", [128, 2, T], F32)
    ones_bf, _ = P.sb(es, "ones_bf", [128, 128], BF16)
    id_bf, _ = P.sb(es, "id_bf", [128, 128], BF16)
    tri_bf, _ = P.sb(es, "tri_bf", [128, 2, 128], BF16)
    lnp, _ = P.sb(es, "lnp", [128, L, 4, 16], F32)
    convp, _ = P.sb(es, "convp", [128, L, 86, 4], F32)
    gateb, _ = P.sb(es, "gateb", [128, L, 20], F32)
    blam, _ = P.sb(es, "blam", [128, L, 4, 64], F32)
    subln, _ = P.sb(es, "subln", [128, L], F32)
    cnorm, _ = P.sb(es, "cnorm", [128, L, 128], F32)
    cftab, _ = P.sb(es, "cftab", [128, 2, 5, 4, 5], F32)
    vtab, _ = P.sb(es, "vtab", [128, 2, 5, 5], F32)
    sel, _ = P.sb(es, "sel", [128, 8], F32)
    modv, B_modv = P.sb(es, "modv", [128, L, 96, 2], F32)
    nlam, B_nlam = P.sb(es, "nlam", [128, L], F32)
    sublns, _ = P.sb(es, "sublns", [128, L], F32)

    def CM(i):
        return cst[:, i, :]

    pbanks = [es.enter_context(nc.psum_tensor("ps%d" % i, [128, 512], F32)) for i in range(8)]
    pbufs = [Buf("ps%d" % i) for i in range(8)]
    pidx = {"s": 0, "l": 0}

    def ps_short():
        i = pidx["s"]
        pidx["s"] = (i + 1) % 5
        return pbanks[i], pbufs[i]

    def ps_long():
        i = pidx["l"]
        pidx["l"] = (i + 1) % 3
        return pbanks[5 + i], pbufs[5 + i]

    def bcast(ap1d, n):
        return bass.AP(ap1d.tensor, 0, [[0, 128], [1, n]])

    fw.dma("sp", cst[:], consts_d.rearrange("p (k n) -> p k n", k=NCONST), writes=[B_c])
    fw.dma("sp", negm[:], negm_d.rearrange("p (k n) -> p k n", k=2), writes=[B_c])
    fw.dma("sp", rope[:], rope_d.rearrange("p (k n) -> p k n", k=2), writes=[B_c])
    fw.dma("sp", lnp[:], lnp_d.rearrange("p (l k c) -> p l k c", l=L, k=4), writes=[B_c])
    fw.dma("sp", convp[:], convp_d.rearrange("p (l c k) -> p l c k", l=L, k=4), writes=[B_c])
    fw.dma("sp", gateb[:], bcast(gateb_d, L * 20).rearrange("p (l k) -> p l k", l=L), writes=[B_c])
    fw.dma("sp", blam[:], bcast(blam_d, L * 256).rearrange("p (l k c) -> p l k c", l=L, k=4), writes=[B_c])
    fw.dma("sp", subln[:], subln_d, writes=[B_c])
    fw.dma("sp", cnorm[:], bcast(cnorm_d, L * 128).rearrange("p (l k) -> p l k", l=L), writes=[B_c])
    fw.dma("sp", cftab[:], cftab_d.rearrange("p (d i r h) -> p d i r h", d=2, i=5, r=4), writes=[B_c])
    fw.dma("sp", vtab[:], vtab_d.rearrange("p (d i h) -> p d i h", d=2, i=5), writes=[B_c])
    fw.dma("sp", sel[:], sel_d, writes=[B_c])
    A(lambda: nc.scalar.copy(out=ones_bf[:], in_=CM(C_ONE)), [B_c], [B_c])
    A(lambda: nc.scalar.copy(out=id_bf[:], in_=CM(C_ID)), [B_c], [B_c])
    A(lambda: nc.scalar.copy(out=tri_bf[:, 0, :], in_=CM(C_TRIF)), [B_c], [B_c])
    A(lambda: nc.scalar.copy(out=tri_bf[:, 1, :], in_=CM(C_TRIB)), [B_c], [B_c])

    def wload(src2d, kc, ncols):
        t, b = wring()
        view = t[:, 0:kc * ncols].rearrange("p (c n) -> p c n", n=ncols)
        srcv = src2d.rearrange("(c p) n -> p c n", p=128)
        step = max(1, 2048 // 128 // 1 if ncols >= 256 else 8)
        step = 16 if ncols >= 256 else 22
        for c0 in range(0, kc, step):
            c1 = min(kc, c0 + step)
            fw.dma("pool", view[:, c0:c1, :], srcv[:, c0:c1, :], writes=[b])
        return view, b

    with ExitStack() as s0:
        c2, B_c2 = P.sb(s0, "c2", [128, 16, 2], F32)
        c2b, _ = P.sb(s0, "c2b", [128, 16, 2], BF16)
        bm, B_bm = P.sb(s0, "bm", [128, L, 24], F32)
        mloc, B_ml = P.sb(s0, "mloc", [128, L, 24, 2], F32)
        mall, B_ma = P.sb(s0, "mall", [128, 4, L, 24, 2], F32)
        fw.dma("sp", c2[:], cond2.rearrange("p (c r) -> p c r", r=2), writes=[B_c2])
        fw.dma("sp", bm[:], bmod.rearrange("p (l c) -> p l c", l=L), writes=[B_bm])
        A(lambda: nc.scalar.activation(out=c2b[:], in_=c2[:], func=AF.Silu), [B_c2], [B_c2])
        with extra_slots(3):
            for l in range(L):
                for t4 in range(6):
                    wv, wb = wload(wmod[l, :, t4 * 512:(t4 + 1) * 512], 16, 512)
                    for q in range(4):
                        cc = t4 * 4 + q
                        ps, pb = ps_short()
                        for c in range(16):
                            mm(ps[:, 0:2], wv[:, c, q * 128:(q + 1) * 128], c2b[:, c, :], c == 0, c == 15, [wb, B_c2], [pb])
                        A(lambda: nc.scalar.activation(out=mloc[:, l, cc, :], in_=ps[:, 0:2], func=AF.Identity,
                                                       bias=bm[:, l, cc:cc + 1], scale=1.0), [pb, B_bm], [B_ml])
        fw.dma("sp", mg_in, mloc[:].rearrange("p l c r -> p (l c r)"), reads=[B_ml], writes=[B_mgi])
        fw.allgather(mg_in, mg_out, reads=[B_mgi], writes=[B_mgo])
        fw.dma("sp", mall[:].rearrange("p r l c w -> p r (l c w)"), mg_out.rearrange("(r p) f -> p r f", p=128),
               reads=[B_mgo], writes=[B_ma])
        for r in range(4):
            for l in range(L):
                A(lambda: nc.scalar.copy(out=modv[:, l, r * 24:(r + 1) * 24, :], in_=mall[:, r, l, :, :]), [B_ma], [B_modv])
        for l in range(L):
            for v0 in (16, 64):
                A(lambda: nc.scalar.activation(out=modv[:, l, v0:v0 + 16, :], in_=modv[:, l, v0:v0 + 16, :], func=AF.Identity, bias=1.0, scale=1.0),
                  [B_modv], [B_modv])
        lt, B_lt = P.sb(s0, "lt", [128, 64], F32)
        ld, B_ld = P.sb(s0, "ld", [128, 4], F32)
        for l in range(L):
            lam_init = 0.8 - 0.6 * math.exp(-0.3 * l)
            for k in range(2):
                V(lambda: nc.vector.tensor_tensor(out=lt[:], in0=blam[:, l, 2 * k, :], in1=blam[:, l, 2 * k + 1, :], op=ALU.mult), [B_c], [B_lt])
                V(lambda: nc.vector.reduce_sum(out=ld[:, k:k + 1], in_=lt[:], axis=AX.X), [B_lt], [B_ld])
            A(lambda: nc.scalar.activation(out=ld[:, 2:4], in_=ld[:, 0:2], func=AF.Exp), [B_ld], [B_ld])
            V(lambda: nc.vector.tensor_tensor(out=nlam[:, l:l + 1], in0=ld[:, 3:4], in1=ld[:, 2:3], op=ALU.subtract), [B_ld], [B_nlam])
            A(lambda: nc.scalar.activation(out=nlam[:, l:l + 1], in_=nlam[:, l:l + 1], func=AF.Identity, bias=-lam_init, scale=1.0), [B_nlam], [B_nlam])
            A(lambda: nc.scalar.activation(out=sublns[:, l:l + 1], in_=subln[:, l:l + 1], func=AF.Identity, scale=1.0 - lam_init), [B_c], [B_nlam])
        fw.barrier()

    def modp(l, v, fc, row):
        return modv[:, l, v * 16 + fc, row:row + 1]

    def modulate(l, vsh, vsc, row):
        for c in range(16):
            A(lambda: nc.scalar.activation(out=h_sb[:, c, :], in_=x_sb[:, c, :], func=AF.Identity,
                                           bias=modp(l, vsh, c, row), scale=modp(l, vsc, c, row)), [B_x, B_modv], [B_h])

    def layernorm(l, k, scope):
        sq_ring = P.ring(scope, "lnsq%d" % k, 2, [128, T], F32)
        st, B_st = P.sb(scope, "lnst%d" % k, [128, 2, T], F32)
        p1, b1 = ps_long()
        p2, b2 = ps_long()
        for c in range(16):
            sq, bq = sq_ring()
            A(lambda: nc.scalar.activation(out=sq[:], in_=x_sb[:, c, :], func=AF.Square), [B_x], [bq])
            mm(p1[:], CM(C_ONE), x_sb[:, c, :], c == 0, c == 15, [B_c, B_x], [b1], inc=True)
            mm(p2[:], CM(C_ONE), sq[:], c == 0, c == 15, [B_c, bq], [b2], inc=True)
        mean, var = st[:, 0, :], st[:, 1, :]
        A(lambda: nc.scalar.activation(out=mean, in_=p1[:], func=AF.Identity, scale=1.0 / D), [b1], [B_st])
        V(lambda: nc.vector.tensor_tensor(out=var, in0=mean, in1=mean, op=ALU.mult), [B_st], [B_st])
        V(lambda: nc.vector.scalar_tensor_tensor(out=var, in0=p2[:], scalar=1.0 / D, in1=var, op0=ALU.mult, op1=ALU.subtract), [b2, B_st], [B_st])
        A(lambda: nc.scalar.activation(out=var, in_=var, func=AF.Ln, bias=LN_EPS, scale=1.0), [B_st], [B_st])
        A(lambda: nc.scalar.activation(out=var, in_=var, func=AF.Exp, scale=-0.5), [B_st], [B_st])
        for c in range(16):
            V(lambda: nc.vector.tensor_tensor(out=x_sb[:, c, :], in0=x_sb[:, c, :], in1=mean, op=ALU.subtract), [B_x, B_st], [B_x])
            V(lambda: nc.vector.tensor_tensor(out=x_sb[:, c, :], in0=x_sb[:, c, :], in1=var, op=ALU.mult), [B_x, B_st], [B_x])
            A(lambda: nc.scalar.activation(out=x_sb[:, c, :], in_=x_sb[:, c, :], func=AF.Identity,
                                           bias=lnp[:, l, 2 * k + 1, c:c + 1], scale=lnp[:, l, 2 * k, c:c + 1]), [B_x, B_c], [B_x])

    def residual_proj(l, wsrc, kc, ncols_tile, rhs_fn, rhs_bufs, vgate, row, scope, tag):
        tmp_ring = P.ring(scope, "rp" + tag, 2, [128, T], F32)
        per = ncols_tile // 128
        with extra_slots(3):
            for tcol in range(D // ncols_tile):
                wv, wb = wload(wsrc[:, tcol * ncols_tile:(tcol + 1) * ncols_tile], kc, ncols_tile)
                for q in range(per):
                    fc = tcol * per + q
                    ps, pb = ps_short()
                    for c in range(kc):
                        mm(ps[:], wv[:, c, q * 128:(q + 1) * 128], rhs_fn(c), c == 0, c == kc - 1, [wb] + rhs_bufs, [pb])
                    tmp, tb = tmp_ring()
                    A(lambda: nc.scalar.activation(out=tmp[:], in_=ps[:], func=AF.Identity, scale=modp(l, vgate, fc, row)), [pb, B_modv], [tb])
                    V(lambda: nc.vector.scalar_tensor_tensor(out=x_sb[:, fc, :], in0=x_sb[:, fc, :], scalar=ALPHA, in1=tmp[:],
                                                             op0=ALU.mult, op1=ALU.add), [B_x, tb], [B_x])

    def attention(groups, scale, et_ring, btmp_ring, finish):
        yps, yb = ps_long()
        dps, db = ps_long()
        ng = len(groups)
        for gi, grp in enumerate(groups):
            st, sb_ = ps_short()
            for si, s in enumerate(grp):
                mm(st[:, s["c0"]:s["c0"] + s["n"]], s["k"], s["q"], si == 0, True, [s["kb"], s["qb"]], [sb_], inc=(si == len(grp) - 1), sgc=True)
            et, eb = et_ring()
            bias = grp[0].get("bias")
            if bias is not None:
                bt, btb = btmp_ring()
                V(lambda: nc.vector.scalar_tensor_tensor(out=bt[:], in0=st[:], scalar=scale, in1=bias, op0=ALU.mult, op1=ALU.add),
                  [sb_, grp[0]["bb"]], [btb])
                A(lambda: nc.scalar.activation(out=et[:], in_=bt[:], func=AF.Exp), [btb], [eb])
            else:
                A(lambda: nc.scalar.activation(out=et[:], in_=st[:], func=AF.Exp, scale=scale), [sb_], [eb])
            for si, s in enumerate(grp):
                mm(yps[:, s["c0"]:s["c0"] + s["n"]], s["v"], et[:, s["c0"]:s["c0"] + s["n"]],
                   gi == 0 and si == 0, gi == ng - 1, [s["vb"], eb], [yb], inc=False, sgc=True)
            mm(dps[:], ones_bf[:], et[:], gi == 0, gi == ng - 1, [B_c, eb], [db], inc=True)
        finish(yps, yb, dps, db)

    def block(l, g):
        row = 0 if g == "P" else 1
        lam_init = 0.8 - 0.6 * math.exp(-0.3 * l)
        xsrc = xin[g] if l == 0 else xspill[g]
        fw.dma("sp", x_sb[:], xsrc.rearrange("(c p) t -> p c t", p=128), reads=[B_spill[g]], writes=[B_x])
        modulate(l, 0, 1, row)
        with ExitStack() as sm:
            ycat, B_y = P.sb(sm, "ycat", [128, 16, T], BF16)
            with ExitStack() as sc:
                qtc, B_qtc = P.sb(sc, "qtc", [128, 5, T], BF16)
                ktc, B_ktc = P.sb(sc, "ktc", [128, 5, T], BF16)
                kcA, B_kcA = P.sb(sc, "kcA", [128, 4, 640], BF16)
                kcB, B_kcB = P.sb(sc, "kcB", [128, 4, 640], BF16)
                vc, B_vc = P.sb(sc, "vc", [128, 4, 640], BF16)
                sigoc, B_so = P.sb(sc, "sigoc", [128, 4, 640], F32)
                gat, B_gat = P.sb(sc, "gat", [128, 4, 20], F32)
                G(lambda: nc.gpsimd.memset(kcA[:], 0.0), [], [B_kcA])
                G(lambda: nc.gpsimd.memset(kcB[:], 0.0), [], [B_kcB])
                ctiles = [(4224, 512), (4736, 512), (5248, 512), (5760, 512), (6272, 532)]
                with extra_slots(2):
                    for (c0, ncol) in ctiles:
                        wv, wb = wload(w_in[l, :, c0:c0 + ncol], 16, ncol)
                        for q in range(min(4, ncol // 128)):
                            col = c0 + q * 128
                            k = col // 128
                            if 33 <= k <= 42:
                                ps, pb = ps_short()
                                for c in range(16):
                                    mm(ps[:], wv[:, c, q * 128:(q + 1) * 128], h_sb[:, c, :], c == 0, c == 15, [wb, B_h], [pb])
                                if k <= 37:
                                    A(lambda: nc.scalar.activation(out=qtc[:, k - 33, :], in_=ps[:], func=AF.Copy, scale=128.0 ** -0.5), [pb], [B_qtc])
                                else:
                                    A(lambda: nc.scalar.copy(out=ktc[:, k - 38, :], in_=ps[:]), [pb], [B_ktc])
                        segs = []
                        for (name, lo, hi) in (("kc", 4864, 5504), ("vc", 5504, 6144), ("oc", 6144, 6784), ("gc", 6784, 6804)):
                            a, b_ = max(lo, c0), min(hi, c0 + ncol)
                            if a < b_:
                                segs.append((name, a, b_, lo))
                        for (name, a, b_, lo) in segs:
                            for tt in range(4):
                                ps, pb = ps_short()
                                n = b_ - a
                                for c in range(16):
                                    mm(ps[:, 0:n], h_sb[:, c, tt * 128:(tt + 1) * 128], wv[:, c, a - c0:b_ - c0], c == 0, c == 15, [wb, B_h], [pb])
                                o0 = a - lo
                                if name == "kc":
                                    A(lambda: nc.scalar.copy(out=kcA[0:64, tt, o0:o0 + n], in_=ps[0:64, 0:n]), [pb], [B_kcA])
                                    A(lambda: nc.scalar.copy(out=kcB[64:128, tt, o0:o0 + n], in_=ps[64:128, 0:n]), [pb], [B_kcB])
                                elif name == "vc":
                                    A(lambda: nc.scalar.copy(out=vc[:, tt, o0:o0 + n], in_=ps[:, 0:n]), [pb], [B_vc])
                                elif name == "oc":
                                    A(lambda: nc.scalar.activation(out=sigoc[:, tt, o0:o0 + n], in_=ps[:, 0:n], func=AF.Sigmoid), [pb], [B_so])
                                else:
                                    V(lambda: nc.vector.tensor_tensor(out=gat[:, tt, :], in0=ps[:, 0:20], in1=gateb[:, l, :], op=ALU.add), [pb, B_c], [B_gat])
                kstop("c1")
                bc, B_bc = P.sb(sc, "bc", [128, 2, 4, 10], F32)
                ea, B_ea = P.sb(sc, "ea", [128, 2, 4, 5], F32)
                bl, B_bl = P.sb(sc, "bl", [128, 2, 8, 10], F32)
                t5, B_t5 = P.sb(sc, "t5", [128, 4, 5], F32)
                vp, B_vp = P.sb(sc, "vp", [128, 2, 4, 5, 130], BF16)
                sgp = ExitStack()
                dg, B_dg = P.sb(sgp, "dg", [128, 5, 128], F32)
                mk, B_mk = P.sb(sgp, "mk", [128, 5, 128], F32)
                for d in range(2):
                    tri = CM(C_TRIF if d == 0 else C_TRIB)
                    for tt in range(4):
                        ig = gat[:, tt, d * 10:d * 10 + 5]
                        fg = gat[:, tt, d * 10 + 5:d * 10 + 10]
                        A(lambda: nc.scalar.activation(out=t5[:, 0, :], in_=fg, func=AF.Exp, scale=-1.0), [B_gat], [B_t5])
                        A(lambda: nc.scalar.activation(out=t5[:, 1, :], in_=t5[:, 0, :], func=AF.Ln, bias=1.0, scale=1.0), [B_t5], [B_t5])
                        ps, pb = ps_short()
                        mm(ps[:, 0:5], tri, t5[:, 1, :], True, True, [B_c, B_t5], [pb])
                        A(lambda: nc.scalar.copy(out=bc[:, d, tt, 0:5], in_=ps[:, 0:5]), [pb], [B_bc])
                        V(lambda: nc.vector.tensor_tensor(out=t5[:, 2, :], in0=ig, in1=bc[:, d, tt, 0:5], op=ALU.add), [B_gat, B_bc], [B_t5])
                        A(lambda: nc.scalar.activation(out=ea[:, d, tt, :], in_=t5[:, 2, :], func=AF.Exp), [B_t5], [B_ea])
                        for h in range(5):
                            A(lambda: nc.scalar.activation(out=dg[:, h, :], in_=CM(C_ID), func=AF.Identity, scale=t5[:, 2, h:h + 1]), [B_c, B_t5], [B_dg])
                        ps1, pb1 = ps_short()
                        mm(ps1[:, 0:384], CM(C_ONE), dg[:, 0:3, :].rearrange("p h s -> p (h s)"), True, True, [B_c, B_dg], [pb1])
                        ps2, pb2 = ps_short()
                        mm(ps2[:, 0:256], CM(C_ONE), dg[:, 3:5, :].rearrange("p h s -> p (h s)"), True, True, [B_c, B_dg], [pb2])
                        V(lambda: nc.vector.tensor_tensor(out=mk[:, 0:3, :].rearrange("p h s -> p (h s)"), in0=ps1[:, 0:384],
                                                          in1=negm[:, d, 0:384], op=ALU.add), [pb1, B_c], [B_mk])
                        V(lambda: nc.vector.tensor_tensor(out=mk[:, 3:5, :].rearrange("p h s -> p (h s)"), in0=ps2[:, 0:256],
                                                          in1=negm[:, d, 384:640], op=ALU.add), [pb2, B_c], [B_mk])
                        V(lambda: nc.vector.tensor_reduce(out=bc[:, d, tt, 5:10], in_=mk[:], axis=AX.X, op=ALU.max), [B_mk], [B_bc])
                        for X in range(2):
                            ep = (C_E63, C_E127)[X] if d == 0 else (C_E0, C_E64)[X]
                            ps, pb = ps_short()
                            mm(ps[:, 0:10], CM(ep), bc[:, d, tt, :], True, True, [B_c, B_bc], [pb])
                            A(lambda: nc.scalar.copy(out=bl[:, d, 2 * tt + X, :], in_=ps[:, 0:10]), [pb], [B_bl])
                        for h in range(5):
                            A(lambda: nc.scalar.activation(out=vp[:, d, tt, h, 0:128], in_=vc[:, tt, h * 128:(h + 1) * 128], func=AF.Identity,
                                                           scale=ea[:, d, tt, h:h + 1]), [B_vc, B_ea], [B_vp])
                        A(lambda: nc.scalar.copy(out=vp[:, d, tt, :, 128], in_=ea[:, d, tt, :]), [B_ea], [B_vp])

                fw.barrier()
                sgp.close()
                kstop("c2")
                mc, B_mc = P.sb(sc, "mc", [128, 2, 8, 5], F32)
                wold, B_wo = P.sb(sc, "wold", [128, 2, 8, 5], F32)
                snew, B_sn = P.sb(sc, "snew", [128, 2, 8, 5], F32)
                mcur, B_mcur = P.sb(sc, "mcur", [128, 2, 5], F32)
                mt, B_mt = P.sb(sc, "mt", [128, 2, 5], F32)
                nfacc, B_nf = P.sb(sc, "nfacc", [128, 2, 5], F32)
                cn, B_cn = P.sb(sc, "cn", [128, 10, 129], F32)
                cnb, B_cnb = P.sb(sc, "cnb", [128, 10, 130], BF16)
                tmpu_ring = P.ring(sc, "tmpu", 2, [128, 129], F32)

                def chunk_order(d, runs):
                    out = []
                    rr = runs if d == 0 else [list(reversed(r)) for r in reversed(runs)]
                    for r in rr:
                        out.append(r)
                    return out

                def mchain(d, run, m_init_fn):
                    m_init_fn(mcur[:, d, :])
                    for c in run:
                        A(lambda: nc.scalar.copy(out=mc[:, d, c, :], in_=mcur[:, d, :]), [B_mcur], [B_mc])
                        V(lambda: nc.vector.tensor_tensor(out=mt[:, 0, :], in0=mcur[:, d, :], in1=bl[:, d, c, 5:10], op=ALU.max), [B_mcur, B_bl], [B_mt])
                        V(lambda: nc.vector.tensor_tensor(out=mt[:, 1, :], in0=mcur[:, d, :], in1=mt[:, 0, :], op=ALU.subtract), [B_mcur, B_mt], [B_mt])
                        A(lambda: nc.scalar.activation(out=wold[:, d, c, :], in_=mt[:, 1, :], func=AF.Exp), [B_mt], [B_wo])
                        A(lambda: nc.scalar.activation(out=snew[:, d, c, :], in_=mt[:, 0, :], func=AF.Exp, scale=-1.0), [B_mt], [B_sn])
                        V(lambda: nc.vector.tensor_tensor(out=mcur[:, d, :], in0=mt[:, 0, :], in1=bl[:, d, c, 0:5], op=ALU.subtract), [B_mt, B_bl], [B_mcur])
                        V(lambda: nc.vector.tensor_tensor(out=nfacc[:, d, :], in0=nfacc[:, d, :], in1=bl[:, d, c, 0:5], op=ALU.add), [B_nf, B_bl], [B_nf])

                def state_update(d, h, c):
                    tt, X = c // 2, c % 2
                    kk = kcA if X == 0 else kcB
                    kkb = B_kcA if X == 0 else B_kcB
                    ps, pb = ps_short()
                    mm(ps[:, 0:129], kk[:, tt, h * 128:(h + 1) * 128], vp[:, d, tt, h, 0:129], True, True, [kkb, B_vp], [pb])
                    tu, tub = tmpu_ring()
                    A(lambda: nc.scalar.activation(out=tu[:], in_=ps[:, 0:129], func=AF.Identity, scale=snew[:, d, c, h:h + 1]), [pb, B_sn], [tub])
                    V(lambda: nc.vector.scalar_tensor_tensor(out=cn[:, d * 5 + h, :], in0=cn[:, d * 5 + h, :], scalar=wold[:, d, c, h:h + 1],
                                                             in1=tu[:], op0=ALU.mult, op1=ALU.add), [B_cn, B_wo, tub], [B_cn])
                    A(lambda: nc.scalar.copy(out=cnb[:, d * 5 + h, 0:129], in_=cn[:, d * 5 + h, :]), [B_cn], [B_cnb])

                def zero_state(d):
                    G(lambda: nc.gpsimd.memset(cn[:, d * 5:(d + 1) * 5, :], 0.0), [], [B_cn])
                    G(lambda: nc.gpsimd.memset(cnb[:, d * 5:(d + 1) * 5, :], 0.0), [], [B_cnb])

                def alloc_scan_bufs():
                    a_ = P.sb(sc, "hc", [128, 4, 640], F32)
                    b_ = P.sb(sc, "tok", [128, 2, 4, 15], F32)
                    c_ = P.sb(sc, "mcol", [128, 5], F32)
                    return (a_[0], a_[1], b_[0], b_[1], c_[0], c_[1], P.ring(sc, "gm", 3, [128, 128], BF16),
                            P.ring(sc, "ti", 3, [128, 129], F32), P.ring(sc, "hn", 3, [128, 129], F32), P.ring(sc, "s3", 3, [128, 3], F32))

                def token_scalars(d, tt):
                    A(lambda: nc.scalar.copy(out=mcol[0:64, :], in_=mc[0:64, d, 2 * tt, :]), [B_mc], [B_mcol])
                    A(lambda: nc.scalar.copy(out=mcol[64:128, :], in_=mc[64:128, d, 2 * tt + 1, :]), [B_mc], [B_mcol])
                    V(lambda: nc.vector.tensor_tensor(out=t5[:, 3, :], in0=bc[:, d, tt, 5:10], in1=mcol[:], op=ALU.max), [B_bc, B_mcol], [B_t5])
                    A(lambda: nc.scalar.activation(out=tok[:, d, tt, 0:5], in_=t5[:, 3, :], func=AF.Exp, scale=-1.0), [B_t5], [B_tok])
                    V(lambda: nc.vector.tensor_tensor(out=t5[:, 0, :], in0=mcol[:], in1=t5[:, 3, :], op=ALU.subtract), [B_mcol, B_t5], [B_t5])
                    A(lambda: nc.scalar.activation(out=tok[:, d, tt, 5:10], in_=t5[:, 0, :], func=AF.Exp), [B_t5], [B_tok])
                    V(lambda: nc.vector.tensor_tensor(out=t5[:, 1, :], in0=bc[:, d, tt, 0:5], in1=t5[:, 3, :], op=ALU.subtract), [B_bc, B_t5], [B_t5])
                    A(lambda: nc.scalar.activation(out=tok[:, d, tt, 10:15], in_=t5[:, 1, :], func=AF.Exp), [B_t5], [B_tok])

                def scan_outputs(runs, on_run_end, on_run_start):
                    G(lambda: nc.gpsimd.memset(hc[:], 0.0), [], [B_hc])
                    for d in range(2):
                        for tt in range(4):
                            token_scalars(d, tt)
                    order = {d: chunk_order(d, runs) for d in range(2)}
                    nsteps = sum(len(r) for r in runs) // 2
                    flat = {d: [c for r in order[d] for c in r] for d in range(2)}
                    run_start = {d: {r[0]: ri for ri, r in enumerate(order[d])} for d in range(2)}
                    run_end = {d: {r[-1]: ri for ri, r in enumerate(order[d])} for d in range(2)}
                    for step in range(nsteps):
                        for d in range(2):
                            c_pair = flat[d][2 * step:2 * step + 2]
                            tt = c_pair[0] // 2
                            if c_pair[0] in run_start[d]:
                                on_run_start(d, run_start[d][c_pair[0]], order[d])
                            for h in range(5):
                                gps, gpb = ps_short()
                                mm(gps[:, 0:128], ktc[:, h, tt * 128:(tt + 1) * 128], qtc[:, h, tt * 128:(tt + 1) * 128], True, True, [B_ktc, B_qtc], [gpb])
                                gm, gmb = gm_ring()
                                V(lambda: nc.vector.tensor_tensor(out=gm[:], in0=gps[:, 0:128], in1=CM(C_TRIF if d == 0 else C_TRIB), op=ALU.mult), [gpb, B_c], [gmb])
                                ips, ipb = ps_short()
                                mm(ips[:, 0:129], gm[:], vp[:, d, tt, h, 0:129], True, True, [gmb, B_vp], [ipb])
                                ti, tib = ti_ring()
                                A(lambda: nc.scalar.activation(out=ti[:], in_=ips[:, 0:129], func=AF.Identity, scale=tok[:, d, tt, h:h + 1]), [ipb, B_tok], [tib])
                                hn, hnb = hn_ring()
                                for c in c_pair:
                                    X = c % 2
                                    rs = slice(0, 64) if X == 0 else slice(64, 128)
                                    xps, xpb = ps_short()
                                    mm(xps[:, 0:129], qtc[:, h, tt * 128:(tt + 1) * 128], cnb[:, d * 5 + h, 0:129], True, True, [B_qtc, B_cnb], [xpb])
                                    V(lambda: nc.vector.scalar_tensor_tensor(out=hn[rs, :], in0=xps[rs, 0:129], scalar=tok[rs, d, tt, 5 + h:6 + h],
                                                                             in1=ti[rs, :], op0=ALU.mult, op1=ALU.add), [xpb, B_tok, tib], [hnb])
                                    state_update(d, h, c)
                                s3, s3b = s3_ring()
                                V(lambda: nc.vector.scalar_tensor_tensor(out=s3[:, 0:1], in0=hn[:, 128:129], scalar=-1.0, in1=hn[:, 128:129],
                                                                         op0=ALU.mult, op1=ALU.max), [hnb], [s3b])
                                V(lambda: nc.vector.tensor_tensor(out=s3[:, 1:2], in0=s3[:, 0:1], in1=tok[:, d, tt, 10 + h:11 + h], op=ALU.max), [s3b, B_tok], [s3b])
                                A(lambda: nc.scalar.activation(out=s3[:, 2:3], in_=s3[:, 1:2], func=AF.Ln), [s3b], [s3b])
                                A(lambda: nc.scalar.activation(out=s3[:, 2:3], in_=s3[:, 2:3], func=AF.Exp, scale=-1.0), [s3b], [s3b])
                                hsl = hc[:, tt, h * 128:(h + 1) * 128]
                                V(lambda: nc.vector.scalar_tensor_tensor(out=hsl, in0=hn[:, 0:128], scalar=s3[:, 2:3], in1=hsl,
                                                                         op0=ALU.mult, op1=ALU.add), [hnb, s3b, B_hc], [B_hc])
                            if c_pair[1] in run_end[d]:
                                on_run_end(d, run_end[d][c_pair[1]], order[d])

                def set_const(val):
                    def f(ap):
                        G(lambda: nc.gpsimd.memset(ap, val), [], [B_mcur])
                    return f

                G(lambda: nc.gpsimd.memset(nfacc[:], 0.0), [], [B_nf])
                if g == "P":
                    runs = [[0, 1, 2, 3], [4, 5, 6, 7]]
                    mfin, B_mfin = P.sb(sc, "mfin", [128, 2, 2, 5], F32)
                    for d in range(2):
                        for ri, r in enumerate(chunk_order(d, runs)):
                            mchain(d, r, set_const(0.0))
                            seq = r[0] // 4
                            A(lambda: nc.scalar.copy(out=mfin[:, seq, d, :], in_=mcur[:, d, :]), [B_mcur], [B_mfin])
                    for seq in range(2):
                        fw.dma("sp", o_cm[seq, l].rearrange("(o d) h -> o (d h)", o=1), mfin[0:1, seq, :, :].rearrange("p d h -> p (d h)"),
                               reads=[B_mfin], writes=[B_out])

                    def on_start(d, ri, order):
                        zero_state(d)

                    def on_end(d, ri, order):
                        seq = order[ri][0] // 4
                        fw.dma("sp", o_cC[seq, l, d].rearrange("h k v -> k h v"), cn[:, d * 5:(d + 1) * 5, 0:128], reads=[B_cn], writes=[B_out])
                        fw.dma("sp", o_cn[seq, l, d], cn[:, d * 5:(d + 1) * 5, 128], reads=[B_cn], writes=[B_out], slow=True)
                    hc, B_hc, tok, B_tok, mcol, B_mcol, gm_ring, ti_ring, hn_ring, s3_ring = alloc_scan_bufs()
                    scan_outputs(runs, on_end, on_start)
                else:
                    runs = [[0, 1, 2, 3, 4, 5, 6, 7]]
                    for d in range(2):
                        zero_state(d)
                        r = chunk_order(d, runs)[0]
                        mchain(d, r, set_const(NEG))
                        for c in r:
                            for h in range(5):
                                state_update(d, h, c)
                    with ExitStack() as sg:
                        cst_t, B_cst = P.sb(sg, "cst_t", [128, 20], F32)
                        A(lambda: nc.scalar.copy(out=cst_t[:, 0:10], in_=mcur[:].rearrange("p d h -> p (d h)")), [B_mcur], [B_cst])
                        A(lambda: nc.scalar.copy(out=cst_t[:, 10:20], in_=nfacc[:].rearrange("p d h -> p (d h)")), [B_nf], [B_cst])
                        fw.dma("sp", cs_in[:, 0:1290], cn[:].rearrange("p a b -> p (a b)"), reads=[B_cn], writes=[B_csi])
                        fw.dma("sp", cs_in[:, 1290:1310], cst_t[:], reads=[B_cst], writes=[B_csi])
                        fw.allgather(cs_in, cs_out, reads=[B_csi], writes=[B_cso])
                        gs, B_gs = P.sb(sg, "gs", [128, 4, CSF], F32)
                        fw.dma("sp", gs[:], cs_out.rearrange("(r p) f -> p r f", p=128), reads=[B_cso], writes=[B_gs])
                        c0t, B_c0 = P.sb(sg, "c0t", [128, 10, 129], F32)
                        m0t, B_m0 = P.sb(sg, "m0t", [128, 10], F32)
                        fw.dma("sp", c0t[:, :, 0:128], cC_d[l].rearrange("d h k v -> k (d h) v"), writes=[B_c0])
                        fw.dma("sp", c0t[:, :, 128], cn_d[l], writes=[B_c0], slow=True)
                        fw.dma("sp", m0t[:], bass.AP(cm_d.tensor, l * 10, [[0, 128], [1, 10]]), writes=[B_m0])
                        av, B_av = P.sb(sg, "av", [128, 2, 6, 5], F32)
                        wv5, B_wv5 = P.sb(sg, "wv5", [128, 2, 5, 5], F32)
                        for d in range(2):
                            for i in range(5):
                                src = m0t[:, d * 5:(d + 1) * 5] if i == 0 else gs[:, i - 1, 1290 + d * 5:1290 + d * 5 + 5]
                                V(lambda: nc.vector.tensor_tensor(out=av[:, d, i, :], in0=src, in1=vtab[:, d, i, :], op=ALU.add), [B_m0, B_gs, B_c], [B_av])
                                for r in range(4):
                                    V(lambda: nc.vector.tensor_tensor(out=t5[:, 0, :], in0=cftab[:, d, i, r, :], in1=gs[:, r, 1300 + d * 5:1305 + d * 5], op=ALU.mult),
                                      [B_c, B_gs], [B_t5])
                                    V(lambda: nc.vector.tensor_tensor(out=av[:, d, i, :], in0=av[:, d, i, :], in1=t5[:, 0, :], op=ALU.subtract), [B_av, B_t5], [B_av])
                            V(lambda: nc.vector.tensor_tensor(out=av[:, d, 5, :], in0=av[:, d, 0, :], in1=av[:, d, 1, :], op=ALU.max), [B_av], [B_av])
                            for i in range(2, 5):
                                V(lambda: nc.vector.tensor_tensor(out=av[:, d, 5, :], in0=av[:, d, 5, :], in1=av[:, d, i, :], op=ALU.max), [B_av], [B_av])
                            for i in range(5):
                                V(lambda: nc.vector.tensor_tensor(out=t5[:, 1, :], in0=av[:, d, i, :], in1=av[:, d, 5, :], op=ALU.subtract), [B_av], [B_t5])
                                A(lambda: nc.scalar.activation(out=wv5[:, d, i, :], in_=t5[:, 1, :], func=AF.Exp), [B_t5], [B_wv5])
                            for h in range(5):
                                dh = d * 5 + h
                                A(lambda: nc.scalar.activation(out=cn[:, dh, :], in_=c0t[:, dh, :], func=AF.Identity, scale=wv5[:, d, 0, h:h + 1]), [B_c0, B_wv5], [B_cn])
                                for r in range(4):
                                    V(lambda: nc.vector.scalar_tensor_tensor(out=cn[:, dh, :], in0=gs[:, r, dh * 129:(dh + 1) * 129], scalar=wv5[:, d, 1 + r, h:h + 1],
                                                                             in1=cn[:, dh, :], op0=ALU.mult, op1=ALU.add), [B_gs, B_wv5, B_cn], [B_cn])
                                A(lambda: nc.scalar.copy(out=cnb[:, dh, 0:129], in_=cn[:, dh, :]), [B_cn], [B_cnb])

                        def m_from_av(d):
                            def f(ap):
                                A(lambda: nc.scalar.copy(out=ap, in_=av[:, d, 5, :]), [B_av], [B_mcur])
                            return f
                        for d in range(2):
                            mchain(d, chunk_order(d, runs)[0], m_from_av(d))
                        fw.barrier()
                    hc, B_hc, tok, B_tok, mcol, B_mcol, gm_ring, ti_ring, hn_ring, s3_ring = alloc_scan_bufs()
                    scan_outputs(runs, lambda *a: None, lambda *a: None)

                kstop("c3")
                ss, B_ss = P.sb(sc, "ss", [128, 20], F32)
                junk, B_junk = P.sb(sc, "junk", [128, 128], F32)
                yct_ring = P.ring(sc, "yct", 3, [128, 128], BF16)
                ytmp_ring = P.ring(sc, "ytmp", 2, [128, 128], F32)
                G(lambda: nc.gpsimd.memset(ss[:], 0.0), [], [B_ss])
                for tt in range(4):
                    for h in range(5):
                        A(lambda: nc.scalar.activation(out=junk[:], in_=hc[:, tt, h * 128:(h + 1) * 128], func=AF.Square,
                                                       accum_out=ss[:, tt * 5 + h:tt * 5 + h + 1]), [B_hc], [B_junk, B_ss])
                A(lambda: nc.scalar.activation(out=ss[:], in_=ss[:], func=AF.Ln, scale=1.0 / 128, bias=RMS_EPS), [B_ss], [B_ss])
                A(lambda: nc.scalar.activation(out=ss[:], in_=ss[:], func=AF.Exp, scale=-0.5), [B_ss], [B_ss])
                kstop("ca")
                for h in range(5):
                    for tt in range(4):
                        yt, ytb = ytmp_ring()
                        A(lambda: nc.scalar.activation(out=yt[:], in_=hc[:, tt, h * 128:(h + 1) * 128], func=AF.Identity, scale=ss[:, tt * 5 + h:tt * 5 + h + 1]),
                          [B_hc, B_ss], [ytb])
                        V(lambda: nc.vector.tensor_tensor(out=yt[:], in0=yt[:], in1=cnorm[:, l, :], op=ALU.mult), [ytb, B_c], [ytb])
                        yc, ycb = yct_ring()
                        V(lambda: nc.vector.tensor_tensor(out=yc[:], in0=yt[:], in1=sigoc[:, tt, h * 128:(h + 1) * 128], op=ALU.mult), [ytb, B_so], [ycb])
                        if KSTOP == "cb":
                            continue
                        ps, pb = ps_short()
                        mm(ps[:, 0:128], yc[:], id_bf[:], True, True, [ycb, B_c], [pb])
                        if KSTOP == "cc":
                            continue
                        A(lambda: nc.scalar.copy(out=ycat[:, 11 + h, tt * 128:(tt + 1) * 128], in_=ps[:, 0:128]), [pb], [B_y])
                fw.barrier()

            kstop("c4")
            kstop("cb")
            kstop("cc")
            with ExitStack() as sa:
                qta, B_qta = P.sb(sa, "qta", [128, 6, T], BF16)
                q1p, B_q1p = P.sb(sa, "q1p", [128, 5, T], BF16)
                q2p, B_q2p = P.sb(sa, "q2p", [128, 5, T], BF16)
                G(lambda: nc.gpsimd.memset(q1p[:], 0.0), [], [B_q1p])
                G(lambda: nc.gpsimd.memset(q2p[:], 0.0), [], [B_q2p])
                if g == "S":
                    q1r, B_q1r = P.sb(sa, "q1r", [128, 5, T], BF16)
                    q2r, B_q2r = P.sb(sa, "q2r", [128, 5, T], BF16)
                    G(lambda: nc.gpsimd.memset(q1r[:], 0.0), [], [B_q1r])
                    G(lambda: nc.gpsimd.memset(q2r[:], 0.0), [], [B_q2r])
                    sip = ExitStack()
                    kst_ring = P.ring(sip, "kst", 3, [128, T], BF16)
                    vst, B_vst = P.sb(sip, "vst", [128, 4, 1408], BF16)
                    rp_ring = P.ring(sip, "rpx", 2, [128, T], F32)
                    rp2_ring = P.ring(sip, "rpy", 2, [128, T], F32)
                else:
                    kta, B_kta = P.sb(sa, "kta", [128, 6, T], BF16)
                    ktb, B_ktb = P.sb(sa, "ktb", [128, 5, T], BF16)
                    vab, B_vab = P.sb(sa, "vab", [128, 4, 1408], BF16)
                    stg_ring = P.ring(sa, "stg", 3, [128, T], F32)

                def rope_apply(ps, pb, outs):
                    xf, xb = rp_ring()
                    A(lambda: nc.scalar.copy(out=xf[:], in_=ps[:]), [pb], [xb])
                    p2, pb2 = ps_short()
                    mm(p2[:], CM(C_PERM), xf[:], True, True, [B_c, xb], [pb2])
                    x2, x2b = rp2_ring()
                    V(lambda: nc.vector.tensor_tensor(out=x2[:], in0=p2[:], in1=rope[:, 1, :], op=ALU.mult), [pb2, B_c], [x2b])
                    V(lambda: nc.vector.tensor_tensor(out=xf[:], in0=xf[:], in1=rope[:, 0, :], op=ALU.mult), [xb, B_c], [xb])
                    for (rs, dst, db_) in outs:
                        V(lambda: nc.vector.tensor_tensor(out=dst[rs, :], in0=xf[rs, :], in1=x2[rs, :], op=ALU.add), [xb, x2b], [db_])

                abtiles = [(i * 512, 512) for i in range(8)] + [(4096, 128)]
                if "t" in KSKIP:
                    abtiles = abtiles[:int(KSKIP[KSKIP.index("t") + 1])]
                with extra_slots(2):
                    for (c0, ncol) in abtiles:
                        wv, wb = wload(w_in[l, :, c0:c0 + ncol], 16, ncol)
                        for q in range(ncol // 128):
                            k = (c0 + q * 128) // 128
                            fm = (k <= 11) or (18 <= k <= 27)
                            if not fm:
                                continue
                            ps, pb = ps_short()
                            for c in range(16):
                                mm(ps[:], wv[:, c, q * 128:(q + 1) * 128], h_sb[:, c, :], c == 0, c == 15, [wb, B_h], [pb])
                            if k <= 5:
                                A(lambda: nc.scalar.copy(out=qta[:, k, :], in_=ps[:]), [pb], [B_qta])
                            elif k <= 11:
                                hh = k - 6
                                if g == "P":
                                    A(lambda: nc.scalar.copy(out=kta[:, hh, :], in_=ps[:]), [pb], [B_kta])
                                    sg, sgb = stg_ring()
                                    A(lambda: nc.scalar.copy(out=sg[:], in_=ps[:]), [pb], [sgb])
                                    if "k" not in KSKIP:
                                        fw.dma("sp", o_ak[:, l, hh].rearrange("s d t -> d s t"), sg[:].rearrange("p (s t) -> p s t", s=2), reads=[sgb], writes=[B_out])
                                else:
                                    ks, ksb = kst_ring()
                                    A(lambda: nc.scalar.copy(out=ks[:], in_=ps[:]), [pb], [ksb])
                                    kr, krb = bnc_k_rows(hh)
                                    fw.dma("sp", kr, ks[:], reads=[ksb], writes=[krb])
                            elif k <= 22:
                                hh = k - 18
                                A(lambda: nc.scalar.copy(out=q1p[0:64, hh, :], in_=ps[0:64, :]), [pb], [B_q1p])
                                A(lambda: nc.scalar.copy(out=q2p[64:128, hh, :], in_=ps[64:128, :]), [pb], [B_q2p])
                                if g == "S":
                                    rope_apply(ps, pb, [(slice(0, 64), q1r[:, hh, :], B_q1r), (slice(64, 128), q2r[:, hh, :], B_q2r)])
                            else:
                                hh = k - 23
                                if g == "P":
                                    A(lambda: nc.scalar.copy(out=ktb[:, hh, :], in_=ps[:]), [pb], [B_ktb])
                                    sg, sgb = stg_ring()
                                    A(lambda: nc.scalar.copy(out=sg[:], in_=ps[:]), [pb], [sgb])
                                    if "k" not in KSKIP:
                                        fw.dma("sp", o_bk[:, l, hh].rearrange("s d t -> d s t"), sg[:].rearrange("p (s t) -> p s t", s=2), reads=[sgb], writes=[B_out])
                                else:
                                    ks, ksb = kst_ring()
                                    rope_apply(ps, pb, [(slice(0, 128), ks, ksb)])
                                    kr, krb = bnc_k_rows(6 + hh)
                                    fw.dma("sp", kr, ks[:], reads=[ksb], writes=[krb])
                        for (name, lo, hi, o_base) in (("va", 1536, 2304, 0), ("vb", 3584, 4224, 768)):
                            a, b_ = max(lo, c0), min(hi, c0 + ncol)
                            if a >= b_:
                                continue
                            n = b_ - a
                            o0 = o_base + a - lo
                            for tt in range(4):
                                ps, pb = ps_short()
                                for c in range(16):
                                    mm(ps[:, 0:n], h_sb[:, c, tt * 128:(tt + 1) * 128], wv[:, c, a - c0:b_ - c0], c == 0, c == 15, [wb, B_h], [pb])
                                if g == "P":
                                    A(lambda: nc.scalar.copy(out=vab[:, tt, o0:o0 + n], in_=ps[:, 0:n]), [pb], [B_vab])
                                    sg, sgb = stg_ring()
                                    A(lambda: nc.scalar.copy(out=sg[:, 0:n], in_=ps[:, 0:n]), [pb], [sgb])
                                    seq, s0_ = tt // 2, (tt % 2) * 128
                                    h0 = (a - lo) // 128
                                    nh = n // 128
                                    dst = (o_av if name == "va" else o_bv)[seq, l, h0:h0 + nh, s0_:s0_ + 128, :].rearrange("h s d -> s h d")
                                    if "v" not in KSKIP:
                                        fw.dma("sp", dst, sg[:, 0:n].rearrange("p (h d) -> p h d", d=128), reads=[sgb], writes=[B_out])
                                else:
                                    A(lambda: nc.scalar.copy(out=vst[:, tt, o0:o0 + n], in_=ps[:, 0:n]), [pb], [B_vst])

                kstop("c5")

                def alloc_attn_rings():
                    return (P.ring(sa, "et", 3, [128, T], BF16), P.ring(sa, "bt", 2, [128, T], F32),
                            P.ring(sa, "rd", 2, [128, T], F32), P.ring(sa, "ybt", 3, [128, T], F32))

                def fin_A(h):
                    def f(yps, yb, dps, db):
                        rd, rdb = rd_ring()
                        A(lambda: nc.scalar.activation(out=rd[:], in_=dps[:], func=AF.Ln), [db], [rdb])
                        A(lambda: nc.scalar.activation(out=rd[:], in_=rd[:], func=AF.Exp, scale=-1.0), [rdb], [rdb])
                        V(lambda: nc.vector.tensor_tensor(out=ycat[:, h, :], in0=yps[:], in1=rd[:], op=ALU.mult), [yb, rdb], [B_y])
                    return f

                def fin_B(dst, dstb):
                    def f(yps, yb, dps, db):
                        rd, rdb = rd_ring()
                        A(lambda: nc.scalar.activation(out=rd[:], in_=dps[:], func=AF.Ln), [db], [rdb])
                        A(lambda: nc.scalar.activation(out=rd[:], in_=rd[:], func=AF.Exp, scale=-1.0), [rdb], [rdb])
                        V(lambda: nc.vector.tensor_tensor(out=dst[:], in0=yps[:], in1=rd[:], op=ALU.mult), [yb, rdb], [dstb])
                    return f

                def diff_finish(h, y1, y1b, y2, y2b):
                    V(lambda: nc.vector.scalar_tensor_tensor(out=y1[:], in0=y2[:], scalar=nlam[:, l:l + 1], in1=y1[:], op0=ALU.mult, op1=ALU.add),
                      [y2b, y1b, B_nlam], [y1b])
                    A(lambda: nc.scalar.activation(out=y2[:], in_=y1[:], func=AF.Square), [y1b], [y2b])
                    sp_, spb = ps_short()
                    mm(sp_[:], CM(C_ONE), y2[:], True, True, [B_c, y2b], [spb])
                    A(lambda: nc.scalar.activation(out=y2[:], in_=sp_[:], func=AF.Ln, scale=1.0 / 128, bias=RMS_EPS), [spb], [y2b])
                    A(lambda: nc.scalar.activation(out=y2[:], in_=y2[:], func=AF.Exp, scale=-0.5), [y2b], [y2b])
                    V(lambda: nc.vector.tensor_tensor(out=y1[:], in0=y1[:], in1=y2[:], op=ALU.mult), [y1b, y2b], [y1b])
                    A(lambda: nc.scalar.activation(out=ycat[:, 6 + h, :], in_=y1[:], func=AF.Identity, scale=sublns[:, l:l + 1]), [y1b, B_nlam], [B_y])

                if g == "P":
                    et_ring, bt_ring, rd_ring, yb_ring = alloc_attn_rings()
                    for h in range(6):
                        groups = []
                        for kb in range(2):
                            grp = []
                            for s in range(2):
                                t0 = s * 256 + kb * 128
                                grp.append(dict(k=kta[:, h, t0:t0 + 128], kb=B_kta, q=qta[:, h, s * 256:(s + 1) * 256], qb=B_qta,
                                                v=vab[:, 2 * s + kb, h * 128:(h + 1) * 128], vb=B_vab, c0=s * 256, n=256))
                            groups.append(grp)
                        attention(groups, 128.0 ** -0.5, et_ring, bt_ring, fin_A(h))
                    for h in range(5):
                        ys = []
                        for (qp, qpb) in ((q1p, B_q1p), (q2p, B_q2p)):
                            groups = []
                            for kb in range(2):
                                grp = []
                                for s in range(2):
                                    t0 = s * 256 + kb * 128
                                    grp.append(dict(k=ktb[:, h, t0:t0 + 128], kb=B_ktb, q=qp[:, h, s * 256:(s + 1) * 256], qb=qpb,
                                                    v=vab[:, 2 * s + kb, 768 + h * 128:768 + (h + 1) * 128], vb=B_vab, c0=s * 256, n=256))
                                groups.append(grp)
                            yt, ytb = yb_ring()
                            attention(groups, 64.0 ** -0.5, et_ring, bt_ring, fin_B(yt, ytb))
                            ys.append((yt, ytb))
                        diff_finish(h, ys[0][0], ys[0][1], ys[1][0], ys[1][1])
                else:
                    for hh in range(11):
                        vr, vrb = bnc_v_rows(hh)
                        fw.dma("sp", vr.rearrange("p (tt d) -> p tt d", d=128),
                               vst[:, :, hh * 128:(hh + 1) * 128], reads=[B_vst], writes=[vrb])
                    for ci in range(3):
                        fw.allgather(bnc_in[ci], bnc_out[ci], reads=[B_bi[ci]], writes=[B_bo[ci]])
                    fw.barrier()
                    sip.close()
                    et_ring, bt_ring, rd_ring, yb_ring = alloc_attn_rings()
                    kall_ring = P.ring(sa, "kall", 2, [128, 4, T], BF16)
                    vall_ring = P.ring(sa, "vall", 2, [128, 4, T], BF16)
                    kctx_ring = P.ring(sa, "kctx", 2, [128, 512], BF16)
                    vctx_ring = P.ring(sa, "vctx", 2, [128, 4, 128], BF16)
                    nb_ring = P.ring(sa, "nbias", 3, [128, T], F32)
                    gviews = [bo.rearrange("(r x) t -> x r t", r=4) for bo in bnc_out]
                    for hh in range(11):
                        isA = hh < 6
                        h = hh if isA else hh - 6
                        ka, kab = kall_ring()
                        va_, vab_ = vall_ring()
                        kc_, kcb_ = kctx_ring()
                        vc_, vcb_ = vctx_ring()
                        gch = HH_CH[hh]
                        krow = HH_IX[hh] * 128
                        vrow = (CH_NH[gch] + HH_IX[hh]) * 128
                        fw.dma("sp", ka[:], gviews[gch][krow:krow + 128], reads=[B_bo[gch]], writes=[kab])
                        fw.dma("sp", va_[:], gviews[gch][vrow:vrow + 128], reads=[B_bo[gch]], writes=[vab_])
                        if isA:
                            fw.dma("pool", kc_[:], cakT[l, h], writes=[kcb_])
                            fw.dma("pool", vc_[:], cav[l, h].rearrange("(b p) d -> p b d", p=128), writes=[vcb_])
                        else:
                            fw.dma("pool", kc_[:], cbkT[l, h], writes=[kcb_])
                            fw.dma("pool", vc_[:], cbv[l, h].rearrange("(b p) d -> p b d", p=128), writes=[vcb_])

                        def mkgroups(q_lat, q_latb, q_ctx, q_ctxb):
                            groups = []
                            for kb in range(16):
                                r, t4 = kb // 4, kb % 4
                                sub = dict(k=ka[:, r, t4 * 128:(t4 + 1) * 128], kb=kab, q=q_lat, qb=q_latb,
                                           v=va_[:, r, t4 * 128:(t4 + 1) * 128], vb=vab_, c0=0, n=T)
                                if isA:
                                    nbt, nbb = nb_ring()
                                    fw.dma("sp", nbt[:], nbias_d[l, h, kb], writes=[nbb])
                                    sub["bias"] = nbt[:]
                                    sub["bb"] = nbb
                                groups.append([sub])
                            for kb in range(4):
                                groups.append([dict(k=kc_[:, kb * 128:(kb + 1) * 128], kb=kcb_, q=q_ctx, qb=q_ctxb,
                                                    v=vc_[:, kb, :], vb=vcb_, c0=0, n=T)])
                            return groups
                        if isA:
                            attention(mkgroups(qta[:, h, :], B_qta, qta[:, h, :], B_qta), 128.0 ** -0.5, et_ring, bt_ring, fin_A(h))
                        else:
                            ys = []
                            for (qr, qrb, qp, qpb) in ((q1r, B_q1r, q1p, B_q1p), (q2r, B_q2r, q2p, B_q2p)):
                                yt, ytb = yb_ring()
                                attention(mkgroups(qr[:, h, :], qrb, qp[:, h, :], qpb), 64.0 ** -0.5, et_ring, bt_ring, fin_B(yt, ytb))
                                ys.append((yt, ytb))
                            diff_finish(h, ys[0][0], ys[0][1], ys[1][0], ys[1][1])
                fw.barrier()

            kstop("c6")
            with ExitStack() as so:
                residual_proj(l, w_out[l], 16, 512, lambda c: ycat[:, c, :], [B_y], 2, row, so, "o")
                layernorm(l, 0, so)
                fw.barrier()
        kstop("c7")
        modulate(l, 3, 4, row)
        with ExitStack() as sf:
            actb, B_act = P.sb(sf, "actb", [128, 43, T], BF16)
            u_ring = P.ring(sf, "u1", 4, [128, T], F32)
            hal, B_hal = P.sb(sf, "hal", [128, 2, 2], F32)
            if g == "S":
                hbs, B_hbs = P.sb(sf, "hbs", [128, 16, 2], F32)
                hba, B_hba = P.sb(sf, "hba", [128, 4, 32], F32)
                hbf, B_hbf = P.sb(sf, "hbf", [128, 16, 2], F32)
                A(lambda: nc.scalar.copy(out=hbs[:, :, 0], in_=h_sb[:, :, 0]), [B_h], [B_hbs])
                A(lambda: nc.scalar.copy(out=hbs[:, :, 1], in_=h_sb[:, :, T - 1]), [B_h], [B_hbs])
                fw.dma("sp", hb_in, hbs[:].rearrange("p c k -> p (c k)"), reads=[B_hbs], writes=[B_hbi])
                fw.allgather(hb_in, hb_out, reads=[B_hbi], writes=[B_hbo])
                fw.dma("sp", hba[:], hb_out.rearrange("(r p) f -> p r f", p=128), reads=[B_hbo], writes=[B_hba])
                hv = hba[:].rearrange("p r (c k) -> p r c k", k=2)
                for (side, kk, so_) in ((0, 1, 0), (1, 0, 4)):
                    A(lambda: nc.scalar.activation(out=hbf[:, :, side], in_=hv[:, 0, :, kk], func=AF.Identity, scale=sel[:, so_:so_ + 1]), [B_hba, B_c], [B_hbf])
                    for r in range(1, 4):
                        V(lambda: nc.vector.scalar_tensor_tensor(out=hbf[:, :, side], in0=hv[:, r, :, kk], scalar=sel[:, so_ + r:so_ + r + 1],
                                                                 in1=hbf[:, :, side], op0=ALU.mult, op1=ALU.add), [B_hba, B_c, B_hbf], [B_hbf])
                A(lambda: nc.scalar.copy(out=hh_sb[:], in_=hbf[:]), [B_hbf], [B_hh])
            segs = [(0, 256), (256, 256)] if g == "P" else [(0, 512)]

            def conv_chunk(ps, pb, hps, hpb, ch):
                u, ub = u_ring()
                cp = convp[:, l, ch, :]
                A(lambda: nc.scalar.activation(out=u[:], in_=ps[:], func=AF.Identity, scale=cp[:, 1:2], bias=cp[:, 3:4]), [pb, B_c], [ub])
                for (s0_, n) in segs:
                    V(lambda: nc.vector.scalar_tensor_tensor(out=u[:, s0_ + 1:s0_ + n], in0=ps[:, s0_:s0_ + n - 1], scalar=cp[:, 0:1],
                                                             in1=u[:, s0_ + 1:s0_ + n], op0=ALU.mult, op1=ALU.add), [pb, B_c, ub], [ub])
                    V(lambda: nc.vector.scalar_tensor_tensor(out=u[:, s0_:s0_ + n - 1], in0=ps[:, s0_ + 1:s0_ + n], scalar=cp[:, 2:3],
                                                             in1=u[:, s0_:s0_ + n - 1], op0=ALU.mult, op1=ALU.add), [pb, B_c, ub], [ub])
                if g == "S":
                    V(lambda: nc.vector.scalar_tensor_tensor(out=u[:, 0:1], in0=hps[:, 0:1], scalar=cp[:, 0:1], in1=u[:, 0:1],
                                                             op0=ALU.mult, op1=ALU.add), [hpb, B_c, ub], [ub])
                    V(lambda: nc.vector.scalar_tensor_tensor(out=u[:, T - 1:T], in0=hps[:, 1:2], scalar=cp[:, 2:3], in1=u[:, T - 1:T],
                                                             op0=ALU.mult, op1=ALU.add), [hpb, B_c, ub], [ub])
                return u, ub

            with extra_slots(2):
                for ti in range(11):
                    ncol = 512 if ti < 10 else 384
                    wa, wab = wload(w_up[l, :, ti * 512:ti * 512 + ncol], 16, ncol)
                    wg, wgb = wload(w_up[l, :, DFF + ti * 512:DFF + ti * 512 + ncol], 16, ncol)
                    for q in range(ncol // 128):
                        j = ti * 4 + q
                        res = []
                        for (wv, wb, ch) in ((wa, wab, j), (wg, wgb, 43 + j)):
                            ps, pb = ps_short()
                            for c in range(16):
                                mm(ps[:], wv[:, c, q * 128:(q + 1) * 128], h_sb[:, c, :], c == 0, c == 15, [wb, B_h], [pb])
                            hps, hpb = None, None
                            if g == "S":
                                hps, hpb = ps_short()
                                for c in range(16):
                                    mm(hps[:, 0:2], wv[:, c, q * 128:(q + 1) * 128], hh_sb[:, c, :], c == 0, c == 15, [wb, B_hh], [hpb])
                            res.append(conv_chunk(ps, pb, hps, hpb, ch))
                        (ua, uab), (ug, ugb) = res
                        A(lambda: nc.scalar.activation(out=ug[:], in_=ug[:], func=AF.Silu), [ugb], [ugb])
                        V(lambda: nc.vector.tensor_tensor(out=actb[:, j, :], in0=ug[:], in1=ua[:], op=ALU.mult), [ugb, uab], [B_act])
            residual_proj(l, w_down[l], 43, 128, lambda c: actb[:, c, :], [B_act], 5, row, sf, "d")
            layernorm(l, 1, sf)
            fw.barrier()
        if l == L - 1:
            fw.dma("sp", yout[g].rearrange("(c p) t -> p c t", p=128), x_sb[:], reads=[B_x], writes=[B_out])
        else:
            fw.dma("sp", xspill[g].rearrange("(c p) t -> p c t", p=128), x_sb[:], reads=[B_x], writes=[B_spill[g]])

    nblk = 0
    try:
        for l in range(L):
            for g in ("P", "S"):
                if KSTOP.startswith("b") and nblk >= int(KSTOP[1]):
                    break
                if KSTOP.startswith("c") and nblk >= 1:
                    break
                block(l, g)
                nblk += 1
    except StopBuild:
        fw.barrier()
        fw.finish()
        return P
    fw.finish()
    P.es.close()
    fw.close()
    return P


def _consts():
    c = np.zeros((NCONST, 128, 128), np.float32)
    idx = np.arange(128)
    c[C_ID] = np.eye(128)
    c[C_ONE] = 1.0
    same = (idx[:, None] // 64) == (idx[None, :] // 64)
    c[C_TRIF] = (same & (idx[:, None] <= idx[None, :]))
    c[C_TRIB] = (same & (idx[:, None] >= idx[None, :]))
    for k, p in ((C_E0, 0), (C_E63, 63), (C_E64, 64), (C_E127, 127)):
        c[k][p, :] = 1.0
    dd = idx % 32
    partner = np.where(dd < 16, idx + 16, idx - 16)
    c[C_PERM][partner, idx] = 1.0
    negm = np.zeros((2, 128, 5, 128), np.float32)
    negm[0] = np.where(c[C_TRIF].T[:, None, :] > 0, 0.0, NEG)
    negm[1] = np.where(c[C_TRIB].T[:, None, :] > 0, 0.0, NEG)
    return (np.ascontiguousarray(c.transpose(1, 0, 2)).reshape(128, NCONST * 128),
            np.ascontiguousarray(negm.transpose(1, 0, 2, 3)).reshape(128, 2 * 640))


def _rope_tables(j):
    t = np.arange(T) + j * T
    rows = (t // 64).astype(np.float32)
    cols = (t % 64).astype(np.float32)
    freqs = (np.float32(10000.0) ** (-np.arange(0, 32, 2, dtype=np.float32) / np.float32(32))).astype(np.float32)
    p = np.arange(128)
    dd = p % 64
    idx = dd % 32
    f = idx % 16
    first = idx < 16
    pos = np.where((dd < 32)[:, None], rows[None, :], cols[None, :]).astype(np.float32)
    ang = (pos * freqs[f][:, None]).astype(np.float32)
    cos = np.cos(ang).astype(np.float32)
    sin = np.sin(ang).astype(np.float32)
    sins = np.where(first[:, None], -sin, sin).astype(np.float32)
    return np.concatenate([cos, sins], axis=1)


def _natten_bias(a_rpb, j):
    kt = np.arange(2048)
    krow, kcol = kt // 64, kt % 64
    qt = np.arange(T) + j * T
    qrow, qcol = qt // 64, qt % 64
    rs = np.clip(qrow - 4, 0, 24)
    cs = np.clip(qcol - 8, 0, 48)
    vr = (krow[:, None] >= rs[None, :]) & (krow[:, None] < rs[None, :] + 8)
    vcm = (kcol[:, None] >= cs[None, :]) & (kcol[:, None] < cs[None, :] + 16)
    valid = vr & vcm
    ri = np.clip(7 + krow[:, None] - qrow[None, :], 0, 14)
    ci = np.clip(kcol[:, None] - qcol[None, :] + 15, 0, 30)
    out = np.empty((L, 6, 2048, T), np.float32)
    for l in range(L):
        for h in range(6):
            out[l, h] = np.where(valid, a_rpb[l, h][ri, ci], np.float32(NEG))
    return out.reshape(L, 6, 16, 128, T)


def _combine_tables(j):
    cf = np.zeros((2, 5, 4), np.float32)
    vt = np.zeros((2, 5), np.float32)
    for r2 in range(4):
        if r2 < j:
            cf[0, 0, r2] = 1
        if r2 > j:
            cf[1, 0, r2] = 1
    for r in range(4):
        vt[0, 1 + r] = 0.0 if r < j else NEG
        vt[1, 1 + r] = 0.0 if r > j else NEG
        for r2 in range(4):
            if r < r2 < j:
                cf[0, 1 + r, r2] = 1
            if j < r2 < r:
                cf[1, 1 + r, r2] = 1
    cft = np.broadcast_to(cf[None, :, :, :, None], (128, 2, 5, 4, 5)).reshape(128, -1)
    vtt = np.broadcast_to(vt[None, :, :, None], (128, 2, 5, 5)).reshape(128, -1)
    sel = np.zeros((8,), np.float32)
    if j > 0:
        sel[j - 1] = 1
    if j < 3:
        sel[4 + j + 1] = 1
    return np.ascontiguousarray(cft), np.ascontiguousarray(vtt), np.ascontiguousarray(np.broadcast_to(sel[None], (128, 8)))


_PROG = None


def kernel(x_prompt, x_sample, cache_a_k, cache_a_v, cache_b_k, cache_b_v, state_c_C, state_c_n, state_c_m,
           c, c_ctx, w_mod, b_mod, w_in, c_gate_b, a_rpb, b_lambda, b_subln, c_norm, w_out,
           ln1_g, ln1_b, ln2_g, ln2_b, w_up, conv_w, conv_b, w_down):
    global _PROG
    f = lambda a: np.ascontiguousarray(np.asarray(a, dtype=np.float32))
    x_prompt, x_sample = f(x_prompt), f(x_sample)
    if _PROG is None:
        _PROG = build_program()
    P = _PROG
    consts, negm = _consts()
    lnp = np.stack([f(ln1_g), f(ln1_b), f(ln2_g), f(ln2_b)], 1).reshape(L, 4, 16, 128).transpose(3, 0, 1, 2).reshape(128, -1)
    cw = np.concatenate([f(conv_w), f(conv_b)[:, None, :]], 1)
    convp = cw.reshape(L, 4, 86, 128).transpose(3, 0, 2, 1).reshape(128, -1)
    shared = {
        "w_in": f(w_in), "w_out": f(w_out), "w_up": f(w_up), "w_down": f(w_down),
        "lnp": np.ascontiguousarray(lnp), "convp": np.ascontiguousarray(convp),
        "gateb": f(c_gate_b).reshape(-1), "blam": f(b_lambda).reshape(-1),
        "subln": np.ascontiguousarray(f(b_subln).T), "cnorm": f(c_norm).reshape(-1),
        "consts": consts, "negm": negm,
    }
    w_mod, b_mod = f(w_mod), f(b_mod)
    in_maps = []
    for i in range(8):
        b, j = i // 4, i % 4
        m = dict(shared)
        m["xpT"] = np.ascontiguousarray(x_prompt[2 * i:2 * i + 2].reshape(T, D).T)
        m["xsT"] = np.ascontiguousarray(x_sample[b, j * T:(j + 1) * T].T)
        m["wmod"] = np.ascontiguousarray(w_mod[:, :, j * 3072:(j + 1) * 3072])
        m["bmod"] = np.ascontiguousarray(b_mod[:, j * 3072:(j + 1) * 3072].reshape(L, 24, 128).transpose(2, 0, 1).reshape(128, -1))
        cond = np.stack([f(c_ctx), f(c)[b]], 1)
        m["cond2"] = np.ascontiguousarray(cond.reshape(16, 128, 2).transpose(1, 0, 2).reshape(128, 32))
        m["rope"] = _rope_tables(j)
        m["nbias"] = _natten_bias(f(a_rpb), j)
        m["cakT"] = np.ascontiguousarray(f(cache_a_k)[b].transpose(0, 1, 3, 2))
        m["cav"] = f(cache_a_v)[b]
        m["cbkT"] = np.ascontiguousarray(f(cache_b_k)[b].transpose(0, 1, 3, 2))
        m["cbv"] = f(cache_b_v)[b]
        m["cC"] = f(state_c_C)[b]
        m["cn"] = np.ascontiguousarray(f(state_c_n)[b].reshape(L, 10, 128).transpose(0, 2, 1))
        m["cm"] = f(state_c_m)[b].reshape(-1)
        m["cftab"], m["vtab"], m["sel"] = _combine_tables(j)
        in_maps.append(m)
    in_maps = [{k: v for k, v in m.items() if k in P.din} for m in in_maps]
    res = run_bass_kernel_spmd(P.nc, in_maps, core_ids=list(range(8))).results
    yp = np.stack([r["ypT"].T.reshape(2, 256, D) for r in res], 0).reshape(16, 256, D)
    ys = np.stack([r["ysT"].T for r in res], 0).reshape(2, 4 * T, D)
    cat = lambda k: np.concatenate([r[k] for r in res], 0)
    n_ak = np.ascontiguousarray(cat("o_ak").transpose(0, 1, 2, 4, 3))
    n_av = cat("o_av")
    n_bk = np.ascontiguousarray(cat("o_bk").transpose(0, 1, 2, 4, 3))
    n_bv = cat("o_bv")
    return (np.ascontiguousarray(yp, dtype=np.float32), np.ascontiguousarray(ys, dtype=np.float32), n_ak, n_av, n_bk, n_bv,
            cat("o_cC"), np.ascontiguousarray(cat("o_cn").transpose(0, 1, 2, 4, 3)), cat("o_cm"))
```

```python
import math
import os
KSTOP = os.environ.get('KSTOP', '')
KSKIP = os.environ.get('KSKIP', '')
SKIP_IN = set()
if KSTOP.startswith('c'):
    SKIP_IN = {"nbias", "cakT", "cav", "cbkT", "cbv", "cC", "cn", "cm", "xsT"}
    if not KSTOP[1].isdigit() or int(KSTOP[1]) < 8:
        SKIP_IN |= {"w_up", "w_down"}
    if not KSTOP[1].isdigit() or int(KSTOP[1]) < 7:
        SKIP_IN |= {"w_out"}


class StopBuild(Exception):
    pass


def kstop(tag):
    if KSTOP == tag:
        raise StopBuild(tag)
from contextlib import ExitStack
import numpy as np
import ml_dtypes
import concourse.bass as bass
import concourse.mybir as mybir
from concourse.bass_utils import run_bass_kernel_spmd

F32 = mybir.dt.float32
BF16 = mybir.dt.bfloat16
AF = mybir.ActivationFunctionType
ALU = mybir.AluOpType
AX = mybir.AxisListType

L = 2
D = 2048
T = 512
DFF = 5504
NEG = -30000.0
ALPHA = (2 * L) ** 0.25
LN_EPS = 1e-5
RMS_EPS = 1e-6
RG = [[0, 1, 2, 3], [4, 5, 6, 7]]
NCONST = 11
C_ID, C_ONE, C_TRIF, C_TRIB, C_E0, C_E63, C_E64, C_E127, C_PERM, C_HA, C_HB = range(NCONST)


class Buf:
    __slots__ = ("name", "w", "r")

    def __init__(self, name=""):
        self.name = name
        self.w = None
        self.r = []


class FW:
    NDMA = 48

    def __init__(self, nc):
        self.nc = nc
        self.eng = {"pe": nc.tensor, "act": nc.scalar, "dve": nc.vector, "pool": nc.gpsimd, "sp": nc.sync}
        self.sem, self.cnt, self._stack = {}, {}, []
        for e in self.eng:
            cm = nc.semaphore("s_" + e)
            self.sem[e] = cm.__enter__()
            self._stack.append(cm)
            self.cnt[e] = 0
        self.dsem, self.dcnt = [], []
        for i in range(self.NDMA):
            cm = nc.semaphore("d_%d" % i)
            self.dsem.append(cm.__enter__())
            self._stack.append(cm)
            self.dcnt.append(0)
        self.dnext = 0
        self.seen = {e: {} for e in self.eng}
        self.ninst = 0
        self.nwaits = 0

    def close(self):
        for cm in reversed(self._stack):
            cm.__exit__(None, None, None)

    def _semobj(self, key):
        return self.sem[key] if isinstance(key, str) else self.dsem[key]

    def _wait(self, e, tok):
        if tok is None:
            return
        key, val = tok
        if self.seen[e].get(key, 0) >= val:
            return
        self.eng[e].wait_ge(self._semobj(key), val)
        self.seen[e][key] = val
        self.nwaits += 1

    def _deps(self, e, reads, writes, is_dma=False):
        for b in reads:
            if b.w is not None:
                if b.w[0] == e and (e == "pe" or is_dma):
                    continue
                self._wait(e, b.w)
        skip_same = (e == "pe" or is_dma)
        for b in writes:
            if b.w is not None and not (skip_same and b.w[0] == e):
                self._wait(e, b.w)
            for t in b.r:
                if not (skip_same and t[0] == e):
                    self._wait(e, t)

    def _commit(self, tok, reads, writes):
        for b in reads:
            b.r.append(tok)
            if len(b.r) > 16:
                best = {}
                for k, v in b.r:
                    if best.get(k, 0) < v:
                        best[k] = v
                b.r = list(best.items())
        for b in writes:
            b.w = tok
            b.r = []

    def op(self, e, fn, reads=(), writes=(), inc=True):
        self._deps(e, reads, writes)
        ins = fn()
        self.ninst += 1
        if inc:
            self.cnt[e] += 1
            ins.then_inc(self.sem[e], 1)
            tok = (e, self.cnt[e])
        else:
            tok = (e, self.cnt[e] + 1)
        self._commit(tok, reads, writes)
        return tok

    def _next_dsem(self, q, kind=None):
        kind = kind or q
        lo, hi = {"sp": (0, 32), "pool": (32, 44), "cc": (44, 48)}[kind]
        if not hasattr(self, "dnx"):
            self.dnx = {}
        i = self.dnx.get(kind, lo)
        self.dnx[kind] = lo + (i + 1 - lo) % (hi - lo)
        if self.dcnt[i] > 0:
            self._wait(q, (i, self.dcnt[i]))
        return i

    def dma(self, q, out, in_, reads=(), writes=(), slow=False):
        self._deps(q, reads, writes, is_dma=True)
        i = self._next_dsem(q)
        if slow:
            ins = self.eng[q].dma_start(out=out, in_=in_, allow_slow_non_contiguous=True)
        else:
            ins = self.eng[q].dma_start(out=out, in_=in_)
        self.dcnt[i] += 16
        ins.then_inc(self.dsem[i], 16)
        self.ninst += 1
        tok = (i, self.dcnt[i])
        self._commit(tok, reads, writes)
        return tok

    def allgather(self, in_ap, out_ap, reads=(), writes=()):
        q = "pool"
        self._deps(q, reads, writes, is_dma=True)
        i = self._next_dsem(q, "cc")
        ins = self.nc.gpsimd.collective_compute("AllGather", ALU.bypass, replica_groups=RG, ins=[in_ap], outs=[out_ap])
        self.dcnt[i] += 1
        ins.then_inc(self.dsem[i], 1)
        self.ninst += 1
        tok = (i, self.dcnt[i])
        self._commit(tok, reads, writes)
        return tok

    def soft_barrier(self):
        for e in self.eng:
            if e != "pe" and self.cnt["pe"] > 0:
                self._wait(e, ("pe", self.cnt["pe"]))
            for i in range(32, 44):
                if self.dcnt[i] > 0:
                    self._wait(e, (i, self.dcnt[i]))

    def barrier(self):
        for e in self.eng:
            for f in self.eng:
                if f != e and self.cnt[f] > 0:
                    self._wait(e, (f, self.cnt[f]))
            for i in range(self.NDMA):
                if self.dcnt[i] > 0:
                    self._wait(e, (i, self.dcnt[i]))

    def finish(self):
        for i in range(self.NDMA):
            if self.dcnt[i] > 0:
                self._wait("sp", (i, self.dcnt[i]))


class Prog:
    def __init__(self):
        nc = bass.Bass("TRN2", target_bir_lowering=False)
        self.nc = nc
        self.fw = FW(nc)
        self.es = ExitStack()
        self.ring_idx = {}
        self.din = {}
        self.dout = {}

    def inp(self, name, shape, dt=F32):
        if name in SKIP_IN:
            return None
        t = self.nc.dram_tensor(name, list(shape), dt, kind="ExternalInput").ap()
        self.din[name] = t
        return t

    def outp(self, name, shape, dt=F32):
        t = self.nc.dram_tensor(name, list(shape), dt, kind="ExternalOutput").ap()
        self.dout[name] = t
        return t

    def scratch(self, name, shape, dt=F32):
        return self.nc.dram_tensor(name, list(shape), dt).ap()

    def sb(self, es, name, shape, dt=F32):
        self.uid = getattr(self, "uid", 0) + 1
        t = es.enter_context(self.nc.sbuf_tensor("%s_%d" % (name, self.uid), list(shape), dt))
        return t, Buf(name)

    def ring(self, es, name, n, shape, dt=F32):
        items = [self.sb(es, "%s%d" % (name, i), shape, dt) for i in range(n)]
        key = name
        self.ring_idx[key] = 0

        def nxt():
            i = self.ring_idx[key]
            self.ring_idx[key] = (i + 1) % n
            return items[i]
        return nxt

    def V(self, fn, reads=(), writes=()):
        return self.fw.op("dve", fn, reads, writes)

    def A(self, fn, reads=(), writes=()):
        return self.fw.op("act", fn, reads, writes)

    def G(self, fn, reads=(), writes=()):
        return self.fw.op("pool", fn, reads, writes)

    def PE(self, fn, reads=(), writes=(), inc=True):
        return self.fw.op("pe", fn, reads, writes, inc=inc)

    def mm(self, out, lhsT, rhs, start, stop, reads, writes, inc=None, sgc=False):
        nc = self.nc
        return self.fw.op("pe", lambda: nc.tensor.matmul(out, lhsT=lhsT, rhs=rhs, start=start, stop=stop, skip_group_check=sgc),
                          reads, writes, inc=(stop if inc is None else inc))


def build_program():
    P = Prog()
    nc, fw = P.nc, P.fw
    V, A, PE, mm, G = P.V, P.A, P.PE, P.mm, P.G

    xin = {"P": P.inp("xpT", [D, T]), "S": P.inp("xsT", [D, T])}
    w_in = P.inp("w_in", [L, D, 6804])
    w_out = P.inp("w_out", [L, D, D])
    w_up = P.inp("w_up", [L, D, 2 * DFF])
    w_down = P.inp("w_down", [L, DFF, D])
    wmod = P.inp("wmod", [L, D, 3072])
    bmod = P.inp("bmod", [128, L * 24])
    cond2 = P.inp("cond2", [128, 32])
    lnp_d = P.inp("lnp", [128, L * 4 * 16])
    convp_d = P.inp("convp", [128, L * 86 * 4])
    gateb_d = P.inp("gateb", [L * 20])
    blam_d = P.inp("blam", [L * 256])
    subln_d = P.inp("subln", [128, L])
    cnorm_d = P.inp("cnorm", [L * 128])
    consts_d = P.inp("consts", [128, NCONST * 128])
    negm_d = P.inp("negm", [128, 2 * 640])
    rope_d = P.inp("rope", [128, 2 * T])
    nbias_d = P.inp("nbias", [L, 6, 16, 128, T])
    cakT = P.inp("cakT", [L, 6, 128, 512])
    cav = P.inp("cav", [L, 6, 512, 128])
    cbkT = P.inp("cbkT", [L, 5, 128, 512])
    cbv = P.inp("cbv", [L, 5, 512, 128])
    cC_d = P.inp("cC", [L, 2, 5, 128, 128])
    cn_d = P.inp("cn", [L, 128, 10])
    cm_d = P.inp("cm", [L * 10])
    cftab_d = P.inp("cftab", [128, 2 * 5 * 4 * 5])
    vtab_d = P.inp("vtab", [128, 2 * 5 * 5])
    sel_d = P.inp("sel", [128, 8])

    yout = {"P": P.outp("ypT", [D, T]), "S": P.outp("ysT", [D, T])}
    o_ak = P.outp("o_ak", [2, L, 6, 128, 256])
    o_av = P.outp("o_av", [2, L, 6, 256, 128])
    o_bk = P.outp("o_bk", [2, L, 5, 128, 256])
    o_bv = P.outp("o_bv", [2, L, 5, 256, 128])
    o_cC = P.outp("o_cC", [2, L, 2, 5, 128, 128])
    o_cn = P.outp("o_cn", [2, L, 2, 128, 5])
    o_cm = P.outp("o_cm", [2, L, 2, 5])
    B_out = Buf("outputs")

    xspill = {"P": P.scratch("xspP", [D, T]), "S": P.scratch("xspS", [D, T])}
    B_spill = {"P": Buf(), "S": Buf()}
    mg_in = P.scratch("mg_in", [128, 96]); mg_out = P.scratch("mg_out", [512, 96])
    B_mgi, B_mgo = Buf(), Buf()
    CH_NH = [4, 4, 3]
    HH_CH = [0, 0, 0, 0, 1, 1, 1, 1, 2, 2, 2]
    HH_IX = [0, 1, 2, 3, 0, 1, 2, 3, 0, 1, 2]
    bnc_in = [P.scratch("bnc_in%d" % i, [2 * n * 128, T], BF16) for i, n in enumerate(CH_NH)]
    bnc_out = [P.scratch("bnc_out%d" % i, [4 * 2 * n * 128, T], BF16) for i, n in enumerate(CH_NH)]
    B_bi = [Buf() for _ in CH_NH]
    B_bo = [Buf() for _ in CH_NH]

    def bnc_k_rows(hh):
        i = HH_IX[hh]
        return bnc_in[HH_CH[hh]][i * 128:(i + 1) * 128, :], B_bi[HH_CH[hh]]

    def bnc_v_rows(hh):
        c = HH_CH[hh]
        i = CH_NH[c] + HH_IX[hh]
        return bnc_in[c][i * 128:(i + 1) * 128, :], B_bi[c]
    CSF = 1310
    cs_in = P.scratch("cs_in", [128, CSF]); cs_out = P.scratch("cs_out", [512, CSF])
    B_csi, B_cso = Buf(), Buf()
    hb_in = P.scratch("hb_in", [128, 32]); hb_out = P.scratch("hb_out", [512, 32])
    B_hbi, B_hbo = Buf(), Buf()

    es = P.es
    x_sb, B_x = P.sb(es, "x_sb", [128, 16, T], F32)
    h_sb, B_h = P.sb(es, "h_sb", [128, 16, T], BF16)
    hh_sb, B_hh = P.sb(es, "hh_sb", [128, 16, 2], BF16)
    WSL = 8704
    wslots = [P.sb(es, "wr%d" % i, [128, WSL], BF16) for i in range(2)]
    wstate = {"i": 0}

    def wring():
        i = wstate["i"] % len(wslots)
        wstate["i"] += 1
        return wslots[i]

    class extra_slots:
        def __init__(self, want, reserve=2048):
            self.want, self.reserve = want, reserve

        def __enter__(self):
            self.sx = ExitStack()
            self.n = 0
            while self.n < self.want and nc.sbuf_bytes_remaining >= WSL * 2 + self.reserve + 256:
                wslots.append(P.sb(self.sx, "wx", [128, WSL], BF16))
                self.n += 1
            return self

        def __exit__(self, *a):
            if a[0] is None:
                fw.soft_barrier()
                for _ in range(self.n):
                    wslots.pop()
                self.sx.close()
            return False
    cst, B_c = P.sb(es, "cst", [128, NCONST, 128], F32)
    negm, _ = P.sb(es, "negm", [128, 2, 640], F32)
    rope, _ = P.sb(es, "rope", [128, 2, T], F32)
    ones_bf, _ = P.sb(es, "ones_bf", [128, 128], BF16)
    id_bf, _ = P.sb(es, "id_bf", [128, 128], BF16)
    tri_bf, _ = P.sb(es, "tri_bf", [128, 2, 128], BF16)
    lnp, _ = P.sb(es, "lnp", [128, L, 4, 16], F32)
    convp, _ = P.sb(es, "convp", [128, L, 86, 4], F32)
    gateb, _ = P.sb(es, "gateb", [128, L, 20], F32)
    blam, _ = P.sb(es, "blam", [128, L, 4, 64], F32)
    subln, _ = P.sb(es, "subln", [128, L], F32)
    cnorm, _ = P.sb(es, "cnorm", [128, L, 128], F32)
    cftab, _ = P.sb(es, "cftab", [128, 2, 5, 4, 5], F32)
    vtab, _ = P.sb(es, "vtab", [128, 2, 5, 5], F32)
    sel, _ = P.sb(es, "sel", [128, 8], F32)
    modv, B_modv = P.sb(es, "modv", [128, L, 96, 2], F32)
    nlam, B_nlam = P.sb(es, "nlam", [128, L], F32)
    sublns, _ = P.sb(es, "sublns", [128, L], F32)

    def CM(i):
        return cst[:, i, :]

    pbanks = [es.enter_context(nc.psum_tensor("ps%d" % i, [128, 512], F32)) for i in range(8)]
    pbufs = [Buf("ps%d" % i) for i in range(8)]
    pidx = {"s": 0, "l": 0}

    def ps_short():
        i = pidx["s"]
        pidx["s"] = (i + 1) % 5
        return pbanks[i], pbufs[i]

    def ps_long():
        i = pidx["l"]
        pidx["l"] = (i + 1) % 3
        return pbanks[5 + i], pbufs[5 + i]

    def bcast(ap1d, n):
        return bass.AP(ap1d.tensor, 0, [[0, 128], [1, n]])

    fw.dma("sp", cst[:], consts_d.rearrange("p (k n) -> p k n", k=NCONST), writes=[B_c])
    fw.dma("sp", negm[:], negm_d.rearrange("p (k n) -> p k n", k=2), writes=[B_c])
    fw.dma("sp", rope[:], rope_d.rearrange("p (k n) -> p k n", k=2), writes=[B_c])
    fw.dma("sp", lnp[:], lnp_d.rearrange("p (l k c) -> p l k c", l=L, k=4), writes=[B_c])
    fw.dma("sp", convp[:], convp_d.rearrange("p (l c k) -> p l c k", l=L, k=4), writes=[B_c])
    fw.dma("sp", gateb[:], bcast(gateb_d, L * 20).rearrange("p (l k) -> p l k", l=L), writes=[B_c])
    fw.dma("sp", blam[:], bcast(blam_d, L * 256).rearrange("p (l k c) -> p l k c", l=L, k=4), writes=[B_c])
    fw.dma("sp", subln[:], subln_d, writes=[B_c])
    fw.dma("sp", cnorm[:], bcast(cnorm_d, L * 128).rearrange("p (l k) -> p l k", l=L), writes=[B_c])
    fw.dma("sp", cftab[:], cftab_d.rearrange("p (d i r h) -> p d i r h", d=2, i=5, r=4), writes=[B_c])
    fw.dma("sp", vtab[:], vtab_d.rearrange("p (d i h) -> p d i h", d=2, i=5), writes=[B_c])
    fw.dma("sp", sel[:], sel_d, writes=[B_c])
    A(lambda: nc.scalar.copy(out=ones_bf[:], in_=CM(C_ONE)), [B_c], [B_c])
    A(lambda: nc.scalar.copy(out=id_bf[:], in_=CM(C_ID)), [B_c], [B_c])
    A(lambda: nc.scalar.copy(out=tri_bf[:, 0, :], in_=CM(C_TRIF)), [B_c], [B_c])
    A(lambda: nc.scalar.copy(out=tri_bf[:, 1, :], in_=CM(C_TRIB)), [B_c], [B_c])

    def wload(src2d, kc, ncols):
        t, b = wring()
        view = t[:, 0:kc * ncols].rearrange("p (c n) -> p c n", n=ncols)
        srcv = src2d.rearrange("(c p) n -> p c n", p=128)
        step = max(1, 2048 // 128 // 1 if ncols >= 256 else 8)
        step = 16 if ncols >= 256 else 22
        for c0 in range(0, kc, step):
            c1 = min(kc, c0 + step)
            fw.dma("pool", view[:, c0:c1, :], srcv[:, c0:c1, :], writes=[b])
        return view, b

    with ExitStack() as s0:
        c2, B_c2 = P.sb(s0, "c2", [128, 16, 2], F32)
        c2b, _ = P.sb(s0, "c2b", [128, 16, 2], BF16)
        bm, B_bm = P.sb(s0, "bm", [128, L, 24], F32)
        mloc, B_ml = P.sb(s0, "mloc", [128, L, 24, 2], F32)
        mall, B_ma = P.sb(s0, "mall", [128, 4, L, 24, 2], F32)
        fw.dma("sp", c2[:], cond2.rearrange("p (c r) -> p c r", r=2), writes=[B_c2])
        fw.dma("sp", bm[:], bmod.rearrange("p (l c) -> p l c", l=L), writes=[B_bm])
        A(lambda: nc.scalar.activation(out=c2b[:], in_=c2[:], func=AF.Silu), [B_c2], [B_c2])
        with extra_slots(3):
            for l in range(L):
                for t4 in range(6):
                    wv, wb = wload(wmod[l, :, t4 * 512:(t4 + 1) * 512], 16, 512)
                    for q in range(4):
                        cc = t4 * 4 + q
                        ps, pb = ps_short()
                        for c in range(16):
                            mm(ps[:, 0:2], wv[:, c, q * 128:(q + 1) * 128], c2b[:, c, :], c == 0, c == 15, [wb, B_c2], [pb])
                        A(lambda: nc.scalar.activation(out=mloc[:, l, cc, :], in_=ps[:, 0:2], func=AF.Identity,
                                                       bias=bm[:, l, cc:cc + 1], scale=1.0), [pb, B_bm], [B_ml])
        fw.dma("sp", mg_in, mloc[:].rearrange("p l c r -> p (l c r)"), reads=[B_ml], writes=[B_mgi])
        fw.allgather(mg_in, mg_out, reads=[B_mgi], writes=[B_mgo])
        fw.dma("sp", mall[:].rearrange("p r l c w -> p r (l c w)"), mg_out.rearrange("(r p) f -> p r f", p=128),
               reads=[B_mgo], writes=[B_ma])
        for r in range(4):
            for l in range(L):
                A(lambda: nc.scalar.copy(out=modv[:, l, r * 24:(r + 1) * 24, :], in_=mall[:, r, l, :, :]), [B_ma], [B_modv])
        for l in range(L):
            for v0 in (16, 64):
                A(lambda: nc.scalar.activation(out=modv[:, l, v0:v0 + 16, :], in_=modv[:, l, v0:v0 + 16, :], func=AF.Identity, bias=1.0, scale=1.0),
                  [B_modv], [B_modv])
        lt, B_lt = P.sb(s0, "lt", [128, 64], F32)
        ld, B_ld = P.sb(s0, "ld", [128, 4], F32)
        for l in range(L):
            lam_init = 0.8 - 0.6 * math.exp(-0.3 * l)
            for k in range(2):
                V(lambda: nc.vector.tensor_tensor(out=lt[:], in0=blam[:, l, 2 * k, :], in1=blam[:, l, 2 * k + 1, :], op=ALU.mult), [B_c], [B_lt])
                V(lambda: nc.vector.reduce_sum(out=ld[:, k:k + 1], in_=lt[:], axis=AX.X), [B_lt], [B_ld])
            A(lambda: nc.scalar.activation(out=ld[:, 2:4], in_=ld[:, 0:2], func=AF.Exp), [B_ld], [B_ld])
            V(lambda: nc.vector.tensor_tensor(out=nlam[:, l:l + 1], in0=ld[:, 3:4], in1=ld[:, 2:3], op=ALU.subtract), [B_ld], [B_nlam])
            A(lambda: nc.scalar.activation(out=nlam[:, l:l + 1], in_=nlam[:, l:l + 1], func=AF.Identity, bias=-lam_init, scale=1.0), [B_nlam], [B_nlam])
            A(lambda: nc.scalar.activation(out=sublns[:, l:l + 1], in_=subln[:, l:l + 1], func=AF.Identity, scale=1.0 - lam_init), [B_c], [B_nlam])
        fw.barrier()

    def modp(l, v, fc, row):
        return modv[:, l, v * 16 + fc, row:row + 1]

    def modulate(l, vsh, vsc, row):
        for c in range(16):
            A(lambda: nc.scalar.activation(out=h_sb[:, c, :], in_=x_sb[:, c, :], func=AF.Identity,
                                           bias=modp(l, vsh, c, row), scale=modp(l, vsc, c, row)), [B_x, B_modv], [B_h])

    def layernorm(l, k, scope):
        sq_ring = P.ring(scope, "lnsq%d" % k, 2, [128, T], F32)
        st, B_st = P.sb(scope, "lnst%d" % k, [128, 2, T], F32)
        p1, b1 = ps_long()
        p2, b2 = ps_long()
        for c in range(16):
            sq, bq = sq_ring()
            A(lambda: nc.scalar.activation(out=sq[:], in_=x_sb[:, c, :], func=AF.Square), [B_x], [bq])
            mm(p1[:], CM(C_ONE), x_sb[:, c, :], c == 0, c == 15, [B_c, B_x], [b1], inc=True)
            mm(p2[:], CM(C_ONE), sq[:], c == 0, c == 15, [B_c, bq], [b2], inc=True)
        mean, var = st[:, 0, :], st[:, 1, :]
        A(lambda: nc.scalar.activation(out=mean, in_=p1[:], func=AF.Identity, scale=1.0 / D), [b1], [B_st])
        V(lambda: nc.vector.tensor_tensor(out=var, in0=mean, in1=mean, op=ALU.mult), [B_st], [B_st])
        V(lambda: nc.vector.scalar_tensor_tensor(out=var, in0=p2[:], scalar=1.0 / D, in1=var, op0=ALU.mult, op1=ALU.subtract), [b2, B_st], [B_st])
        A(lambda: nc.scalar.activation(out=var, in_=var, func=AF.Ln, bias=LN_EPS, scale=1.0), [B_st], [B_st])
        A(lambda: nc.scalar.activation(out=var, in_=var, func=AF.Exp, scale=-0.5), [B_st], [B_st])
        for c in range(16):
            V(lambda: nc.vector.tensor_tensor(out=x_sb[:, c, :], in0=x_sb[:, c, :], in1=mean, op=ALU.subtract), [B_x, B_st], [B_x])
            V(lambda: nc.vector.tensor_tensor(out=x_sb[:, c, :], in0=x_sb[:, c, :], in1=var, op=ALU.mult), [B_x, B_st], [B_x])
            A(lambda: nc.scalar.activation(out=x_sb[:, c, :], in_=x_sb[:, c, :], func=AF.Identity,
                                           bias=lnp[:, l, 2 * k + 1, c:c + 1], scale=lnp[:, l, 2 * k, c:c + 1]), [B_x, B_c], [B_x])

    def residual_proj(l, wsrc, kc, ncols_tile, rhs_fn, rhs_bufs, vgate, row, scope, tag):
        tmp_ring = P.ring(scope, "rp" + tag, 2, [128, T], F32)
        per = ncols_tile // 128
        with extra_slots(3):
            for tcol in range(D // ncols_tile):
                wv, wb = wload(wsrc[:, tcol * ncols_tile:(tcol + 1) * ncols_tile], kc, ncols_tile)
                for q in range(per):
                    fc = tcol * per + q
                    ps, pb = ps_short()
                    for c in range(kc):
                        mm(ps[:], wv[:, c, q * 128:(q + 1) * 128], rhs_fn(c), c == 0, c == kc - 1, [wb] + rhs_bufs, [pb])
                    tmp, tb = tmp_ring()
                    A(lambda: nc.scalar.activation(out=tmp[:], in_=ps[:], func=AF.Identity, scale=modp(l, vgate, fc, row)), [pb, B_modv], [tb])
                    V(lambda: nc.vector.scalar_tensor_tensor(out=x_sb[:, fc, :], in0=x_sb[:, fc, :], scalar=ALPHA, in1=tmp[:],
                                                             op0=ALU.mult, op1=ALU.add), [B_x, tb], [B_x])

    def attention(groups, scale, et_ring, btmp_ring, finish):
        yps, yb = ps_long()
        dps, db = ps_long()
        ng = len(groups)
        for gi, grp in enumerate(groups):
            st, sb_ = ps_short()
            for si, s in enumerate(grp):
                mm(st[:, s["c0"]:s["c0"] + s["n"]], s["k"], s["q"], si == 0, True, [s["kb"], s["qb"]], [sb_], inc=(si == len(grp) - 1), sgc=True)
            et, eb = et_ring()
            bias = grp[0].get("bias")
            if bias is not None:
                bt, btb = btmp_ring()
                V(lambda: nc.vector.scalar_tensor_tensor(out=bt[:], in0=st[:], scalar=scale, in1=bias, op0=ALU.mult, op1=ALU.add),
                  [sb_, grp[0]["bb"]], [btb])
                A(lambda: nc.scalar.activation(out=et[:], in_=bt[:], func=AF.Exp), [btb], [eb])
            else:
                A(lambda: nc.scalar.activation(out=et[:], in_=st[:], func=AF.Exp, scale=scale), [sb_], [eb])
            for si, s in enumerate(grp):
                mm(yps[:, s["c0"]:s["c0"] + s["n"]], s["v"], et[:, s["c0"]:s["c0"] + s["n"]],
                   gi == 0 and si == 0, gi == ng - 1, [s["vb"], eb], [yb], inc=False, sgc=True)
            mm(dps[:], ones_bf[:], et[:], gi == 0, gi == ng - 1, [B_c, eb], [db], inc=True)
        finish(yps, yb, dps, db)

    def block(l, g):
        row = 0 if g == "P" else 1
        lam_init = 0.8 - 0.6 * math.exp(-0.3 * l)
        xsrc = xin[g] if l == 0 else xspill[g]
        fw.dma("sp", x_sb[:], xsrc.rearrange("(c p) t -> p c t", p=128), reads=[B_spill[g]], writes=[B_x])
        modulate(l, 0, 1, row)
        with ExitStack() as sm:
            ycat, B_y = P.sb(sm, "ycat", [128, 16, T], BF16)
            with ExitStack() as sc:
                qtc, B_qtc = P.sb(sc, "qtc", [128, 5, T], BF16)
                ktc, B_ktc = P.sb(sc, "ktc", [128, 5, T], BF16)
                kcA, B_kcA = P.sb(sc, "kcA", [128, 4, 640], BF16)
                kcB, B_kcB = P.sb(sc, "kcB", [128, 4, 640], BF16)
                vc, B_vc = P.sb(sc, "vc", [128, 4, 640], BF16)
                sigoc, B_so = P.sb(sc, "sigoc", [128, 4, 640], F32)
                gat, B_gat = P.sb(sc, "gat", [128, 4, 20], F32)
                G(lambda: nc.gpsimd.memset(kcA[:], 0.0), [], [B_kcA])
                G(lambda: nc.gpsimd.memset(kcB[:], 0.0), [], [B_kcB])
                ctiles = [(4224, 512), (4736, 512), (5248, 512), (5760, 512), (6272, 532)]
                with extra_slots(2):
                    for (c0, ncol) in ctiles:
                        wv, wb = wload(w_in[l, :, c0:c0 + ncol], 16, ncol)
                        for q in range(min(4, ncol // 128)):
                            col = c0 + q * 128
                            k = col // 128
                            if 33 <= k <= 42:
                                ps, pb = ps_short()
                                for c in range(16):
                                    mm(ps[:], wv[:, c, q * 128:(q + 1) * 128], h_sb[:, c, :], c == 0, c == 15, [wb, B_h], [pb])
                                if k <= 37:
                                    A(lambda: nc.scalar.activation(out=qtc[:, k - 33, :], in_=ps[:], func=AF.Copy, scale=128.0 ** -0.5), [pb], [B_qtc])
                                else:
                                    A(lambda: nc.scalar.copy(out=ktc[:, k - 38, :], in_=ps[:]), [pb], [B_ktc])
                        segs = []
                        for (name, lo, hi) in (("kc", 4864, 5504), ("vc", 5504, 6144), ("oc", 6144, 6784), ("gc", 6784, 6804)):
                            a, b_ = max(lo, c0), min(hi, c0 + ncol)
                            if a < b_:
                                segs.append((name, a, b_, lo))
                        for (name, a, b_, lo) in segs:
                            for tt in range(4):
                                ps, pb = ps_short()
                                n = b_ - a
                                for c in range(16):
                                    mm(ps[:, 0:n], h_sb[:, c, tt * 128:(tt + 1) * 128], wv[:, c, a - c0:b_ - c0], c == 0, c == 15, [wb, B_h], [pb])
                                o0 = a - lo
                                if name == "kc":
                                    A(lambda: nc.scalar.copy(out=kcA[0:64, tt, o0:o0 + n], in_=ps[0:64, 0:n]), [pb], [B_kcA])
                                    A(lambda: nc.scalar.copy(out=kcB[64:128, tt, o0:o0 + n], in_=ps[64:128, 0:n]), [pb], [B_kcB])
                                elif name == "vc":
                                    A(lambda: nc.scalar.copy(out=vc[:, tt, o0:o0 + n], in_=ps[:, 0:n]), [pb], [B_vc])
                                elif name == "oc":
                                    A(lambda: nc.scalar.activation(out=sigoc[:, tt, o0:o0 + n], in_=ps[:, 0:n], func=AF.Sigmoid), [pb], [B_so])
                                else:
                                    V(lambda: nc.vector.tensor_tensor(out=gat[:, tt, :], in0=ps[:, 0:20], in1=gateb[:, l, :], op=ALU.add), [pb, B_c], [B_gat])
                kstop("c1")
                bc, B_bc = P.sb(sc, "bc", [128, 2, 4, 10], F32)
                ea, B_ea = P.sb(sc, "ea", [128, 2, 4, 5], F32)
                bl, B_bl = P.sb(sc, "bl", [128, 2, 8, 10], F32)
                t5, B_t5 = P.sb(sc, "t5", [128, 4, 5], F32)
                vp, B_vp = P.sb(sc, "vp", [128, 2, 4, 5, 130], BF16)
                sgp = ExitStack()
                dg, B_dg = P.sb(sgp, "dg", [128, 5, 128], F32)
                mk, B_mk = P.sb(sgp, "mk", [128, 5, 128], F32)
                for d in range(2):
                    tri = CM(C_TRIF if d == 0 else C_TRIB)
                    for tt in range(4):
                        ig = gat[:, tt, d * 10:d * 10 + 5]
                        fg = gat[:, tt, d * 10 + 5:d * 10 + 10]
                        A(lambda: nc.scalar.activation(out=t5[:, 0, :], in_=fg, func=AF.Exp, scale=-1.0), [B_gat], [B_t5])
                        A(lambda: nc.scalar.activation(out=t5[:, 1, :], in_=t5[:, 0, :], func=AF.Ln, bias=1.0, scale=1.0), [B_t5], [B_t5])
                        ps, pb = ps_short()
                        mm(ps[:, 0:5], tri, t5[:, 1, :], True, True, [B_c, B_t5], [pb])
                        A(lambda: nc.scalar.copy(out=bc[:, d, tt, 0:5], in_=ps[:, 0:5]), [pb], [B_bc])
                        V(lambda: nc.vector.tensor_tensor(out=t5[:, 2, :], in0=ig, in1=bc[:, d, tt, 0:5], op=ALU.add), [B_gat, B_bc], [B_t5])
                        A(lambda: nc.scalar.activation(out=ea[:, d, tt, :], in_=t5[:, 2, :], func=AF.Exp), [B_t5], [B_ea])
                        for h in range(5):
                            A(lambda: nc.scalar.activation(out=dg[:, h, :], in_=CM(C_ID), func=AF.Identity, scale=t5[:, 2, h:h + 1]), [B_c, B_t5], [B_dg])
                        ps1, pb1 = ps_short()
                        mm(ps1[:, 0:384], CM(C_ONE), dg[:, 0:3, :].rearrange("p h s -> p (h s)"), True, True, [B_c, B_dg], [pb1])
                        ps2, pb2 = ps_short()
                        mm(ps2[:, 0:256], CM(C_ONE), dg[:, 3:5, :].rearrange("p h s -> p (h s)"), True, True, [B_c, B_dg], [pb2])
                        V(lambda: nc.vector.tensor_tensor(out=mk[:, 0:3, :].rearrange("p h s -> p (h s)"), in0=ps1[:, 0:384],
                                                          in1=negm[:, d, 0:384], op=ALU.add), [pb1, B_c], [B_mk])
                        V(lambda: nc.vector.tensor_tensor(out=mk[:, 3:5, :].rearrange("p h s -> p (h s)"), in0=ps2[:, 0:256],
                                                          in1=negm[:, d, 384:640], op=ALU.add), [pb2, B_c], [B_mk])
                        V(lambda: nc.vector.tensor_reduce(out=bc[:, d, tt, 5:10], in_=mk[:], axis=AX.X, op=ALU.max), [B_mk], [B_bc])
                        for X in range(2):
                            ep = (C_E63, C_E127)[X] if d == 0 else (C_E0, C_E64)[X]
                            ps, pb = ps_short()
                            mm(ps[:, 0:10], CM(ep), bc[:, d, tt, :], True, True, [B_c, B_bc], [pb])
                            A(lambda: nc.scalar.copy(out=bl[:, d, 2 * tt + X, :], in_=ps[:, 0:10]), [pb], [B_bl])
                        for h in range(5):
                            A(lambda: nc.scalar.activation(out=vp[:, d, tt, h, 0:128], in_=vc[:, tt, h * 128:(h + 1) * 128], func=AF.Identity,
                                                           scale=ea[:, d, tt, h:h + 1]), [B_vc, B_ea], [B_vp])
                        A(lambda: nc.scalar.copy(out=vp[:, d, tt, :, 128], in_=ea[:, d, tt, :]), [B_ea], [B_vp])

                fw.barrier()
                sgp.close()
                kstop("c2")
                mc, B_mc = P.sb(sc, "mc", [128, 2, 8, 5], F32)
                wold, B_wo = P.sb(sc, "wold", [128, 2, 8, 5], F32)
                snew, B_sn = P.sb(sc, "snew", [128, 2, 8, 5], F32)
                mcur, B_mcur = P.sb(sc, "mcur", [128, 2, 5], F32)
                mt, B_mt = P.sb(sc, "mt", [128, 2, 5], F32)
                nfacc, B_nf = P.sb(sc, "nfacc", [128, 2, 5], F32)
                cn, B_cn = P.sb(sc, "cn", [128, 10, 129], F32)
                cnb, B_cnb = P.sb(sc, "cnb", [128, 10, 130], BF16)
                tmpu_ring = P.ring(sc, "tmpu", 2, [128, 129], F32)

                def chunk_order(d, runs):
                    out = []
                    rr = runs if d == 0 else [list(reversed(r)) for r in reversed(runs)]
                    for r in rr:
                        out.append(r)
                    return out

                def mchain(d, run, m_init_fn):
                    m_init_fn(mcur[:, d, :])
                    for c in run:
                        A(lambda: nc.scalar.copy(out=mc[:, d, c, :], in_=mcur[:, d, :]), [B_mcur], [B_mc])
                        V(lambda: nc.vector.tensor_tensor(out=mt[:, 0, :], in0=mcur[:, d, :], in1=bl[:, d, c, 5:10], op=ALU.max), [B_mcur, B_bl], [B_mt])
                        V(lambda: nc.vector.tensor_tensor(out=mt[:, 1, :], in0=mcur[:, d, :], in1=mt[:, 0, :], op=ALU.subtract), [B_mcur, B_mt], [B_mt])
                        A(lambda: nc.scalar.activation(out=wold[:, d, c, :], in_=mt[:, 1, :], func=AF.Exp), [B_mt], [B_wo])
                        A(lambda: nc.scalar.activation(out=snew[:, d, c, :], in_=mt[:, 0, :], func=AF.Exp, scale=-1.0), [B_mt], [B_sn])
                        V(lambda: nc.vector.tensor_tensor(out=mcur[:, d, :], in0=mt[:, 0, :], in1=bl[:, d, c, 0:5], op=ALU.subtract), [B_mt, B_bl], [B_mcur])
                        V(lambda: nc.vector.tensor_tensor(out=nfacc[:, d, :], in0=nfacc[:, d, :], in1=bl[:, d, c, 0:5], op=ALU.add), [B_nf, B_bl], [B_nf])

                def state_update(d, h, c, need_bf=True):
                    tt, X = c // 2, c % 2
                    kk = kcA if X == 0 else kcB
                    kkb = B_kcA if X == 0 else B_kcB
                    ps, pb = ps_short()
                    mm(ps[:, 0:129], kk[:, tt, h * 128:(h + 1) * 128], vp[:, d, tt, h, 0:129], True, True, [kkb, B_vp], [pb])
                    tu, tub = tmpu_ring()
                    A(lambda: nc.scalar.activation(out=tu[:], in_=ps[:, 0:129], func=AF.Identity, scale=snew[:, d, c, h:h + 1]), [pb, B_sn], [tub])
                    V(lambda: nc.vector.scalar_tensor_tensor(out=cn[:, d * 5 + h, :], in0=cn[:, d * 5 + h, :], scalar=wold[:, d, c, h:h + 1],
                                                             in1=tu[:], op0=ALU.mult, op1=ALU.add), [B_cn, B_wo, tub], [B_cn])
                    if need_bf:
                        A(lambda: nc.scalar.copy(out=cnb[:, d * 5 + h, 0:129], in_=cn[:, d * 5 + h, :]), [B_cn], [B_cnb])

                def zero_state(d):
                    G(lambda: nc.gpsimd.memset(cn[:, d * 5:(d + 1) * 5, :], 0.0), [], [B_cn])
                    G(lambda: nc.gpsimd.memset(cnb[:, d * 5:(d + 1) * 5, :], 0.0), [], [B_cnb])

                def alloc_scan_bufs():
                    a_ = P.sb(sc, "hc", [128, 4, 640], F32)
                    b_ = P.sb(sc, "tok", [128, 2, 4, 15], F32)
                    c_ = P.sb(sc, "mcol", [128, 5], F32)
                    return (a_[0], a_[1], b_[0], b_[1], c_[0], c_[1], P.ring(sc, "gm", 3, [128, 128], BF16),
                            P.ring(sc, "ti", 3, [128, 129], F32), P.ring(sc, "hn", 3, [128, 129], F32), P.ring(sc, "s3", 3, [128, 3], F32))

                def token_scalars(d, tt):
                    A(lambda: nc.scalar.copy(out=mcol[0:64, :], in_=mc[0:64, d, 2 * tt, :]), [B_mc], [B_mcol])
                    A(lambda: nc.scalar.copy(out=mcol[64:128, :], in_=mc[64:128, d, 2 * tt + 1, :]), [B_mc], [B_mcol])
                    V(lambda: nc.vector.tensor_tensor(out=t5[:, 3, :], in0=bc[:, d, tt, 5:10], in1=mcol[:], op=ALU.max), [B_bc, B_mcol], [B_t5])
                    A(lambda: nc.scalar.activation(out=tok[:, d, tt, 0:5], in_=t5[:, 3, :], func=AF.Exp, scale=-1.0), [B_t5], [B_tok])
                    V(lambda: nc.vector.tensor_tensor(out=t5[:, 0, :], in0=mcol[:], in1=t5[:, 3, :], op=ALU.subtract), [B_mcol, B_t5], [B_t5])
                    A(lambda: nc.scalar.activation(out=tok[:, d, tt, 5:10], in_=t5[:, 0, :], func=AF.Exp), [B_t5], [B_tok])
                    V(lambda: nc.vector.tensor_tensor(out=t5[:, 1, :], in0=bc[:, d, tt, 0:5], in1=t5[:, 3, :], op=ALU.subtract), [B_bc, B_t5], [B_t5])
                    A(lambda: nc.scalar.activation(out=tok[:, d, tt, 10:15], in_=t5[:, 1, :], func=AF.Exp), [B_t5], [B_tok])

                def scan_outputs(runs, on_run_end, on_run_start):
                    G(lambda: nc.gpsimd.memset(hc[:], 0.0), [], [B_hc])
                    for d in range(2):
                        for tt in range(4):
                            token_scalars(d, tt)
                    order = {d: chunk_order(d, runs) for d in range(2)}
                    nsteps = sum(len(r) for r in runs) // 2
                    flat = {d: [c for r in order[d] for c in r] for d in range(2)}
                    run_start = {d: {r[0]: ri for ri, r in enumerate(order[d])} for d in range(2)}
                    run_end = {d: {r[-1]: ri for ri, r in enumerate(order[d])} for d in range(2)}
                    for step in range(nsteps):
                        for d in range(2):
                            c_pair = flat[d][2 * step:2 * step + 2]
                            tt = c_pair[0] // 2
                            if c_pair[0] in run_start[d]:
                                on_run_start(d, run_start[d][c_pair[0]], order[d])
                            for h in range(5):
                                gps, gpb = ps_short()
                                mm(gps[:, 0:128], ktc[:, h, tt * 128:(tt + 1) * 128], qtc[:, h, tt * 128:(tt + 1) * 128], True, True, [B_ktc, B_qtc], [gpb])
                                gm, gmb = gm_ring()
                                V(lambda: nc.vector.tensor_tensor(out=gm[:], in0=gps[:, 0:128], in1=CM(C_TRIF if d == 0 else C_TRIB), op=ALU.mult), [gpb, B_c], [gmb])
                                ips, ipb = ps_short()
                                mm(ips[:, 0:129], gm[:], vp[:, d, tt, h, 0:129], True, True, [gmb, B_vp], [ipb])
                                ti, tib = ti_ring()
                                A(lambda: nc.scalar.activation(out=ti[:], in_=ips[:, 0:129], func=AF.Identity, scale=tok[:, d, tt, h:h + 1]), [ipb, B_tok], [tib])
                                hn, hnb = hn_ring()
                                for c in c_pair:
                                    X = c % 2
                                    rs = slice(0, 64) if X == 0 else slice(64, 128)
                                    xps, xpb = ps_short()
                                    mm(xps[:, 0:129], qtc[:, h, tt * 128:(tt + 1) * 128], cnb[:, d * 5 + h, 0:129], True, True, [B_qtc, B_cnb], [xpb])
                                    V(lambda: nc.vector.scalar_tensor_tensor(out=hn[rs, :], in0=xps[rs, 0:129], scalar=tok[rs, d, tt, 5 + h:6 + h],
                                                                             in1=ti[rs, :], op0=ALU.mult, op1=ALU.add), [xpb, B_tok, tib], [hnb])
                                    state_update(d, h, c)
                                s3, s3b = s3_ring()
                                V(lambda: nc.vector.scalar_tensor_tensor(out=s3[:, 0:1], in0=hn[:, 128:129], scalar=-1.0, in1=hn[:, 128:129],
                                                                         op0=ALU.mult, op1=ALU.max), [hnb], [s3b])
                                V(lambda: nc.vector.tensor_tensor(out=s3[:, 1:2], in0=s3[:, 0:1], in1=tok[:, d, tt, 10 + h:11 + h], op=ALU.max), [s3b, B_tok], [s3b])
                                A(lambda: nc.scalar.activation(out=s3[:, 2:3], in_=s3[:, 1:2], func=AF.Ln), [s3b], [s3b])
                                A(lambda: nc.scalar.activation(out=s3[:, 2:3], in_=s3[:, 2:3], func=AF.Exp, scale=-1.0), [s3b], [s3b])
                                hsl = hc[:, tt, h * 128:(h + 1) * 128]
                                V(lambda: nc.vector.scalar_tensor_tensor(out=hsl, in0=hn[:, 0:128], scalar=s3[:, 2:3], in1=hsl,
                                                                         op0=ALU.mult, op1=ALU.add), [hnb, s3b, B_hc], [B_hc])
                            if c_pair[1] in run_end[d]:
                                on_run_end(d, run_end[d][c_pair[1]], order[d])

                def set_const(val):
                    def f(ap):
                        G(lambda: nc.gpsimd.memset(ap, val), [], [B_mcur])
                    return f

                G(lambda: nc.gpsimd.memset(nfacc[:], 0.0), [], [B_nf])
                if g == "P":
                    runs = [[0, 1, 2, 3], [4, 5, 6, 7]]
                    mfin, B_mfin = P.sb(sc, "mfin", [128, 2, 2, 5], F32)
                    for d in range(2):
                        for ri, r in enumerate(chunk_order(d, runs)):
                            mchain(d, r, set_const(0.0))
                            seq = r[0] // 4
                            A(lambda: nc.scalar.copy(out=mfin[:, seq, d, :], in_=mcur[:, d, :]), [B_mcur], [B_mfin])
                    for seq in range(2):
                        fw.dma("sp", o_cm[seq, l].rearrange("(o d) h -> o (d h)", o=1), mfin[0:1, seq, :, :].rearrange("p d h -> p (d h)"),
                               reads=[B_mfin], writes=[B_out])

                    def on_start(d, ri, order):
                        zero_state(d)

                    def on_end(d, ri, order):
                        seq = order[ri][0] // 4
                        fw.dma("sp", o_cC[seq, l, d].rearrange("h k v -> k h v"), cn[:, d * 5:(d + 1) * 5, 0:128], reads=[B_cn], writes=[B_out])
                        fw.dma("sp", o_cn[seq, l, d], cn[:, d * 5:(d + 1) * 5, 128], reads=[B_cn], writes=[B_out], slow=True)
                    hc, B_hc, tok, B_tok, mcol, B_mcol, gm_ring, ti_ring, hn_ring, s3_ring = alloc_scan_bufs()
                    scan_outputs(runs, on_end, on_start)
                else:
                    runs = [[0, 1, 2, 3, 4, 5, 6, 7]]
                    for d in range(2):
                        zero_state(d)
                        r = chunk_order(d, runs)[0]
                        mchain(d, r, set_const(NEG))
                        for c in r:
                            for h in range(5):
                                state_update(d, h, c, need_bf=False)
                    with ExitStack() as sg:
                        cst_t, B_cst = P.sb(sg, "cst_t", [128, 20], F32)
                        A(lambda: nc.scalar.copy(out=cst_t[:, 0:10], in_=mcur[:].rearrange("p d h -> p (d h)")), [B_mcur], [B_cst])
                        A(lambda: nc.scalar.copy(out=cst_t[:, 10:20], in_=nfacc[:].rearrange("p d h -> p (d h)")), [B_nf], [B_cst])
                        fw.dma("sp", cs_in[:, 0:1290], cn[:].rearrange("p a b -> p (a b)"), reads=[B_cn], writes=[B_csi])
                        fw.dma("sp", cs_in[:, 1290:1310], cst_t[:], reads=[B_cst], writes=[B_csi])
                        fw.allgather(cs_in, cs_out, reads=[B_csi], writes=[B_cso])
                        gs, B_gs = P.sb(sg, "gs", [128, 4, CSF], F32)
                        fw.dma("sp", gs[:], cs_out.rearrange("(r p) f -> p r f", p=128), reads=[B_cso], writes=[B_gs])
                        c0t, B_c0 = P.sb(sg, "c0t", [128, 10, 129], F32)
                        m0t, B_m0 = P.sb(sg, "m0t", [128, 10], F32)
                        fw.dma("sp", c0t[:, :, 0:128], cC_d[l].rearrange("d h k v -> k (d h) v"), writes=[B_c0])
                        fw.dma("sp", c0t[:, :, 128], cn_d[l], writes=[B_c0], slow=True)
                        fw.dma("sp", m0t[:], bass.AP(cm_d.tensor, l * 10, [[0, 128], [1, 10]]), writes=[B_m0])
                        av, B_av = P.sb(sg, "av", [128, 2, 6, 5], F32)
                        wv5, B_wv5 = P.sb(sg, "wv5", [128, 2, 5, 5], F32)
                        for d in range(2):
                            for i in range(5):
                                src = m0t[:, d * 5:(d + 1) * 5] if i == 0 else gs[:, i - 1, 1290 + d * 5:1290 + d * 5 + 5]
                                V(lambda: nc.vector.tensor_tensor(out=av[:, d, i, :], in0=src, in1=vtab[:, d, i, :], op=ALU.add), [B_m0, B_gs, B_c], [B_av])
                                for r in range(4):
                                    V(lambda: nc.vector.tensor_tensor(out=t5[:, 0, :], in0=cftab[:, d, i, r, :], in1=gs[:, r, 1300 + d * 5:1305 + d * 5], op=ALU.mult),
                                      [B_c, B_gs], [B_t5])
                                    V(lambda: nc.vector.tensor_tensor(out=av[:, d, i, :], in0=av[:, d, i, :], in1=t5[:, 0, :], op=ALU.subtract), [B_av, B_t5], [B_av])
                            V(lambda: nc.vector.tensor_tensor(out=av[:, d, 5, :], in0=av[:, d, 0, :], in1=av[:, d, 1, :], op=ALU.max), [B_av], [B_av])
                            for i in range(2, 5):
                                V(lambda: nc.vector.tensor_tensor(out=av[:, d, 5, :], in0=av[:, d, 5, :], in1=av[:, d, i, :], op=ALU.max), [B_av], [B_av])
                            for i in range(5):
                                V(lambda: nc.vector.tensor_tensor(out=t5[:, 1, :], in0=av[:, d, i, :], in1=av[:, d, 5, :], op=ALU.subtract), [B_av], [B_t5])
                                A(lambda: nc.scalar.activation(out=wv5[:, d, i, :], in_=t5[:, 1, :], func=AF.Exp), [B_t5], [B_wv5])
                            for h in range(5):
                                dh = d * 5 + h
                                A(lambda: nc.scalar.activation(out=cn[:, dh, :], in_=c0t[:, dh, :], func=AF.Identity, scale=wv5[:, d, 0, h:h + 1]), [B_c0, B_wv5], [B_cn])
                                for r in range(4):
                                    V(lambda: nc.vector.scalar_tensor_tensor(out=cn[:, dh, :], in0=gs[:, r, dh * 129:(dh + 1) * 129], scalar=wv5[:, d, 1 + r, h:h + 1],
                                                                             in1=cn[:, dh, :], op0=ALU.mult, op1=ALU.add), [B_gs, B_wv5, B_cn], [B_cn])
                                A(lambda: nc.scalar.copy(out=cnb[:, dh, 0:129], in_=cn[:, dh, :]), [B_cn], [B_cnb])

                        def m_from_av(d):
                            def f(ap):
                                A(lambda: nc.scalar.copy(out=ap, in_=av[:, d, 5, :]), [B_av], [B_mcur])
                            return f
                        for d in range(2):
                            mchain(d, chunk_order(d, runs)[0], m_from_av(d))
                        fw.barrier()
                    hc, B_hc, tok, B_tok, mcol, B_mcol, gm_ring, ti_ring, hn_ring, s3_ring = alloc_scan_bufs()
                    scan_outputs(runs, lambda *a: None, lambda *a: None)

                kstop("c3")
                ss, B_ss = P.sb(sc, "ss", [128, 20], F32)
                junk, B_junk = P.sb(sc, "junk", [128, 128], F32)
                yct_ring = P.ring(sc, "yct", 3, [128, 128], BF16)
                ytmp_ring = P.ring(sc, "ytmp", 2, [128, 128], F32)
                G(lambda: nc.gpsimd.memset(ss[:], 0.0), [], [B_ss])
                for tt in range(4):
                    for h in range(5):
                        A(lambda: nc.scalar.activation(out=junk[:], in_=hc[:, tt, h * 128:(h + 1) * 128], func=AF.Square,
                                                       accum_out=ss[:, tt * 5 + h:tt * 5 + h + 1]), [B_hc], [B_junk, B_ss])
                A(lambda: nc.scalar.activation(out=ss[:], in_=ss[:], func=AF.Ln, scale=1.0 / 128, bias=RMS_EPS), [B_ss], [B_ss])
                A(lambda: nc.scalar.activation(out=ss[:], in_=ss[:], func=AF.Exp, scale=-0.5), [B_ss], [B_ss])
                kstop("ca")
                for h in range(5):
                    for tt in range(4):
                        yt, ytb = ytmp_ring()
                        A(lambda: nc.scalar.activation(out=yt[:], in_=hc[:, tt, h * 128:(h + 1) * 128], func=AF.Identity, scale=ss[:, tt * 5 + h:tt * 5 + h + 1]),
                          [B_hc, B_ss], [ytb])
                        V(lambda: nc.vector.tensor_tensor(out=yt[:], in0=yt[:], in1=cnorm[:, l, :], op=ALU.mult), [ytb, B_c], [ytb])
                        yc, ycb = yct_ring()
                        V(lambda: nc.vector.tensor_tensor(out=yc[:], in0=yt[:], in1=sigoc[:, tt, h * 128:(h + 1) * 128], op=ALU.mult), [ytb, B_so], [ycb])
                        if KSTOP == "cb":
                            continue
                        ps, pb = ps_short()
                        mm(ps[:, 0:128], yc[:], id_bf[:], True, True, [ycb, B_c], [pb])
                        if KSTOP == "cc":
                            continue
                        A(lambda: nc.scalar.copy(out=ycat[:, 11 + h, tt * 128:(tt + 1) * 128], in_=ps[:, 0:128]), [pb], [B_y])
                fw.barrier()

            kstop("c4")
            kstop("cb")
            kstop("cc")
            with ExitStack() as sa:
                qta, B_qta = P.sb(sa, "qta", [128, 6, T], BF16)
                q1p, B_q1p = P.sb(sa, "q1p", [128, 5, T], BF16)
                q2p, B_q2p = P.sb(sa, "q2p", [128, 5, T], BF16)
                G(lambda: nc.gpsimd.memset(q1p[:], 0.0), [], [B_q1p])
                G(lambda: nc.gpsimd.memset(q2p[:], 0.0), [], [B_q2p])
                if g == "S":
                    q1r, B_q1r = P.sb(sa, "q1r", [128, 5, T], BF16)
                    q2r, B_q2r = P.sb(sa, "q2r", [128, 5, T], BF16)
                    G(lambda: nc.gpsimd.memset(q1r[:], 0.0), [], [B_q1r])
                    G(lambda: nc.gpsimd.memset(q2r[:], 0.0), [], [B_q2r])
                    sip = ExitStack()
                    kst_ring = P.ring(sip, "kst", 3, [128, T], BF16)
                    vst, B_vst = P.sb(sip, "vst", [128, 4, 1408], BF16)
                    rp_ring = P.ring(sip, "rpx", 2, [128, T], F32)
                    rp2_ring = P.ring(sip, "rpy", 2, [128, T], F32)
                else:
                    kta, B_kta = P.sb(sa, "kta", [128, 6, T], BF16)
                    ktb, B_ktb = P.sb(sa, "ktb", [128, 5, T], BF16)
                    vab, B_vab = P.sb(sa, "vab", [128, 4, 1408], BF16)
                    stg_ring = P.ring(sa, "stg", 3, [128, T], F32)

                def rope_apply(ps, pb, outs):
                    xf, xb = rp_ring()
                    A(lambda: nc.scalar.copy(out=xf[:], in_=ps[:]), [pb], [xb])
                    p2, pb2 = ps_short()
                    mm(p2[:], CM(C_PERM), xf[:], True, True, [B_c, xb], [pb2])
                    x2, x2b = rp2_ring()
                    V(lambda: nc.vector.tensor_tensor(out=x2[:], in0=p2[:], in1=rope[:, 1, :], op=ALU.mult), [pb2, B_c], [x2b])
                    V(lambda: nc.vector.tensor_tensor(out=xf[:], in0=xf[:], in1=rope[:, 0, :], op=ALU.mult), [xb, B_c], [xb])
                    for (rs, dst, db_) in outs:
                        V(lambda: nc.vector.tensor_tensor(out=dst[rs, :], in0=xf[rs, :], in1=x2[rs, :], op=ALU.add), [xb, x2b], [db_])

                abtiles = [(i * 512, 512) for i in range(8)] + [(4096, 128)]
                if "t" in KSKIP:
                    abtiles = abtiles[:int(KSKIP[KSKIP.index("t") + 1])]
                with extra_slots(2):
                    for (c0, ncol) in abtiles:
                        wv, wb = wload(w_in[l, :, c0:c0 + ncol], 16, ncol)
                        for q in range(ncol // 128):
                            k = (c0 + q * 128) // 128
                            fm = (k <= 11) or (18 <= k <= 27)
                            if not fm:
                                continue
                            ps, pb = ps_short()
                            for c in range(16):
                                mm(ps[:], wv[:, c, q * 128:(q + 1) * 128], h_sb[:, c, :], c == 0, c == 15, [wb, B_h], [pb])
                            if k <= 5:
                                A(lambda: nc.scalar.copy(out=qta[:, k, :], in_=ps[:]), [pb], [B_qta])
                            elif k <= 11:
                                hh = k - 6
                                if g == "P":
                                    A(lambda: nc.scalar.copy(out=kta[:, hh, :], in_=ps[:]), [pb], [B_kta])
                                    sg, sgb = stg_ring()
                                    A(lambda: nc.scalar.copy(out=sg[:], in_=ps[:]), [pb], [sgb])
                                    if "k" not in KSKIP:
                                        fw.dma("sp", o_ak[:, l, hh].rearrange("s d t -> d s t"), sg[:].rearrange("p (s t) -> p s t", s=2), reads=[sgb], writes=[B_out])
                                else:
                                    ks, ksb = kst_ring()
                                    A(lambda: nc.scalar.copy(out=ks[:], in_=ps[:]), [pb], [ksb])
                                    kr, krb = bnc_k_rows(hh)
                                    fw.dma("sp", kr, ks[:], reads=[ksb], writes=[krb])
                            elif k <= 22:
                                hh = k - 18
                                A(lambda: nc.scalar.copy(out=q1p[0:64, hh, :], in_=ps[0:64, :]), [pb], [B_q1p])
                                A(lambda: nc.scalar.copy(out=q2p[64:128, hh, :], in_=ps[64:128, :]), [pb], [B_q2p])
                                if g == "S":
                                    rope_apply(ps, pb, [(slice(0, 64), q1r[:, hh, :], B_q1r), (slice(64, 128), q2r[:, hh, :], B_q2r)])
                            else:
                                hh = k - 23
                                if g == "P":
                                    A(lambda: nc.scalar.copy(out=ktb[:, hh, :], in_=ps[:]), [pb], [B_ktb])
                                    sg, sgb = stg_ring()
                                    A(lambda: nc.scalar.copy(out=sg[:], in_=ps[:]), [pb], [sgb])
                                    if "k" not in KSKIP:
                                        fw.dma("sp", o_bk[:, l, hh].rearrange("s d t -> d s t"), sg[:].rearrange("p (s t) -> p s t", s=2), reads=[sgb], writes=[B_out])
                                else:
                                    ks, ksb = kst_ring()
                                    rope_apply(ps, pb, [(slice(0, 128), ks, ksb)])
                                    kr, krb = bnc_k_rows(6 + hh)
                                    fw.dma("sp", kr, ks[:], reads=[ksb], writes=[krb])
                        for (name, lo, hi, o_base) in (("va", 1536, 2304, 0), ("vb", 3584, 4224, 768)):
                            a, b_ = max(lo, c0), min(hi, c0 + ncol)
                            if a >= b_:
                                continue
                            n = b_ - a
                            o0 = o_base + a - lo
                            for tt in range(4):
                                ps, pb = ps_short()
                                for c in range(16):
                                    mm(ps[:, 0:n], h_sb[:, c, tt * 128:(tt + 1) * 128], wv[:, c, a - c0:b_ - c0], c == 0, c == 15, [wb, B_h], [pb])
                                if g == "P":
                                    A(lambda: nc.scalar.copy(out=vab[:, tt, o0:o0 + n], in_=ps[:, 0:n]), [pb], [B_vab])
                                    sg, sgb = stg_ring()
                                    A(lambda: nc.scalar.copy(out=sg[:, 0:n], in_=ps[:, 0:n]), [pb], [sgb])
                                    seq, s0_ = tt // 2, (tt % 2) * 128
                                    h0 = (a - lo) // 128
                                    nh = n // 128
                                    dst = (o_av if name == "va" else o_bv)[seq, l, h0:h0 + nh, s0_:s0_ + 128, :].rearrange("h s d -> s h d")
                                    if "v" not in KSKIP:
                                        fw.dma("sp", dst, sg[:, 0:n].rearrange("p (h d) -> p h d", d=128), reads=[sgb], writes=[B_out])
                                else:
                                    A(lambda: nc.scalar.copy(out=vst[:, tt, o0:o0 + n], in_=ps[:, 0:n]), [pb], [B_vst])

                kstop("c5")

                def alloc_attn_rings():
                    return (P.ring(sa, "et", 3, [128, T], BF16), P.ring(sa, "bt", 2, [128, T], F32),
                            P.ring(sa, "rd", 2, [128, T], F32), P.ring(sa, "ybt", 3, [128, T], F32))

                def fin_A(h):
                    def f(yps, yb, dps, db):
                        rd, rdb = rd_ring()
                        A(lambda: nc.scalar.activation(out=rd[:], in_=dps[:], func=AF.Ln), [db], [rdb])
                        A(lambda: nc.scalar.activation(out=rd[:], in_=rd[:], func=AF.Exp, scale=-1.0), [rdb], [rdb])
                        V(lambda: nc.vector.tensor_tensor(out=ycat[:, h, :], in0=yps[:], in1=rd[:], op=ALU.mult), [yb, rdb], [B_y])
                    return f

                def fin_B(dst, dstb):
                    def f(yps, yb, dps, db):
                        rd, rdb = rd_ring()
                        A(lambda: nc.scalar.activation(out=rd[:], in_=dps[:], func=AF.Ln), [db], [rdb])
                        A(lambda: nc.scalar.activation(out=rd[:], in_=rd[:], func=AF.Exp, scale=-1.0), [rdb], [rdb])
                        V(lambda: nc.vector.tensor_tensor(out=dst[:], in0=yps[:], in1=rd[:], op=ALU.mult), [yb, rdb], [dstb])
                    return f

                def diff_finish(h, y1, y1b, y2, y2b):
                    V(lambda: nc.vector.scalar_tensor_tensor(out=y1[:], in0=y2[:], scalar=nlam[:, l:l + 1], in1=y1[:], op0=ALU.mult, op1=ALU.add),
                      [y2b, y1b, B_nlam], [y1b])
                    A(lambda: nc.scalar.activation(out=y2[:], in_=y1[:], func=AF.Square), [y1b], [y2b])
                    sp_, spb = ps_short()
                    mm(sp_[:], CM(C_ONE), y2[:], True, True, [B_c, y2b], [spb])
                    A(lambda: nc.scalar.activation(out=y2[:], in_=sp_[:], func=AF.Ln, scale=1.0 / 128, bias=RMS_EPS), [spb], [y2b])
                    A(lambda: nc.scalar.activation(out=y2[:], in_=y2[:], func=AF.Exp, scale=-0.5), [y2b], [y2b])
                    V(lambda: nc.vector.tensor_tensor(out=y1[:], in0=y1[:], in1=y2[:], op=ALU.mult), [y1b, y2b], [y1b])
                    A(lambda: nc.scalar.activation(out=ycat[:, 6 + h, :], in_=y1[:], func=AF.Identity, scale=sublns[:, l:l + 1]), [y1b, B_nlam], [B_y])

                if g == "P":
                    et_ring, bt_ring, rd_ring, yb_ring = alloc_attn_rings()
                    for h in range(6):
                        groups = []
                        for kb in range(2):
                            grp = []
                            for s in range(2):
                                t0 = s * 256 + kb * 128
                                grp.append(dict(k=kta[:, h, t0:t0 + 128], kb=B_kta, q=qta[:, h, s * 256:(s + 1) * 256], qb=B_qta,
                                                v=vab[:, 2 * s + kb, h * 128:(h + 1) * 128], vb=B_vab, c0=s * 256, n=256))
                            groups.append(grp)
                        attention(groups, 128.0 ** -0.5, et_ring, bt_ring, fin_A(h))
                    for h in range(5):
                        ys = []
                        for (qp, qpb) in ((q1p, B_q1p), (q2p, B_q2p)):
                            groups = []
                            for kb in range(2):
                                grp = []
                                for s in range(2):
                                    t0 = s * 256 + kb * 128
                                    grp.append(dict(k=ktb[:, h, t0:t0 + 128], kb=B_ktb, q=qp[:, h, s * 256:(s + 1) * 256], qb=qpb,
                                                    v=vab[:, 2 * s + kb, 768 + h * 128:768 + (h + 1) * 128], vb=B_vab, c0=s * 256, n=256))
                                groups.append(grp)
                            yt, ytb = yb_ring()
                            attention(groups, 64.0 ** -0.5, et_ring, bt_ring, fin_B(yt, ytb))
                            ys.append((yt, ytb))
                        diff_finish(h, ys[0][0], ys[0][1], ys[1][0], ys[1][1])
                else:
                    for hh in range(11):
                        vr, vrb = bnc_v_rows(hh)
                        fw.dma("sp", vr.rearrange("p (tt d) -> p tt d", d=128),
                               vst[:, :, hh * 128:(hh + 1) * 128], reads=[B_vst], writes=[vrb])
                    for ci in range(3):
                        fw.allgather(bnc_in[ci], bnc_out[ci], reads=[B_bi[ci]], writes=[B_bo[ci]])
                    fw.barrier()
                    sip.close()
                    et_ring, bt_ring, rd_ring, yb_ring = alloc_attn_rings()
                    kall_ring = P.ring(sa, "kall", 2, [128, 4, T], BF16)
                    vall_ring = P.ring(sa, "vall", 2, [128, 4, T], BF16)
                    kctx_ring = P.ring(sa, "kctx", 2, [128, 512], BF16)
                    vctx_ring = P.ring(sa, "vctx", 2, [128, 4, 128], BF16)
                    nb_ring = P.ring(sa, "nbias", 3, [128, T], F32)
                    gviews = [bo.rearrange("(r x) t -> x r t", r=4) for bo in bnc_out]
                    for hh in range(11):
                        isA = hh < 6
                        h = hh if isA else hh - 6
                        ka, kab = kall_ring()
                        va_, vab_ = vall_ring()
                        kc_, kcb_ = kctx_ring()
                        vc_, vcb_ = vctx_ring()
                        gch = HH_CH[hh]
                        krow = HH_IX[hh] * 128
                        vrow = (CH_NH[gch] + HH_IX[hh]) * 128
                        fw.dma("sp", ka[:], gviews[gch][krow:krow + 128], reads=[B_bo[gch]], writes=[kab])
                        fw.dma("sp", va_[:], gviews[gch][vrow:vrow + 128], reads=[B_bo[gch]], writes=[vab_])
                        if isA:
                            fw.dma("pool", kc_[:], cakT[l, h], writes=[kcb_])
                            fw.dma("pool", vc_[:], cav[l, h].rearrange("(b p) d -> p b d", p=128), writes=[vcb_])
                        else:
                            fw.dma("pool", kc_[:], cbkT[l, h], writes=[kcb_])
                            fw.dma("pool", vc_[:], cbv[l, h].rearrange("(b p) d -> p b d", p=128), writes=[vcb_])

                        def mkgroups(q_lat, q_latb, q_ctx, q_ctxb):
                            groups = []
                            for kb in range(16):
                                r, t4 = kb // 4, kb % 4
                                sub = dict(k=ka[:, r, t4 * 128:(t4 + 1) * 128], kb=kab, q=q_lat, qb=q_latb,
                                           v=va_[:, r, t4 * 128:(t4 + 1) * 128], vb=vab_, c0=0, n=T)
                                if isA:
                                    nbt, nbb = nb_ring()
                                    fw.dma("sp", nbt[:], nbias_d[l, h, kb], writes=[nbb])
                                    sub["bias"] = nbt[:]
                                    sub["bb"] = nbb
                                groups.append([sub])
                            for kb in range(4):
                                groups.append([dict(k=kc_[:, kb * 128:(kb + 1) * 128], kb=kcb_, q=q_ctx, qb=q_ctxb,
                                                    v=vc_[:, kb, :], vb=vcb_, c0=0, n=T)])
                            return groups
                        if isA:
                            attention(mkgroups(qta[:, h, :], B_qta, qta[:, h, :], B_qta), 128.0 ** -0.5, et_ring, bt_ring, fin_A(h))
                        else:
                            ys = []
                            for (qr, qrb, qp, qpb) in ((q1r, B_q1r, q1p, B_q1p), (q2r, B_q2r, q2p, B_q2p)):
                                yt, ytb = yb_ring()
                                attention(mkgroups(qr[:, h, :], qrb, qp[:, h, :], qpb), 64.0 ** -0.5, et_ring, bt_ring, fin_B(yt, ytb))
                                ys.append((yt, ytb))
                            diff_finish(h, ys[0][0], ys[0][1], ys[1][0], ys[1][1])
                fw.barrier()

            kstop("c6")
            with ExitStack() as so:
                residual_proj(l, w_out[l], 16, 512, lambda c: ycat[:, c, :], [B_y], 2, row, so, "o")
                layernorm(l, 0, so)
                fw.barrier()
        kstop("c7")
        modulate(l, 3, 4, row)
        with ExitStack() as sf:
            actb, B_act = P.sb(sf, "actb", [128, 43, T], BF16)
            u_ring = P.ring(sf, "u1", 4, [128, T], F32)
            hal, B_hal = P.sb(sf, "hal", [128, 2, 2], F32)
            if g == "S":
                hbs, B_hbs = P.sb(sf, "hbs", [128, 16, 2], F32)
                hba, B_hba = P.sb(sf, "hba", [128, 4, 32], F32)
                hbf, B_hbf = P.sb(sf, "hbf", [128, 16, 2], F32)
                A(lambda: nc.scalar.copy(out=hbs[:, :, 0], in_=h_sb[:, :, 0]), [B_h], [B_hbs])
                A(lambda: nc.scalar.copy(out=hbs[:, :, 1], in_=h_sb[:, :, T - 1]), [B_h], [B_hbs])
                fw.dma("sp", hb_in, hbs[:].rearrange("p c k -> p (c k)"), reads=[B_hbs], writes=[B_hbi])
                fw.allgather(hb_in, hb_out, reads=[B_hbi], writes=[B_hbo])
                fw.dma("sp", hba[:], hb_out.rearrange("(r p) f -> p r f", p=128), reads=[B_hbo], writes=[B_hba])
                hv = hba[:].rearrange("p r (c k) -> p r c k", k=2)
                for (side, kk, so_) in ((0, 1, 0), (1, 0, 4)):
                    A(lambda: nc.scalar.activation(out=hbf[:, :, side], in_=hv[:, 0, :, kk], func=AF.Identity, scale=sel[:, so_:so_ + 1]), [B_hba, B_c], [B_hbf])
                    for r in range(1, 4):
                        V(lambda: nc.vector.scalar_tensor_tensor(out=hbf[:, :, side], in0=hv[:, r, :, kk], scalar=sel[:, so_ + r:so_ + r + 1],
                                                                 in1=hbf[:, :, side], op0=ALU.mult, op1=ALU.add), [B_hba, B_c, B_hbf], [B_hbf])
                A(lambda: nc.scalar.copy(out=hh_sb[:], in_=hbf[:]), [B_hbf], [B_hh])
            segs = [(0, 256), (256, 256)] if g == "P" else [(0, 512)]

            def conv_chunk(ps, pb, hps, hpb, ch):
                u, ub = u_ring()
                cp = convp[:, l, ch, :]
                A(lambda: nc.scalar.activation(out=u[:], in_=ps[:], func=AF.Identity, scale=cp[:, 1:2], bias=cp[:, 3:4]), [pb, B_c], [ub])
                for (s0_, n) in segs:
                    V(lambda: nc.vector.scalar_tensor_tensor(out=u[:, s0_ + 1:s0_ + n], in0=ps[:, s0_:s0_ + n - 1], scalar=cp[:, 0:1],
                                                             in1=u[:, s0_ + 1:s0_ + n], op0=ALU.mult, op1=ALU.add), [pb, B_c, ub], [ub])
                    V(lambda: nc.vector.scalar_tensor_tensor(out=u[:, s0_:s0_ + n - 1], in0=ps[:, s0_ + 1:s0_ + n], scalar=cp[:, 2:3],
                                                             in1=u[:, s0_:s0_ + n - 1], op0=ALU.mult, op1=ALU.add), [pb, B_c, ub], [ub])
                if g == "S":
                    V(lambda: nc.vector.scalar_tensor_tensor(out=u[:, 0:1], in0=hps[:, 0:1], scalar=cp[:, 0:1], in1=u[:, 0:1],
                                                             op0=ALU.mult, op1=ALU.add), [hpb, B_c, ub], [ub])
                    V(lambda: nc.vector.scalar_tensor_tensor(out=u[:, T - 1:T], in0=hps[:, 1:2], scalar=cp[:, 2:3], in1=u[:, T - 1:T],
                                                             op0=ALU.mult, op1=ALU.add), [hpb, B_c, ub], [ub])
                return u, ub

            with extra_slots(2):
                for ti in range(11):
                    ncol = 512 if ti < 10 else 384
                    wa, wab = wload(w_up[l, :, ti * 512:ti * 512 + ncol], 16, ncol)
                    wg, wgb = wload(w_up[l, :, DFF + ti * 512:DFF + ti * 512 + ncol], 16, ncol)
                    for q in range(ncol // 128):
                        j = ti * 4 + q
                        res = []
                        for (wv, wb, ch) in ((wa, wab, j), (wg, wgb, 43 + j)):
                            ps, pb = ps_short()
                            for c in range(16):
                                mm(ps[:], wv[:, c, q * 128:(q + 1) * 128], h_sb[:, c, :], c == 0, c == 15, [wb, B_h], [pb])
                            hps, hpb = None, None
                            if g == "S":
                                hps, hpb = ps_short()
                                for c in range(16):
                                    mm(hps[:, 0:2], wv[:, c, q * 128:(q + 1) * 128], hh_sb[:, c, :], c == 0, c == 15, [wb, B_hh], [hpb])
                            res.append(conv_chunk(ps, pb, hps, hpb, ch))
                        (ua, uab), (ug, ugb) = res
                        A(lambda: nc.scalar.activation(out=ug[:], in_=ug[:], func=AF.Silu), [ugb], [ugb])
                        V(lambda: nc.vector.tensor_tensor(out=actb[:, j, :], in0=ug[:], in1=ua[:], op=ALU.mult), [ugb, uab], [B_act])
            residual_proj(l, w_down[l], 43, 128, lambda c: actb[:, c, :], [B_act], 5, row, sf, "d")
            layernorm(l, 1, sf)
            fw.barrier()
        if l == L - 1:
            fw.dma("sp", yout[g].rearrange("(c p) t -> p c t", p=128), x_sb[:], reads=[B_x], writes=[B_out])
        else:
            fw.dma("sp", xspill[g].rearrange("(c p) t -> p c t", p=128), x_sb[:], reads=[B_x], writes=[B_spill[g]])

    nblk = 0
    try:
        for l in range(L):
            for g in ("P", "S"):
                if KSTOP.startswith("b") and nblk >= int(KSTOP[1]):
                    break
                if KSTOP.startswith("c") and nblk >= 1:
                    break
                block(l, g)
                nblk += 1
    except StopBuild:
        fw.barrier()
        fw.finish()
        return P
    fw.finish()
    P.es.close()
    fw.close()
    return P


def _consts():
    c = np.zeros((NCONST, 128, 128), np.float32)
    idx = np.arange(128)
    c[C_ID] = np.eye(128)
    c[C_ONE] = 1.0
    same = (idx[:, None] // 64) == (idx[None, :] // 64)
    c[C_TRIF] = (same & (idx[:, None] <= idx[None, :]))
    c[C_TRIB] = (same & (idx[:, None] >= idx[None, :]))
    for k, p in ((C_E0, 0), (C_E63, 63), (C_E64, 64), (C_E127, 127)):
        c[k][p, :] = 1.0
    dd = idx % 32
    partner = np.where(dd < 16, idx + 16, idx - 16)
    c[C_PERM][partner, idx] = 1.0
    negm = np.zeros((2, 128, 5, 128), np.float32)
    negm[0] = np.where(c[C_TRIF].T[:, None, :] > 0, 0.0, NEG)
    negm[1] = np.where(c[C_TRIB].T[:, None, :] > 0, 0.0, NEG)
    return (np.ascontiguousarray(c.transpose(1, 0, 2)).reshape(128, NCONST * 128),
            np.ascontiguousarray(negm.transpose(1, 0, 2, 3)).reshape(128, 2 * 640))


def _rope_tables(j):
    t = np.arange(T) + j * T
    rows = (t // 64).astype(np.float32)
    cols = (t % 64).astype(np.float32)
    freqs = (np.float32(10000.0) ** (-np.arange(0, 32, 2, dtype=np.float32) / np.float32(32))).astype(np.float32)
    p = np.arange(128)
    dd = p % 64
    idx = dd % 32
    f = idx % 16
    first = idx < 16
    pos = np.where((dd < 32)[:, None], rows[None, :], cols[None, :]).astype(np.float32)
    ang = (pos * freqs[f][:, None]).astype(np.float32)
    cos = np.cos(ang).astype(np.float32)
    sin = np.sin(ang).astype(np.float32)
    sins = np.where(first[:, None], -sin, sin).astype(np.float32)
    return np.concatenate([cos, sins], axis=1)


def _natten_bias(a_rpb, j):
    kt = np.arange(2048)
    krow, kcol = kt // 64, kt % 64
    qt = np.arange(T) + j * T
    qrow, qcol = qt // 64, qt % 64
    rs = np.clip(qrow - 4, 0, 24)
    cs = np.clip(qcol - 8, 0, 48)
    vr = (krow[:, None] >= rs[None, :]) & (krow[:, None] < rs[None, :] + 8)
    vcm = (kcol[:, None] >= cs[None, :]) & (kcol[:, None] < cs[None, :] + 16)
    valid = vr & vcm
    ri = np.clip(7 + krow[:, None] - qrow[None, :], 0, 14)
    ci = np.clip(kcol[:, None] - qcol[None, :] + 15, 0, 30)
    out = np.empty((L, 6, 2048, T), np.float32)
    for l in range(L):
        for h in range(6):
            out[l, h] = np.where(valid, a_rpb[l, h][ri, ci], np.float32(NEG))
    return out.reshape(L, 6, 16, 128, T)


def _combine_tables(j):
    cf = np.zeros((2, 5, 4), np.float32)
    vt = np.zeros((2, 5), np.float32)
    for r2 in range(4):
        if r2 < j:
            cf[0, 0, r2] = 1
        if r2 > j:
            cf[1, 0, r2] = 1
    for r in range(4):
        vt[0, 1 + r] = 0.0 if r < j else NEG
        vt[1, 1 + r] = 0.0 if r > j else NEG
        for r2 in range(4):
            if r < r2 < j:
                cf[0, 1 + r, r2] = 1
            if j < r2 < r:
                cf[1, 1 + r, r2] = 1
    cft = np.broadcast_to(cf[None, :, :, :, None], (128, 2, 5, 4, 5)).reshape(128, -1)
    vtt = np.broadcast_to(vt[None, :, :, None], (128, 2, 5, 5)).reshape(128, -1)
    sel = np.zeros((8,), np.float32)
    if j > 0:
        sel[j - 1] = 1
    if j < 3:
        sel[4 + j + 1] = 1
    return np.ascontiguousarray(cft), np.ascontiguousarray(vtt), np.ascontiguousarray(np.broadcast_to(sel[None], (128, 8)))


_PROG = None


def kernel(x_prompt, x_sample, cache_a_k, cache_a_v, cache_b_k, cache_b_v, state_c_C, state_c_n, state_c_m,
           c, c_ctx, w_mod, b_mod, w_in, c_gate_b, a_rpb, b_lambda, b_subln, c_norm, w_out,
           ln1_g, ln1_b, ln2_g, ln2_b, w_up, conv_w, conv_b, w_down):
    global _PROG
    f = lambda a: np.ascontiguousarray(np.asarray(a, dtype=np.float32))
    x_prompt, x_sample = f(x_prompt), f(x_sample)
    if _PROG is None:
        _PROG = build_program()
    P = _PROG
    consts, negm = _consts()
    lnp = np.stack([f(ln1_g), f(ln1_b), f(ln2_g), f(ln2_b)], 1).reshape(L, 4, 16, 128).transpose(3, 0, 1, 2).reshape(128, -1)
    cw = np.concatenate([f(conv_w), f(conv_b)[:, None, :]], 1)
    convp = cw.reshape(L, 4, 86, 128).transpose(3, 0, 2, 1).reshape(128, -1)
    shared = {
        "w_in": f(w_in), "w_out": f(w_out), "w_up": f(w_up), "w_down": f(w_down),
        "lnp": np.ascontiguousarray(lnp), "convp": np.ascontiguousarray(convp),
        "gateb": f(c_gate_b).reshape(-1), "blam": f(b_lambda).reshape(-1),
        "subln": np.ascontiguousarray(f(b_subln).T), "cnorm": f(c_norm).reshape(-1),
        "consts": consts, "negm": negm,
    }
    w_mod, b_mod = f(w_mod), f(b_mod)
    in_maps = []
    for i in range(8):
        b, j = i // 4, i % 4
        m = dict(shared)
        m["xpT"] = np.ascontiguousarray(x_prompt[2 * i:2 * i + 2].reshape(T, D).T)
        m["xsT"] = np.ascontiguousarray(x_sample[b, j * T:(j + 1) * T].T)
        m["wmod"] = np.ascontiguousarray(w_mod[:, :, j * 3072:(j + 1) * 3072])
        m["bmod"] = np.ascontiguousarray(b_mod[:, j * 3072:(j + 1) * 3072].reshape(L, 24, 128).transpose(2, 0, 1).reshape(128, -1))
        cond = np.stack([f(c_ctx), f(c)[b]], 1)
        m["cond2"] = np.ascontiguousarray(cond.reshape(16, 128, 2).transpose(1, 0, 2).reshape(128, 32))
        m["rope"] = _rope_tables(j)
        m["nbias"] = _natten_bias(f(a_rpb), j)
        m["cakT"] = np.ascontiguousarray(f(cache_a_k)[b].transpose(0, 1, 3, 2))
        m["cav"] = f(cache_a_v)[b]
        m["cbkT"] = np.ascontiguousarray(f(cache_b_k)[b].transpose(0, 1, 3, 2))
        m["cbv"] = f(cache_b_v)[b]
        m["cC"] = f(state_c_C)[b]
        m["cn"] = np.ascontiguousarray(f(state_c_n)[b].reshape(L, 10, 128).transpose(0, 2, 1))
        m["cm"] = f(state_c_m)[b].reshape(-1)
        m["cftab"], m["vtab"], m["sel"] = _combine_tables(j)
        in_maps.append(m)
    in_maps = [{k: v for k, v in m.items() if k in P.din} for m in in_maps]
    res = run_bass_kernel_spmd(P.nc, in_maps, core_ids=list(range(8))).results
    yp = np.stack([r["ypT"].T.reshape(2, 256, D) for r in res], 0).reshape(16, 256, D)
    ys = np.stack([r["ysT"].T for r in res], 0).reshape(2, 4 * T, D)
    cat = lambda k: np.concatenate([r[k] for r in res], 0)
    n_ak = np.ascontiguousarray(cat("o_ak").transpose(0, 1, 2, 4, 3))
    n_av = cat("o_av")
    n_bk = np.ascontiguousarray(cat("o_bk").transpose(0, 1, 2, 4, 3))
    n_bv = cat("o_bv")
    return (np.ascontiguousarray(yp, dtype=np.float32), np.ascontiguousarray(ys, dtype=np.float32), n_ak, n_av, n_bk, n_bv,
            cat("o_cC"), np.ascontiguousarray(cat("o_cn").transpose(0, 1, 2, 4, 3)), cat("o_cm"))
```

```python
import math
import os
KSTOP = os.environ.get('KSTOP', '')
KSKIP = os.environ.get('KSKIP', '')
SKIP_IN = set()
if KSTOP.startswith('c'):
    SKIP_IN = {"nbias", "cakT", "cav", "cbkT", "cbv", "cC", "cn", "cm", "xsT"}
    if not KSTOP[1].isdigit() or int(KSTOP[1]) < 8:
        SKIP_IN |= {"w_up", "w_down"}
    if not KSTOP[1].isdigit() or int(KSTOP[1]) < 7:
        SKIP_IN |= {"w_out"}


class StopBuild(Exception):
    pass


def kstop(tag):
    if KSTOP == tag:
        raise StopBuild(tag)
from contextlib import ExitStack
import numpy as np
import ml_dtypes
import concourse.bass as bass
import concourse.mybir as mybir
from concourse.bass_utils import run_bass_kernel_spmd

F32 = mybir.dt.float32
BF16 = mybir.dt.bfloat16
AF = mybir.ActivationFunctionType
ALU = mybir.AluOpType
AX = mybir.AxisListType

L = 2
D = 2048
T = 512
DFF = 5504
NEG = -30000.0
ALPHA = (2 * L) ** 0.25
LN_EPS = 1e-5
RMS_EPS = 1e-6
RG = [[0, 1, 2, 3], [4, 5, 6, 7]]
NCONST = 11
C_ID, C_ONE, C_TRIF, C_TRIB, C_E0, C_E63, C_E64, C_E127, C_PERM, C_HA, C_HB = range(NCONST)


class Buf:
    __slots__ = ("name", "w", "r")

    def __init__(self, name=""):
        self.name = name
        self.w = None
        self.r = []


class FW:
    NDMA = 48

    def __init__(self, nc):
        self.nc = nc
        self.eng = {"pe": nc.tensor, "act": nc.scalar, "dve": nc.vector, "pool": nc.gpsimd, "sp": nc.sync}
        self.sem, self.cnt, self._stack = {}, {}, []
        for e in self.eng:
            cm = nc.semaphore("s_" + e)
            self.sem[e] = cm.__enter__()
            self._stack.append(cm)
            self.cnt[e] = 0
        self.dsem, self.dcnt = [], []
        for i in range(self.NDMA):
            cm = nc.semaphore("d_%d" % i)
            self.dsem.append(cm.__enter__())
            self._stack.append(cm)
            self.dcnt.append(0)
        self.dnext = 0
        self.seen = {e: {} for e in self.eng}
        self.ninst = 0
        self.nwaits = 0

    def close(self):
        for cm in reversed(self._stack):
            cm.__exit__(None, None, None)

    def _semobj(self, key):
        return self.sem[key] if isinstance(key, str) else self.dsem[key]

    def _wait(self, e, tok):
        if tok is None:
            return
        key, val = tok
        if self.seen[e].get(key, 0) >= val:
            return
        self.eng[e].wait_ge(self._semobj(key), val)
        self.seen[e][key] = val
        self.nwaits += 1

    def _deps(self, e, reads, writes, is_dma=False):
        for b in reads:
            if b.w is not None:
                if b.w[0] == e and (e == "pe" or is_dma):
                    continue
                self._wait(e, b.w)
        skip_same = (e == "pe" or is_dma)
        for b in writes:
            if b.w is not None and not (skip_same and b.w[0] == e):
                self._wait(e, b.w)
            for t in b.r:
                if not (skip_same and t[0] == e):
                    self._wait(e, t)

    def _commit(self, tok, reads, writes):
        for b in reads:
            b.r.append(tok)
            if len(b.r) > 16:
                best = {}
                for k, v in b.r:
                    if best.get(k, 0) < v:
                        best[k] = v
                b.r = list(best.items())
        for b in writes:
            b.w = tok
            b.r = []

    def op(self, e, fn, reads=(), writes=(), inc=True):
        self._deps(e, reads, writes)
        ins = fn()
        self.ninst += 1
        if inc:
            self.cnt[e] += 1
            ins.then_inc(self.sem[e], 1)
            tok = (e, self.cnt[e])
        else:
            tok = (e, self.cnt[e] + 1)
        self._commit(tok, reads, writes)
        return tok

    def _next_dsem(self, q, kind=None):
        kind = kind or q
        lo, hi = {"sp": (0, 32), "pool": (32, 44), "cc": (44, 48)}[kind]
        if not hasattr(self, "dnx"):
            self.dnx = {}
        i = self.dnx.get(kind, lo)
        self.dnx[kind] = lo + (i + 1 - lo) % (hi - lo)
        if self.dcnt[i] > 0:
            self._wait(q, (i, self.dcnt[i]))
        return i

    def dma(self, q, out, in_, reads=(), writes=(), slow=False):
        self._deps(q, reads, writes, is_dma=True)
        i = self._next_dsem(q)
        if slow:
            ins = self.eng[q].dma_start(out=out, in_=in_, allow_slow_non_contiguous=True)
        else:
            ins = self.eng[q].dma_start(out=out, in_=in_)
        self.dcnt[i] += 16
        ins.then_inc(self.dsem[i], 16)
        self.ninst += 1
        tok = (i, self.dcnt[i])
        self._commit(tok, reads, writes)
        return tok

    def allgather(self, in_ap, out_ap, reads=(), writes=()):
        q = "pool"
        self._deps(q, reads, writes, is_dma=True)
        i = self._next_dsem(q, "cc")
        ins = self.nc.gpsimd.collective_compute("AllGather", ALU.bypass, replica_groups=RG, ins=[in_ap], outs=[out_ap])
        self.dcnt[i] += 1
        ins.then_inc(self.dsem[i], 1)
        self.ninst += 1
        tok = (i, self.dcnt[i])
        self._commit(tok, reads, writes)
        return tok

    def soft_barrier(self):
        for e in self.eng:
            if e != "pe" and self.cnt["pe"] > 0:
                self._wait(e, ("pe", self.cnt["pe"]))
            for i in range(32, 44):
                if self.dcnt[i] > 0:
                    self._wait(e, (i, self.dcnt[i]))

    def barrier(self):
        for e in self.eng:
            for f in self.eng:
                if f != e and self.cnt[f] > 0:
                    self._wait(e, (f, self.cnt[f]))
            for i in range(self.NDMA):
                if self.dcnt[i] > 0:
                    self._wait(e, (i, self.dcnt[i]))

    def finish(self):
        for i in range(self.NDMA):
            if self.dcnt[i] > 0:
                self._wait("sp", (i, self.dcnt[i]))


class Prog:
    def __init__(self):
        nc = bass.Bass("TRN2", target_bir_lowering=False)
        self.nc = nc
        self.fw = FW(nc)
        self.es = ExitStack()
        self.ring_idx = {}
        self.din = {}
        self.dout = {}

    def inp(self, name, shape, dt=F32):
        if name in SKIP_IN:
            return None
        t = self.nc.dram_tensor(name, list(shape), dt, kind="ExternalInput").ap()
        self.din[name] = t
        return t

    def outp(self, name, shape, dt=F32):
        t = self.nc.dram_tensor(name, list(shape), dt, kind="ExternalOutput").ap()
        self.dout[name] = t
        return t

    def scratch(self, name, shape, dt=F32):
        return self.nc.dram_tensor(name, list(shape), dt).ap()

    def sb(self, es, name, shape, dt=F32):
        self.uid = getattr(self, "uid", 0) + 1
        t = es.enter_context(self.nc.sbuf_tensor("%s_%d" % (name, self.uid), list(shape), dt))
        return t, Buf(name)

    def ring(self, es, name, n, shape, dt=F32):
        items = [self.sb(es, "%s%d" % (name, i), shape, dt) for i in range(n)]
        key = name
        self.ring_idx[key] = 0

        def nxt():
            i = self.ring_idx[key]
            self.ring_idx[key] = (i + 1) % n
            return items[i]
        return nxt

    def V(self, fn, reads=(), writes=()):
        return self.fw.op("dve", fn, reads, writes)

    def A(self, fn, reads=(), writes=()):
        return self.fw.op("act", fn, reads, writes)

    def G(self, fn, reads=(), writes=()):
        return self.fw.op("pool", fn, reads, writes)

    def PE(self, fn, reads=(), writes=(), inc=True):
        return self.fw.op("pe", fn, reads, writes, inc=inc)

    def mm(self, out, lhsT, rhs, start, stop, reads, writes, inc=None, sgc=False):
        nc = self.nc
        return self.fw.op("pe", lambda: nc.tensor.matmul(out, lhsT=lhsT, rhs=rhs, start=start, stop=stop, skip_group_check=sgc),
                          reads, writes, inc=(stop if inc is None else inc))


def build_program():
    P = Prog()
    nc, fw = P.nc, P.fw
    V, A, PE, mm, G = P.V, P.A, P.PE, P.mm, P.G

    xin = {"P": P.inp("xpT", [D, T]), "S": P.inp("xsT", [D, T])}
    w_in = P.inp("w_in", [L, D, 6804])
    w_out = P.inp("w_out", [L, D, D])
    w_up = P.inp("w_up", [L, D, 2 * DFF])
    w_down = P.inp("w_down", [L, DFF, D])
    wmod = P.inp("wmod", [L, D, 3072])
    bmod = P.inp("bmod", [128, L * 24])
    cond2 = P.inp("cond2", [128, 32])
    lnp_d = P.inp("lnp", [128, L * 4 * 16])
    convp_d = P.inp("convp", [128, L * 86 * 4])
    gateb_d = P.inp("gateb", [L * 20])
    blam_d = P.inp("blam", [L * 256])
    subln_d = P.inp("subln", [128, L])
    cnorm_d = P.inp("cnorm", [L * 128])
    consts_d = P.inp("consts", [128, NCONST * 128])
    negm_d = P.inp("negm", [128, 2 * 640])
    rope_d = P.inp("rope", [128, 2 * T])
    nbias_d = P.inp("nbias", [L, 6, 16, 128, T])
    cakT = P.inp("cakT", [L, 6, 128, 512])
    cav = P.inp("cav", [L, 6, 512, 128])
    cbkT = P.inp("cbkT", [L, 5, 128, 512])
    cbv = P.inp("cbv", [L, 5, 512, 128])
    cC_d = P.inp("cC", [L, 2, 5, 128, 128])
    cn_d = P.inp("cn", [L, 128, 10])
    cm_d = P.inp("cm", [L * 10])
    cftab_d = P.inp("cftab", [128, 2 * 5 * 4 * 5])
    vtab_d = P.inp("vtab", [128, 2 * 5 * 5])
    sel_d = P.inp("sel", [128, 8])

    yout = {"P": P.outp("ypT", [D, T]), "S": P.outp("ysT", [D, T])}
    o_ak = P.outp("o_ak", [2, L, 6, 128, 256])
    o_av = P.outp("o_av", [2, L, 6, 256, 128])
    o_bk = P.outp("o_bk", [2, L, 5, 128, 256])
    o_bv = P.outp("o_bv", [2, L, 5, 256, 128])
    o_cC = P.outp("o_cC", [2, L, 2, 5, 128, 128])
    o_cn = P.outp("o_cn", [2, L, 2, 128, 5])
    o_cm = P.outp("o_cm", [2, L, 2, 5])
    B_out = Buf("outputs")

    xspill = {"P": P.scratch("xspP", [D, T]), "S": P.scratch("xspS", [D, T])}
    B_spill = {"P": Buf(), "S": Buf()}
    mg_in = P.scratch("mg_in", [128, 96]); mg_out = P.scratch("mg_out", [512, 96])
    B_mgi, B_mgo = Buf(), Buf()
    CH_NH = [4, 4, 3]
    HH_CH = [0, 0, 0, 0, 1, 1, 1, 1, 2, 2, 2]
    HH_IX = [0, 1, 2, 3, 0, 1, 2, 3, 0, 1, 2]
    bnc_in = [P.scratch("bnc_in%d" % i, [2 * n * 128, T], BF16) for i, n in enumerate(CH_NH)]
    bnc_out = [P.scratch("bnc_out%d" % i, [4 * 2 * n * 128, T], BF16) for i, n in enumerate(CH_NH)]
    B_bi = [Buf() for _ in CH_NH]
    B_bo = [Buf() for _ in CH_NH]

    def bnc_k_rows(hh):
        i = HH_IX[hh]
        return bnc_in[HH_CH[hh]][i * 128:(i + 1) * 128, :], B_bi[HH_CH[hh]]

    def bnc_v_rows(hh):
        c = HH_CH[hh]
        i = CH_NH[c] + HH_IX[hh]
        return bnc_in[c][i * 128:(i + 1) * 128, :], B_bi[c]
    CSF = 1310
    cs_in = P.scratch("cs_in", [128, CSF]); cs_out = P.scratch("cs_out", [512, CSF])
    B_csi, B_cso = Buf(), Buf()
    hb_in = P.scratch("hb_in", [128, 32]); hb_out = P.scratch("hb_out", [512, 32])
    B_hbi, B_hbo = Buf(), Buf()

    es = P.es
    x_sb, B_x = P.sb(es, "x_sb", [128, 16, T], F32)
    h_sb, B_h = P.sb(es, "h_sb", [128, 16, T], BF16)
    hh_sb, B_hh = P.sb(es, "hh_sb", [128, 16, 2], BF16)
    WSL = 8704
    wslots = [P.sb(es, "wr%d" % i, [128, WSL], BF16) for i in range(2)]
    wstate = {"i": 0}

    def wring():
        i = wstate["i"] % len(wslots)
        wstate["i"] += 1
        return wslots[i]

    class extra_slots:
        def __init__(self, want, reserve=2048):
            self.want, self.reserve = want, reserve

        def __enter__(self):
            self.sx = ExitStack()
            self.n = 0
            while self.n < self.want and nc.sbuf_bytes_remaining >= WSL * 2 + self.reserve + 256:
                wslots.append(P.sb(self.sx, "wx", [128, WSL], BF16))
                self.n += 1
            return self

        def __exit__(self, *a):
            if a[0] is None:
                fw.soft_barrier()
                for _ in range(self.n):
                    wslots.pop()
                self.sx.close()
            return False
    cst, B_c = P.sb(es, "cst", [128, NCONST, 128], F32)
    negm, _ = P.sb(es, "negm", [128, 2, 640], F32)
    rope, _ = P.sb(es, "rope", [128, 2, T], F32)
    ones_bf, _ = P.sb(es, "ones_bf", [128, 128], BF16)
    id_bf, _ = P.sb(es, "id_bf", [128, 128], BF16)
    tri_bf, _ = P.sb(es, "tri_bf", [128, 2, 128], BF16)
    lnp, _ = P.sb(es, "lnp", [128, L, 4, 16], F32)
    convp, _ = P.sb(es, "convp", [128, L, 86, 4], F32)
    gateb, _ = P.sb(es, "gateb", [128, L, 20], F32)
    blam, _ = P.sb(es, "blam", [128, L, 4, 64], F32)
    subln, _ = P.sb(es, "subln", [128, L], F32)
    cnorm, _ = P.sb(es, "cnorm", [128, L, 128], F32)
    cftab, _ = P.sb(es, "cftab", [128, 2, 5, 4, 5], F32)
    vtab, _ = P.sb(es, "vtab", [128, 2, 5, 5], F32)
    sel, _ = P.sb(es, "sel", [128, 8], F32)
    modv, B_modv = P.sb(es, "modv", [128, L, 96, 2], F32)
    nlam, B_nlam = P.sb(es, "nlam", [128, L], F32)
    sublns, _ = P.sb(es, "sublns", [128, L], F32)

    def CM(i):
        return cst[:, i, :]

    pbanks = [es.enter_context(nc.psum_tensor("ps%d" % i, [128, 512], F32)) for i in range(8)]
    pbufs = [Buf("ps%d" % i) for i in range(8)]
    pidx = {"s": 0, "l": 0}

    def ps_short():
        i = pidx["s"]
        pidx["s"] = (i + 1) % 5
        return pbanks[i], pbufs[i]

    def ps_long():
        i = pidx["l"]
        pidx["l"] = (i + 1) % 3
        return pbanks[5 + i], pbufs[5 + i]

    def bcast(ap1d, n):
        return bass.AP(ap1d.tensor, 0, [[0, 128], [1, n]])

    fw.dma("sp", cst[:], consts_d.rearrange("p (k n) -> p k n", k=NCONST), writes=[B_c])
    fw.dma("sp", negm[:], negm_d.rearrange("p (k n) -> p k n", k=2), writes=[B_c])
    fw.dma("sp", rope[:], rope_d.rearrange("p (k n) -> p k n", k=2), writes=[B_c])
    fw.dma("sp", lnp[:], lnp_d.rearrange("p (l k c) -> p l k c", l=L, k=4), writes=[B_c])
    fw.dma("sp", convp[:], convp_d.rearrange("p (l c k) -> p l c k", l=L, k=4), writes=[B_c])
    fw.dma("sp", gateb[:], bcast(gateb_d, L * 20).rearrange("p (l k) -> p l k", l=L), writes=[B_c])
    fw.dma("sp", blam[:], bcast(blam_d, L * 256).rearrange("p (l k c) -> p l k c", l=L, k=4), writes=[B_c])
    fw.dma("sp", subln[:], subln_d, writes=[B_c])
    fw.dma("sp", cnorm[:], bcast(cnorm_d, L * 128).rearrange("p (l k) -> p l k", l=L), writes=[B_c])
    fw.dma("sp", cftab[:], cftab_d.rearrange("p (d i r h) -> p d i r h", d=2, i=5, r=4), writes=[B_c])
    fw.dma("sp", vtab[:], vtab_d.rearrange("p (d i h) -> p d i h", d=2, i=5), writes=[B_c])
    fw.dma("sp", sel[:], sel_d, writes=[B_c])
    A(lambda: nc.scalar.copy(out=ones_bf[:], in_=CM(C_ONE)), [B_c], [B_c])
    A(lambda: nc.scalar.copy(out=id_bf[:], in_=CM(C_ID)), [B_c], [B_c])
    A(lambda: nc.scalar.copy(out=tri_bf[:, 0, :], in_=CM(C_TRIF)), [B_c], [B_c])
    A(lambda: nc.scalar.copy(out=tri_bf[:, 1, :], in_=CM(C_TRIB)), [B_c], [B_c])

    pre = {}

    def wload(src2d, kc, ncols, key=None):
        if key is not None and key in pre:
            return pre.pop(key)
        return _wload(src2d, kc, ncols)

    def prefetch(key, src2d, kc, ncols):
        pre[key] = _wload(src2d, kc, ncols)

    def _wload(src2d, kc, ncols):
        t, b = wring()
        view = t[:, 0:kc * ncols].rearrange("p (c n) -> p c n", n=ncols)
        srcv = src2d.rearrange("(c p) n -> p c n", p=128)
        step = max(1, 2048 // 128 // 1 if ncols >= 256 else 8)
        step = 16 if ncols >= 256 else 22
        for c0 in range(0, kc, step):
            c1 = min(kc, c0 + step)
            fw.dma("pool", view[:, c0:c1, :], srcv[:, c0:c1, :], writes=[b])
        return view, b

    with ExitStack() as s0:
        c2, B_c2 = P.sb(s0, "c2", [128, 16, 2], F32)
        c2b, _ = P.sb(s0, "c2b", [128, 16, 2], BF16)
        bm, B_bm = P.sb(s0, "bm", [128, L, 24], F32)
        mloc, B_ml = P.sb(s0, "mloc", [128, L, 24, 2], F32)
        mall, B_ma = P.sb(s0, "mall", [128, 4, L, 24, 2], F32)
        fw.dma("sp", c2[:], cond2.rearrange("p (c r) -> p c r", r=2), writes=[B_c2])
        fw.dma("sp", bm[:], bmod.rearrange("p (l c) -> p l c", l=L), writes=[B_bm])
        A(lambda: nc.scalar.activation(out=c2b[:], in_=c2[:], func=AF.Silu), [B_c2], [B_c2])
        with extra_slots(3):
            for l in range(L):
                for t4 in range(6):
                    wv, wb = wload(wmod[l, :, t4 * 512:(t4 + 1) * 512], 16, 512)
                    for q in range(4):
                        cc = t4 * 4 + q
                        ps, pb = ps_short()
                        for c in range(16):
                            mm(ps[:, 0:2], wv[:, c, q * 128:(q + 1) * 128], c2b[:, c, :], c == 0, c == 15, [wb, B_c2], [pb])
                        A(lambda: nc.scalar.activation(out=mloc[:, l, cc, :], in_=ps[:, 0:2], func=AF.Identity,
                                                       bias=bm[:, l, cc:cc + 1], scale=1.0), [pb, B_bm], [B_ml])
        fw.dma("sp", mg_in, mloc[:].rearrange("p l c r -> p (l c r)"), reads=[B_ml], writes=[B_mgi])
        fw.allgather(mg_in, mg_out, reads=[B_mgi], writes=[B_mgo])
        fw.dma("sp", mall[:].rearrange("p r l c w -> p r (l c w)"), mg_out.rearrange("(r p) f -> p r f", p=128),
               reads=[B_mgo], writes=[B_ma])
        for r in range(4):
            for l in range(L):
                A(lambda: nc.scalar.copy(out=modv[:, l, r * 24:(r + 1) * 24, :], in_=mall[:, r, l, :, :]), [B_ma], [B_modv])
        for l in range(L):
            for v0 in (16, 64):
                A(lambda: nc.scalar.activation(out=modv[:, l, v0:v0 + 16, :], in_=modv[:, l, v0:v0 + 16, :], func=AF.Identity, bias=1.0, scale=1.0),
                  [B_modv], [B_modv])
        lt, B_lt = P.sb(s0, "lt", [128, 64], F32)
        ld, B_ld = P.sb(s0, "ld", [128, 4], F32)
        for l in range(L):
            lam_init = 0.8 - 0.6 * math.exp(-0.3 * l)
            for k in range(2):
                V(lambda: nc.vector.tensor_tensor(out=lt[:], in0=blam[:, l, 2 * k, :], in1=blam[:, l, 2 * k + 1, :], op=ALU.mult), [B_c], [B_lt])
                V(lambda: nc.vector.reduce_sum(out=ld[:, k:k + 1], in_=lt[:], axis=AX.X), [B_lt], [B_ld])
            A(lambda: nc.scalar.activation(out=ld[:, 2:4], in_=ld[:, 0:2], func=AF.Exp), [B_ld], [B_ld])
            V(lambda: nc.vector.tensor_tensor(out=nlam[:, l:l + 1], in0=ld[:, 3:4], in1=ld[:, 2:3], op=ALU.subtract), [B_ld], [B_nlam])
            A(lambda: nc.scalar.activation(out=nlam[:, l:l + 1], in_=nlam[:, l:l + 1], func=AF.Identity, bias=-lam_init, scale=1.0), [B_nlam], [B_nlam])
            A(lambda: nc.scalar.activation(out=sublns[:, l:l + 1], in_=subln[:, l:l + 1], func=AF.Identity, scale=1.0 - lam_init), [B_c], [B_nlam])
        fw.barrier()

    def modp(l, v, fc, row):
        return modv[:, l, v * 16 + fc, row:row + 1]

    def modulate(l, vsh, vsc, row):
        for c in range(16):
            A(lambda: nc.scalar.activation(out=h_sb[:, c, :], in_=x_sb[:, c, :], func=AF.Identity,
                                           bias=modp(l, vsh, c, row), scale=modp(l, vsc, c, row)), [B_x, B_modv], [B_h])

    def layernorm(l, k, scope):
        sq_ring = P.ring(scope, "lnsq%d" % k, 2, [128, T], F32)
        st, B_st = P.sb(scope, "lnst%d" % k, [128, 2, T], F32)
        p1, b1 = ps_long()
        p2, b2 = ps_long()
        for c in range(16):
            sq, bq = sq_ring()
            A(lambda: nc.scalar.activation(out=sq[:], in_=x_sb[:, c, :], func=AF.Square), [B_x], [bq])
            mm(p1[:], CM(C_ONE), x_sb[:, c, :], c == 0, c == 15, [B_c, B_x], [b1], inc=True)
            mm(p2[:], CM(C_ONE), sq[:], c == 0, c == 15, [B_c, bq], [b2], inc=True)
        mean, var = st[:, 0, :], st[:, 1, :]
        A(lambda: nc.scalar.activation(out=mean, in_=p1[:], func=AF.Identity, scale=1.0 / D), [b1], [B_st])
        V(lambda: nc.vector.tensor_tensor(out=var, in0=mean, in1=mean, op=ALU.mult), [B_st], [B_st])
        V(lambda: nc.vector.scalar_tensor_tensor(out=var, in0=p2[:], scalar=1.0 / D, in1=var, op0=ALU.mult, op1=ALU.subtract), [b2, B_st], [B_st])
        A(lambda: nc.scalar.activation(out=var, in_=var, func=AF.Ln, bias=LN_EPS, scale=1.0), [B_st], [B_st])
        A(lambda: nc.scalar.activation(out=var, in_=var, func=AF.Exp, scale=-0.5), [B_st], [B_st])
        for c in range(16):
            V(lambda: nc.vector.tensor_tensor(out=x_sb[:, c, :], in0=x_sb[:, c, :], in1=mean, op=ALU.subtract), [B_x, B_st], [B_x])
            V(lambda: nc.vector.tensor_tensor(out=x_sb[:, c, :], in0=x_sb[:, c, :], in1=var, op=ALU.mult), [B_x, B_st], [B_x])
            A(lambda: nc.scalar.activation(out=x_sb[:, c, :], in_=x_sb[:, c, :], func=AF.Identity,
                                           bias=lnp[:, l, 2 * k + 1, c:c + 1], scale=lnp[:, l, 2 * k, c:c + 1]), [B_x, B_c], [B_x])

    def residual_proj(l, wsrc, kc, ncols_tile, rhs_fn, rhs_bufs, vgate, row, scope, tag, after=None):
        tmp_ring = P.ring(scope, "rp" + tag, 2, [128, T], F32)
        per = ncols_tile // 128
        with extra_slots(3):
            for tcol in range(D // ncols_tile):
                wv, wb = wload(wsrc[:, tcol * ncols_tile:(tcol + 1) * ncols_tile], kc, ncols_tile, key=(tag, l, tcol))
                for q in range(per):
                    fc = tcol * per + q
                    ps, pb = ps_short()
                    for c in range(kc):
                        mm(ps[:], wv[:, c, q * 128:(q + 1) * 128], rhs_fn(c), c == 0, c == kc - 1, [wb] + rhs_bufs, [pb])
                    tmp, tb = tmp_ring()
                    A(lambda: nc.scalar.activation(out=tmp[:], in_=ps[:], func=AF.Identity, scale=modp(l, vgate, fc, row)), [pb, B_modv], [tb])
                    V(lambda: nc.vector.scalar_tensor_tensor(out=x_sb[:, fc, :], in0=x_sb[:, fc, :], scalar=ALPHA, in1=tmp[:],
                                                             op0=ALU.mult, op1=ALU.add), [B_x, tb], [B_x])
        if after is not None:
            after()

    def attention(groups, scale, et_ring, btmp_ring, finish):
        yps, yb = ps_long()
        dps, db = ps_long()
        ng = len(groups)
        for gi, grp in enumerate(groups):
            st, sb_ = ps_short()
            for si, s in enumerate(grp):
                mm(st[:, s["c0"]:s["c0"] + s["n"]], s["k"], s["q"], si == 0, True, [s["kb"], s["qb"]], [sb_], inc=(si == len(grp) - 1), sgc=True)
            et, eb = et_ring()
            bias = grp[0].get("bias")
            if bias is not None:
                bt, btb = btmp_ring()
                V(lambda: nc.vector.scalar_tensor_tensor(out=bt[:], in0=st[:], scalar=scale, in1=bias, op0=ALU.mult, op1=ALU.add),
                  [sb_, grp[0]["bb"]], [btb])
                A(lambda: nc.scalar.activation(out=et[:], in_=bt[:], func=AF.Exp), [btb], [eb])
            else:
                A(lambda: nc.scalar.activation(out=et[:], in_=st[:], func=AF.Exp, scale=scale), [sb_], [eb])
            for si, s in enumerate(grp):
                mm(yps[:, s["c0"]:s["c0"] + s["n"]], s["v"], et[:, s["c0"]:s["c0"] + s["n"]],
                   gi == 0 and si == 0, gi == ng - 1, [s["vb"], eb], [yb], inc=False, sgc=True)
            mm(dps[:], ones_bf[:], et[:], gi == 0, gi == ng - 1, [B_c, eb], [db], inc=True)
        finish(yps, yb, dps, db)

    def block(l, g):
        row = 0 if g == "P" else 1
        lam_init = 0.8 - 0.6 * math.exp(-0.3 * l)
        xsrc = xin[g] if l == 0 else xspill[g]
        fw.dma("sp", x_sb[:], xsrc.rearrange("(c p) t -> p c t", p=128), reads=[B_spill[g]], writes=[B_x])
        modulate(l, 0, 1, row)
        with ExitStack() as sm:
            ycat, B_y = P.sb(sm, "ycat", [128, 16, T], BF16)
            with ExitStack() as sc:
                qtc, B_qtc = P.sb(sc, "qtc", [128, 5, T], BF16)
                ktc, B_ktc = P.sb(sc, "ktc", [128, 5, T], BF16)
                kcA, B_kcA = P.sb(sc, "kcA", [128, 4, 640], BF16)
                kcB, B_kcB = P.sb(sc, "kcB", [128, 4, 640], BF16)
                vc, B_vc = P.sb(sc, "vc", [128, 4, 640], BF16)
                sigoc, B_so = P.sb(sc, "sigoc", [128, 4, 640], F32)
                gat, B_gat = P.sb(sc, "gat", [128, 4, 20], F32)
                G(lambda: nc.gpsimd.memset(kcA[:], 0.0), [], [B_kcA])
                G(lambda: nc.gpsimd.memset(kcB[:], 0.0), [], [B_kcB])
                ctiles = [(4224, 512), (4736, 512), (5248, 512), (5760, 512), (6272, 532)]
                with extra_slots(2):
                    for (c0, ncol) in ctiles:
                        wv, wb = wload(w_in[l, :, c0:c0 + ncol], 16, ncol, key=("w_in", l, c0))
                        for q in range(min(4, ncol // 128)):
                            col = c0 + q * 128
                            k = col // 128
                            if 33 <= k <= 42:
                                ps, pb = ps_short()
                                for c in range(16):
                                    mm(ps[:], wv[:, c, q * 128:(q + 1) * 128], h_sb[:, c, :], c == 0, c == 15, [wb, B_h], [pb])
                                if k <= 37:
                                    A(lambda: nc.scalar.activation(out=qtc[:, k - 33, :], in_=ps[:], func=AF.Copy, scale=128.0 ** -0.5), [pb], [B_qtc])
                                else:
                                    A(lambda: nc.scalar.copy(out=ktc[:, k - 38, :], in_=ps[:]), [pb], [B_ktc])
                        segs = []
                        for (name, lo, hi) in (("kc", 4864, 5504), ("vc", 5504, 6144), ("oc", 6144, 6784), ("gc", 6784, 6804)):
                            a, b_ = max(lo, c0), min(hi, c0 + ncol)
                            if a < b_:
                                segs.append((name, a, b_, lo))
                        for (name, a, b_, lo) in segs:
                            for tt in range(4):
                                ps, pb = ps_short()
                                n = b_ - a
                                for c in range(16):
                                    mm(ps[:, 0:n], h_sb[:, c, tt * 128:(tt + 1) * 128], wv[:, c, a - c0:b_ - c0], c == 0, c == 15, [wb, B_h], [pb])
                                o0 = a - lo
                                if name == "kc":
                                    A(lambda: nc.scalar.copy(out=kcA[0:64, tt, o0:o0 + n], in_=ps[0:64, 0:n]), [pb], [B_kcA])
                                    A(lambda: nc.scalar.copy(out=kcB[64:128, tt, o0:o0 + n], in_=ps[64:128, 0:n]), [pb], [B_kcB])
                                elif name == "vc":
                                    A(lambda: nc.scalar.copy(out=vc[:, tt, o0:o0 + n], in_=ps[:, 0:n]), [pb], [B_vc])
                                elif name == "oc":
                                    A(lambda: nc.scalar.activation(out=sigoc[:, tt, o0:o0 + n], in_=ps[:, 0:n], func=AF.Sigmoid), [pb], [B_so])
                                else:
                                    V(lambda: nc.vector.tensor_tensor(out=gat[:, tt, :], in0=ps[:, 0:20], in1=gateb[:, l, :], op=ALU.add), [pb, B_c], [B_gat])
                for c0_ in (0, 512):
                    prefetch(("w_in", l, c0_), w_in[l, :, c0_:c0_ + 512], 16, 512)
                kstop("c1")
                bc, B_bc = P.sb(sc, "bc", [128, 2, 4, 10], F32)
                ea, B_ea = P.sb(sc, "ea", [128, 2, 4, 5], F32)
                bl, B_bl = P.sb(sc, "bl", [128, 2, 8, 10], F32)
                t5, B_t5 = P.sb(sc, "t5", [128, 4, 5], F32)
                vp, B_vp = P.sb(sc, "vp", [128, 2, 4, 5, 130], BF16)
                sgp = ExitStack()
                dg, B_dg = P.sb(sgp, "dg", [128, 5, 128], F32)
                mk, B_mk = P.sb(sgp, "mk", [128, 5, 128], F32)
                for d in range(2):
                    tri = CM(C_TRIF if d == 0 else C_TRIB)
                    for tt in range(4):
                        ig = gat[:, tt, d * 10:d * 10 + 5]
                        fg = gat[:, tt, d * 10 + 5:d * 10 + 10]
                        A(lambda: nc.scalar.activation(out=t5[:, 0, :], in_=fg, func=AF.Exp, scale=-1.0), [B_gat], [B_t5])
                        A(lambda: nc.scalar.activation(out=t5[:, 1, :], in_=t5[:, 0, :], func=AF.Ln, bias=1.0, scale=1.0), [B_t5], [B_t5])
                        ps, pb = ps_short()
                        mm(ps[:, 0:5], tri, t5[:, 1, :], True, True, [B_c, B_t5], [pb])
                        A(lambda: nc.scalar.copy(out=bc[:, d, tt, 0:5], in_=ps[:, 0:5]), [pb], [B_bc])
                        V(lambda: nc.vector.tensor_tensor(out=t5[:, 2, :], in0=ig, in1=bc[:, d, tt, 0:5], op=ALU.add), [B_gat, B_bc], [B_t5])
                        A(lambda: nc.scalar.activation(out=ea[:, d, tt, :], in_=t5[:, 2, :], func=AF.Exp), [B_t5], [B_ea])
                        for h in range(5):
                            A(lambda: nc.scalar.activation(out=dg[:, h, :], in_=CM(C_ID), func=AF.Identity, scale=t5[:, 2, h:h + 1]), [B_c, B_t5], [B_dg])
                        ps1, pb1 = ps_short()
                        mm(ps1[:, 0:384], CM(C_ONE), dg[:, 0:3, :].rearrange("p h s -> p (h s)"), True, True, [B_c, B_dg], [pb1])
                        ps2, pb2 = ps_short()
                        mm(ps2[:, 0:256], CM(C_ONE), dg[:, 3:5, :].rearrange("p h s -> p (h s)"), True, True, [B_c, B_dg], [pb2])
                        V(lambda: nc.vector.tensor_tensor(out=mk[:, 0:3, :].rearrange("p h s -> p (h s)"), in0=ps1[:, 0:384],
                                                          in1=negm[:, d, 0:384], op=ALU.add), [pb1, B_c], [B_mk])
                        V(lambda: nc.vector.tensor_tensor(out=mk[:, 3:5, :].rearrange("p h s -> p (h s)"), in0=ps2[:, 0:256],
                                                          in1=negm[:, d, 384:640], op=ALU.add), [pb2, B_c], [B_mk])
                        V(lambda: nc.vector.tensor_reduce(out=bc[:, d, tt, 5:10], in_=mk[:], axis=AX.X, op=ALU.max), [B_mk], [B_bc])
                        for X in range(2):
                            ep = (C_E63, C_E127)[X] if d == 0 else (C_E0, C_E64)[X]
                            ps, pb = ps_short()
                            mm(ps[:, 0:10], CM(ep), bc[:, d, tt, :], True, True, [B_c, B_bc], [pb])
                            A(lambda: nc.scalar.copy(out=bl[:, d, 2 * tt + X, :], in_=ps[:, 0:10]), [pb], [B_bl])
                        for h in range(5):
                            A(lambda: nc.scalar.activation(out=vp[:, d, tt, h, 0:128], in_=vc[:, tt, h * 128:(h + 1) * 128], func=AF.Identity,
                                                           scale=ea[:, d, tt, h:h + 1]), [B_vc, B_ea], [B_vp])
                        A(lambda: nc.scalar.copy(out=vp[:, d, tt, :, 128], in_=ea[:, d, tt, :]), [B_ea], [B_vp])

                fw.barrier()
                sgp.close()
                kstop("c2")
                mc, B_mc = P.sb(sc, "mc", [128, 2, 8, 5], F32)
                wold, B_wo = P.sb(sc, "wold", [128, 2, 8, 5], F32)
                snew, B_sn = P.sb(sc, "snew", [128, 2, 8, 5], F32)
                mcur, B_mcur = P.sb(sc, "mcur", [128, 2, 5], F32)
                mt, B_mt = P.sb(sc, "mt", [128, 2, 5], F32)
                nfacc, B_nf = P.sb(sc, "nfacc", [128, 2, 5], F32)
                cn, B_cn = P.sb(sc, "cn", [128, 10, 129], F32)
                cnb, B_cnb = P.sb(sc, "cnb", [128, 10, 130], BF16)
                tmpu_ring = P.ring(sc, "tmpu", 2, [128, 129], F32)

                def chunk_order(d, runs):
                    out = []
                    rr = runs if d == 0 else [list(reversed(r)) for r in reversed(runs)]
                    for r in rr:
                        out.append(r)
                    return out

                def mchain(d, run, m_init_fn):
                    m_init_fn(mcur[:, d, :])
                    for c in run:
                        A(lambda: nc.scalar.copy(out=mc[:, d, c, :], in_=mcur[:, d, :]), [B_mcur], [B_mc])
                        V(lambda: nc.vector.tensor_tensor(out=mt[:, 0, :], in0=mcur[:, d, :], in1=bl[:, d, c, 5:10], op=ALU.max), [B_mcur, B_bl], [B_mt])
                        V(lambda: nc.vector.tensor_tensor(out=mt[:, 1, :], in0=mcur[:, d, :], in1=mt[:, 0, :], op=ALU.subtract), [B_mcur, B_mt], [B_mt])
                        A(lambda: nc.scalar.activation(out=wold[:, d, c, :], in_=mt[:, 1, :], func=AF.Exp), [B_mt], [B_wo])
                        A(lambda: nc.scalar.activation(out=snew[:, d, c, :], in_=mt[:, 0, :], func=AF.Exp, scale=-1.0), [B_mt], [B_sn])
                        V(lambda: nc.vector.tensor_tensor(out=mcur[:, d, :], in0=mt[:, 0, :], in1=bl[:, d, c, 0:5], op=ALU.subtract), [B_mt, B_bl], [B_mcur])
                        V(lambda: nc.vector.tensor_tensor(out=nfacc[:, d, :], in0=nfacc[:, d, :], in1=bl[:, d, c, 0:5], op=ALU.add), [B_nf, B_bl], [B_nf])

                def state_update(d, h, c, need_bf=True):
                    tt, X = c // 2, c % 2
                    kk = kcA if X == 0 else kcB
                    kkb = B_kcA if X == 0 else B_kcB
                    ps, pb = ps_short()
                    mm(ps[:, 0:129], kk[:, tt, h * 128:(h + 1) * 128], vp[:, d, tt, h, 0:129], True, True, [kkb, B_vp], [pb])
                    tu, tub = tmpu_ring()
                    A(lambda: nc.scalar.activation(out=tu[:], in_=ps[:, 0:129], func=AF.Identity, scale=snew[:, d, c, h:h + 1]), [pb, B_sn], [tub])
                    V(lambda: nc.vector.scalar_tensor_tensor(out=cn[:, d * 5 + h, :], in0=cn[:, d * 5 + h, :], scalar=wold[:, d, c, h:h + 1],
                                                             in1=tu[:], op0=ALU.mult, op1=ALU.add), [B_cn, B_wo, tub], [B_cn])
                    if need_bf:
                        A(lambda: nc.scalar.copy(out=cnb[:, d * 5 + h, 0:129], in_=cn[:, d * 5 + h, :]), [B_cn], [B_cnb])

                def zero_state(d):
                    G(lambda: nc.gpsimd.memset(cn[:, d * 5:(d + 1) * 5, :], 0.0), [], [B_cn])
                    G(lambda: nc.gpsimd.memset(cnb[:, d * 5:(d + 1) * 5, :], 0.0), [], [B_cnb])

                def alloc_scan_bufs():
                    a_ = P.sb(sc, "hc", [128, 4, 640], F32)
                    b_ = P.sb(sc, "tok", [128, 2, 4, 15], F32)
                    c_ = P.sb(sc, "mcol", [128, 5], F32)
                    return (a_[0], a_[1], b_[0], b_[1], c_[0], c_[1], P.ring(sc, "gm", 3, [128, 128], BF16),
                            P.ring(sc, "ti", 3, [128, 129], F32), P.ring(sc, "hn", 3, [128, 129], F32), P.ring(sc, "s3", 3, [128, 3], F32))

                def token_scalars(d, tt):
                    A(lambda: nc.scalar.copy(out=mcol[0:64, :], in_=mc[0:64, d, 2 * tt, :]), [B_mc], [B_mcol])
                    A(lambda: nc.scalar.copy(out=mcol[64:128, :], in_=mc[64:128, d, 2 * tt + 1, :]), [B_mc], [B_mcol])
                    V(lambda: nc.vector.tensor_tensor(out=t5[:, 3, :], in0=bc[:, d, tt, 5:10], in1=mcol[:], op=ALU.max), [B_bc, B_mcol], [B_t5])
                    A(lambda: nc.scalar.activation(out=tok[:, d, tt, 0:5], in_=t5[:, 3, :], func=AF.Exp, scale=-1.0), [B_t5], [B_tok])
                    V(lambda: nc.vector.tensor_tensor(out=t5[:, 0, :], in0=mcol[:], in1=t5[:, 3, :], op=ALU.subtract), [B_mcol, B_t5], [B_t5])
                    A(lambda: nc.scalar.activation(out=tok[:, d, tt, 5:10], in_=t5[:, 0, :], func=AF.Exp), [B_t5], [B_tok])
                    V(lambda: nc.vector.tensor_tensor(out=t5[:, 1, :], in0=bc[:, d, tt, 0:5], in1=t5[:, 3, :], op=ALU.subtract), [B_bc, B_t5], [B_t5])
                    A(lambda: nc.scalar.activation(out=tok[:, d, tt, 10:15], in_=t5[:, 1, :], func=AF.Exp), [B_t5], [B_tok])

                def scan_outputs(runs, on_run_end, on_run_start):
                    G(lambda: nc.gpsimd.memset(hc[:], 0.0), [], [B_hc])
                    for d in range(2):
                        for tt in range(4):
                            token_scalars(d, tt)
                    order = {d: chunk_order(d, runs) for d in range(2)}
                    nsteps = sum(len(r) for r in runs) // 2
                    flat = {d: [c for r in order[d] for c in r] for d in range(2)}
                    run_start = {d: {r[0]: ri for ri, r in enumerate(order[d])} for d in range(2)}
                    run_end = {d: {r[-1]: ri for ri, r in enumerate(order[d])} for d in range(2)}
                    for step in range(nsteps):
                        for d in range(2):
                            c_pair = flat[d][2 * step:2 * step + 2]
                            tt = c_pair[0] // 2
                            if c_pair[0] in run_start[d]:
                                on_run_start(d, run_start[d][c_pair[0]], order[d])
                            for h in range(5):
                                gps, gpb = ps_short()
                                mm(gps[:, 0:128], ktc[:, h, tt * 128:(tt + 1) * 128], qtc[:, h, tt * 128:(tt + 1) * 128], True, True, [B_ktc, B_qtc], [gpb])
                                gm, gmb = gm_ring()
                                V(lambda: nc.vector.tensor_tensor(out=gm[:], in0=gps[:, 0:128], in1=CM(C_TRIF if d == 0 else C_TRIB), op=ALU.mult), [gpb, B_c], [gmb])
                                ips, ipb = ps_short()
                                mm(ips[:, 0:129], gm[:], vp[:, d, tt, h, 0:129], True, True, [gmb, B_vp], [ipb])
                                ti, tib = ti_ring()
                                A(lambda: nc.scalar.activation(out=ti[:], in_=ips[:, 0:129], func=AF.Identity, scale=tok[:, d, tt, h:h + 1]), [ipb, B_tok], [tib])
                                hn, hnb = hn_ring()
                                for c in c_pair:
                                    X = c % 2
                                    rs = slice(0, 64) if X == 0 else slice(64, 128)
                                    xps, xpb = ps_short()
                                    mm(xps[:, 0:129], qtc[:, h, tt * 128:(tt + 1) * 128], cnb[:, d * 5 + h, 0:129], True, True, [B_qtc, B_cnb], [xpb])
                                    V(lambda: nc.vector.scalar_tensor_tensor(out=hn[rs, :], in0=xps[rs, 0:129], scalar=tok[rs, d, tt, 5 + h:6 + h],
                                                                             in1=ti[rs, :], op0=ALU.mult, op1=ALU.add), [xpb, B_tok, tib], [hnb])
                                    state_update(d, h, c)
                                s3, s3b = s3_ring()
                                V(lambda: nc.vector.scalar_tensor_tensor(out=s3[:, 0:1], in0=hn[:, 128:129], scalar=-1.0, in1=hn[:, 128:129],
                                                                         op0=ALU.mult, op1=ALU.max), [hnb], [s3b])
                                V(lambda: nc.vector.tensor_tensor(out=s3[:, 1:2], in0=s3[:, 0:1], in1=tok[:, d, tt, 10 + h:11 + h], op=ALU.max), [s3b, B_tok], [s3b])
                                A(lambda: nc.scalar.activation(out=s3[:, 2:3], in_=s3[:, 1:2], func=AF.Ln), [s3b], [s3b])
                                A(lambda: nc.scalar.activation(out=s3[:, 2:3], in_=s3[:, 2:3], func=AF.Exp, scale=-1.0), [s3b], [s3b])
                                hsl = hc[:, tt, h * 128:(h + 1) * 128]
                                V(lambda: nc.vector.scalar_tensor_tensor(out=hsl, in0=hn[:, 0:128], scalar=s3[:, 2:3], in1=hsl,
                                                                         op0=ALU.mult, op1=ALU.add), [hnb, s3b, B_hc], [B_hc])
                            if c_pair[1] in run_end[d]:
                                on_run_end(d, run_end[d][c_pair[1]], order[d])

                def set_const(val):
                    def f(ap):
                        G(lambda: nc.gpsimd.memset(ap, val), [], [B_mcur])
                    return f

                G(lambda: nc.gpsimd.memset(nfacc[:], 0.0), [], [B_nf])
                if g == "P":
                    runs = [[0, 1, 2, 3], [4, 5, 6, 7]]
                    mfin, B_mfin = P.sb(sc, "mfin", [128, 2, 2, 5], F32)
                    for d in range(2):
                        for ri, r in enumerate(chunk_order(d, runs)):
                            mchain(d, r, set_const(0.0))
                            seq = r[0] // 4
                            A(lambda: nc.scalar.copy(out=mfin[:, seq, d, :], in_=mcur[:, d, :]), [B_mcur], [B_mfin])
                    for seq in range(2):
                        fw.dma("sp", o_cm[seq, l].rearrange("(o d) h -> o (d h)", o=1), mfin[0:1, seq, :, :].rearrange("p d h -> p (d h)"),
                               reads=[B_mfin], writes=[B_out])

                    def on_start(d, ri, order):
                        zero_state(d)

                    def on_end(d, ri, order):
                        seq = order[ri][0] // 4
                        fw.dma("sp", o_cC[seq, l, d].rearrange("h k v -> k h v"), cn[:, d * 5:(d + 1) * 5, 0:128], reads=[B_cn], writes=[B_out])
                        fw.dma("sp", o_cn[seq, l, d], cn[:, d * 5:(d + 1) * 5, 128], reads=[B_cn], writes=[B_out], slow=True)
                    hc, B_hc, tok, B_tok, mcol, B_mcol, gm_ring, ti_ring, hn_ring, s3_ring = alloc_scan_bufs()
                    scan_outputs(runs, on_end, on_start)
                else:
                    runs = [[0, 1, 2, 3, 4, 5, 6, 7]]
                    for d in range(2):
                        zero_state(d)
                        r = chunk_order(d, runs)[0]
                        mchain(d, r, set_const(NEG))
                        for c in r:
                            for h in range(5):
                                state_update(d, h, c, need_bf=False)
                    with ExitStack() as sg:
                        cst_t, B_cst = P.sb(sg, "cst_t", [128, 20], F32)
                        A(lambda: nc.scalar.copy(out=cst_t[:, 0:10], in_=mcur[:].rearrange("p d h -> p (d h)")), [B_mcur], [B_cst])
                        A(lambda: nc.scalar.copy(out=cst_t[:, 10:20], in_=nfacc[:].rearrange("p d h -> p (d h)")), [B_nf], [B_cst])
                        fw.dma("sp", cs_in[:, 0:1290], cn[:].rearrange("p a b -> p (a b)"), reads=[B_cn], writes=[B_csi])
                        fw.dma("sp", cs_in[:, 1290:1310], cst_t[:], reads=[B_cst], writes=[B_csi])
                        fw.allgather(cs_in, cs_out, reads=[B_csi], writes=[B_cso])
                        gs, B_gs = P.sb(sg, "gs", [128, 4, CSF], F32)
                        fw.dma("sp", gs[:], cs_out.rearrange("(r p) f -> p r f", p=128), reads=[B_cso], writes=[B_gs])
                        c0t, B_c0 = P.sb(sg, "c0t", [128, 10, 129], F32)
                        m0t, B_m0 = P.sb(sg, "m0t", [128, 10], F32)
                        fw.dma("sp", c0t[:, :, 0:128], cC_d[l].rearrange("d h k v -> k (d h) v"), writes=[B_c0])
                        fw.dma("sp", c0t[:, :, 128], cn_d[l], writes=[B_c0], slow=True)
                        fw.dma("sp", m0t[:], bass.AP(cm_d.tensor, l * 10, [[0, 128], [1, 10]]), writes=[B_m0])
                        av, B_av = P.sb(sg, "av", [128, 2, 6, 5], F32)
                        wv5, B_wv5 = P.sb(sg, "wv5", [128, 2, 5, 5], F32)
                        for d in range(2):
                            for i in range(5):
                                src = m0t[:, d * 5:(d + 1) * 5] if i == 0 else gs[:, i - 1, 1290 + d * 5:1290 + d * 5 + 5]
                                V(lambda: nc.vector.tensor_tensor(out=av[:, d, i, :], in0=src, in1=vtab[:, d, i, :], op=ALU.add), [B_m0, B_gs, B_c], [B_av])
                                for r in range(4):
                                    V(lambda: nc.vector.tensor_tensor(out=t5[:, 0, :], in0=cftab[:, d, i, r, :], in1=gs[:, r, 1300 + d * 5:1305 + d * 5], op=ALU.mult),
                                      [B_c, B_gs], [B_t5])
                                    V(lambda: nc.vector.tensor_tensor(out=av[:, d, i, :], in0=av[:, d, i, :], in1=t5[:, 0, :], op=ALU.subtract), [B_av, B_t5], [B_av])
                            V(lambda: nc.vector.tensor_tensor(out=av[:, d, 5, :], in0=av[:, d, 0, :], in1=av[:, d, 1, :], op=ALU.max), [B_av], [B_av])
                            for i in range(2, 5):
                                V(lambda: nc.vector.tensor_tensor(out=av[:, d, 5, :], in0=av[:, d, 5, :], in1=av[:, d, i, :], op=ALU.max), [B_av], [B_av])
                            for i in range(5):
                                V(lambda: nc.vector.tensor_tensor(out=t5[:, 1, :], in0=av[:, d, i, :], in1=av[:, d, 5, :], op=ALU.subtract), [B_av], [B_t5])
                                A(lambda: nc.scalar.activation(out=wv5[:, d, i, :], in_=t5[:, 1, :], func=AF.Exp), [B_t5], [B_wv5])
                            for h in range(5):
                                dh = d * 5 + h
                                A(lambda: nc.scalar.activation(out=cn[:, dh, :], in_=c0t[:, dh, :], func=AF.Identity, scale=wv5[:, d, 0, h:h + 1]), [B_c0, B_wv5], [B_cn])
                                for r in range(4):
                                    V(lambda: nc.vector.scalar_tensor_tensor(out=cn[:, dh, :], in0=gs[:, r, dh * 129:(dh + 1) * 129], scalar=wv5[:, d, 1 + r, h:h + 1],
                                                                             in1=cn[:, dh, :], op0=ALU.mult, op1=ALU.add), [B_gs, B_wv5, B_cn], [B_cn])
                                A(lambda: nc.scalar.copy(out=cnb[:, dh, 0:129], in_=cn[:, dh, :]), [B_cn], [B_cnb])

                        def m_from_av(d):
                            def f(ap):
                                A(lambda: nc.scalar.copy(out=ap, in_=av[:, d, 5, :]), [B_av], [B_mcur])
                            return f
                        for d in range(2):
                            mchain(d, chunk_order(d, runs)[0], m_from_av(d))
                        fw.barrier()
                    hc, B_hc, tok, B_tok, mcol, B_mcol, gm_ring, ti_ring, hn_ring, s3_ring = alloc_scan_bufs()
                    scan_outputs(runs, lambda *a: None, lambda *a: None)

                kstop("c3")
                ss, B_ss = P.sb(sc, "ss", [128, 20], F32)
                junk, B_junk = P.sb(sc, "junk", [128, 128], F32)
                yct_ring = P.ring(sc, "yct", 3, [128, 128], BF16)
                ytmp_ring = P.ring(sc, "ytmp", 2, [128, 128], F32)
                G(lambda: nc.gpsimd.memset(ss[:], 0.0), [], [B_ss])
                for tt in range(4):
                    for h in range(5):
                        A(lambda: nc.scalar.activation(out=junk[:], in_=hc[:, tt, h * 128:(h + 1) * 128], func=AF.Square,
                                                       accum_out=ss[:, tt * 5 + h:tt * 5 + h + 1]), [B_hc], [B_junk, B_ss])
                A(lambda: nc.scalar.activation(out=ss[:], in_=ss[:], func=AF.Ln, scale=1.0 / 128, bias=RMS_EPS), [B_ss], [B_ss])
                A(lambda: nc.scalar.activation(out=ss[:], in_=ss[:], func=AF.Exp, scale=-0.5), [B_ss], [B_ss])
                kstop("ca")
                for h in range(5):
                    for tt in range(4):
                        yt, ytb = ytmp_ring()
                        A(lambda: nc.scalar.activation(out=yt[:], in_=hc[:, tt, h * 128:(h + 1) * 128], func=AF.Identity, scale=ss[:, tt * 5 + h:tt * 5 + h + 1]),
                          [B_hc, B_ss], [ytb])
                        V(lambda: nc.vector.tensor_tensor(out=yt[:], in0=yt[:], in1=cnorm[:, l, :], op=ALU.mult), [ytb, B_c], [ytb])
                        yc, ycb = yct_ring()
                        V(lambda: nc.vector.tensor_tensor(out=yc[:], in0=yt[:], in1=sigoc[:, tt, h * 128:(h + 1) * 128], op=ALU.mult), [ytb, B_so], [ycb])
                        if KSTOP == "cb":
                            continue
                        ps, pb = ps_short()
                        mm(ps[:, 0:128], yc[:], id_bf[:], True, True, [ycb, B_c], [pb])
                        if KSTOP == "cc":
                            continue
                        A(lambda: nc.scalar.copy(out=ycat[:, 11 + h, tt * 128:(tt + 1) * 128], in_=ps[:, 0:128]), [pb], [B_y])
                fw.barrier()

            kstop("c4")
            kstop("cb")
            kstop("cc")
            with ExitStack() as sa:
                qta, B_qta = P.sb(sa, "qta", [128, 6, T], BF16)
                q1p, B_q1p = P.sb(sa, "q1p", [128, 5, T], BF16)
                q2p, B_q2p = P.sb(sa, "q2p", [128, 5, T], BF16)
                G(lambda: nc.gpsimd.memset(q1p[:], 0.0), [], [B_q1p])
                G(lambda: nc.gpsimd.memset(q2p[:], 0.0), [], [B_q2p])
                if g == "S":
                    q1r, B_q1r = P.sb(sa, "q1r", [128, 5, T], BF16)
                    q2r, B_q2r = P.sb(sa, "q2r", [128, 5, T], BF16)
                    G(lambda: nc.gpsimd.memset(q1r[:], 0.0), [], [B_q1r])
                    G(lambda: nc.gpsimd.memset(q2r[:], 0.0), [], [B_q2r])
                    sip = ExitStack()
                    kst_ring = P.ring(sip, "kst", 3, [128, T], BF16)
                    vst, B_vst = P.sb(sip, "vst", [128, 4, 1408], BF16)
                    rp_ring = P.ring(sip, "rpx", 2, [128, T], F32)
                    rp2_ring = P.ring(sip, "rpy", 2, [128, T], F32)
                else:
                    kta, B_kta = P.sb(sa, "kta", [128, 6, T], BF16)
                    ktb, B_ktb = P.sb(sa, "ktb", [128, 5, T], BF16)
                    vab, B_vab = P.sb(sa, "vab", [128, 4, 1408], BF16)
                    stg_ring = P.ring(sa, "stg", 3, [128, T], F32)

                def rope_apply(ps, pb, outs):
                    xf, xb = rp_ring()
                    A(lambda: nc.scalar.copy(out=xf[:], in_=ps[:]), [pb], [xb])
                    p2, pb2 = ps_short()
                    mm(p2[:], CM(C_PERM), xf[:], True, True, [B_c, xb], [pb2])
                    x2, x2b = rp2_ring()
                    V(lambda: nc.vector.tensor_tensor(out=x2[:], in0=p2[:], in1=rope[:, 1, :], op=ALU.mult), [pb2, B_c], [x2b])
                    V(lambda: nc.vector.tensor_tensor(out=xf[:], in0=xf[:], in1=rope[:, 0, :], op=ALU.mult), [xb, B_c], [xb])
                    for (rs, dst, db_) in outs:
                        V(lambda: nc.vector.tensor_tensor(out=dst[rs, :], in0=xf[rs, :], in1=x2[rs, :], op=ALU.add), [xb, x2b], [db_])

                abtiles = [(i * 512, 512) for i in range(8)] + [(4096, 128)]
                if "t" in KSKIP:
                    abtiles = abtiles[:int(KSKIP[KSKIP.index("t") + 1])]
                with extra_slots(2):
                    for (c0, ncol) in abtiles:
                        wv, wb = wload(w_in[l, :, c0:c0 + ncol], 16, ncol, key=("w_in", l, c0))
                        for q in range(ncol // 128):
                            k = (c0 + q * 128) // 128
                            fm = (k <= 11) or (18 <= k <= 27)
                            if not fm:
                                continue
                            ps, pb = ps_short()
                            for c in range(16):
                                mm(ps[:], wv[:, c, q * 128:(q + 1) * 128], h_sb[:, c, :], c == 0, c == 15, [wb, B_h], [pb])
                            if k <= 5:
                                A(lambda: nc.scalar.copy(out=qta[:, k, :], in_=ps[:]), [pb], [B_qta])
                            elif k <= 11:
                                hh = k - 6
                                if g == "P":
                                    A(lambda: nc.scalar.copy(out=kta[:, hh, :], in_=ps[:]), [pb], [B_kta])
                                    sg, sgb = stg_ring()
                                    A(lambda: nc.scalar.copy(out=sg[:], in_=ps[:]), [pb], [sgb])
                                    if "k" not in KSKIP:
                                        fw.dma("sp", o_ak[:, l, hh].rearrange("s d t -> d s t"), sg[:].rearrange("p (s t) -> p s t", s=2), reads=[sgb], writes=[B_out])
                                else:
                                    ks, ksb = kst_ring()
                                    A(lambda: nc.scalar.copy(out=ks[:], in_=ps[:]), [pb], [ksb])
                                    kr, krb = bnc_k_rows(hh)
                                    fw.dma("sp", kr, ks[:], reads=[ksb], writes=[krb])
                            elif k <= 22:
                                hh = k - 18
                                A(lambda: nc.scalar.copy(out=q1p[0:64, hh, :], in_=ps[0:64, :]), [pb], [B_q1p])
                                A(lambda: nc.scalar.copy(out=q2p[64:128, hh, :], in_=ps[64:128, :]), [pb], [B_q2p])
                                if g == "S":
                                    rope_apply(ps, pb, [(slice(0, 64), q1r[:, hh, :], B_q1r), (slice(64, 128), q2r[:, hh, :], B_q2r)])
                            else:
                                hh = k - 23
                                if g == "P":
                                    A(lambda: nc.scalar.copy(out=ktb[:, hh, :], in_=ps[:]), [pb], [B_ktb])
                                    sg, sgb = stg_ring()
                                    A(lambda: nc.scalar.copy(out=sg[:], in_=ps[:]), [pb], [sgb])
                                    if "k" not in KSKIP:
                                        fw.dma("sp", o_bk[:, l, hh].rearrange("s d t -> d s t"), sg[:].rearrange("p (s t) -> p s t", s=2), reads=[sgb], writes=[B_out])
                                else:
                                    ks, ksb = kst_ring()
                                    rope_apply(ps, pb, [(slice(0, 128), ks, ksb)])
                                    kr, krb = bnc_k_rows(6 + hh)
                                    fw.dma("sp", kr, ks[:], reads=[ksb], writes=[krb])
                        for (name, lo, hi, o_base) in (("va", 1536, 2304, 0), ("vb", 3584, 4224, 768)):
                            a, b_ = max(lo, c0), min(hi, c0 + ncol)
                            if a >= b_:
                                continue
                            n = b_ - a
                            o0 = o_base + a - lo
                            for tt in range(4):
                                ps, pb = ps_short()
                                for c in range(16):
                                    mm(ps[:, 0:n], h_sb[:, c, tt * 128:(tt + 1) * 128], wv[:, c, a - c0:b_ - c0], c == 0, c == 15, [wb, B_h], [pb])
                                if g == "P":
                                    A(lambda: nc.scalar.copy(out=vab[:, tt, o0:o0 + n], in_=ps[:, 0:n]), [pb], [B_vab])
                                    sg, sgb = stg_ring()
                                    A(lambda: nc.scalar.copy(out=sg[:, 0:n], in_=ps[:, 0:n]), [pb], [sgb])
                                    seq, s0_ = tt // 2, (tt % 2) * 128
                                    h0 = (a - lo) // 128
                                    nh = n // 128
                                    dst = (o_av if name == "va" else o_bv)[seq, l, h0:h0 + nh, s0_:s0_ + 128, :].rearrange("h s d -> s h d")
                                    if "v" not in KSKIP:
                                        fw.dma("sp", dst, sg[:, 0:n].rearrange("p (h d) -> p h d", d=128), reads=[sgb], writes=[B_out])
                                else:
                                    A(lambda: nc.scalar.copy(out=vst[:, tt, o0:o0 + n], in_=ps[:, 0:n]), [pb], [B_vst])

                for tc_ in (0, 1):
                    prefetch(("o", l, tc_), w_out[l][:, tc_ * 512:(tc_ + 1) * 512], 16, 512)
                kstop("c5")

                def alloc_attn_rings():
                    return (P.ring(sa, "et", 3, [128, T], BF16), P.ring(sa, "bt", 2, [128, T], F32),
                            P.ring(sa, "rd", 2, [128, T], F32), P.ring(sa, "ybt", 3, [128, T], F32))

                def fin_A(h):
                    def f(yps, yb, dps, db):
                        rd, rdb = rd_ring()
                        A(lambda: nc.scalar.activation(out=rd[:], in_=dps[:], func=AF.Ln), [db], [rdb])
                        A(lambda: nc.scalar.activation(out=rd[:], in_=rd[:], func=AF.Exp, scale=-1.0), [rdb], [rdb])
                        V(lambda: nc.vector.tensor_tensor(out=ycat[:, h, :], in0=yps[:], in1=rd[:], op=ALU.mult), [yb, rdb], [B_y])
                    return f

                def fin_B(dst, dstb):
                    def f(yps, yb, dps, db):
                        rd, rdb = rd_ring()
                        A(lambda: nc.scalar.activation(out=rd[:], in_=dps[:], func=AF.Ln), [db], [rdb])
                        A(lambda: nc.scalar.activation(out=rd[:], in_=rd[:], func=AF.Exp, scale=-1.0), [rdb], [rdb])
                        V(lambda: nc.vector.tensor_tensor(out=dst[:], in0=yps[:], in1=rd[:], op=ALU.mult), [yb, rdb], [dstb])
                    return f

                def diff_finish(h, y1, y1b, y2, y2b):
                    V(lambda: nc.vector.scalar_tensor_tensor(out=y1[:], in0=y2[:], scalar=nlam[:, l:l + 1], in1=y1[:], op0=ALU.mult, op1=ALU.add),
                      [y2b, y1b, B_nlam], [y1b])
                    A(lambda: nc.scalar.activation(out=y2[:], in_=y1[:], func=AF.Square), [y1b], [y2b])
                    sp_, spb = ps_short()
                    mm(sp_[:], CM(C_ONE), y2[:], True, True, [B_c, y2b], [spb])
                    A(lambda: nc.scalar.activation(out=y2[:], in_=sp_[:], func=AF.Ln, scale=1.0 / 128, bias=RMS_EPS), [spb], [y2b])
                    A(lambda: nc.scalar.activation(out=y2[:], in_=y2[:], func=AF.Exp, scale=-0.5), [y2b], [y2b])
                    V(lambda: nc.vector.tensor_tensor(out=y1[:], in0=y1[:], in1=y2[:], op=ALU.mult), [y1b, y2b], [y1b])
                    A(lambda: nc.scalar.activation(out=ycat[:, 6 + h, :], in_=y1[:], func=AF.Identity, scale=sublns[:, l:l + 1]), [y1b, B_nlam], [B_y])

                if g == "P":
                    et_ring, bt_ring, rd_ring, yb_ring = alloc_attn_rings()
                    for h in range(6):
                        groups = []
                        for kb in range(2):
                            grp = []
                            for s in range(2):
                                t0 = s * 256 + kb * 128
                                grp.append(dict(k=kta[:, h, t0:t0 + 128], kb=B_kta, q=qta[:, h, s * 256:(s + 1) * 256], qb=B_qta,
                                                v=vab[:, 2 * s + kb, h * 128:(h + 1) * 128], vb=B_vab, c0=s * 256, n=256))
                            groups.append(grp)
                        attention(groups, 128.0 ** -0.5, et_ring, bt_ring, fin_A(h))
                    for h in range(5):
                        ys = []
                        for (qp, qpb) in ((q1p, B_q1p), (q2p, B_q2p)):
                            groups = []
                            for kb in range(2):
                                grp = []
                                for s in range(2):
                                    t0 = s * 256 + kb * 128
                                    grp.append(dict(k=ktb[:, h, t0:t0 + 128], kb=B_ktb, q=qp[:, h, s * 256:(s + 1) * 256], qb=qpb,
                                                    v=vab[:, 2 * s + kb, 768 + h * 128:768 + (h + 1) * 128], vb=B_vab, c0=s * 256, n=256))
                                groups.append(grp)
                            yt, ytb = yb_ring()
                            attention(groups, 64.0 ** -0.5, et_ring, bt_ring, fin_B(yt, ytb))
                            ys.append((yt, ytb))
                        diff_finish(h, ys[0][0], ys[0][1], ys[1][0], ys[1][1])
                else:
                    for hh in range(11):
                        vr, vrb = bnc_v_rows(hh)
                        fw.dma("sp", vr.rearrange("p (tt d) -> p tt d", d=128),
                               vst[:, :, hh * 128:(hh + 1) * 128], reads=[B_vst], writes=[vrb])
                    for ci in range(3):
                        fw.allgather(bnc_in[ci], bnc_out[ci], reads=[B_bi[ci]], writes=[B_bo[ci]])
                    fw.barrier()
                    sip.close()
                    et_ring, bt_ring, rd_ring, yb_ring = alloc_attn_rings()
                    kall_ring = P.ring(sa, "kall", 2, [128, 4, T], BF16)
                    vall_ring = P.ring(sa, "vall", 2, [128, 4, T], BF16)
                    kctx_ring = P.ring(sa, "kctx", 2, [128, 512], BF16)
                    vctx_ring = P.ring(sa, "vctx", 2, [128, 4, 128], BF16)
                    nb_ring = P.ring(sa, "nbias", 3, [128, T], F32)
                    gviews = [bo.rearrange("(r x) t -> x r t", r=4) for bo in bnc_out]
                    for hh in range(11):
                        isA = hh < 6
                        h = hh if isA else hh - 6
                        ka, kab = kall_ring()
                        va_, vab_ = vall_ring()
                        kc_, kcb_ = kctx_ring()
                        vc_, vcb_ = vctx_ring()
                        gch = HH_CH[hh]
                        krow = HH_IX[hh] * 128
                        vrow = (CH_NH[gch] + HH_IX[hh]) * 128
                        fw.dma("sp", ka[:], gviews[gch][krow:krow + 128], reads=[B_bo[gch]], writes=[kab])
                        fw.dma("sp", va_[:], gviews[gch][vrow:vrow + 128], reads=[B_bo[gch]], writes=[vab_])
                        if isA:
                            fw.dma("pool", kc_[:], cakT[l, h], writes=[kcb_])
                            fw.dma("pool", vc_[:], cav[l, h].rearrange("(b p) d -> p b d", p=128), writes=[vcb_])
                        else:
                            fw.dma("pool", kc_[:], cbkT[l, h], writes=[kcb_])
                            fw.dma("pool", vc_[:], cbv[l, h].rearrange("(b p) d -> p b d", p=128), writes=[vcb_])

                        def mkgroups(q_lat, q_latb, q_ctx, q_ctxb):
                            groups = []
                            for kb in range(16):
                                r, t4 = kb // 4, kb % 4
                                sub = dict(k=ka[:, r, t4 * 128:(t4 + 1) * 128], kb=kab, q=q_lat, qb=q_latb,
                                           v=va_[:, r, t4 * 128:(t4 + 1) * 128], vb=vab_, c0=0, n=T)
                                if isA:
                                    nbt, nbb = nb_ring()
                                    fw.dma("sp", nbt[:], nbias_d[l, h, kb], writes=[nbb])
                                    sub["bias"] = nbt[:]
                                    sub["bb"] = nbb
                                groups.append([sub])
                            for kb in range(4):
                                groups.append([dict(k=kc_[:, kb * 128:(kb + 1) * 128], kb=kcb_, q=q_ctx, qb=q_ctxb,
                                                    v=vc_[:, kb, :], vb=vcb_, c0=0, n=T)])
                            return groups
                        if isA:
                            attention(mkgroups(qta[:, h, :], B_qta, qta[:, h, :], B_qta), 128.0 ** -0.5, et_ring, bt_ring, fin_A(h))
                        else:
                            ys = []
                            for (qr, qrb, qp, qpb) in ((q1r, B_q1r, q1p, B_q1p), (q2r, B_q2r, q2p, B_q2p)):
                                yt, ytb = yb_ring()
                                attention(mkgroups(qr[:, h, :], qrb, qp[:, h, :], qpb), 64.0 ** -0.5, et_ring, bt_ring, fin_B(yt, ytb))
                                ys.append((yt, ytb))
                            diff_finish(h, ys[0][0], ys[0][1], ys[1][0], ys[1][1])
                fw.barrier()

            kstop("c6")
            with ExitStack() as so:
                def pf_up():
                    prefetch(("up", l, 0), w_up[l, :, 0:512], 16, 512)
                    prefetch(("up", l, DFF), w_up[l, :, DFF:DFF + 512], 16, 512)
                residual_proj(l, w_out[l], 16, 512, lambda c: ycat[:, c, :], [B_y], 2, row, so, "o", after=pf_up)
                layernorm(l, 0, so)
                fw.barrier()
        kstop("c7")
        modulate(l, 3, 4, row)
        with ExitStack() as sf:
            actb, B_act = P.sb(sf, "actb", [128, 43, T], BF16)
            u_ring = P.ring(sf, "u1", 4, [128, T], F32)
            hal, B_hal = P.sb(sf, "hal", [128, 2, 2], F32)
            if g == "S":
                hbs, B_hbs = P.sb(sf, "hbs", [128, 16, 2], F32)
                hba, B_hba = P.sb(sf, "hba", [128, 4, 32], F32)
                hbf, B_hbf = P.sb(sf, "hbf", [128, 16, 2], F32)
                A(lambda: nc.scalar.copy(out=hbs[:, :, 0], in_=h_sb[:, :, 0]), [B_h], [B_hbs])
                A(lambda: nc.scalar.copy(out=hbs[:, :, 1], in_=h_sb[:, :, T - 1]), [B_h], [B_hbs])
                fw.dma("sp", hb_in, hbs[:].rearrange("p c k -> p (c k)"), reads=[B_hbs], writes=[B_hbi])
                fw.allgather(hb_in, hb_out, reads=[B_hbi], writes=[B_hbo])
                fw.dma("sp", hba[:], hb_out.rearrange("(r p) f -> p r f", p=128), reads=[B_hbo], writes=[B_hba])
                hv = hba[:].rearrange("p r (c k) -> p r c k", k=2)
                for (side, kk, so_) in ((0, 1, 0), (1, 0, 4)):
                    A(lambda: nc.scalar.activation(out=hbf[:, :, side], in_=hv[:, 0, :, kk], func=AF.Identity, scale=sel[:, so_:so_ + 1]), [B_hba, B_c], [B_hbf])
                    for r in range(1, 4):
                        V(lambda: nc.vector.scalar_tensor_tensor(out=hbf[:, :, side], in0=hv[:, r, :, kk], scalar=sel[:, so_ + r:so_ + r + 1],
                                                                 in1=hbf[:, :, side], op0=ALU.mult, op1=ALU.add), [B_hba, B_c, B_hbf], [B_hbf])
                A(lambda: nc.scalar.copy(out=hh_sb[:], in_=hbf[:]), [B_hbf], [B_hh])
            segs = [(0, 256), (256, 256)] if g == "P" else [(0, 512)]

            def conv_chunk(ps, pb, hps, hpb, ch):
                u, ub = u_ring()
                cp = convp[:, l, ch, :]
                A(lambda: nc.scalar.activation(out=u[:], in_=ps[:], func=AF.Identity, scale=cp[:, 1:2], bias=cp[:, 3:4]), [pb, B_c], [ub])
                for (s0_, n) in segs:
                    V(lambda: nc.vector.scalar_tensor_tensor(out=u[:, s0_ + 1:s0_ + n], in0=ps[:, s0_:s0_ + n - 1], scalar=cp[:, 0:1],
                                                             in1=u[:, s0_ + 1:s0_ + n], op0=ALU.mult, op1=ALU.add), [pb, B_c, ub], [ub])
                    V(lambda: nc.vector.scalar_tensor_tensor(out=u[:, s0_:s0_ + n - 1], in0=ps[:, s0_ + 1:s0_ + n], scalar=cp[:, 2:3],
                                                             in1=u[:, s0_:s0_ + n - 1], op0=ALU.mult, op1=ALU.add), [pb, B_c, ub], [ub])
                if g == "S":
                    V(lambda: nc.vector.scalar_tensor_tensor(out=u[:, 0:1], in0=hps[:, 0:1], scalar=cp[:, 0:1], in1=u[:, 0:1],
                                                             op0=ALU.mult, op1=ALU.add), [hpb, B_c, ub], [ub])
                    V(lambda: nc.vector.scalar_tensor_tensor(out=u[:, T - 1:T], in0=hps[:, 1:2], scalar=cp[:, 2:3], in1=u[:, T - 1:T],
                                                             op0=ALU.mult, op1=ALU.add), [hpb, B_c, ub], [ub])
                return u, ub

            with extra_slots(2):
                for ti in range(11):
                    ncol = 512 if ti < 10 else 384
                    wa, wab = wload(w_up[l, :, ti * 512:ti * 512 + ncol], 16, ncol, key=("up", l, ti * 512))
                    wg, wgb = wload(w_up[l, :, DFF + ti * 512:DFF + ti * 512 + ncol], 16, ncol, key=("up", l, DFF + ti * 512))
                    for q in range(ncol // 128):
                        j = ti * 4 + q
                        res = []
                        for (wv, wb, ch) in ((wa, wab, j), (wg, wgb, 43 + j)):
                            ps, pb = ps_short()
                            for c in range(16):
                                mm(ps[:], wv[:, c, q * 128:(q + 1) * 128], h_sb[:, c, :], c == 0, c == 15, [wb, B_h], [pb])
                            hps, hpb = None, None
                            if g == "S":
                                hps, hpb = ps_short()
                                for c in range(16):
                                    mm(hps[:, 0:2], wv[:, c, q * 128:(q + 1) * 128], hh_sb[:, c, :], c == 0, c == 15, [wb, B_hh], [hpb])
                            res.append(conv_chunk(ps, pb, hps, hpb, ch))
                        (ua, uab), (ug, ugb) = res
                        A(lambda: nc.scalar.activation(out=ug[:], in_=ug[:], func=AF.Silu), [ugb], [ugb])
                        V(lambda: nc.vector.tensor_tensor(out=actb[:, j, :], in0=ug[:], in1=ua[:], op=ALU.mult), [ugb, uab], [B_act])
            def pf_next():
                nl = l if g == "P" else l + 1
                if nl < L:
                    for c0_ in (4224, 4736):
                        prefetch(("w_in", nl, c0_), w_in[nl, :, c0_:c0_ + 512], 16, 512)
            residual_proj(l, w_down[l], 43, 128, lambda c: actb[:, c, :], [B_act], 5, row, sf, "d", after=pf_next)
            layernorm(l, 1, sf)
            fw.barrier()
        if l == L - 1:
            fw.dma("sp", yout[g].rearrange("(c p) t -> p c t", p=128), x_sb[:], reads=[B_x], writes=[B_out])
        else:
            fw.dma("sp", xspill[g].rearrange("(c p) t -> p c t", p=128), x_sb[:], reads=[B_x], writes=[B_spill[g]])

    nblk = 0
    try:
        for l in range(L):
            for g in ("P", "S"):
                if KSTOP.startswith("b") and nblk >= int(KSTOP[1]):
                    break
                if KSTOP.startswith("c") and nblk >= 1:
                    break
                block(l, g)
                nblk += 1
    except StopBuild:
        fw.barrier()
        fw.finish()
        return P
    fw.finish()
    P.es.close()
    fw.close()
    return P


def _consts():
    c = np.zeros((NCONST, 128, 128), np.float32)
    idx = np.arange(128)
    c[C_ID] = np.eye(128)
    c[C_ONE] = 1.0
    same = (idx[:, None] // 64) == (idx[None, :] // 64)
    c[C_TRIF] = (same & (idx[:, None] <= idx[None, :]))
    c[C_TRIB] = (same & (idx[:, None] >= idx[None, :]))
    for k, p in ((C_E0, 0), (C_E63, 63), (C_E64, 64), (C_E127, 127)):
        c[k][p, :] = 1.0
    dd = idx % 32
    partner = np.where(dd < 16, idx + 16, idx - 16)
    c[C_PERM][partner, idx] = 1.0
    negm = np.zeros((2, 128, 5, 128), np.float32)
    negm[0] = np.where(c[C_TRIF].T[:, None, :] > 0, 0.0, NEG)
    negm[1] = np.where(c[C_TRIB].T[:, None, :] > 0, 0.0, NEG)
    return (np.ascontiguousarray(c.transpose(1, 0, 2)).reshape(128, NCONST * 128),
            np.ascontiguousarray(negm.transpose(1, 0, 2, 3)).reshape(128, 2 * 640))


def _rope_tables(j):
    t = np.arange(T) + j * T
    rows = (t // 64).astype(np.float32)
    cols = (t % 64).astype(np.float32)
    freqs = (np.float32(10000.0) ** (-np.arange(0, 32, 2, dtype=np.float32) / np.float32(32))).astype(np.float32)
    p = np.arange(128)
    dd = p % 64
    idx = dd % 32
    f = idx % 16
    first = idx < 16
    pos = np.where((dd < 32)[:, None], rows[None, :], cols[None, :]).astype(np.float32)
    ang = (pos * freqs[f][:, None]).astype(np.float32)
    cos = np.cos(ang).astype(np.float32)
    sin = np.sin(ang).astype(np.float32)
    sins = np.where(first[:, None], -sin, sin).astype(np.float32)
    return np.concatenate([cos, sins], axis=1)


def _natten_bias(a_rpb, j):
    kt = np.arange(2048)
    krow, kcol = kt // 64, kt % 64
    qt = np.arange(T) + j * T
    qrow, qcol = qt // 64, qt % 64
    rs = np.clip(qrow - 4, 0, 24)
    cs = np.clip(qcol - 8, 0, 48)
    vr = (krow[:, None] >= rs[None, :]) & (krow[:, None] < rs[None, :] + 8)
    vcm = (kcol[:, None] >= cs[None, :]) & (kcol[:, None] < cs[None, :] + 16)
    valid = vr & vcm
    ri = np.clip(7 + krow[:, None] - qrow[None, :], 0, 14)
    ci = np.clip(kcol[:, None] - qcol[None, :] + 15, 0, 30)
    out = np.empty((L, 6, 2048, T), np.float32)
    for l in range(L):
        for h in range(6):
            out[l, h] = np.where(valid, a_rpb[l, h][ri, ci], np.float32(NEG))
    return out.reshape(L, 6, 16, 128, T)


def _combine_tables(j):
    cf = np.zeros((2, 5, 4), np.float32)
    vt = np.zeros((2, 5), np.float32)
    for r2 in range(4):
        if r2 < j:
            cf[0, 0, r2] = 1
        if r2 > j:
            cf[1, 0, r2] = 1
    for r in range(4):
        vt[0, 1 + r] = 0.0 if r < j else NEG
        vt[1, 1 + r] = 0.0 if r > j else NEG
        for r2 in range(4):
            if r < r2 < j:
                cf[0, 1 + r, r2] = 1
            if j < r2 < r:
                cf[1, 1 + r, r2] = 1
    cft = np.broadcast_to(cf[None, :, :, :, None], (128, 2, 5, 4, 5)).reshape(128, -1)
    vtt = np.broadcast_to(vt[None, :, :, None], (128, 2, 5, 5)).reshape(128, -1)
    sel = np.zeros((8,), np.float32)
    if j > 0:
        sel[j - 1] = 1
    if j < 3:
        sel[4 + j + 1] = 1
    return np.ascontiguousarray(cft), np.ascontiguousarray(vtt), np.ascontiguousarray(np.broadcast_to(sel[None], (128, 8)))


_PROG = None


def kernel(x_prompt, x_sample, cache_a_k, cache_a_v, cache_b_k, cache_b_v, state_c_C, state_c_n, state_c_m,
           c, c_ctx, w_mod, b_mod, w_in, c_gate_b, a_rpb, b_lambda, b_subln, c_norm, w_out,
           ln1_g, ln1_b, ln2_g, ln2_b, w_up, conv_w, conv_b, w_down):
    global _PROG
    f = lambda a: np.ascontiguousarray(np.asarray(a, dtype=np.float32))
    x_prompt, x_sample = f(x_prompt), f(x_sample)
    if _PROG is None:
        _PROG = build_program()
    P = _PROG
    consts, negm = _consts()
    lnp = np.stack([f(ln1_g), f(ln1_b), f(ln2_g), f(ln2_b)], 1).reshape(L, 4, 16, 128).transpose(3, 0, 1, 2).reshape(128, -1)
    cw = np.concatenate([f(conv_w), f(conv_b)[:, None, :]], 1)
    convp = cw.reshape(L, 4, 86, 128).transpose(3, 0, 2, 1).reshape(128, -1)
    shared = {
        "w_in": f(w_in), "w_out": f(w_out), "w_up": f(w_up), "w_down": f(w_down),
        "lnp": np.ascontiguousarray(lnp), "convp": np.ascontiguousarray(convp),
        "gateb": f(c_gate_b).reshape(-1), "blam": f(b_lambda).reshape(-1),
        "subln": np.ascontiguousarray(f(b_subln).T), "cnorm": f(c_norm).reshape(-1),
        "consts": consts, "negm": negm,
    }
    w_mod, b_mod = f(w_mod), f(b_mod)
    in_maps = []
    for i in range(8):
        b, j = i // 4, i % 4
        m = dict(shared)
        m["xpT"] = np.ascontiguousarray(x_prompt[2 * i:2 * i + 2].reshape(T, D).T)
        m["xsT"] = np.ascontiguousarray(x_sample[b, j * T:(j + 1) * T].T)
        m["wmod"] = np.ascontiguousarray(w_mod[:, :, j * 3072:(j + 1) * 3072])
        m["bmod"] = np.ascontiguousarray(b_mod[:, j * 3072:(j + 1) * 3072].reshape(L, 24, 128).transpose(2, 0, 1).reshape(128, -1))
        cond = np.stack([f(c_ctx), f(c)[b]], 1)
        m["cond2"] = np.ascontiguousarray(cond.reshape(16, 128, 2).transpose(1, 0, 2).reshape(128, 32))
        m["rope"] = _rope_tables(j)
        m["nbias"] = _natten_bias(f(a_rpb), j)
        m["cakT"] = np.ascontiguousarray(f(cache_a_k)[b].transpose(0, 1, 3, 2))
        m["cav"] = f(cache_a_v)[b]
        m["cbkT"] = np.ascontiguousarray(f(cache_b_k)[b].transpose(0, 1, 3, 2))
        m["cbv"] = f(cache_b_v)[b]
        m["cC"] = f(state_c_C)[b]
        m["cn"] = np.ascontiguousarray(f(state_c_n)[b].reshape(L, 10, 128).transpose(0, 2, 1))
        m["cm"] = f(state_c_m)[b].reshape(-1)
        m["cftab"], m["vtab"], m["sel"] = _combine_tables(j)
        in_maps.append(m)
    in_maps = [{k: v for k, v in m.items() if k in P.din} for m in in_maps]
    res = run_bass_kernel_spmd(P.nc, in_maps, core_ids=list(range(8))).results
    yp = np.stack([r["ypT"].T.reshape(2, 256, D) for r in res], 0).reshape(16, 256, D)
    ys = np.stack([r["ysT"].T for r in res], 0).reshape(2, 4 * T, D)
    cat = lambda k: np.concatenate([r[k] for r in res], 0)
    n_ak = np.ascontiguousarray(cat("o_ak").transpose(0, 1, 2, 4, 3))
    n_av = cat("o_av")
    n_bk = np.ascontiguousarray(cat("o_bk").transpose(0, 1, 2, 4, 3))
    n_bv = cat("o_bv")
    return (np.ascontiguousarray(yp, dtype=np.float32), np.ascontiguousarray(ys, dtype=np.float32), n_ak, n_av, n_bk, n_bv,
            cat("o_cC"), np.ascontiguousarray(cat("o_cn").transpose(0, 1, 2, 4, 3)), cat("o_cm"))
```

```python
import math
import os
KSTOP = os.environ.get('KSTOP', '')
KSKIP = os.environ.get('KSKIP', '')
SKIP_IN = set()
if KSTOP.startswith('c'):
    SKIP_IN = {"nbias", "cakT", "cav", "cbkT", "cbv", "cC", "cn", "cm", "xsT"}
    if not KSTOP[1].isdigit() or int(KSTOP[1]) < 8:
        SKIP_IN |= {"w_up", "w_down"}
    if not KSTOP[1].isdigit() or int(KSTOP[1]) < 7:
        SKIP_IN |= {"w_out"}


class StopBuild(Exception):
    pass


def kstop(tag):
    if KSTOP == tag:
        raise StopBuild(tag)
from contextlib import ExitStack
import numpy as np
import ml_dtypes
import concourse.bass as bass
import concourse.mybir as mybir
from concourse.bass_utils import run_bass_kernel_spmd

F32 = mybir.dt.float32
BF16 = mybir.dt.bfloat16
AF = mybir.ActivationFunctionType
ALU = mybir.AluOpType
AX = mybir.AxisListType

L = 2
D = 2048
T = 512
DFF = 5504
NEG = -30000.0
ALPHA = (2 * L) ** 0.25
LN_EPS = 1e-5
RMS_EPS = 1e-6
RG = [[0, 1, 2, 3], [4, 5, 6, 7]]
NCONST = 11
C_ID, C_ONE, C_TRIF, C_TRIB, C_E0, C_E63, C_E64, C_E127, C_PERM, C_HA, C_HB = range(NCONST)


class Buf:
    __slots__ = ("name", "w", "r")

    def __init__(self, name=""):
        self.name = name
        self.w = None
        self.r = []


class FW:
    NDMA = 48

    def __init__(self, nc):
        self.nc = nc
        self.eng = {"pe": nc.tensor, "act": nc.scalar, "dve": nc.vector, "pool": nc.gpsimd, "sp": nc.sync}
        self.sem, self.cnt, self._stack = {}, {}, []
        for e in self.eng:
            cm = nc.semaphore("s_" + e)
            self.sem[e] = cm.__enter__()
            self._stack.append(cm)
            self.cnt[e] = 0
        self.dsem, self.dcnt = [], []
        for i in range(self.NDMA):
            cm = nc.semaphore("d_%d" % i)
            self.dsem.append(cm.__enter__())
            self._stack.append(cm)
            self.dcnt.append(0)
        self.dnext = 0
        self.seen = {e: {} for e in self.eng}
        self.ninst = 0
        self.nwaits = 0

    def close(self):
        for cm in reversed(self._stack):
            cm.__exit__(None, None, None)

    def _semobj(self, key):
        return self.sem[key] if isinstance(key, str) else self.dsem[key]

    def _wait(self, e, tok):
        if tok is None:
            return
        key, val = tok
        if self.seen[e].get(key, 0) >= val:
            return
        self.eng[e].wait_ge(self._semobj(key), val)
        self.seen[e][key] = val
        self.nwaits += 1

    def _deps(self, e, reads, writes, is_dma=False):
        for b in reads:
            if b.w is not None:
                if b.w[0] == e and (e == "pe" or is_dma):
                    continue
                self._wait(e, b.w)
        skip_same = (e == "pe" or is_dma)
        for b in writes:
            if b.w is not None and not (skip_same and b.w[0] == e):
                self._wait(e, b.w)
            for t in b.r:
                if not (skip_same and t[0] == e):
                    self._wait(e, t)

    def _commit(self, tok, reads, writes):
        for b in reads:
            b.r.append(tok)
            if len(b.r) > 16:
                best = {}
                for k, v in b.r:
                    if best.get(k, 0) < v:
                        best[k] = v
                b.r = list(best.items())
        for b in writes:
            b.w = tok
            b.r = []

    def op(self, e, fn, reads=(), writes=(), inc=True):
        self._deps(e, reads, writes)
        ins = fn()
        self.ninst += 1
        if inc:
            self.cnt[e] += 1
            ins.then_inc(self.sem[e], 1)
            tok = (e, self.cnt[e])
        else:
            tok = (e, self.cnt[e] + 1)
        self._commit(tok, reads, writes)
        return tok

    def _next_dsem(self, q, kind=None):
        kind = kind or q
        lo, hi = {"sp": (0, 32), "pool": (32, 44), "cc": (44, 48)}[kind]
        if not hasattr(self, "dnx"):
            self.dnx = {}
        i = self.dnx.get(kind, lo)
        self.dnx[kind] = lo + (i + 1 - lo) % (hi - lo)
        if self.dcnt[i] > 0:
            self._wait(q, (i, self.dcnt[i]))
        return i

    def dma(self, q, out, in_, reads=(), writes=(), slow=False):
        self._deps(q, reads, writes, is_dma=True)
        i = self._next_dsem(q)
        if slow:
            ins = self.eng[q].dma_start(out=out, in_=in_, allow_slow_non_contiguous=True)
        else:
            ins = self.eng[q].dma_start(out=out, in_=in_)
        self.dcnt[i] += 16
        ins.then_inc(self.dsem[i], 16)
        self.ninst += 1
        tok = (i, self.dcnt[i])
        self._commit(tok, reads, writes)
        return tok

    def allgather(self, in_ap, out_ap, reads=(), writes=()):
        q = "pool"
        self._deps(q, reads, writes, is_dma=True)
        i = self._next_dsem(q, "cc")
        ins = self.nc.gpsimd.collective_compute("AllGather", ALU.bypass, replica_groups=RG, ins=[in_ap], outs=[out_ap])
        self.dcnt[i] += 1
        ins.then_inc(self.dsem[i], 1)
        self.ninst += 1
        tok = (i, self.dcnt[i])
        self._commit(tok, reads, writes)
        return tok

    def soft_barrier(self):
        for e in self.eng:
            if e != "pe" and self.cnt["pe"] > 0:
                self._wait(e, ("pe", self.cnt["pe"]))
            for i in range(32, 44):
                if self.dcnt[i] > 0:
                    self._wait(e, (i, self.dcnt[i]))

    def barrier(self):
        for e in self.eng:
            for f in self.eng:
                if f != e and self.cnt[f] > 0:
                    self._wait(e, (f, self.cnt[f]))
            for i in range(self.NDMA):
                if self.dcnt[i] > 0:
                    self._wait(e, (i, self.dcnt[i]))

    def finish(self):
        for i in range(self.NDMA):
            if self.dcnt[i] > 0:
                self._wait("sp", (i, self.dcnt[i]))


class Prog:
    def __init__(self):
        nc = bass.Bass("TRN2", target_bir_lowering=False)
        self.nc = nc
        self.fw = FW(nc)
        self.es = ExitStack()
        self.ring_idx = {}
        self.din = {}
        self.dout = {}

    def inp(self, name, shape, dt=F32):
        if name in SKIP_IN:
            return None
        t = self.nc.dram_tensor(name, list(shape), dt, kind="ExternalInput").ap()
        self.din[name] = t
        return t

    def outp(self, name, shape, dt=F32):
        t = self.nc.dram_tensor(name, list(shape), dt, kind="ExternalOutput").ap()
        self.dout[name] = t
        return t

    def scratch(self, name, shape, dt=F32):
        return self.nc.dram_tensor(name, list(shape), dt).ap()

    def sb(self, es, name, shape, dt=F32):
        self.uid = getattr(self, "uid", 0) + 1
        t = es.enter_context(self.nc.sbuf_tensor("%s_%d" % (name, self.uid), list(shape), dt))
        return t, Buf(name)

    def ring(self, es, name, n, shape, dt=F32):
        items = [self.sb(es, "%s%d" % (name, i), shape, dt) for i in range(n)]
        key = name
        self.ring_idx[key] = 0

        def nxt():
            i = self.ring_idx[key]
            self.ring_idx[key] = (i + 1) % n
            return items[i]
        return nxt

    def V(self, fn, reads=(), writes=()):
        return self.fw.op("dve", fn, reads, writes)

    def A(self, fn, reads=(), writes=()):
        return self.fw.op("act", fn, reads, writes)

    def G(self, fn, reads=(), writes=()):
        return self.fw.op("pool", fn, reads, writes)

    def PE(self, fn, reads=(), writes=(), inc=True):
        return self.fw.op("pe", fn, reads, writes, inc=inc)

    def mm(self, out, lhsT, rhs, start, stop, reads, writes, inc=None, sgc=False):
        nc = self.nc
        return self.fw.op("pe", lambda: nc.tensor.matmul(out, lhsT=lhsT, rhs=rhs, start=start, stop=stop, skip_group_check=sgc),
                          reads, writes, inc=(stop if inc is None else inc))


def build_program():
    P = Prog()
    nc, fw = P.nc, P.fw
    V, A, PE, mm, G = P.V, P.A, P.PE, P.mm, P.G

    xin = {"P": P.inp("xpT", [D, T]), "S": P.inp("xsT", [D, T])}
    w_in = P.inp("w_in", [L, D, 6804])
    w_out = P.inp("w_out", [L, D, D])
    w_up = P.inp("w_up", [L, D, 2 * DFF])
    w_down = P.inp("w_down", [L, DFF, D])
    wmod = P.inp("wmod", [L, D, 3072])
    bmod = P.inp("bmod", [128, L * 24])
    cond2 = P.inp("cond2", [128, 32])
    lnp_d = P.inp("lnp", [128, L * 4 * 16])
    convp_d = P.inp("convp", [128, L * 86 * 4])
    gateb_d = P.inp("gateb", [L * 20])
    blam_d = P.inp("blam", [L * 256])
    subln_d = P.inp("subln", [128, L])
    cnorm_d = P.inp("cnorm", [L * 128])
    consts_d = P.inp("consts", [128, NCONST * 128])
    negm_d = P.inp("negm", [128, 2 * 640])
    rope_d = P.inp("rope", [128, 2 * T])
    nbias_d = P.inp("nbias", [L, 6, 16, 128, T])
    cakT = P.inp("cakT", [L, 6, 128, 512])
    cav = P.inp("cav", [L, 6, 512, 128])
    cbkT = P.inp("cbkT", [L, 5, 128, 512])
    cbv = P.inp("cbv", [L, 5, 512, 128])
    cC_d = P.inp("cC", [L, 2, 5, 128, 128])
    cn_d = P.inp("cn", [L, 128, 10])
    cm_d = P.inp("cm", [L * 10])
    cftab_d = P.inp("cftab", [128, 2 * 5 * 4 * 5])
    vtab_d = P.inp("vtab", [128, 2 * 5 * 5])
    sel_d = P.inp("sel", [128, 8])

    yout = {"P": P.outp("ypT", [D, T]), "S": P.outp("ysT", [D, T])}
    o_ak = P.outp("o_ak", [2, L, 6, 128, 256])
    o_av = P.outp("o_av", [2, L, 6, 256, 128])
    o_bk = P.outp("o_bk", [2, L, 5, 128, 256])
    o_bv = P.outp("o_bv", [2, L, 5, 256, 128])
    o_cC = P.outp("o_cC", [2, L, 2, 5, 128, 128])
    o_cn = P.outp("o_cn", [2, L, 2, 128, 5])
    o_cm = P.outp("o_cm", [2, L, 2, 5])
    B_out = Buf("outputs")

    xspill = {"P": P.scratch("xspP", [D, T]), "S": P.scratch("xspS", [D, T])}
    B_spill = {"P": Buf(), "S": Buf()}
    mg_in = P.scratch("mg_in", [128, 96]); mg_out = P.scratch("mg_out", [512, 96])
    B_mgi, B_mgo = Buf(), Buf()
    CH_NH = [4, 4, 3]
    HH_CH = [0, 0, 0, 0, 1, 1, 1, 1, 2, 2, 2]
    HH_IX = [0, 1, 2, 3, 0, 1, 2, 3, 0, 1, 2]
    bnc_in = [P.scratch("bnc_in%d" % i, [2 * n * 128, T], BF16) for i, n in enumerate(CH_NH)]
    bnc_out = [P.scratch("bnc_out%d" % i, [4 * 2 * n * 128, T], BF16) for i, n in enumerate(CH_NH)]
    B_bi = [Buf() for _ in CH_NH]
    B_bo = [Buf() for _ in CH_NH]

    def bnc_k_rows(hh):
        i = HH_IX[hh]
        return bnc_in[HH_CH[hh]][i * 128:(i + 1) * 128, :], B_bi[HH_CH[hh]]

    def bnc_v_rows(hh):
        c = HH_CH[hh]
        i = CH_NH[c] + HH_IX[hh]
        return bnc_in[c][i * 128:(i + 1) * 128, :], B_bi[c]
    CSF = 1310
    cs_in = P.scratch("cs_in", [128, CSF]); cs_out = P.scratch("cs_out", [512, CSF])
    B_csi, B_cso = Buf(), Buf()
    hb_in = P.scratch("hb_in", [128, 32]); hb_out = P.scratch("hb_out", [512, 32])
    B_hbi, B_hbo = Buf(), Buf()

    es = P.es
    x_sb, B_x = P.sb(es, "x_sb", [128, 16, T], F32)
    h_sb, B_h = P.sb(es, "h_sb", [128, 16, T], BF16)
    hh_sb, B_hh = P.sb(es, "hh_sb", [128, 16, 2], BF16)
    WSL = 8704
    wslots = [P.sb(es, "wr%d" % i, [128, WSL], BF16) for i in range(2)]
    wstate = {"i": 0}

    def wring():
        i = wstate["i"] % len(wslots)
        wstate["i"] += 1
        return wslots[i]

    class extra_slots:
        def __init__(self, want, reserve=2048):
            self.want, self.reserve = want, reserve

        def __enter__(self):
            self.sx = ExitStack()
            self.n = 0
            while self.n < self.want and nc.sbuf_bytes_remaining >= WSL * 2 + self.reserve + 256:
                wslots.append(P.sb(self.sx, "wx", [128, WSL], BF16))
                self.n += 1
            return self

        def __exit__(self, *a):
            if a[0] is None:
                fw.soft_barrier()
                for _ in range(self.n):
                    wslots.pop()
                self.sx.close()
            return False
    cst, B_c = P.sb(es, "cst", [128, NCONST, 128], F32)
    negm, _ = P.sb(es, "negm", [128, 2, 640], F32)
    rope, _ = P.sb(es, "rope", [128, 2, T], F32)
    ones_bf, _ = P.sb(es, "ones_bf", [128, 128], BF16)
    id_bf, _ = P.sb(es, "id_bf", [128, 128], BF16)
    tri_bf, _ = P.sb(es, "tri_bf", [128, 2, 128], BF16)
    lnp, _ = P.sb(es, "lnp", [128, L, 4, 16], F32)
    convp, _ = P.sb(es, "convp", [128, L, 86, 4], F32)
    gateb, _ = P.sb(es, "gateb", [128, L, 20], F32)
    blam, _ = P.sb(es, "blam", [128, L, 4, 64], F32)
    subln, _ = P.sb(es, "subln", [128, L], F32)
    cnorm, _ = P.sb(es, "cnorm", [128, L, 128], F32)
    cftab, _ = P.sb(es, "cftab", [128, 2, 5, 4, 5], F32)
    vtab, _ = P.sb(es, "vtab", [128, 2, 5, 5], F32)
    sel, _ = P.sb(es, "sel", [128, 8], F32)
    modv, B_modv = P.sb(es, "modv", [128, L, 96, 2], F32)
    nlam, B_nlam = P.sb(es, "nlam", [128, L], F32)
    sublns, _ = P.sb(es, "sublns", [128, L], F32)

    def CM(i):
        return cst[:, i, :]

    pbanks = [es.enter_context(nc.psum_tensor("ps%d" % i, [128, 512], F32)) for i in range(8)]
    pbufs = [Buf("ps%d" % i) for i in range(8)]
    pidx = {"s": 0, "l": 0}

    def ps_short():
        i = pidx["s"]
        pidx["s"] = (i + 1) % 5
        return pbanks[i], pbufs[i]

    def ps_long():
        i = pidx["l"]
        pidx["l"] = (i + 1) % 3
        return pbanks[5 + i], pbufs[5 + i]

    def bcast(ap1d, n):
        return bass.AP(ap1d.tensor, 0, [[0, 128], [1, n]])

    fw.dma("sp", cst[:], consts_d.rearrange("p (k n) -> p k n", k=NCONST), writes=[B_c])
    fw.dma("sp", negm[:], negm_d.rearrange("p (k n) -> p k n", k=2), writes=[B_c])
    fw.dma("sp", rope[:], rope_d.rearrange("p (k n) -> p k n", k=2), writes=[B_c])
    fw.dma("sp", lnp[:], lnp_d.rearrange("p (l k c) -> p l k c", l=L, k=4), writes=[B_c])
    fw.dma("sp", convp[:], convp_d.rearrange("p (l c k) -> p l c k", l=L, k=4), writes=[B_c])
    fw.dma("sp", gateb[:], bcast(gateb_d, L * 20).rearrange("p (l k) -> p l k", l=L), writes=[B_c])
    fw.dma("sp", blam[:], bcast(blam_d, L * 256).rearrange("p (l k c) -> p l k c", l=L, k=4), writes=[B_c])
    fw.dma("sp", subln[:], subln_d, writes=[B_c])
    fw.dma("sp", cnorm[:], bcast(cnorm_d, L * 128).rearrange("p (l k) -> p l k", l=L), writes=[B_c])
    fw.dma("sp", cftab[:], cftab_d.rearrange("p (d i r h) -> p d i r h", d=2, i=5, r=4), writes=[B_c])
    fw.dma("sp", vtab[:], vtab_d.rearrange("p (d i h) -> p d i h", d=2, i=5), writes=[B_c])
    fw.dma("sp", sel[:], sel_d, writes=[B_c])
    A(lambda: nc.scalar.copy(out=ones_bf[:], in_=CM(C_ONE)), [B_c], [B_c])
    A(lambda: nc.scalar.copy(out=id_bf[:], in_=CM(C_ID)), [B_c], [B_c])
    A(lambda: nc.scalar.copy(out=tri_bf[:, 0, :], in_=CM(C_TRIF)), [B_c], [B_c])
    A(lambda: nc.scalar.copy(out=tri_bf[:, 1, :], in_=CM(C_TRIB)), [B_c], [B_c])

    pre = {}

    def wload(src2d, kc, ncols, key=None):
        if key is not None and key in pre:
            return pre.pop(key)
        return _wload(src2d, kc, ncols)

    def prefetch(key, src2d, kc, ncols):
        pre[key] = _wload(src2d, kc, ncols)

    def _wload(src2d, kc, ncols):
        t, b = wring()
        view = t[:, 0:kc * ncols].rearrange("p (c n) -> p c n", n=ncols)
        srcv = src2d.rearrange("(c p) n -> p c n", p=128)
        step = max(1, 2048 // 128 // 1 if ncols >= 256 else 8)
        step = 16 if ncols >= 256 else 22
        for c0 in range(0, kc, step):
            c1 = min(kc, c0 + step)
            fw.dma("pool", view[:, c0:c1, :], srcv[:, c0:c1, :], writes=[b])
        return view, b

    with ExitStack() as s0:
        c2, B_c2 = P.sb(s0, "c2", [128, 16, 2], F32)
        c2b, _ = P.sb(s0, "c2b", [128, 16, 2], BF16)
        bm, B_bm = P.sb(s0, "bm", [128, L, 24], F32)
        mloc, B_ml = P.sb(s0, "mloc", [128, L, 24, 2], F32)
        mall, B_ma = P.sb(s0, "mall", [128, 4, L, 24, 2], F32)
        fw.dma("sp", c2[:], cond2.rearrange("p (c r) -> p c r", r=2), writes=[B_c2])
        fw.dma("sp", bm[:], bmod.rearrange("p (l c) -> p l c", l=L), writes=[B_bm])
        A(lambda: nc.scalar.activation(out=c2b[:], in_=c2[:], func=AF.Silu), [B_c2], [B_c2])
        with extra_slots(3):
            for l in range(L):
                for t4 in range(6):
                    wv, wb = wload(wmod[l, :, t4 * 512:(t4 + 1) * 512], 16, 512)
                    for q in range(4):
                        cc = t4 * 4 + q
                        ps, pb = ps_short()
                        for c in range(16):
                            mm(ps[:, 0:2], wv[:, c, q * 128:(q + 1) * 128], c2b[:, c, :], c == 0, c == 15, [wb, B_c2], [pb])
                        A(lambda: nc.scalar.activation(out=mloc[:, l, cc, :], in_=ps[:, 0:2], func=AF.Identity,
                                                       bias=bm[:, l, cc:cc + 1], scale=1.0), [pb, B_bm], [B_ml])
        fw.dma("sp", mg_in, mloc[:].rearrange("p l c r -> p (l c r)"), reads=[B_ml], writes=[B_mgi])
        fw.allgather(mg_in, mg_out, reads=[B_mgi], writes=[B_mgo])
        fw.dma("sp", mall[:].rearrange("p r l c w -> p r (l c w)"), mg_out.rearrange("(r p) f -> p r f", p=128),
               reads=[B_mgo], writes=[B_ma])
        for r in range(4):
            for l in range(L):
                A(lambda: nc.scalar.copy(out=modv[:, l, r * 24:(r + 1) * 24, :], in_=mall[:, r, l, :, :]), [B_ma], [B_modv])
        for l in range(L):
            for v0 in (16, 64):
                A(lambda: nc.scalar.activation(out=modv[:, l, v0:v0 + 16, :], in_=modv[:, l, v0:v0 + 16, :], func=AF.Identity, bias=1.0, scale=1.0),
                  [B_modv], [B_modv])
        lt, B_lt = P.sb(s0, "lt", [128, 64], F32)
        ld, B_ld = P.sb(s0, "ld", [128, 4], F32)
        for l in range(L):
            lam_init = 0.8 - 0.6 * math.exp(-0.3 * l)
            for k in range(2):
                V(lambda: nc.vector.tensor_tensor(out=lt[:], in0=blam[:, l, 2 * k, :], in1=blam[:, l, 2 * k + 1, :], op=ALU.mult), [B_c], [B_lt])
                V(lambda: nc.vector.reduce_sum(out=ld[:, k:k + 1], in_=lt[:], axis=AX.X), [B_lt], [B_ld])
            A(lambda: nc.scalar.activation(out=ld[:, 2:4], in_=ld[:, 0:2], func=AF.Exp), [B_ld], [B_ld])
            V(lambda: nc.vector.tensor_tensor(out=nlam[:, l:l + 1], in0=ld[:, 3:4], in1=ld[:, 2:3], op=ALU.subtract), [B_ld], [B_nlam])
            A(lambda: nc.scalar.activation(out=nlam[:, l:l + 1], in_=nlam[:, l:l + 1], func=AF.Identity, bias=-lam_init, scale=1.0), [B_nlam], [B_nlam])
            A(lambda: nc.scalar.activation(out=sublns[:, l:l + 1], in_=subln[:, l:l + 1], func=AF.Identity, scale=1.0 - lam_init), [B_c], [B_nlam])
        fw.barrier()

    def modp(l, v, fc, row):
        return modv[:, l, v * 16 + fc, row:row + 1]

    def modulate(l, vsh, vsc, row):
        for c in range(16):
            A(lambda: nc.scalar.activation(out=h_sb[:, c, :], in_=x_sb[:, c, :], func=AF.Identity,
                                           bias=modp(l, vsh, c, row), scale=modp(l, vsc, c, row)), [B_x, B_modv], [B_h])

    def layernorm(l, k, scope):
        sq_ring = P.ring(scope, "lnsq%d" % k, 2, [128, T], F32)
        st, B_st = P.sb(scope, "lnst%d" % k, [128, 2, T], F32)
        p1, b1 = ps_long()
        p2, b2 = ps_long()
        for c in range(16):
            sq, bq = sq_ring()
            A(lambda: nc.scalar.activation(out=sq[:], in_=x_sb[:, c, :], func=AF.Square), [B_x], [bq])
            mm(p1[:], CM(C_ONE), x_sb[:, c, :], c == 0, c == 15, [B_c, B_x], [b1], inc=True)
            mm(p2[:], CM(C_ONE), sq[:], c == 0, c == 15, [B_c, bq], [b2], inc=True)
        mean, var = st[:, 0, :], st[:, 1, :]
        A(lambda: nc.scalar.activation(out=mean, in_=p1[:], func=AF.Identity, scale=1.0 / D), [b1], [B_st])
        V(lambda: nc.vector.tensor_tensor(out=var, in0=mean, in1=mean, op=ALU.mult), [B_st], [B_st])
        V(lambda: nc.vector.scalar_tensor_tensor(out=var, in0=p2[:], scalar=1.0 / D, in1=var, op0=ALU.mult, op1=ALU.subtract), [b2, B_st], [B_st])
        A(lambda: nc.scalar.activation(out=var, in_=var, func=AF.Ln, bias=LN_EPS, scale=1.0), [B_st], [B_st])
        A(lambda: nc.scalar.activation(out=var, in_=var, func=AF.Exp, scale=-0.5), [B_st], [B_st])
        for c in range(16):
            V(lambda: nc.vector.tensor_tensor(out=x_sb[:, c, :], in0=x_sb[:, c, :], in1=mean, op=ALU.subtract), [B_x, B_st], [B_x])
            V(lambda: nc.vector.tensor_tensor(out=x_sb[:, c, :], in0=x_sb[:, c, :], in1=var, op=ALU.mult), [B_x, B_st], [B_x])
            A(lambda: nc.scalar.activation(out=x_sb[:, c, :], in_=x_sb[:, c, :], func=AF.Identity,
                                           bias=lnp[:, l, 2 * k + 1, c:c + 1], scale=lnp[:, l, 2 * k, c:c + 1]), [B_x, B_c], [B_x])

    def residual_proj(l, wsrc, kc, ncols_tile, rhs_fn, rhs_bufs, vgate, row, scope, tag, after=None):
        tmp_ring = P.ring(scope, "rp" + tag, 2, [128, T], F32)
        per = ncols_tile // 128
        with extra_slots(3):
            for tcol in range(D // ncols_tile):
                wv, wb = wload(wsrc[:, tcol * ncols_tile:(tcol + 1) * ncols_tile], kc, ncols_tile, key=(tag, l, tcol))
                for q in range(per):
                    fc = tcol * per + q
                    ps, pb = ps_short()
                    for c in range(kc):
                        mm(ps[:], wv[:, c, q * 128:(q + 1) * 128], rhs_fn(c), c == 0, c == kc - 1, [wb] + rhs_bufs, [pb])
                    tmp, tb = tmp_ring()
                    A(lambda: nc.scalar.activation(out=tmp[:], in_=ps[:], func=AF.Identity, scale=modp(l, vgate, fc, row)), [pb, B_modv], [tb])
                    V(lambda: nc.vector.scalar_tensor_tensor(out=x_sb[:, fc, :], in0=x_sb[:, fc, :], scalar=ALPHA, in1=tmp[:],
                                                             op0=ALU.mult, op1=ALU.add), [B_x, tb], [B_x])
        if after is not None:
            after()

    def attention(groups, scale, et_ring, btmp_ring, finish):
        yps, yb = ps_long()
        dps, db = ps_long()
        ng = len(groups)
        for gi, grp in enumerate(groups):
            st, sb_ = ps_short()
            for si, s in enumerate(grp):
                mm(st[:, s["c0"]:s["c0"] + s["n"]], s["k"], s["q"], si == 0, True, [s["kb"], s["qb"]], [sb_], inc=(si == len(grp) - 1), sgc=True)
            et, eb = et_ring()
            bias_fn = grp[0].get("bias_fn")
            if bias_fn is not None:
                bias, bias_b = bias_fn()
                bt, btb = btmp_ring()
                V(lambda: nc.vector.scalar_tensor_tensor(out=bt[:], in0=st[:], scalar=scale, in1=bias, op0=ALU.mult, op1=ALU.add),
                  [sb_, bias_b], [btb])
                A(lambda: nc.scalar.activation(out=et[:], in_=bt[:], func=AF.Exp), [btb], [eb])
            else:
                A(lambda: nc.scalar.activation(out=et[:], in_=st[:], func=AF.Exp, scale=scale), [sb_], [eb])
            for si, s in enumerate(grp):
                mm(yps[:, s["c0"]:s["c0"] + s["n"]], s["v"], et[:, s["c0"]:s["c0"] + s["n"]],
                   gi == 0 and si == 0, gi == ng - 1, [s["vb"], eb], [yb], inc=False, sgc=True)
            mm(dps[:], ones_bf[:], et[:], gi == 0, gi == ng - 1, [B_c, eb], [db], inc=True)
        finish(yps, yb, dps, db)

    def block(l, g):
        row = 0 if g == "P" else 1
        lam_init = 0.8 - 0.6 * math.exp(-0.3 * l)
        xsrc = xin[g] if l == 0 else xspill[g]
        fw.dma("sp", x_sb[:], xsrc.rearrange("(c p) t -> p c t", p=128), reads=[B_spill[g]], writes=[B_x])
        modulate(l, 0, 1, row)
        with ExitStack() as sm:
            ycat, B_y = P.sb(sm, "ycat", [128, 16, T], BF16)
            with ExitStack() as sc:
                qtc, B_qtc = P.sb(sc, "qtc", [128, 5, T], BF16)
                ktc, B_ktc = P.sb(sc, "ktc", [128, 5, T], BF16)
                kcA, B_kcA = P.sb(sc, "kcA", [128, 4, 640], BF16)
                kcB, B_kcB = P.sb(sc, "kcB", [128, 4, 640], BF16)
                vc, B_vc = P.sb(sc, "vc", [128, 4, 640], BF16)
                sigoc, B_so = P.sb(sc, "sigoc", [128, 4, 640], F32)
                gat, B_gat = P.sb(sc, "gat", [128, 4, 20], F32)
                G(lambda: nc.gpsimd.memset(kcA[:], 0.0), [], [B_kcA])
                G(lambda: nc.gpsimd.memset(kcB[:], 0.0), [], [B_kcB])
                ctiles = [(4224, 512), (4736, 512), (5248, 512), (5760, 512), (6272, 532)]
                with extra_slots(2):
                    for (c0, ncol) in ctiles:
                        wv, wb = wload(w_in[l, :, c0:c0 + ncol], 16, ncol, key=("w_in", l, c0))
                        for q in range(min(4, ncol // 128)):
                            col = c0 + q * 128
                            k = col // 128
                            if 33 <= k <= 42:
                                ps, pb = ps_short()
                                for c in range(16):
                                    mm(ps[:], wv[:, c, q * 128:(q + 1) * 128], h_sb[:, c, :], c == 0, c == 15, [wb, B_h], [pb])
                                if k <= 37:
                                    A(lambda: nc.scalar.activation(out=qtc[:, k - 33, :], in_=ps[:], func=AF.Copy, scale=128.0 ** -0.5), [pb], [B_qtc])
                                else:
                                    A(lambda: nc.scalar.copy(out=ktc[:, k - 38, :], in_=ps[:]), [pb], [B_ktc])
                        segs = []
                        for (name, lo, hi) in (("kc", 4864, 5504), ("vc", 5504, 6144), ("oc", 6144, 6784), ("gc", 6784, 6804)):
                            a, b_ = max(lo, c0), min(hi, c0 + ncol)
                            if a < b_:
                                segs.append((name, a, b_, lo))
                        for (name, a, b_, lo) in segs:
                            for tt in range(4):
                                ps, pb = ps_short()
                                n = b_ - a
                                for c in range(16):
                                    mm(ps[:, 0:n], h_sb[:, c, tt * 128:(tt + 1) * 128], wv[:, c, a - c0:b_ - c0], c == 0, c == 15, [wb, B_h], [pb])
                                o0 = a - lo
                                if name == "kc":
                                    A(lambda: nc.scalar.copy(out=kcA[0:64, tt, o0:o0 + n], in_=ps[0:64, 0:n]), [pb], [B_kcA])
                                    A(lambda: nc.scalar.copy(out=kcB[64:128, tt, o0:o0 + n], in_=ps[64:128, 0:n]), [pb], [B_kcB])
                                elif name == "vc":
                                    A(lambda: nc.scalar.copy(out=vc[:, tt, o0:o0 + n], in_=ps[:, 0:n]), [pb], [B_vc])
                                elif name == "oc":
                                    A(lambda: nc.scalar.activation(out=sigoc[:, tt, o0:o0 + n], in_=ps[:, 0:n], func=AF.Sigmoid), [pb], [B_so])
                                else:
                                    V(lambda: nc.vector.tensor_tensor(out=gat[:, tt, :], in0=ps[:, 0:20], in1=gateb[:, l, :], op=ALU.add), [pb, B_c], [B_gat])
                for c0_ in (0, 512):
                    prefetch(("w_in", l, c0_), w_in[l, :, c0_:c0_ + 512], 16, 512)
                kstop("c1")
                bc, B_bc = P.sb(sc, "bc", [128, 2, 4, 10], F32)
                ea, B_ea = P.sb(sc, "ea", [128, 2, 4, 5], F32)
                bl, B_bl = P.sb(sc, "bl", [128, 2, 8, 10], F32)
                t5, B_t5 = P.sb(sc, "t5", [128, 4, 5], F32)
                vp, B_vp = P.sb(sc, "vp", [128, 2, 4, 5, 130], BF16)
                sgp = ExitStack()
                dg, B_dg = P.sb(sgp, "dg", [128, 5, 128], F32)
                mk, B_mk = P.sb(sgp, "mk", [128, 5, 128], F32)
                for d in range(2):
                    tri = CM(C_TRIF if d == 0 else C_TRIB)
                    for tt in range(4):
                        ig = gat[:, tt, d * 10:d * 10 + 5]
                        fg = gat[:, tt, d * 10 + 5:d * 10 + 10]
                        A(lambda: nc.scalar.activation(out=t5[:, 0, :], in_=fg, func=AF.Exp, scale=-1.0), [B_gat], [B_t5])
                        A(lambda: nc.scalar.activation(out=t5[:, 1, :], in_=t5[:, 0, :], func=AF.Ln, bias=1.0, scale=1.0), [B_t5], [B_t5])
                        ps, pb = ps_short()
                        mm(ps[:, 0:5], tri, t5[:, 1, :], True, True, [B_c, B_t5], [pb])
                        A(lambda: nc.scalar.copy(out=bc[:, d, tt, 0:5], in_=ps[:, 0:5]), [pb], [B_bc])
                        V(lambda: nc.vector.tensor_tensor(out=t5[:, 2, :], in0=ig, in1=bc[:, d, tt, 0:5], op=ALU.add), [B_gat, B_bc], [B_t5])
                        A(lambda: nc.scalar.activation(out=ea[:, d, tt, :], in_=t5[:, 2, :], func=AF.Exp), [B_t5], [B_ea])
                        for h in range(5):
                            A(lambda: nc.scalar.activation(out=dg[:, h, :], in_=CM(C_ID), func=AF.Identity, scale=t5[:, 2, h:h + 1]), [B_c, B_t5], [B_dg])
                        ps1, pb1 = ps_short()
                        mm(ps1[:, 0:384], CM(C_ONE), dg[:, 0:3, :].rearrange("p h s -> p (h s)"), True, True, [B_c, B_dg], [pb1])
                        ps2, pb2 = ps_short()
                        mm(ps2[:, 0:256], CM(C_ONE), dg[:, 3:5, :].rearrange("p h s -> p (h s)"), True, True, [B_c, B_dg], [pb2])
                        V(lambda: nc.vector.tensor_tensor(out=mk[:, 0:3, :].rearrange("p h s -> p (h s)"), in0=ps1[:, 0:384],
                                                          in1=negm[:, d, 0:384], op=ALU.add), [pb1, B_c], [B_mk])
                        V(lambda: nc.vector.tensor_tensor(out=mk[:, 3:5, :].rearrange("p h s -> p (h s)"), in0=ps2[:, 0:256],
                                                          in1=negm[:, d, 384:640], op=ALU.add), [pb2, B_c], [B_mk])
                        V(lambda: nc.vector.tensor_reduce(out=bc[:, d, tt, 5:10], in_=mk[:], axis=AX.X, op=ALU.max), [B_mk], [B_bc])
                        for X in range(2):
                            ep = (C_E63, C_E127)[X] if d == 0 else (C_E0, C_E64)[X]
                            ps, pb = ps_short()
                            mm(ps[:, 0:10], CM(ep), bc[:, d, tt, :], True, True, [B_c, B_bc], [pb])
                            A(lambda: nc.scalar.copy(out=bl[:, d, 2 * tt + X, :], in_=ps[:, 0:10]), [pb], [B_bl])
                        for h in range(5):
                            A(lambda: nc.scalar.activation(out=vp[:, d, tt, h, 0:128], in_=vc[:, tt, h * 128:(h + 1) * 128], func=AF.Identity,
                                                           scale=ea[:, d, tt, h:h + 1]), [B_vc, B_ea], [B_vp])
                        A(lambda: nc.scalar.copy(out=vp[:, d, tt, :, 128], in_=ea[:, d, tt, :]), [B_ea], [B_vp])

                fw.barrier()
                sgp.close()
                kstop("c2")
                mc, B_mc = P.sb(sc, "mc", [128, 2, 8, 5], F32)
                wold, B_wo = P.sb(sc, "wold", [128, 2, 8, 5], F32)
                snew, B_sn = P.sb(sc, "snew", [128, 2, 8, 5], F32)
                mcur, B_mcur = P.sb(sc, "mcur", [128, 2, 5], F32)
                mt, B_mt = P.sb(sc, "mt", [128, 2, 5], F32)
                nfacc, B_nf = P.sb(sc, "nfacc", [128, 2, 5], F32)
                cn, B_cn = P.sb(sc, "cn", [128, 10, 129], F32)
                cnb, B_cnb = P.sb(sc, "cnb", [128, 10, 130], BF16)
                tmpu_ring = P.ring(sc, "tmpu", 2, [128, 129], F32)

                def chunk_order(d, runs):
                    out = []
                    rr = runs if d == 0 else [list(reversed(r)) for r in reversed(runs)]
                    for r in rr:
                        out.append(r)
                    return out

                def mchain(d, run, m_init_fn):
                    m_init_fn(mcur[:, d, :])
                    for c in run:
                        A(lambda: nc.scalar.copy(out=mc[:, d, c, :], in_=mcur[:, d, :]), [B_mcur], [B_mc])
                        V(lambda: nc.vector.tensor_tensor(out=mt[:, 0, :], in0=mcur[:, d, :], in1=bl[:, d, c, 5:10], op=ALU.max), [B_mcur, B_bl], [B_mt])
                        V(lambda: nc.vector.tensor_tensor(out=mt[:, 1, :], in0=mcur[:, d, :], in1=mt[:, 0, :], op=ALU.subtract), [B_mcur, B_mt], [B_mt])
                        A(lambda: nc.scalar.activation(out=wold[:, d, c, :], in_=mt[:, 1, :], func=AF.Exp), [B_mt], [B_wo])
                        A(lambda: nc.scalar.activation(out=snew[:, d, c, :], in_=mt[:, 0, :], func=AF.Exp, scale=-1.0), [B_mt], [B_sn])
                        V(lambda: nc.vector.tensor_tensor(out=mcur[:, d, :], in0=mt[:, 0, :], in1=bl[:, d, c, 0:5], op=ALU.subtract), [B_mt, B_bl], [B_mcur])
                        V(lambda: nc.vector.tensor_tensor(out=nfacc[:, d, :], in0=nfacc[:, d, :], in1=bl[:, d, c, 0:5], op=ALU.add), [B_nf, B_bl], [B_nf])

                def state_update(d, h, c, need_bf=True):
                    tt, X = c // 2, c % 2
                    kk = kcA if X == 0 else kcB
                    kkb = B_kcA if X == 0 else B_kcB
                    ps, pb = ps_short()
                    mm(ps[:, 0:129], kk[:, tt, h * 128:(h + 1) * 128], vp[:, d, tt, h, 0:129], True, True, [kkb, B_vp], [pb])
                    tu, tub = tmpu_ring()
                    A(lambda: nc.scalar.activation(out=tu[:], in_=ps[:, 0:129], func=AF.Identity, scale=snew[:, d, c, h:h + 1]), [pb, B_sn], [tub])
                    V(lambda: nc.vector.scalar_tensor_tensor(out=cn[:, d * 5 + h, :], in0=cn[:, d * 5 + h, :], scalar=wold[:, d, c, h:h + 1],
                                                             in1=tu[:], op0=ALU.mult, op1=ALU.add), [B_cn, B_wo, tub], [B_cn])
                    if need_bf:
                        A(lambda: nc.scalar.copy(out=cnb[:, d * 5 + h, 0:129], in_=cn[:, d * 5 + h, :]), [B_cn], [B_cnb])

                def zero_state(d):
                    G(lambda: nc.gpsimd.memset(cn[:, d * 5:(d + 1) * 5, :], 0.0), [], [B_cn])
                    G(lambda: nc.gpsimd.memset(cnb[:, d * 5:(d + 1) * 5, :], 0.0), [], [B_cnb])

                def alloc_scan_bufs():
                    a_ = P.sb(sc, "hc", [128, 4, 640], F32)
                    b_ = P.sb(sc, "tok", [128, 2, 4, 15], F32)
                    c_ = P.sb(sc, "mcol", [128, 5], F32)
                    return (a_[0], a_[1], b_[0], b_[1], c_[0], c_[1], P.ring(sc, "gm", 3, [128, 128], BF16),
                            P.ring(sc, "ti", 3, [128, 129], F32), P.ring(sc, "hn", 3, [128, 129], F32), P.ring(sc, "s3", 3, [128, 3], F32))

                def token_scalars(d, tt):
                    A(lambda: nc.scalar.copy(out=mcol[0:64, :], in_=mc[0:64, d, 2 * tt, :]), [B_mc], [B_mcol])
                    A(lambda: nc.scalar.copy(out=mcol[64:128, :], in_=mc[64:128, d, 2 * tt + 1, :]), [B_mc], [B_mcol])
                    V(lambda: nc.vector.tensor_tensor(out=t5[:, 3, :], in0=bc[:, d, tt, 5:10], in1=mcol[:], op=ALU.max), [B_bc, B_mcol], [B_t5])
                    A(lambda: nc.scalar.activation(out=tok[:, d, tt, 0:5], in_=t5[:, 3, :], func=AF.Exp, scale=-1.0), [B_t5], [B_tok])
                    V(lambda: nc.vector.tensor_tensor(out=t5[:, 0, :], in0=mcol[:], in1=t5[:, 3, :], op=ALU.subtract), [B_mcol, B_t5], [B_t5])
                    A(lambda: nc.scalar.activation(out=tok[:, d, tt, 5:10], in_=t5[:, 0, :], func=AF.Exp), [B_t5], [B_tok])
                    V(lambda: nc.vector.tensor_tensor(out=t5[:, 1, :], in0=bc[:, d, tt, 0:5], in1=t5[:, 3, :], op=ALU.subtract), [B_bc, B_t5], [B_t5])
                    A(lambda: nc.scalar.activation(out=tok[:, d, tt, 10:15], in_=t5[:, 1, :], func=AF.Exp), [B_t5], [B_tok])

                def scan_outputs(runs, on_run_end, on_run_start):
                    G(lambda: nc.gpsimd.memset(hc[:], 0.0), [], [B_hc])
                    for d in range(2):
                        for tt in range(4):
                            token_scalars(d, tt)
                    order = {d: chunk_order(d, runs) for d in range(2)}
                    nsteps = sum(len(r) for r in runs) // 2
                    flat = {d: [c for r in order[d] for c in r] for d in range(2)}
                    run_start = {d: {r[0]: ri for ri, r in enumerate(order[d])} for d in range(2)}
                    run_end = {d: {r[-1]: ri for ri, r in enumerate(order[d])} for d in range(2)}
                    for step in range(nsteps):
                        for d in range(2):
                            c_pair = flat[d][2 * step:2 * step + 2]
                            tt = c_pair[0] // 2
                            if c_pair[0] in run_start[d]:
                                on_run_start(d, run_start[d][c_pair[0]], order[d])
                            for h in range(5):
                                gps, gpb = ps_short()
                                mm(gps[:, 0:128], ktc[:, h, tt * 128:(tt + 1) * 128], qtc[:, h, tt * 128:(tt + 1) * 128], True, True, [B_ktc, B_qtc], [gpb])
                                gm, gmb = gm_ring()
                                V(lambda: nc.vector.tensor_tensor(out=gm[:], in0=gps[:, 0:128], in1=CM(C_TRIF if d == 0 else C_TRIB), op=ALU.mult), [gpb, B_c], [gmb])
                                ips, ipb = ps_short()
                                mm(ips[:, 0:129], gm[:], vp[:, d, tt, h, 0:129], True, True, [gmb, B_vp], [ipb])
                                ti, tib = ti_ring()
                                A(lambda: nc.scalar.activation(out=ti[:], in_=ips[:, 0:129], func=AF.Identity, scale=tok[:, d, tt, h:h + 1]), [ipb, B_tok], [tib])
                                hn, hnb = hn_ring()
                                for c in c_pair:
                                    X = c % 2
                                    rs = slice(0, 64) if X == 0 else slice(64, 128)
                                    xps, xpb = ps_short()
                                    mm(xps[:, 0:129], qtc[:, h, tt * 128:(tt + 1) * 128], cnb[:, d * 5 + h, 0:129], True, True, [B_qtc, B_cnb], [xpb])
                                    V(lambda: nc.vector.scalar_tensor_tensor(out=hn[rs, :], in0=xps[rs, 0:129], scalar=tok[rs, d, tt, 5 + h:6 + h],
                                                                             in1=ti[rs, :], op0=ALU.mult, op1=ALU.add), [xpb, B_tok, tib], [hnb])
                                    state_update(d, h, c)
                                s3, s3b = s3_ring()
                                V(lambda: nc.vector.scalar_tensor_tensor(out=s3[:, 0:1], in0=hn[:, 128:129], scalar=-1.0, in1=hn[:, 128:129],
                                                                         op0=ALU.mult, op1=ALU.max), [hnb], [s3b])
                                V(lambda: nc.vector.tensor_tensor(out=s3[:, 1:2], in0=s3[:, 0:1], in1=tok[:, d, tt, 10 + h:11 + h], op=ALU.max), [s3b, B_tok], [s3b])
                                A(lambda: nc.scalar.activation(out=s3[:, 2:3], in_=s3[:, 1:2], func=AF.Ln), [s3b], [s3b])
                                A(lambda: nc.scalar.activation(out=s3[:, 2:3], in_=s3[:, 2:3], func=AF.Exp, scale=-1.0), [s3b], [s3b])
                                hsl = hc[:, tt, h * 128:(h + 1) * 128]
                                V(lambda: nc.vector.scalar_tensor_tensor(out=hsl, in0=hn[:, 0:128], scalar=s3[:, 2:3], in1=hsl,
                                                                         op0=ALU.mult, op1=ALU.add), [hnb, s3b, B_hc], [B_hc])
                            if c_pair[1] in run_end[d]:
                                on_run_end(d, run_end[d][c_pair[1]], order[d])

                def set_const(val):
                    def f(ap):
                        G(lambda: nc.gpsimd.memset(ap, val), [], [B_mcur])
                    return f

                G(lambda: nc.gpsimd.memset(nfacc[:], 0.0), [], [B_nf])
                if g == "P":
                    runs = [[0, 1, 2, 3], [4, 5, 6, 7]]
                    mfin, B_mfin = P.sb(sc, "mfin", [128, 2, 2, 5], F32)
                    for d in range(2):
                        for ri, r in enumerate(chunk_order(d, runs)):
                            mchain(d, r, set_const(0.0))
                            seq = r[0] // 4
                            A(lambda: nc.scalar.copy(out=mfin[:, seq, d, :], in_=mcur[:, d, :]), [B_mcur], [B_mfin])
                    for seq in range(2):
                        fw.dma("sp", o_cm[seq, l].rearrange("(o d) h -> o (d h)", o=1), mfin[0:1, seq, :, :].rearrange("p d h -> p (d h)"),
                               reads=[B_mfin], writes=[B_out])

                    def on_start(d, ri, order):
                        zero_state(d)

                    def on_end(d, ri, order):
                        seq = order[ri][0] // 4
                        fw.dma("sp", o_cC[seq, l, d].rearrange("h k v -> k h v"), cn[:, d * 5:(d + 1) * 5, 0:128], reads=[B_cn], writes=[B_out])
                        fw.dma("sp", o_cn[seq, l, d], cn[:, d * 5:(d + 1) * 5, 128], reads=[B_cn], writes=[B_out], slow=True)
                    hc, B_hc, tok, B_tok, mcol, B_mcol, gm_ring, ti_ring, hn_ring, s3_ring = alloc_scan_bufs()
                    scan_outputs(runs, on_end, on_start)
                else:
                    runs = [[0, 1, 2, 3, 4, 5, 6, 7]]
                    for d in range(2):
                        zero_state(d)
                        r = chunk_order(d, runs)[0]
                        mchain(d, r, set_const(NEG))
                        for c in r:
                            for h in range(5):
                                state_update(d, h, c, need_bf=False)
                    with ExitStack() as sg:
                        cst_t, B_cst = P.sb(sg, "cst_t", [128, 20], F32)
                        A(lambda: nc.scalar.copy(out=cst_t[:, 0:10], in_=mcur[:].rearrange("p d h -> p (d h)")), [B_mcur], [B_cst])
                        A(lambda: nc.scalar.copy(out=cst_t[:, 10:20], in_=nfacc[:].rearrange("p d h -> p (d h)")), [B_nf], [B_cst])
                        fw.dma("sp", cs_in[:, 0:1290], cn[:].rearrange("p a b -> p (a b)"), reads=[B_cn], writes=[B_csi])
                        fw.dma("sp", cs_in[:, 1290:1310], cst_t[:], reads=[B_cst], writes=[B_csi])
                        fw.allgather(cs_in, cs_out, reads=[B_csi], writes=[B_cso])
                        gs, B_gs = P.sb(sg, "gs", [128, 4, CSF], F32)
                        fw.dma("sp", gs[:], cs_out.rearrange("(r p) f -> p r f", p=128), reads=[B_cso], writes=[B_gs])
                        c0t, B_c0 = P.sb(sg, "c0t", [128, 10, 129], F32)
                        m0t, B_m0 = P.sb(sg, "m0t", [128, 10], F32)
                        fw.dma("sp", c0t[:, :, 0:128], cC_d[l].rearrange("d h k v -> k (d h) v"), writes=[B_c0])
                        fw.dma("sp", c0t[:, :, 128], cn_d[l], writes=[B_c0], slow=True)
                        fw.dma("sp", m0t[:], bass.AP(cm_d.tensor, l * 10, [[0, 128], [1, 10]]), writes=[B_m0])
                        av, B_av = P.sb(sg, "av", [128, 2, 6, 5], F32)
                        wv5, B_wv5 = P.sb(sg, "wv5", [128, 2, 5, 5], F32)
                        for d in range(2):
                            for i in range(5):
                                src = m0t[:, d * 5:(d + 1) * 5] if i == 0 else gs[:, i - 1, 1290 + d * 5:1290 + d * 5 + 5]
                                V(lambda: nc.vector.tensor_tensor(out=av[:, d, i, :], in0=src, in1=vtab[:, d, i, :], op=ALU.add), [B_m0, B_gs, B_c], [B_av])
                                for r in range(4):
                                    V(lambda: nc.vector.tensor_tensor(out=t5[:, 0, :], in0=cftab[:, d, i, r, :], in1=gs[:, r, 1300 + d * 5:1305 + d * 5], op=ALU.mult),
                                      [B_c, B_gs], [B_t5])
                                    V(lambda: nc.vector.tensor_tensor(out=av[:, d, i, :], in0=av[:, d, i, :], in1=t5[:, 0, :], op=ALU.subtract), [B_av, B_t5], [B_av])
                            V(lambda: nc.vector.tensor_tensor(out=av[:, d, 5, :], in0=av[:, d, 0, :], in1=av[:, d, 1, :], op=ALU.max), [B_av], [B_av])
                            for i in range(2, 5):
                                V(lambda: nc.vector.tensor_tensor(out=av[:, d, 5, :], in0=av[:, d, 5, :], in1=av[:, d, i, :], op=ALU.max), [B_av], [B_av])
                            for i in range(5):
                                V(lambda: nc.vector.tensor_tensor(out=t5[:, 1, :], in0=av[:, d, i, :], in1=av[:, d, 5, :], op=ALU.subtract), [B_av], [B_t5])
                                A(lambda: nc.scalar.activation(out=wv5[:, d, i, :], in_=t5[:, 1, :], func=AF.Exp), [B_t5], [B_wv5])
                            for h in range(5):
                                dh = d * 5 + h
                                A(lambda: nc.scalar.activation(out=cn[:, dh, :], in_=c0t[:, dh, :], func=AF.Identity, scale=wv5[:, d, 0, h:h + 1]), [B_c0, B_wv5], [B_cn])
                                for r in range(4):
                                    V(lambda: nc.vector.scalar_tensor_tensor(out=cn[:, dh, :], in0=gs[:, r, dh * 129:(dh + 1) * 129], scalar=wv5[:, d, 1 + r, h:h + 1],
                                                                             in1=cn[:, dh, :], op0=ALU.mult, op1=ALU.add), [B_gs, B_wv5, B_cn], [B_cn])
                                A(lambda: nc.scalar.copy(out=cnb[:, dh, 0:129], in_=cn[:, dh, :]), [B_cn], [B_cnb])

                        def m_from_av(d):
                            def f(ap):
                                A(lambda: nc.scalar.copy(out=ap, in_=av[:, d, 5, :]), [B_av], [B_mcur])
                            return f
                        for d in range(2):
                            mchain(d, chunk_order(d, runs)[0], m_from_av(d))
                        fw.barrier()
                    hc, B_hc, tok, B_tok, mcol, B_mcol, gm_ring, ti_ring, hn_ring, s3_ring = alloc_scan_bufs()
                    scan_outputs(runs, lambda *a: None, lambda *a: None)

                kstop("c3")
                ss, B_ss = P.sb(sc, "ss", [128, 20], F32)
                junk, B_junk = P.sb(sc, "junk", [128, 128], F32)
                yct_ring = P.ring(sc, "yct", 3, [128, 128], BF16)
                ytmp_ring = P.ring(sc, "ytmp", 2, [128, 128], F32)
                G(lambda: nc.gpsimd.memset(ss[:], 0.0), [], [B_ss])
                for tt in range(4):
                    for h in range(5):
                        A(lambda: nc.scalar.activation(out=junk[:], in_=hc[:, tt, h * 128:(h + 1) * 128], func=AF.Square,
                                                       accum_out=ss[:, tt * 5 + h:tt * 5 + h + 1]), [B_hc], [B_junk, B_ss])
                A(lambda: nc.scalar.activation(out=ss[:], in_=ss[:], func=AF.Ln, scale=1.0 / 128, bias=RMS_EPS), [B_ss], [B_ss])
                A(lambda: nc.scalar.activation(out=ss[:], in_=ss[:], func=AF.Exp, scale=-0.5), [B_ss], [B_ss])
                kstop("ca")
                for h in range(5):
                    for tt in range(4):
                        yt, ytb = ytmp_ring()
                        A(lambda: nc.scalar.activation(out=yt[:], in_=hc[:, tt, h * 128:(h + 1) * 128], func=AF.Identity, scale=ss[:, tt * 5 + h:tt * 5 + h + 1]),
                          [B_hc, B_ss], [ytb])
                        V(lambda: nc.vector.tensor_tensor(out=yt[:], in0=yt[:], in1=cnorm[:, l, :], op=ALU.mult), [ytb, B_c], [ytb])
                        yc, ycb = yct_ring()
                        V(lambda: nc.vector.tensor_tensor(out=yc[:], in0=yt[:], in1=sigoc[:, tt, h * 128:(h + 1) * 128], op=ALU.mult), [ytb, B_so], [ycb])
                        if KSTOP == "cb":
                            continue
                        ps, pb = ps_short()
                        mm(ps[:, 0:128], yc[:], id_bf[:], True, True, [ycb, B_c], [pb])
                        if KSTOP == "cc":
                            continue
                        A(lambda: nc.scalar.copy(out=ycat[:, 11 + h, tt * 128:(tt + 1) * 128], in_=ps[:, 0:128]), [pb], [B_y])
                fw.barrier()

            kstop("c4")
            kstop("cb")
            kstop("cc")
            with ExitStack() as sa:
                qta, B_qta = P.sb(sa, "qta", [128, 6, T], BF16)
                q1p, B_q1p = P.sb(sa, "q1p", [128, 5, T], BF16)
                q2p, B_q2p = P.sb(sa, "q2p", [128, 5, T], BF16)
                G(lambda: nc.gpsimd.memset(q1p[:], 0.0), [], [B_q1p])
                G(lambda: nc.gpsimd.memset(q2p[:], 0.0), [], [B_q2p])
                if g == "S":
                    q1r, B_q1r = P.sb(sa, "q1r", [128, 5, T], BF16)
                    q2r, B_q2r = P.sb(sa, "q2r", [128, 5, T], BF16)
                    G(lambda: nc.gpsimd.memset(q1r[:], 0.0), [], [B_q1r])
                    G(lambda: nc.gpsimd.memset(q2r[:], 0.0), [], [B_q2r])
                    sip = ExitStack()
                    kst_ring = P.ring(sip, "kst", 3, [128, T], BF16)
                    vst, B_vst = P.sb(sip, "vst", [128, 4, 1408], BF16)
                    rp_ring = P.ring(sip, "rpx", 2, [128, T], F32)
                    rp2_ring = P.ring(sip, "rpy", 2, [128, T], F32)
                else:
                    kta, B_kta = P.sb(sa, "kta", [128, 6, T], BF16)
                    ktb, B_ktb = P.sb(sa, "ktb", [128, 5, T], BF16)
                    vab, B_vab = P.sb(sa, "vab", [128, 4, 1408], BF16)
                    stg_ring = P.ring(sa, "stg", 3, [128, T], F32)

                def rope_apply(ps, pb, outs):
                    xf, xb = rp_ring()
                    A(lambda: nc.scalar.copy(out=xf[:], in_=ps[:]), [pb], [xb])
                    p2, pb2 = ps_short()
                    mm(p2[:], CM(C_PERM), xf[:], True, True, [B_c, xb], [pb2])
                    x2, x2b = rp2_ring()
                    V(lambda: nc.vector.tensor_tensor(out=x2[:], in0=p2[:], in1=rope[:, 1, :], op=ALU.mult), [pb2, B_c], [x2b])
                    V(lambda: nc.vector.tensor_tensor(out=xf[:], in0=xf[:], in1=rope[:, 0, :], op=ALU.mult), [xb, B_c], [xb])
                    for (rs, dst, db_) in outs:
                        V(lambda: nc.vector.tensor_tensor(out=dst[rs, :], in0=xf[rs, :], in1=x2[rs, :], op=ALU.add), [xb, x2b], [db_])

                abtiles = [(i * 512, 512) for i in range(8)] + [(4096, 128)]
                if "t" in KSKIP:
                    abtiles = abtiles[:int(KSKIP[KSKIP.index("t") + 1])]
                with extra_slots(2):
                    for (c0, ncol) in abtiles:
                        wv, wb = wload(w_in[l, :, c0:c0 + ncol], 16, ncol, key=("w_in", l, c0))
                        for q in range(ncol // 128):
                            k = (c0 + q * 128) // 128
                            fm = (k <= 11) or (18 <= k <= 27)
                            if not fm:
                                continue
                            ps, pb = ps_short()
                            for c in range(16):
                                mm(ps[:], wv[:, c, q * 128:(q + 1) * 128], h_sb[:, c, :], c == 0, c == 15, [wb, B_h], [pb])
                            if k <= 5:
                                A(lambda: nc.scalar.copy(out=qta[:, k, :], in_=ps[:]), [pb], [B_qta])
                            elif k <= 11:
                                hh = k - 6
                                if g == "P":
                                    A(lambda: nc.scalar.copy(out=kta[:, hh, :], in_=ps[:]), [pb], [B_kta])
                                    sg, sgb = stg_ring()
                                    A(lambda: nc.scalar.copy(out=sg[:], in_=ps[:]), [pb], [sgb])
                                    if "k" not in KSKIP:
                                        fw.dma("sp", o_ak[:, l, hh].rearrange("s d t -> d s t"), sg[:].rearrange("p (s t) -> p s t", s=2), reads=[sgb], writes=[B_out])
                                else:
                                    ks, ksb = kst_ring()
                                    A(lambda: nc.scalar.copy(out=ks[:], in_=ps[:]), [pb], [ksb])
                                    kr, krb = bnc_k_rows(hh)
                                    fw.dma("sp", kr, ks[:], reads=[ksb], writes=[krb])
                            elif k <= 22:
                                hh = k - 18
                                A(lambda: nc.scalar.copy(out=q1p[0:64, hh, :], in_=ps[0:64, :]), [pb], [B_q1p])
                                A(lambda: nc.scalar.copy(out=q2p[64:128, hh, :], in_=ps[64:128, :]), [pb], [B_q2p])
                                if g == "S":
                                    rope_apply(ps, pb, [(slice(0, 64), q1r[:, hh, :], B_q1r), (slice(64, 128), q2r[:, hh, :], B_q2r)])
                            else:
                                hh = k - 23
                                if g == "P":
                                    A(lambda: nc.scalar.copy(out=ktb[:, hh, :], in_=ps[:]), [pb], [B_ktb])
                                    sg, sgb = stg_ring()
                                    A(lambda: nc.scalar.copy(out=sg[:], in_=ps[:]), [pb], [sgb])
                                    if "k" not in KSKIP:
                                        fw.dma("sp", o_bk[:, l, hh].rearrange("s d t -> d s t"), sg[:].rearrange("p (s t) -> p s t", s=2), reads=[sgb], writes=[B_out])
                                else:
                                    ks, ksb = kst_ring()
                                    rope_apply(ps, pb, [(slice(0, 128), ks, ksb)])
                                    kr, krb = bnc_k_rows(6 + hh)
                                    fw.dma("sp", kr, ks[:], reads=[ksb], writes=[krb])
                        for (name, lo, hi, o_base) in (("va", 1536, 2304, 0), ("vb", 3584, 4224, 768)):
                            a, b_ = max(lo, c0), min(hi, c0 + ncol)
                            if a >= b_:
                                continue
                            n = b_ - a
                            o0 = o_base + a - lo
                            for tt in range(4):
                                ps, pb = ps_short()
                                for c in range(16):
                                    mm(ps[:, 0:n], h_sb[:, c, tt * 128:(tt + 1) * 128], wv[:, c, a - c0:b_ - c0], c == 0, c == 15, [wb, B_h], [pb])
                                if g == "P":
                                    A(lambda: nc.scalar.copy(out=vab[:, tt, o0:o0 + n], in_=ps[:, 0:n]), [pb], [B_vab])
                                    sg, sgb = stg_ring()
                                    A(lambda: nc.scalar.copy(out=sg[:, 0:n], in_=ps[:, 0:n]), [pb], [sgb])
                                    seq, s0_ = tt // 2, (tt % 2) * 128
                                    h0 = (a - lo) // 128
                                    nh = n // 128
                                    dst = (o_av if name == "va" else o_bv)[seq, l, h0:h0 + nh, s0_:s0_ + 128, :].rearrange("h s d -> s h d")
                                    if "v" not in KSKIP:
                                        fw.dma("sp", dst, sg[:, 0:n].rearrange("p (h d) -> p h d", d=128), reads=[sgb], writes=[B_out])
                                else:
                                    A(lambda: nc.scalar.copy(out=vst[:, tt, o0:o0 + n], in_=ps[:, 0:n]), [pb], [B_vst])

                for tc_ in (0, 1):
                    prefetch(("o", l, tc_), w_out[l][:, tc_ * 512:(tc_ + 1) * 512], 16, 512)
                kstop("c5")

                def alloc_attn_rings():
                    return (P.ring(sa, "et", 5, [128, T], BF16), P.ring(sa, "bt", 4, [128, T], F32),
                            P.ring(sa, "rd", 2, [128, T], F32), P.ring(sa, "ybt", 3, [128, T], F32))

                def fin_A(h):
                    def f(yps, yb, dps, db):
                        rd, rdb = rd_ring()
                        A(lambda: nc.scalar.activation(out=rd[:], in_=dps[:], func=AF.Ln), [db], [rdb])
                        A(lambda: nc.scalar.activation(out=rd[:], in_=rd[:], func=AF.Exp, scale=-1.0), [rdb], [rdb])
                        V(lambda: nc.vector.tensor_tensor(out=ycat[:, h, :], in0=yps[:], in1=rd[:], op=ALU.mult), [yb, rdb], [B_y])
                    return f

                def fin_B(dst, dstb):
                    def f(yps, yb, dps, db):
                        rd, rdb = rd_ring()
                        A(lambda: nc.scalar.activation(out=rd[:], in_=dps[:], func=AF.Ln), [db], [rdb])
                        A(lambda: nc.scalar.activation(out=rd[:], in_=rd[:], func=AF.Exp, scale=-1.0), [rdb], [rdb])
                        V(lambda: nc.vector.tensor_tensor(out=dst[:], in0=yps[:], in1=rd[:], op=ALU.mult), [yb, rdb], [dstb])
                    return f

                def diff_finish(h, y1, y1b, y2, y2b):
                    V(lambda: nc.vector.scalar_tensor_tensor(out=y1[:], in0=y2[:], scalar=nlam[:, l:l + 1], in1=y1[:], op0=ALU.mult, op1=ALU.add),
                      [y2b, y1b, B_nlam], [y1b])
                    A(lambda: nc.scalar.activation(out=y2[:], in_=y1[:], func=AF.Square), [y1b], [y2b])
                    sp_, spb = ps_short()
                    mm(sp_[:], CM(C_ONE), y2[:], True, True, [B_c, y2b], [spb])
                    A(lambda: nc.scalar.activation(out=y2[:], in_=sp_[:], func=AF.Ln, scale=1.0 / 128, bias=RMS_EPS), [spb], [y2b])
                    A(lambda: nc.scalar.activation(out=y2[:], in_=y2[:], func=AF.Exp, scale=-0.5), [y2b], [y2b])
                    V(lambda: nc.vector.tensor_tensor(out=y1[:], in0=y1[:], in1=y2[:], op=ALU.mult), [y1b, y2b], [y1b])
                    A(lambda: nc.scalar.activation(out=ycat[:, 6 + h, :], in_=y1[:], func=AF.Identity, scale=sublns[:, l:l + 1]), [y1b, B_nlam], [B_y])

                if g == "P":
                    et_ring, bt_ring, rd_ring, yb_ring = alloc_attn_rings()
                    for h in range(6):
                        groups = []
                        for kb in range(2):
                            grp = []
                            for s in range(2):
                                t0 = s * 256 + kb * 128
                                grp.append(dict(k=kta[:, h, t0:t0 + 128], kb=B_kta, q=qta[:, h, s * 256:(s + 1) * 256], qb=B_qta,
                                                v=vab[:, 2 * s + kb, h * 128:(h + 1) * 128], vb=B_vab, c0=s * 256, n=256))
                            groups.append(grp)
                        attention(groups, 128.0 ** -0.5, et_ring, bt_ring, fin_A(h))
                    for h in range(5):
                        ys = []
                        for (qp, qpb) in ((q1p, B_q1p), (q2p, B_q2p)):
                            groups = []
                            for kb in range(2):
                                grp = []
                                for s in range(2):
                                    t0 = s * 256 + kb * 128
                                    grp.append(dict(k=ktb[:, h, t0:t0 + 128], kb=B_ktb, q=qp[:, h, s * 256:(s + 1) * 256], qb=qpb,
                                                    v=vab[:, 2 * s + kb, 768 + h * 128:768 + (h + 1) * 128], vb=B_vab, c0=s * 256, n=256))
                                groups.append(grp)
                            yt, ytb = yb_ring()
                            attention(groups, 64.0 ** -0.5, et_ring, bt_ring, fin_B(yt, ytb))
                            ys.append((yt, ytb))
                        diff_finish(h, ys[0][0], ys[0][1], ys[1][0], ys[1][1])
                else:
                    for hh in range(11):
                        vr, vrb = bnc_v_rows(hh)
                        fw.dma("sp", vr.rearrange("p (tt d) -> p tt d", d=128),
                               vst[:, :, hh * 128:(hh + 1) * 128], reads=[B_vst], writes=[vrb])
                    for ci in range(3):
                        fw.allgather(bnc_in[ci], bnc_out[ci], reads=[B_bi[ci]], writes=[B_bo[ci]])
                    fw.barrier()
                    sip.close()
                    et_ring, bt_ring, rd_ring, yb_ring = alloc_attn_rings()
                    kall_ring = P.ring(sa, "kall", 2, [128, 4, T], BF16)
                    vall_ring = P.ring(sa, "vall", 2, [128, 4, T], BF16)
                    kctx_ring = P.ring(sa, "kctx", 2, [128, 512], BF16)
                    vctx_ring = P.ring(sa, "vctx", 2, [128, 4, 128], BF16)
                    nb_ring = P.ring(sa, "nbias", 5, [128, T], F32)
                    gviews = [bo.rearrange("(r x) t -> x r t", r=4) for bo in bnc_out]
                    for hh in range(11):
                        isA = hh < 6
                        h = hh if isA else hh - 6
                        ka, kab = kall_ring()
                        va_, vab_ = vall_ring()
                        kc_, kcb_ = kctx_ring()
                        vc_, vcb_ = vctx_ring()
                        gch = HH_CH[hh]
                        krow = HH_IX[hh] * 128
                        vrow = (CH_NH[gch] + HH_IX[hh]) * 128
                        fw.dma("sp", ka[:], gviews[gch][krow:krow + 128], reads=[B_bo[gch]], writes=[kab])
                        fw.dma("sp", va_[:], gviews[gch][vrow:vrow + 128], reads=[B_bo[gch]], writes=[vab_])
                        if isA:
                            fw.dma("pool", kc_[:], cakT[l, h], writes=[kcb_])
                            fw.dma("pool", vc_[:], cav[l, h].rearrange("(b p) d -> p b d", p=128), writes=[vcb_])
                        else:
                            fw.dma("pool", kc_[:], cbkT[l, h], writes=[kcb_])
                            fw.dma("pool", vc_[:], cbv[l, h].rearrange("(b p) d -> p b d", p=128), writes=[vcb_])

                        def mkgroups(q_lat, q_latb, q_ctx, q_ctxb):
                            groups = []
                            for kb in range(16):
                                r, t4 = kb // 4, kb % 4
                                sub = dict(k=ka[:, r, t4 * 128:(t4 + 1) * 128], kb=kab, q=q_lat, qb=q_latb,
                                           v=va_[:, r, t4 * 128:(t4 + 1) * 128], vb=vab_, c0=0, n=T)
                                if isA:
                                    def bias_fn(kb=kb, h=h):
                                        nbt, nbb = nb_ring()
                                        fw.dma("sp", nbt[:], nbias_d[l, h, kb], writes=[nbb])
                                        return nbt[:], nbb
                                    sub["bias_fn"] = bias_fn
                                groups.append([sub])
                            for kb in range(4):
                                groups.append([dict(k=kc_[:, kb * 128:(kb + 1) * 128], kb=kcb_, q=q_ctx, qb=q_ctxb,
                                                    v=vc_[:, kb, :], vb=vcb_, c0=0, n=T)])
                            return groups
                        if isA:
                            attention(mkgroups(qta[:, h, :], B_qta, qta[:, h, :], B_qta), 128.0 ** -0.5, et_ring, bt_ring, fin_A(h))
                        else:
                            ys = []
                            for (qr, qrb, qp, qpb) in ((q1r, B_q1r, q1p, B_q1p), (q2r, B_q2r, q2p, B_q2p)):
                                yt, ytb = yb_ring()
                                attention(mkgroups(qr[:, h, :], qrb, qp[:, h, :], qpb), 64.0 ** -0.5, et_ring, bt_ring, fin_B(yt, ytb))
                                ys.append((yt, ytb))
                            diff_finish(h, ys[0][0], ys[0][1], ys[1][0], ys[1][1])
                fw.barrier()

            kstop("c6")
            with ExitStack() as so:
                def pf_up():
                    prefetch(("up", l, 0), w_up[l, :, 0:512], 16, 512)
                    prefetch(("up", l, DFF), w_up[l, :, DFF:DFF + 512], 16, 512)
                residual_proj(l, w_out[l], 16, 512, lambda c: ycat[:, c, :], [B_y], 2, row, so, "o", after=pf_up)
                layernorm(l, 0, so)
                fw.barrier()
        kstop("c7")
        modulate(l, 3, 4, row)
        with ExitStack() as sf:
            actb, B_act = P.sb(sf, "actb", [128, 43, T], BF16)
            u_ring = P.ring(sf, "u1", 4, [128, T], F32)
            hal, B_hal = P.sb(sf, "hal", [128, 2, 2], F32)
            if g == "S":
                hbs, B_hbs = P.sb(sf, "hbs", [128, 16, 2], F32)
                hba, B_hba = P.sb(sf, "hba", [128, 4, 32], F32)
                hbf, B_hbf = P.sb(sf, "hbf", [128, 16, 2], F32)
                A(lambda: nc.scalar.copy(out=hbs[:, :, 0], in_=h_sb[:, :, 0]), [B_h], [B_hbs])
                A(lambda: nc.scalar.copy(out=hbs[:, :, 1], in_=h_sb[:, :, T - 1]), [B_h], [B_hbs])
                fw.dma("sp", hb_in, hbs[:].rearrange("p c k -> p (c k)"), reads=[B_hbs], writes=[B_hbi])
                fw.allgather(hb_in, hb_out, reads=[B_hbi], writes=[B_hbo])
                fw.dma("sp", hba[:], hb_out.rearrange("(r p) f -> p r f", p=128), reads=[B_hbo], writes=[B_hba])
                hv = hba[:].rearrange("p r (c k) -> p r c k", k=2)
                for (side, kk, so_) in ((0, 1, 0), (1, 0, 4)):
                    A(lambda: nc.scalar.activation(out=hbf[:, :, side], in_=hv[:, 0, :, kk], func=AF.Identity, scale=sel[:, so_:so_ + 1]), [B_hba, B_c], [B_hbf])
                    for r in range(1, 4):
                        V(lambda: nc.vector.scalar_tensor_tensor(out=hbf[:, :, side], in0=hv[:, r, :, kk], scalar=sel[:, so_ + r:so_ + r + 1],
                                                                 in1=hbf[:, :, side], op0=ALU.mult, op1=ALU.add), [B_hba, B_c, B_hbf], [B_hbf])
                A(lambda: nc.scalar.copy(out=hh_sb[:], in_=hbf[:]), [B_hbf], [B_hh])
            segs = [(0, 256), (256, 256)] if g == "P" else [(0, 512)]

            def conv_chunk(ps, pb, hps, hpb, ch):
                u, ub = u_ring()
                cp = convp[:, l, ch, :]
                A(lambda: nc.scalar.activation(out=u[:], in_=ps[:], func=AF.Identity, scale=cp[:, 1:2], bias=cp[:, 3:4]), [pb, B_c], [ub])
                for (s0_, n) in segs:
                    V(lambda: nc.vector.scalar_tensor_tensor(out=u[:, s0_ + 1:s0_ + n], in0=ps[:, s0_:s0_ + n - 1], scalar=cp[:, 0:1],
                                                             in1=u[:, s0_ + 1:s0_ + n], op0=ALU.mult, op1=ALU.add), [pb, B_c, ub], [ub])
                    V(lambda: nc.vector.scalar_tensor_tensor(out=u[:, s0_:s0_ + n - 1], in0=ps[:, s0_ + 1:s0_ + n], scalar=cp[:, 2:3],
                                                             in1=u[:, s0_:s0_ + n - 1], op0=ALU.mult, op1=ALU.add), [pb, B_c, ub], [ub])
                if g == "S":
                    V(lambda: nc.vector.scalar_tensor_tensor(out=u[:, 0:1], in0=hps[:, 0:1], scalar=cp[:, 0:1], in1=u[:, 0:1],
                                                             op0=ALU.mult, op1=ALU.add), [hpb, B_c, ub], [ub])
                    V(lambda: nc.vector.scalar_tensor_tensor(out=u[:, T - 1:T], in0=hps[:, 1:2], scalar=cp[:, 2:3], in1=u[:, T - 1:T],
                                                             op0=ALU.mult, op1=ALU.add), [hpb, B_c, ub], [ub])
                return u, ub

            with extra_slots(2):
                for ti in range(11):
                    ncol = 512 if ti < 10 else 384
                    wa, wab = wload(w_up[l, :, ti * 512:ti * 512 + ncol], 16, ncol, key=("up", l, ti * 512))
                    wg, wgb = wload(w_up[l, :, DFF + ti * 512:DFF + ti * 512 + ncol], 16, ncol, key=("up", l, DFF + ti * 512))
                    for q in range(ncol // 128):
                        j = ti * 4 + q
                        res = []
                        for (wv, wb, ch) in ((wa, wab, j), (wg, wgb, 43 + j)):
                            ps, pb = ps_short()
                            for c in range(16):
                                mm(ps[:], wv[:, c, q * 128:(q + 1) * 128], h_sb[:, c, :], c == 0, c == 15, [wb, B_h], [pb])
                            hps, hpb = None, None
                            if g == "S":
                                hps, hpb = ps_short()
                                for c in range(16):
                                    mm(hps[:, 0:2], wv[:, c, q * 128:(q + 1) * 128], hh_sb[:, c, :], c == 0, c == 15, [wb, B_hh], [hpb])
                            res.append(conv_chunk(ps, pb, hps, hpb, ch))
                        (ua, uab), (ug, ugb) = res
                        A(lambda: nc.scalar.activation(out=ug[:], in_=ug[:], func=AF.Silu), [ugb], [ugb])
                        V(lambda: nc.vector.tensor_tensor(out=actb[:, j, :], in0=ug[:], in1=ua[:], op=ALU.mult), [ugb, uab], [B_act])
            def pf_next():
                nl = l if g == "P" else l + 1
                if nl < L:
                    for c0_ in (4224, 4736):
                        prefetch(("w_in", nl, c0_), w_in[nl, :, c0_:c0_ + 512], 16, 512)
            residual_proj(l, w_down[l], 43, 128, lambda c: actb[:, c, :], [B_act], 5, row, sf, "d", after=pf_next)
            layernorm(l, 1, sf)
            fw.barrier()
        if l == L - 1:
            fw.dma("sp", yout[g].rearrange("(c p) t -> p c t", p=128), x_sb[:], reads=[B_x], writes=[B_out])
        else:
            fw.dma("sp", xspill[g].rearrange("(c p) t -> p c t", p=128), x_sb[:], reads=[B_x], writes=[B_spill[g]])

    nblk = 0
    try:
        for l in range(L):
            for g in ("P", "S"):
                if KSTOP.startswith("b") and nblk >= int(KSTOP[1]):
                    break
                if KSTOP.startswith("c") and nblk >= 1:
                    break
                block(l, g)
                nblk += 1
    except StopBuild:
        fw.barrier()
        fw.finish()
        return P
    fw.finish()
    P.es.close()
    fw.close()
    return P


def _consts():
    c = np.zeros((NCONST, 128, 128), np.float32)
    idx = np.arange(128)
    c[C_ID] = np.eye(128)
    c[C_ONE] = 1.0
    same = (idx[:, None] // 64) == (idx[None, :] // 64)
    c[C_TRIF] = (same & (idx[:, None] <= idx[None, :]))
    c[C_TRIB] = (same & (idx[:, None] >= idx[None, :]))
    for k, p in ((C_E0, 0), (C_E63, 63), (C_E64, 64), (C_E127, 127)):
        c[k][p, :] = 1.0
    dd = idx % 32
    partner = np.where(dd < 16, idx + 16, idx - 16)
    c[C_PERM][partner, idx] = 1.0
    negm = np.zeros((2, 128, 5, 128), np.float32)
    negm[0] = np.where(c[C_TRIF].T[:, None, :] > 0, 0.0, NEG)
    negm[1] = np.where(c[C_TRIB].T[:, None, :] > 0, 0.0, NEG)
    return (np.ascontiguousarray(c.transpose(1, 0, 2)).reshape(128, NCONST * 128),
            np.ascontiguousarray(negm.transpose(1, 0, 2, 3)).reshape(128, 2 * 640))


def _rope_tables(j):
    t = np.arange(T) + j * T
    rows = (t // 64).astype(np.float32)
    cols = (t % 64).astype(np.float32)
    freqs = (np.float32(10000.0) ** (-np.arange(0, 32, 2, dtype=np.float32) / np.float32(32))).astype(np.float32)
    p = np.arange(128)
    dd = p % 64
    idx = dd % 32
    f = idx % 16
    first = idx < 16
    pos = np.where((dd < 32)[:, None], rows[None, :], cols[None, :]).astype(np.float32)
    ang = (pos * freqs[f][:, None]).astype(np.float32)
    cos = np.cos(ang).astype(np.float32)
    sin = np.sin(ang).astype(np.float32)
    sins = np.where(first[:, None], -sin, sin).astype(np.float32)
    return np.concatenate([cos, sins], axis=1)


def _natten_bias(a_rpb, j):
    kt = np.arange(2048)
    krow, kcol = kt // 64, kt % 64
    qt = np.arange(T) + j * T
    qrow, qcol = qt // 64, qt % 64
    rs = np.clip(qrow - 4, 0, 24)
    cs = np.clip(qcol - 8, 0, 48)
    vr = (krow[:, None] >= rs[None, :]) & (krow[:, None] < rs[None, :] + 8)
    vcm = (kcol[:, None] >= cs[None, :]) & (kcol[:, None] < cs[None, :] + 16)
    valid = vr & vcm
    ri = np.clip(7 + krow[:, None] - qrow[None, :], 0, 14)
    ci = np.clip(kcol[:, None] - qcol[None, :] + 15, 0, 30)
    out = np.empty((L, 6, 2048, T), np.float32)
    for l in range(L):
        for h in range(6):
            out[l, h] = np.where(valid, a_rpb[l, h][ri, ci], np.float32(NEG))
    return out.reshape(L, 6, 16, 128, T)


def _combine_tables(j):
    cf = np.zeros((2, 5, 4), np.float32)
    vt = np.zeros((2, 5), np.float32)
    for r2 in range(4):
        if r2 < j:
            cf[0, 0, r2] = 1
        if r2 > j:
            cf[1, 0, r2] = 1
    for r in range(4):
        vt[0, 1 + r] = 0.0 if r < j else NEG
        vt[1, 1 + r] = 0.0 if r > j else NEG
        for r2 in range(4):
            if r < r2 < j:
                cf[0, 1 + r, r2] = 1
            if j < r2 < r:
                cf[1, 1 + r, r2] = 1
    cft = np.broadcast_to(cf[None, :, :, :, None], (128, 2, 5, 4, 5)).reshape(128, -1)
    vtt = np.broadcast_to(vt[None, :, :, None], (128, 2, 5, 5)).reshape(128, -1)
    sel = np.zeros((8,), np.float32)
    if j > 0:
        sel[j - 1] = 1
    if j < 3:
        sel[4 + j + 1] = 1
    return np.ascontiguousarray(cft), np.ascontiguousarray(vtt), np.ascontiguousarray(np.broadcast_to(sel[None], (128, 8)))


_PROG = None


def kernel(x_prompt, x_sample, cache_a_k, cache_a_v, cache_b_k, cache_b_v, state_c_C, state_c_n, state_c_m,
           c, c_ctx, w_mod, b_mod, w_in, c_gate_b, a_rpb, b_lambda, b_subln, c_norm, w_out,
           ln1_g, ln1_b, ln2_g, ln2_b, w_up, conv_w, conv_b, w_down):
    global _PROG
    f = lambda a: np.ascontiguousarray(np.asarray(a, dtype=np.float32))
    x_prompt, x_sample = f(x_prompt), f(x_sample)
    if _PROG is None:
        _PROG = build_program()
    P = _PROG
    consts, negm = _consts()
    lnp = np.stack([f(ln1_g), f(ln1_b), f(ln2_g), f(ln2_b)], 1).reshape(L, 4, 16, 128).transpose(3, 0, 1, 2).reshape(128, -1)
    cw = np.concatenate([f(conv_w), f(conv_b)[:, None, :]], 1)
    convp = cw.reshape(L, 4, 86, 128).transpose(3, 0, 2, 1).reshape(128, -1)
    shared = {
        "w_in": f(w_in), "w_out": f(w_out), "w_up": f(w_up), "w_down": f(w_down),
        "lnp": np.ascontiguousarray(lnp), "convp": np.ascontiguousarray(convp),
        "gateb": f(c_gate_b).reshape(-1), "blam": f(b_lambda).reshape(-1),
        "subln": np.ascontiguousarray(f(b_subln).T), "cnorm": f(c_norm).reshape(-1),
        "consts": consts, "negm": negm,
    }
    w_mod, b_mod = f(w_mod), f(b_mod)
    in_maps = []
    for i in range(8):
        b, j = i // 4, i % 4
        m = dict(shared)
        m["xpT"] = np.ascontiguousarray(x_prompt[2 * i:2 * i + 2].reshape(T, D).T)
        m["xsT"] = np.ascontiguousarray(x_sample[b, j * T:(j + 1) * T].T)
        m["wmod"] = np.ascontiguousarray(w_mod[:, :, j * 3072:(j + 1) * 3072])
        m["bmod"] = np.ascontiguousarray(b_mod[:, j * 3072:(j + 1) * 3072].reshape(L, 24, 128).transpose(2, 0, 1).reshape(128, -1))
        cond = np.stack([f(c_ctx), f(c)[b]], 1)
        m["cond2"] = np.ascontiguousarray(cond.reshape(16, 128, 2).transpose(1, 0, 2).reshape(128, 32))
        m["rope"] = _rope_tables(j)
        m["nbias"] = _natten_bias(f(a_rpb), j)
        m["cakT"] = np.ascontiguousarray(f(cache_a_k)[b].transpose(0, 1, 3, 2))
        m["cav"] = f(cache_a_v)[b]
        m["cbkT"] = np.ascontiguousarray(f(cache_b_k)[b].transpose(0, 1, 3, 2))
        m["cbv"] = f(cache_b_v)[b]
        m["cC"] = f(state_c_C)[b]
        m["cn"] = np.ascontiguousarray(f(state_c_n)[b].reshape(L, 10, 128).transpose(0, 2, 1))
        m["cm"] = f(state_c_m)[b].reshape(-1)
        m["cftab"], m["vtab"], m["sel"] = _combine_tables(j)
        in_maps.append(m)
    in_maps = [{k: v for k, v in m.items() if k in P.din} for m in in_maps]
    res = run_bass_kernel_spmd(P.nc, in_maps, core_ids=list(range(8))).results
    yp = np.stack([r["ypT"].T.reshape(2, 256, D) for r in res], 0).reshape(16, 256, D)
    ys = np.stack([r["ysT"].T for r in res], 0).reshape(2, 4 * T, D)
    cat = lambda k: np.concatenate([r[k] for r in res], 0)
    n_ak = np.ascontiguousarray(cat("o_ak").transpose(0, 1, 2, 4, 3))
    n_av = cat("o_av")
    n_bk = np.ascontiguousarray(cat("o_bk").transpose(0, 1, 2, 4, 3))
    n_bv = cat("o_bv")
    return (np.ascontiguousarray(yp, dtype=np.float32), np.ascontiguousarray(ys, dtype=np.float32), n_ak, n_av, n_bk, n_bv,
            cat("o_cC"), np.ascontiguousarray(cat("o_cn").transpose(0, 1, 2, 4, 3)), cat("o_cm"))
```

```python
import math
import os
KSTOP = os.environ.get('KSTOP', '')
KSKIP = os.environ.get('KSKIP', '')
SKIP_IN = set()
if KSTOP.startswith('c'):
    SKIP_IN = {"nbias", "cakT", "cav", "cbkT", "cbv", "cC", "cn", "cm", "xsT"}
    if not KSTOP[1].isdigit() or int(KSTOP[1]) < 8:
        SKIP_IN |= {"w_up", "w_down"}
    if not KSTOP[1].isdigit() or int(KSTOP[1]) < 7:
        SKIP_IN |= {"w_out"}


class StopBuild(Exception):
    pass


def kstop(tag):
    if KSTOP == tag:
        raise StopBuild(tag)
from contextlib import ExitStack
import numpy as np
import ml_dtypes
import concourse.bass as bass
import concourse.mybir as mybir
from concourse.bass_utils import run_bass_kernel_spmd

F32 = mybir.dt.float32
BF16 = mybir.dt.bfloat16
AF = mybir.ActivationFunctionType
ALU = mybir.AluOpType
AX = mybir.AxisListType

L = 2
D = 2048
T = 512
DFF = 5504
NEG = -30000.0
ALPHA = (2 * L) ** 0.25
LN_EPS = 1e-5
RMS_EPS = 1e-6
RG = [[0, 1, 2, 3], [4, 5, 6, 7]]
NCONST = 11
C_ID, C_ONE, C_TRIF, C_TRIB, C_E0, C_E63, C_E64, C_E127, C_PERM, C_HA, C_HB = range(NCONST)


class Buf:
    __slots__ = ("name", "w", "r")

    def __init__(self, name=""):
        self.name = name
        self.w = None
        self.r = []


class FW:
    NDMA = 48

    def __init__(self, nc):
        self.nc = nc
        self.eng = {"pe": nc.tensor, "act": nc.scalar, "dve": nc.vector, "pool": nc.gpsimd, "sp": nc.sync}
        self.sem, self.cnt, self._stack = {}, {}, []
        for e in self.eng:
            cm = nc.semaphore("s_" + e)
            self.sem[e] = cm.__enter__()
            self._stack.append(cm)
            self.cnt[e] = 0
        self.dsem, self.dcnt = [], []
        for i in range(self.NDMA):
            cm = nc.semaphore("d_%d" % i)
            self.dsem.append(cm.__enter__())
            self._stack.append(cm)
            self.dcnt.append(0)
        self.dnext = 0
        self.seen = {e: {} for e in self.eng}
        self.ninst = 0
        self.nwaits = 0

    def close(self):
        for cm in reversed(self._stack):
            cm.__exit__(None, None, None)

    def _semobj(self, key):
        return self.sem[key] if isinstance(key, str) else self.dsem[key]

    def _wait(self, e, tok):
        if tok is None:
            return
        key, val = tok
        if self.seen[e].get(key, 0) >= val:
            return
        self.eng[e].wait_ge(self._semobj(key), val)
        self.seen[e][key] = val
        self.nwaits += 1

    def _deps(self, e, reads, writes, is_dma=False):
        for b in reads:
            if b.w is not None:
                if b.w[0] == e and (e == "pe" or is_dma):
                    continue
                self._wait(e, b.w)
        skip_same = (e == "pe" or is_dma)
        for b in writes:
            if b.w is not None and not (skip_same and b.w[0] == e):
                self._wait(e, b.w)
            for t in b.r:
                if not (skip_same and t[0] == e):
                    self._wait(e, t)

    def _commit(self, tok, reads, writes):
        for b in reads:
            b.r.append(tok)
            if len(b.r) > 16:
                best = {}
                for k, v in b.r:
                    if best.get(k, 0) < v:
                        best[k] = v
                b.r = list(best.items())
        for b in writes:
            b.w = tok
            b.r = []

    def op(self, e, fn, reads=(), writes=(), inc=True):
        self._deps(e, reads, writes)
        ins = fn()
        self.ninst += 1
        if inc:
            self.cnt[e] += 1
            ins.then_inc(self.sem[e], 1)
            tok = (e, self.cnt[e])
        else:
            tok = (e, self.cnt[e] + 1)
        self._commit(tok, reads, writes)
        return tok

    def _next_dsem(self, q, kind=None):
        kind = kind or q
        lo, hi = {"sp": (0, 32), "pool": (32, 44), "cc": (44, 48)}[kind]
        if not hasattr(self, "dnx"):
            self.dnx = {}
        i = self.dnx.get(kind, lo)
        self.dnx[kind] = lo + (i + 1 - lo) % (hi - lo)
        if self.dcnt[i] > 0:
            self._wait(q, (i, self.dcnt[i]))
        return i

    def dma(self, q, out, in_, reads=(), writes=(), slow=False):
        self._deps(q, reads, writes, is_dma=True)
        i = self._next_dsem(q)
        if slow:
            ins = self.eng[q].dma_start(out=out, in_=in_, allow_slow_non_contiguous=True)
        else:
            ins = self.eng[q].dma_start(out=out, in_=in_)
        self.dcnt[i] += 16
        ins.then_inc(self.dsem[i], 16)
        self.ninst += 1
        tok = (i, self.dcnt[i])
        self._commit(tok, reads, writes)
        return tok

    def allgather(self, in_ap, out_ap, reads=(), writes=()):
        q = "pool"
        self._deps(q, reads, writes, is_dma=True)
        i = self._next_dsem(q, "cc")
        ins = self.nc.gpsimd.collective_compute("AllGather", ALU.bypass, replica_groups=RG, ins=[in_ap], outs=[out_ap])
        self.dcnt[i] += 1
        ins.then_inc(self.dsem[i], 1)
        self.ninst += 1
        tok = (i, self.dcnt[i])
        self._commit(tok, reads, writes)
        return tok

    def soft_barrier(self):
        for e in self.eng:
            if e != "pe" and self.cnt["pe"] > 0:
                self._wait(e, ("pe", self.cnt["pe"]))
            for i in range(32, 44):
                if self.dcnt[i] > 0:
                    self._wait(e, (i, self.dcnt[i]))

    def barrier(self):
        for e in self.eng:
            for f in self.eng:
                if f != e and self.cnt[f] > 0:
                    self._wait(e, (f, self.cnt[f]))
            for i in range(self.NDMA):
                if self.dcnt[i] > 0:
                    self._wait(e, (i, self.dcnt[i]))

    def finish(self):
        for i in range(self.NDMA):
            if self.dcnt[i] > 0:
                self._wait("sp", (i, self.dcnt[i]))


class Prog:
    def __init__(self):
        nc = bass.Bass("TRN2", target_bir_lowering=False)
        self.nc = nc
        self.fw = FW(nc)
        self.es = ExitStack()
        self.ring_idx = {}
        self.din = {}
        self.dout = {}

    def inp(self, name, shape, dt=F32):
        if name in SKIP_IN:
            return None
        t = self.nc.dram_tensor(name, list(shape), dt, kind="ExternalInput").ap()
        self.din[name] = t
        return t

    def outp(self, name, shape, dt=F32):
        t = self.nc.dram_tensor(name, list(shape), dt, kind="ExternalOutput").ap()
        self.dout[name] = t
        return t

    def scratch(self, name, shape, dt=F32):
        return self.nc.dram_tensor(name, list(shape), dt).ap()

    def sb(self, es, name, shape, dt=F32):
        self.uid = getattr(self, "uid", 0) + 1
        t = es.enter_context(self.nc.sbuf_tensor("%s_%d" % (name, self.uid), list(shape), dt))
        return t, Buf(name)

    def ring(self, es, name, n, shape, dt=F32):
        items = [self.sb(es, "%s%d" % (name, i), shape, dt) for i in range(n)]
        key = name
        self.ring_idx[key] = 0

        def nxt():
            i = self.ring_idx[key]
            self.ring_idx[key] = (i + 1) % n
            return items[i]
        return nxt

    def V(self, fn, reads=(), writes=()):
        return self.fw.op("dve", fn, reads, writes)

    def A(self, fn, reads=(), writes=()):
        return self.fw.op("act", fn, reads, writes)

    def G(self, fn, reads=(), writes=()):
        return self.fw.op("pool", fn, reads, writes)

    def PE(self, fn, reads=(), writes=(), inc=True):
        return self.fw.op("pe", fn, reads, writes, inc=inc)

    def mm(self, out, lhsT, rhs, start, stop, reads, writes, inc=None, sgc=False):
        nc = self.nc
        return self.fw.op("pe", lambda: nc.tensor.matmul(out, lhsT=lhsT, rhs=rhs, start=start, stop=stop, skip_group_check=sgc),
                          reads, writes, inc=(stop if inc is None else inc))


def build_program():
    P = Prog()
    nc, fw = P.nc, P.fw
    V, A, PE, mm, G = P.V, P.A, P.PE, P.mm, P.G

    xin = {"P": P.inp("xpT", [D, T]), "S": P.inp("xsT", [D, T])}
    w_in = P.inp("w_in", [L, D, 6804])
    w_out = P.inp("w_out", [L, D, D])
    w_up = P.inp("w_up", [L, D, 2 * DFF])
    w_down = P.inp("w_down", [L, DFF, D])
    wmod = P.inp("wmod", [L, D, 3072])
    bmod = P.inp("bmod", [128, L * 24])
    cond2 = P.inp("cond2", [128, 32])
    lnp_d = P.inp("lnp", [128, L * 4 * 16])
    convp_d = P.inp("convp", [128, L * 86 * 4])
    gateb_d = P.inp("gateb", [L * 20])
    blam_d = P.inp("blam", [L * 256])
    subln_d = P.inp("subln", [128, L])
    cnorm_d = P.inp("cnorm", [L * 128])
    consts_d = P.inp("consts", [128, NCONST * 128])
    negm_d = P.inp("negm", [128, 2 * 640])
    rope_d = P.inp("rope", [128, 2 * T])
    nbias_d = P.inp("nbias", [L, 6, 16, 128, T])
    cakT = P.inp("cakT", [L, 6, 128, 512])
    cav = P.inp("cav", [L, 6, 512, 128])
    cbkT = P.inp("cbkT", [L, 5, 128, 512])
    cbv = P.inp("cbv", [L, 5, 512, 128])
    cC_d = P.inp("cC", [L, 2, 5, 128, 128])
    cn_d = P.inp("cn", [L, 128, 10])
    cm_d = P.inp("cm", [L * 10])
    cftab_d = P.inp("cftab", [128, 2 * 5 * 4 * 5])
    vtab_d = P.inp("vtab", [128, 2 * 5 * 5])
    sel_d = P.inp("sel", [128, 8])

    yout = {"P": P.outp("ypT", [D, T]), "S": P.outp("ysT", [D, T])}
    o_ak = P.outp("o_ak", [2, L, 6, 128, 256])
    o_av = P.outp("o_av", [2, L, 6, 256, 128])
    o_bk = P.outp("o_bk", [2, L, 5, 128, 256])
    o_bv = P.outp("o_bv", [2, L, 5, 256, 128])
    o_cC = P.outp("o_cC", [2, L, 2, 5, 128, 128])
    o_cn = P.outp("o_cn", [2, L, 2, 128, 5])
    o_cm = P.outp("o_cm", [2, L, 2, 5])
    B_out = Buf("outputs")

    xspill = {"P": P.scratch("xspP", [D, T]), "S": P.scratch("xspS", [D, T])}
    B_spill = {"P": Buf(), "S": Buf()}
    mg_in = P.scratch("mg_in", [128, 96]); mg_out = P.scratch("mg_out", [512, 96])
    B_mgi, B_mgo = Buf(), Buf()
    CH_NH = [4, 4, 3]
    HH_CH = [0, 0, 0, 0, 1, 1, 1, 1, 2, 2, 2]
    HH_IX = [0, 1, 2, 3, 0, 1, 2, 3, 0, 1, 2]
    bnc_in = [P.scratch("bnc_in%d" % i, [2 * n * 128, T], BF16) for i, n in enumerate(CH_NH)]
    bnc_out = [P.scratch("bnc_out%d" % i, [4 * 2 * n * 128, T], BF16) for i, n in enumerate(CH_NH)]
    B_bi = [Buf() for _ in CH_NH]
    B_bo = [Buf() for _ in CH_NH]

    def bnc_k_rows(hh):
        i = HH_IX[hh]
        return bnc_in[HH_CH[hh]][i * 128:(i + 1) * 128, :], B_bi[HH_CH[hh]]

    def bnc_v_rows(hh):
        c = HH_CH[hh]
        i = CH_NH[c] + HH_IX[hh]
        return bnc_in[c][i * 128:(i + 1) * 128, :], B_bi[c]
    CSF = 1310
    cs_in = P.scratch("cs_in", [128, CSF]); cs_out = P.scratch("cs_out", [512, CSF])
    B_csi, B_cso = Buf(), Buf()
    hb_in = P.scratch("hb_in", [128, 32]); hb_out = P.scratch("hb_out", [512, 32])
    B_hbi, B_hbo = Buf(), Buf()

    es = P.es
    x_sb, B_x = P.sb(es, "x_sb", [128, 16, T], F32)
    h_sb, B_h = P.sb(es, "h_sb", [128, 16, T], BF16)
    hh_sb, B_hh = P.sb(es, "hh_sb", [128, 16, 2], BF16)
    WSL = 8704
    wslots = [P.sb(es, "wr%d" % i, [128, WSL], BF16) for i in range(2)]
    wstate = {"i": 0}

    def wring():
        i = wstate["i"] % len(wslots)
        wstate["i"] += 1
        return wslots[i]

    class extra_slots:
        def __init__(self, want, reserve=2048):
            self.want, self.reserve = want, reserve

        def __enter__(self):
            self.sx = ExitStack()
            self.n = 0
            while self.n < self.want and nc.sbuf_bytes_remaining >= WSL * 2 + self.reserve + 256:
                wslots.append(P.sb(self.sx, "wx", [128, WSL], BF16))
                self.n += 1
            return self

        def __exit__(self, *a):
            if a[0] is None:
                fw.soft_barrier()
                for _ in range(self.n):
                    wslots.pop()
                self.sx.close()
            return False
    cst, B_c = P.sb(es, "cst", [128, NCONST, 128], F32)
    negm, _ = P.sb(es, "negm", [128, 2, 640], F32)
    rope, _ = P.sb(es, "rope", [128, 2, T], F32)
    ones_bf, _ = P.sb(es, "ones_bf", [128, 128], BF16)
    id_bf, _ = P.sb(es, "id_bf", [128, 128], BF16)
    tri_bf, _ = P.sb(es, "tri_bf", [128, 2, 128], BF16)
    lnp, _ = P.sb(es, "lnp", [128, L, 4, 16], F32)
    convp, _ = P.sb(es, "convp", [128, L, 86, 4], F32)
    gateb, _ = P.sb(es, "gateb", [128, L, 20], F32)
    blam, _ = P.sb(es, "blam", [128, L, 4, 64], F32)
    subln, _ = P.sb(es, "subln", [128, L], F32)
    cnorm, _ = P.sb(es, "cnorm", [128, L, 128], F32)
    cftab, _ = P.sb(es, "cftab", [128, 2, 5, 4, 5], F32)
    vtab, _ = P.sb(es, "vtab", [128, 2, 5, 5], F32)
    sel, _ = P.sb(es, "sel", [128, 8], F32)
    modv, B_modv = P.sb(es, "modv", [128, L, 96, 2], F32)
    nlam, B_nlam = P.sb(es, "nlam", [128, L], F32)
    sublns, _ = P.sb(es, "sublns", [128, L], F32)

    def CM(i):
        return cst[:, i, :]

    pbanks = [es.enter_context(nc.psum_tensor("ps%d" % i, [128, 512], F32)) for i in range(8)]
    pbufs = [Buf("ps%d" % i) for i in range(8)]
    pidx = {"s": 0, "l": 0}

    def ps_short():
        i = pidx["s"]
        pidx["s"] = (i + 1) % 5
        return pbanks[i], pbufs[i]

    def ps_long():
        i = pidx["l"]
        pidx["l"] = (i + 1) % 3
        return pbanks[5 + i], pbufs[5 + i]

    def bcast(ap1d, n):
        return bass.AP(ap1d.tensor, 0, [[0, 128], [1, n]])

    fw.dma("sp", cst[:], consts_d.rearrange("p (k n) -> p k n", k=NCONST), writes=[B_c])
    fw.dma("sp", negm[:], negm_d.rearrange("p (k n) -> p k n", k=2), writes=[B_c])
    fw.dma("sp", rope[:], rope_d.rearrange("p (k n) -> p k n", k=2), writes=[B_c])
    fw.dma("sp", lnp[:], lnp_d.rearrange("p (l k c) -> p l k c", l=L, k=4), writes=[B_c])
    fw.dma("sp", convp[:], convp_d.rearrange("p (l c k) -> p l c k", l=L, k=4), writes=[B_c])
    fw.dma("sp", gateb[:], bcast(gateb_d, L * 20).rearrange("p (l k) -> p l k", l=L), writes=[B_c])
    fw.dma("sp", blam[:], bcast(blam_d, L * 256).rearrange("p (l k c) -> p l k c", l=L, k=4), writes=[B_c])
    fw.dma("sp", subln[:], subln_d, writes=[B_c])
    fw.dma("sp", cnorm[:], bcast(cnorm_d, L * 128).rearrange("p (l k) -> p l k", l=L), writes=[B_c])
    fw.dma("sp", cftab[:], cftab_d.rearrange("p (d i r h) -> p d i r h", d=2, i=5, r=4), writes=[B_c])
    fw.dma("sp", vtab[:], vtab_d.rearrange("p (d i h) -> p d i h", d=2, i=5), writes=[B_c])
    fw.dma("sp", sel[:], sel_d, writes=[B_c])
    A(lambda: nc.scalar.copy(out=ones_bf[:], in_=CM(C_ONE)), [B_c], [B_c])
    A(lambda: nc.scalar.copy(out=id_bf[:], in_=CM(C_ID)), [B_c], [B_c])
    A(lambda: nc.scalar.copy(out=tri_bf[:, 0, :], in_=CM(C_TRIF)), [B_c], [B_c])
    A(lambda: nc.scalar.copy(out=tri_bf[:, 1, :], in_=CM(C_TRIB)), [B_c], [B_c])

    pre = {}

    def wload(src2d, kc, ncols, key=None):
        if key is not None and key in pre:
            return pre.pop(key)
        return _wload(src2d, kc, ncols)

    def prefetch(key, src2d, kc, ncols):
        pre[key] = _wload(src2d, kc, ncols)

    def _wload(src2d, kc, ncols):
        t, b = wring()
        view = t[:, 0:kc * ncols].rearrange("p (c n) -> p c n", n=ncols)
        srcv = src2d.rearrange("(c p) n -> p c n", p=128)
        step = max(1, 2048 // 128 // 1 if ncols >= 256 else 8)
        step = 16 if ncols >= 256 else 22
        for c0 in range(0, kc, step):
            c1 = min(kc, c0 + step)
            fw.dma("pool", view[:, c0:c1, :], srcv[:, c0:c1, :], writes=[b])
        return view, b

    with ExitStack() as s0:
        c2, B_c2 = P.sb(s0, "c2", [128, 16, 2], F32)
        c2b, _ = P.sb(s0, "c2b", [128, 16, 2], BF16)
        bm, B_bm = P.sb(s0, "bm", [128, L, 24], F32)
        mloc, B_ml = P.sb(s0, "mloc", [128, L, 24, 2], F32)
        mall, B_ma = P.sb(s0, "mall", [128, 4, L, 24, 2], F32)
        fw.dma("sp", c2[:], cond2.rearrange("p (c r) -> p c r", r=2), writes=[B_c2])
        fw.dma("sp", bm[:], bmod.rearrange("p (l c) -> p l c", l=L), writes=[B_bm])
        A(lambda: nc.scalar.activation(out=c2b[:], in_=c2[:], func=AF.Silu), [B_c2], [B_c2])
        with extra_slots(3):
            for l in range(L):
                for t4 in range(6):
                    wv, wb = wload(wmod[l, :, t4 * 512:(t4 + 1) * 512], 16, 512)
                    for q in range(4):
                        cc = t4 * 4 + q
                        ps, pb = ps_short()
                        for c in range(16):
                            mm(ps[:, 0:2], wv[:, c, q * 128:(q + 1) * 128], c2b[:, c, :], c == 0, c == 15, [wb, B_c2], [pb])
                        A(lambda: nc.scalar.activation(out=mloc[:, l, cc, :], in_=ps[:, 0:2], func=AF.Identity,
                                                       bias=bm[:, l, cc:cc + 1], scale=1.0), [pb, B_bm], [B_ml])
        fw.dma("sp", mg_in, mloc[:].rearrange("p l c r -> p (l c r)"), reads=[B_ml], writes=[B_mgi])
        fw.allgather(mg_in, mg_out, reads=[B_mgi], writes=[B_mgo])
        fw.dma("sp", mall[:].rearrange("p r l c w -> p r (l c w)"), mg_out.rearrange("(r p) f -> p r f", p=128),
               reads=[B_mgo], writes=[B_ma])
        for r in range(4):
            for l in range(L):
                A(lambda: nc.scalar.copy(out=modv[:, l, r * 24:(r + 1) * 24, :], in_=mall[:, r, l, :, :]), [B_ma], [B_modv])
        for l in range(L):
            for v0 in (16, 64):
                A(lambda: nc.scalar.activation(out=modv[:, l, v0:v0 + 16, :], in_=modv[:, l, v0:v0 + 16, :], func=AF.Identity, bias=1.0, scale=1.0),
                  [B_modv], [B_modv])
        lt, B_lt = P.sb(s0, "lt", [128, 64], F32)
        ld, B_ld = P.sb(s0, "ld", [128, 4], F32)
        for l in range(L):
            lam_init = 0.8 - 0.6 * math.exp(-0.3 * l)
            for k in range(2):
                V(lambda: nc.vector.tensor_tensor(out=lt[:], in0=blam[:, l, 2 * k, :], in1=blam[:, l, 2 * k + 1, :], op=ALU.mult), [B_c], [B_lt])
                V(lambda: nc.vector.reduce_sum(out=ld[:, k:k + 1], in_=lt[:], axis=AX.X), [B_lt], [B_ld])
            A(lambda: nc.scalar.activation(out=ld[:, 2:4], in_=ld[:, 0:2], func=AF.Exp), [B_ld], [B_ld])
            V(lambda: nc.vector.tensor_tensor(out=nlam[:, l:l + 1], in0=ld[:, 3:4], in1=ld[:, 2:3], op=ALU.subtract), [B_ld], [B_nlam])
            A(lambda: nc.scalar.activation(out=nlam[:, l:l + 1], in_=nlam[:, l:l + 1], func=AF.Identity, bias=-lam_init, scale=1.0), [B_nlam], [B_nlam])
            A(lambda: nc.scalar.activation(out=sublns[:, l:l + 1], in_=subln[:, l:l + 1], func=AF.Identity, scale=1.0 - lam_init), [B_c], [B_nlam])
        fw.barrier()

    def modp(l, v, fc, row):
        return modv[:, l, v * 16 + fc, row:row + 1]

    def modulate(l, vsh, vsc, row):
        for c in range(16):
            A(lambda: nc.scalar.activation(out=h_sb[:, c, :], in_=x_sb[:, c, :], func=AF.Identity,
                                           bias=modp(l, vsh, c, row), scale=modp(l, vsc, c, row)), [B_x, B_modv], [B_h])

    def layernorm(l, k, scope):
        sq_ring = P.ring(scope, "lnsq%d" % k, 2, [128, T], F32)
        st, B_st = P.sb(scope, "lnst%d" % k, [128, 2, T], F32)
        p1, b1 = ps_long()
        p2, b2 = ps_long()
        for c in range(16):
            sq, bq = sq_ring()
            A(lambda: nc.scalar.activation(out=sq[:], in_=x_sb[:, c, :], func=AF.Square), [B_x], [bq])
            mm(p1[:], CM(C_ONE), x_sb[:, c, :], c == 0, c == 15, [B_c, B_x], [b1], inc=True)
            mm(p2[:], CM(C_ONE), sq[:], c == 0, c == 15, [B_c, bq], [b2], inc=True)
        mean, var = st[:, 0, :], st[:, 1, :]
        A(lambda: nc.scalar.activation(out=mean, in_=p1[:], func=AF.Identity, scale=1.0 / D), [b1], [B_st])
        V(lambda: nc.vector.tensor_tensor(out=var, in0=mean, in1=mean, op=ALU.mult), [B_st], [B_st])
        V(lambda: nc.vector.scalar_tensor_tensor(out=var, in0=p2[:], scalar=1.0 / D, in1=var, op0=ALU.mult, op1=ALU.subtract), [b2, B_st], [B_st])
        A(lambda: nc.scalar.activation(out=var, in_=var, func=AF.Ln, bias=LN_EPS, scale=1.0), [B_st], [B_st])
        A(lambda: nc.scalar.activation(out=var, in_=var, func=AF.Exp, scale=-0.5), [B_st], [B_st])
        for c in range(16):
            V(lambda: nc.vector.tensor_tensor(out=x_sb[:, c, :], in0=x_sb[:, c, :], in1=mean, op=ALU.subtract), [B_x, B_st], [B_x])
            V(lambda: nc.vector.tensor_tensor(out=x_sb[:, c, :], in0=x_sb[:, c, :], in1=var, op=ALU.mult), [B_x, B_st], [B_x])
            A(lambda: nc.scalar.activation(out=x_sb[:, c, :], in_=x_sb[:, c, :], func=AF.Identity,
                                           bias=lnp[:, l, 2 * k + 1, c:c + 1], scale=lnp[:, l, 2 * k, c:c + 1]), [B_x, B_c], [B_x])

    def residual_proj(l, wsrc, kc, ncols_tile, rhs_fn, rhs_bufs, vgate, row, scope, tag, after=None):
        tmp_ring = P.ring(scope, "rp" + tag, 2, [128, T], F32)
        per = ncols_tile // 128
        with extra_slots(3):
            for tcol in range(D // ncols_tile):
                wv, wb = wload(wsrc[:, tcol * ncols_tile:(tcol + 1) * ncols_tile], kc, ncols_tile, key=(tag, l, tcol))
                for q in range(per):
                    fc = tcol * per + q
                    ps, pb = ps_short()
                    for c in range(kc):
                        mm(ps[:], wv[:, c, q * 128:(q + 1) * 128], rhs_fn(c), c == 0, c == kc - 1, [wb] + rhs_bufs, [pb])
                    tmp, tb = tmp_ring()
                    A(lambda: nc.scalar.activation(out=tmp[:], in_=ps[:], func=AF.Identity, scale=modp(l, vgate, fc, row)), [pb, B_modv], [tb])
                    V(lambda: nc.vector.scalar_tensor_tensor(out=x_sb[:, fc, :], in0=x_sb[:, fc, :], scalar=ALPHA, in1=tmp[:],
                                                             op0=ALU.mult, op1=ALU.add), [B_x, tb], [B_x])
        if after is not None:
            after()

    def attention(groups, scale, et_ring, btmp_ring, finish):
        yps, yb = ps_long()
        dps, db = ps_long()
        ng = len(groups)
        for gi, grp in enumerate(groups):
            st, sb_ = ps_short()
            for si, s in enumerate(grp):
                mm(st[:, s["c0"]:s["c0"] + s["n"]], s["k"], s["q"], si == 0, True, [s["kb"], s["qb"]], [sb_], inc=(si == len(grp) - 1), sgc=True)
            et, eb = et_ring()
            bias_fn = grp[0].get("bias_fn")
            if bias_fn is not None:
                bias, bias_b = bias_fn()
                bt, btb = btmp_ring()
                V(lambda: nc.vector.scalar_tensor_tensor(out=bt[:], in0=st[:], scalar=scale, in1=bias, op0=ALU.mult, op1=ALU.add),
                  [sb_, bias_b], [btb])
                A(lambda: nc.scalar.activation(out=et[:], in_=bt[:], func=AF.Exp), [btb], [eb])
            else:
                A(lambda: nc.scalar.activation(out=et[:], in_=st[:], func=AF.Exp, scale=scale), [sb_], [eb])
            for si, s in enumerate(grp):
                mm(yps[:, s["c0"]:s["c0"] + s["n"]], s["v"], et[:, s["c0"]:s["c0"] + s["n"]],
                   gi == 0 and si == 0, gi == ng - 1, [s["vb"], eb], [yb], inc=False, sgc=True)
            mm(dps[:], ones_bf[:], et[:], gi == 0, gi == ng - 1, [B_c, eb], [db], inc=True)
        finish(yps, yb, dps, db)

    def block(l, g):
        row = 0 if g == "P" else 1
        lam_init = 0.8 - 0.6 * math.exp(-0.3 * l)
        xsrc = xin[g] if l == 0 else xspill[g]
        fw.dma("sp", x_sb[:], xsrc.rearrange("(c p) t -> p c t", p=128), reads=[B_spill[g]], writes=[B_x])
        modulate(l, 0, 1, row)
        with ExitStack() as sm:
            ycat, B_y = P.sb(sm, "ycat", [128, 16, T], BF16)
            with ExitStack() as sc:
                qtc, B_qtc = P.sb(sc, "qtc", [128, 5, T], BF16)
                ktc, B_ktc = P.sb(sc, "ktc", [128, 5, T], BF16)
                kcA, B_kcA = P.sb(sc, "kcA", [128, 4, 640], BF16)
                kcB, B_kcB = P.sb(sc, "kcB", [128, 4, 640], BF16)
                vc, B_vc = P.sb(sc, "vc", [128, 4, 640], BF16)
                sigoc, B_so = P.sb(sc, "sigoc", [128, 4, 640], F32)
                gat, B_gat = P.sb(sc, "gat", [128, 4, 20], F32)
                G(lambda: nc.gpsimd.memset(kcA[:], 0.0), [], [B_kcA])
                G(lambda: nc.gpsimd.memset(kcB[:], 0.0), [], [B_kcB])
                ctiles = [(4224, 512), (4736, 512), (5248, 512), (5760, 512), (6272, 532)]
                with extra_slots(2):
                    for (c0, ncol) in ctiles:
                        wv, wb = wload(w_in[l, :, c0:c0 + ncol], 16, ncol, key=("w_in", l, c0))
                        for q in range(min(4, ncol // 128)):
                            col = c0 + q * 128
                            k = col // 128
                            if 33 <= k <= 42:
                                ps, pb = ps_short()
                                for c in range(16):
                                    mm(ps[:], wv[:, c, q * 128:(q + 1) * 128], h_sb[:, c, :], c == 0, c == 15, [wb, B_h], [pb])
                                if k <= 37:
                                    A(lambda: nc.scalar.activation(out=qtc[:, k - 33, :], in_=ps[:], func=AF.Copy, scale=128.0 ** -0.5), [pb], [B_qtc])
                                else:
                                    A(lambda: nc.scalar.copy(out=ktc[:, k - 38, :], in_=ps[:]), [pb], [B_ktc])
                        segs = []
                        for (name, lo, hi) in (("kc", 4864, 5504), ("vc", 5504, 6144), ("oc", 6144, 6784), ("gc", 6784, 6804)):
                            a, b_ = max(lo, c0), min(hi, c0 + ncol)
                            if a < b_:
                                segs.append((name, a, b_, lo))
                        for (name, a, b_, lo) in segs:
                            for tt in range(4):
                                ps, pb = ps_short()
                                n = b_ - a
                                for c in range(16):
                                    mm(ps[:, 0:n], h_sb[:, c, tt * 128:(tt + 1) * 128], wv[:, c, a - c0:b_ - c0], c == 0, c == 15, [wb, B_h], [pb])
                                o0 = a - lo
                                if name == "kc":
                                    A(lambda: nc.scalar.copy(out=kcA[0:64, tt, o0:o0 + n], in_=ps[0:64, 0:n]), [pb], [B_kcA])
                                    A(lambda: nc.scalar.copy(out=kcB[64:128, tt, o0:o0 + n], in_=ps[64:128, 0:n]), [pb], [B_kcB])
                                elif name == "vc":
                                    A(lambda: nc.scalar.copy(out=vc[:, tt, o0:o0 + n], in_=ps[:, 0:n]), [pb], [B_vc])
                                elif name == "oc":
                                    A(lambda: nc.scalar.activation(out=sigoc[:, tt, o0:o0 + n], in_=ps[:, 0:n], func=AF.Sigmoid), [pb], [B_so])
                                else:
                                    V(lambda: nc.vector.tensor_tensor(out=gat[:, tt, :], in0=ps[:, 0:20], in1=gateb[:, l, :], op=ALU.add), [pb, B_c], [B_gat])
                for c0_ in (0, 512):
                    prefetch(("w_in", l, c0_), w_in[l, :, c0_:c0_ + 512], 16, 512)
                kstop("c1")
                bc, B_bc = P.sb(sc, "bc", [128, 2, 4, 10], F32)
                ea, B_ea = P.sb(sc, "ea", [128, 2, 4, 5], F32)
                bl, B_bl = P.sb(sc, "bl", [128, 2, 8, 10], F32)
                t5, B_t5 = P.sb(sc, "t5", [128, 4, 5], F32)
                vp, B_vp = P.sb(sc, "vp", [128, 2, 4, 5, 130], BF16)
                sgp = ExitStack()
                dg, B_dg = P.sb(sgp, "dg", [128, 5, 128], F32)
                mk, B_mk = P.sb(sgp, "mk", [128, 5, 128], F32)
                for d in range(2):
                    tri = CM(C_TRIF if d == 0 else C_TRIB)
                    for tt in range(4):
                        ig = gat[:, tt, d * 10:d * 10 + 5]
                        fg = gat[:, tt, d * 10 + 5:d * 10 + 10]
                        A(lambda: nc.scalar.activation(out=t5[:, 0, :], in_=fg, func=AF.Exp, scale=-1.0), [B_gat], [B_t5])
                        A(lambda: nc.scalar.activation(out=t5[:, 1, :], in_=t5[:, 0, :], func=AF.Ln, bias=1.0, scale=1.0), [B_t5], [B_t5])
                        ps, pb = ps_short()
                        mm(ps[:, 0:5], tri, t5[:, 1, :], True, True, [B_c, B_t5], [pb])
                        A(lambda: nc.scalar.copy(out=bc[:, d, tt, 0:5], in_=ps[:, 0:5]), [pb], [B_bc])
                        V(lambda: nc.vector.tensor_tensor(out=t5[:, 2, :], in0=ig, in1=bc[:, d, tt, 0:5], op=ALU.add), [B_gat, B_bc], [B_t5])
                        A(lambda: nc.scalar.activation(out=ea[:, d, tt, :], in_=t5[:, 2, :], func=AF.Exp), [B_t5], [B_ea])
                        for h in range(5):
                            A(lambda: nc.scalar.activation(out=dg[:, h, :], in_=CM(C_ID), func=AF.Identity, scale=t5[:, 2, h:h + 1]), [B_c, B_t5], [B_dg])
                        ps1, pb1 = ps_short()
                        mm(ps1[:, 0:384], CM(C_ONE), dg[:, 0:3, :].rearrange("p h s -> p (h s)"), True, True, [B_c, B_dg], [pb1])
                        ps2, pb2 = ps_short()
                        mm(ps2[:, 0:256], CM(C_ONE), dg[:, 3:5, :].rearrange("p h s -> p (h s)"), True, True, [B_c, B_dg], [pb2])
                        V(lambda: nc.vector.tensor_tensor(out=mk[:, 0:3, :].rearrange("p h s -> p (h s)"), in0=ps1[:, 0:384],
                                                          in1=negm[:, d, 0:384], op=ALU.add), [pb1, B_c], [B_mk])
                        V(lambda: nc.vector.tensor_tensor(out=mk[:, 3:5, :].rearrange("p h s -> p (h s)"), in0=ps2[:, 0:256],
                                                          in1=negm[:, d, 384:640], op=ALU.add), [pb2, B_c], [B_mk])
                        V(lambda: nc.vector.tensor_reduce(out=bc[:, d, tt, 5:10], in_=mk[:], axis=AX.X, op=ALU.max), [B_mk], [B_bc])
                        for X in range(2):
                            ep = (C_E63, C_E127)[X] if d == 0 else (C_E0, C_E64)[X]
                            ps, pb = ps_short()
                            mm(ps[:, 0:10], CM(ep), bc[:, d, tt, :], True, True, [B_c, B_bc], [pb])
                            A(lambda: nc.scalar.copy(out=bl[:, d, 2 * tt + X, :], in_=ps[:, 0:10]), [pb], [B_bl])
                        for h in range(5):
                            A(lambda: nc.scalar.activation(out=vp[:, d, tt, h, 0:128], in_=vc[:, tt, h * 128:(h + 1) * 128], func=AF.Identity,
                                                           scale=ea[:, d, tt, h:h + 1]), [B_vc, B_ea], [B_vp])
                        A(lambda: nc.scalar.copy(out=vp[:, d, tt, :, 128], in_=ea[:, d, tt, :]), [B_ea], [B_vp])

                fw.barrier()
                sgp.close()
                kstop("c2")
                mc, B_mc = P.sb(sc, "mc", [128, 2, 8, 5], F32)
                wold, B_wo = P.sb(sc, "wold", [128, 2, 8, 5], F32)
                snew, B_sn = P.sb(sc, "snew", [128, 2, 8, 5], F32)
                mcur, B_mcur = P.sb(sc, "mcur", [128, 2, 5], F32)
                mt, B_mt = P.sb(sc, "mt", [128, 2, 5], F32)
                nfacc, B_nf = P.sb(sc, "nfacc", [128, 2, 5], F32)
                cn, B_cn = P.sb(sc, "cn", [128, 10, 129], F32)
                cnb, B_cnb = P.sb(sc, "cnb", [128, 10, 130], BF16)
                tmpu_ring = P.ring(sc, "tmpu", 2, [128, 129], F32)

                def chunk_order(d, runs):
                    out = []
                    rr = runs if d == 0 else [list(reversed(r)) for r in reversed(runs)]
                    for r in rr:
                        out.append(r)
                    return out

                def mchain(d, run, m_init_fn):
                    m_init_fn(mcur[:, d, :])
                    for c in run:
                        A(lambda: nc.scalar.copy(out=mc[:, d, c, :], in_=mcur[:, d, :]), [B_mcur], [B_mc])
                        V(lambda: nc.vector.tensor_tensor(out=mt[:, 0, :], in0=mcur[:, d, :], in1=bl[:, d, c, 5:10], op=ALU.max), [B_mcur, B_bl], [B_mt])
                        V(lambda: nc.vector.tensor_tensor(out=mt[:, 1, :], in0=mcur[:, d, :], in1=mt[:, 0, :], op=ALU.subtract), [B_mcur, B_mt], [B_mt])
                        A(lambda: nc.scalar.activation(out=wold[:, d, c, :], in_=mt[:, 1, :], func=AF.Exp), [B_mt], [B_wo])
                        A(lambda: nc.scalar.activation(out=snew[:, d, c, :], in_=mt[:, 0, :], func=AF.Exp, scale=-1.0), [B_mt], [B_sn])
                        V(lambda: nc.vector.tensor_tensor(out=mcur[:, d, :], in0=mt[:, 0, :], in1=bl[:, d, c, 0:5], op=ALU.subtract), [B_mt, B_bl], [B_mcur])
                        V(lambda: nc.vector.tensor_tensor(out=nfacc[:, d, :], in0=nfacc[:, d, :], in1=bl[:, d, c, 0:5], op=ALU.add), [B_nf, B_bl], [B_nf])

                def state_update(d, h, c, need_bf=True):
                    tt, X = c // 2, c % 2
                    kk = kcA if X == 0 else kcB
                    kkb = B_kcA if X == 0 else B_kcB
                    ps, pb = ps_short()
                    mm(ps[:, 0:129], kk[:, tt, h * 128:(h + 1) * 128], vp[:, d, tt, h, 0:129], True, True, [kkb, B_vp], [pb])
                    tu, tub = tmpu_ring()
                    A(lambda: nc.scalar.activation(out=tu[:], in_=ps[:, 0:129], func=AF.Identity, scale=snew[:, d, c, h:h + 1]), [pb, B_sn], [tub])
                    V(lambda: nc.vector.scalar_tensor_tensor(out=cn[:, d * 5 + h, :], in0=cn[:, d * 5 + h, :], scalar=wold[:, d, c, h:h + 1],
                                                             in1=tu[:], op0=ALU.mult, op1=ALU.add), [B_cn, B_wo, tub], [B_cn])
                    if need_bf:
                        A(lambda: nc.scalar.copy(out=cnb[:, d * 5 + h, 0:129], in_=cn[:, d * 5 + h, :]), [B_cn], [B_cnb])

                def zero_state(d):
                    G(lambda: nc.gpsimd.memset(cn[:, d * 5:(d + 1) * 5, :], 0.0), [], [B_cn])
                    G(lambda: nc.gpsimd.memset(cnb[:, d * 5:(d + 1) * 5, :], 0.0), [], [B_cnb])

                def alloc_scan_bufs():
                    a_ = P.sb(sc, "hc", [128, 4, 640], F32)
                    b_ = P.sb(sc, "tok", [128, 2, 4, 15], F32)
                    c_ = P.sb(sc, "mcol", [128, 5], F32)
                    return (a_[0], a_[1], b_[0], b_[1], c_[0], c_[1], P.ring(sc, "gm", 3, [128, 128], BF16),
                            P.ring(sc, "ti", 3, [128, 129], F32), P.ring(sc, "hn", 3, [128, 129], F32), P.ring(sc, "s3", 3, [128, 3], F32))

                def token_scalars(d, tt):
                    A(lambda: nc.scalar.copy(out=mcol[0:64, :], in_=mc[0:64, d, 2 * tt, :]), [B_mc], [B_mcol])
                    A(lambda: nc.scalar.copy(out=mcol[64:128, :], in_=mc[64:128, d, 2 * tt + 1, :]), [B_mc], [B_mcol])
                    V(lambda: nc.vector.tensor_tensor(out=t5[:, 3, :], in0=bc[:, d, tt, 5:10], in1=mcol[:], op=ALU.max), [B_bc, B_mcol], [B_t5])
                    A(lambda: nc.scalar.activation(out=tok[:, d, tt, 0:5], in_=t5[:, 3, :], func=AF.Exp, scale=-1.0), [B_t5], [B_tok])
                    V(lambda: nc.vector.tensor_tensor(out=t5[:, 0, :], in0=mcol[:], in1=t5[:, 3, :], op=ALU.subtract), [B_mcol, B_t5], [B_t5])
                    A(lambda: nc.scalar.activation(out=tok[:, d, tt, 5:10], in_=t5[:, 0, :], func=AF.Exp), [B_t5], [B_tok])
                    V(lambda: nc.vector.tensor_tensor(out=t5[:, 1, :], in0=bc[:, d, tt, 0:5], in1=t5[:, 3, :], op=ALU.subtract), [B_bc, B_t5], [B_t5])
                    A(lambda: nc.scalar.activation(out=tok[:, d, tt, 10:15], in_=t5[:, 1, :], func=AF.Exp), [B_t5], [B_tok])

                def scan_outputs(runs, on_run_end, on_run_start):
                    G(lambda: nc.gpsimd.memset(hc[:], 0.0), [], [B_hc])
                    for d in range(2):
                        for tt in range(4):
                            token_scalars(d, tt)
                    order = {d: chunk_order(d, runs) for d in range(2)}
                    nsteps = sum(len(r) for r in runs) // 2
                    flat = {d: [c for r in order[d] for c in r] for d in range(2)}
                    run_start = {d: {r[0]: ri for ri, r in enumerate(order[d])} for d in range(2)}
                    run_end = {d: {r[-1]: ri for ri, r in enumerate(order[d])} for d in range(2)}
                    for step in range(nsteps):
                        for d in range(2):
                            c_pair = flat[d][2 * step:2 * step + 2]
                            tt = c_pair[0] // 2
                            if c_pair[0] in run_start[d]:
                                on_run_start(d, run_start[d][c_pair[0]], order[d])
                            for h in range(5):
                                gps, gpb = ps_short()
                                mm(gps[:, 0:128], ktc[:, h, tt * 128:(tt + 1) * 128], qtc[:, h, tt * 128:(tt + 1) * 128], True, True, [B_ktc, B_qtc], [gpb])
                                gm, gmb = gm_ring()
                                V(lambda: nc.vector.tensor_tensor(out=gm[:], in0=gps[:, 0:128], in1=CM(C_TRIF if d == 0 else C_TRIB), op=ALU.mult), [gpb, B_c], [gmb])
                                ips, ipb = ps_short()
                                mm(ips[:, 0:129], gm[:], vp[:, d, tt, h, 0:129], True, True, [gmb, B_vp], [ipb])
                                ti, tib = ti_ring()
                                A(lambda: nc.scalar.activation(out=ti[:], in_=ips[:, 0:129], func=AF.Identity, scale=tok[:, d, tt, h:h + 1]), [ipb, B_tok], [tib])
                                hn, hnb = hn_ring()
                                for c in c_pair:
                                    X = c % 2
                                    rs = slice(0, 64) if X == 0 else slice(64, 128)
                                    xps, xpb = ps_short()
                                    mm(xps[:, 0:129], qtc[:, h, tt * 128:(tt + 1) * 128], cnb[:, d * 5 + h, 0:129], True, True, [B_qtc, B_cnb], [xpb])
                                    V(lambda: nc.vector.scalar_tensor_tensor(out=hn[rs, :], in0=xps[rs, 0:129], scalar=tok[rs, d, tt, 5 + h:6 + h],
                                                                             in1=ti[rs, :], op0=ALU.mult, op1=ALU.add), [xpb, B_tok, tib], [hnb])
                                    state_update(d, h, c)
                                s3, s3b = s3_ring()
                                V(lambda: nc.vector.scalar_tensor_tensor(out=s3[:, 0:1], in0=hn[:, 128:129], scalar=-1.0, in1=hn[:, 128:129],
                                                                         op0=ALU.mult, op1=ALU.max), [hnb], [s3b])
                                V(lambda: nc.vector.tensor_tensor(out=s3[:, 1:2], in0=s3[:, 0:1], in1=tok[:, d, tt, 10 + h:11 + h], op=ALU.max), [s3b, B_tok], [s3b])
                                A(lambda: nc.scalar.activation(out=s3[:, 2:3], in_=s3[:, 1:2], func=AF.Ln), [s3b], [s3b])
                                A(lambda: nc.scalar.activation(out=s3[:, 2:3], in_=s3[:, 2:3], func=AF.Exp, scale=-1.0), [s3b], [s3b])
                                hsl = hc[:, tt, h * 128:(h + 1) * 128]
                                V(lambda: nc.vector.scalar_tensor_tensor(out=hsl, in0=hn[:, 0:128], scalar=s3[:, 2:3], in1=hsl,
                                                                         op0=ALU.mult, op1=ALU.add), [hnb, s3b, B_hc], [B_hc])
                            if c_pair[1] in run_end[d]:
                                on_run_end(d, run_end[d][c_pair[1]], order[d])

                def set_const(val):
                    def f(ap):
                        G(lambda: nc.gpsimd.memset(ap, val), [], [B_mcur])
                    return f

                G(lambda: nc.gpsimd.memset(nfacc[:], 0.0), [], [B_nf])
                if g == "P":
                    runs = [[0, 1, 2, 3], [4, 5, 6, 7]]
                    mfin, B_mfin = P.sb(sc, "mfin", [128, 2, 2, 5], F32)
                    for d in range(2):
                        for ri, r in enumerate(chunk_order(d, runs)):
                            mchain(d, r, set_const(0.0))
                            seq = r[0] // 4
                            A(lambda: nc.scalar.copy(out=mfin[:, seq, d, :], in_=mcur[:, d, :]), [B_mcur], [B_mfin])
                    for seq in range(2):
                        fw.dma("sp", o_cm[seq, l].rearrange("(o d) h -> o (d h)", o=1), mfin[0:1, seq, :, :].rearrange("p d h -> p (d h)"),
                               reads=[B_mfin], writes=[B_out])

                    def on_start(d, ri, order):
                        zero_state(d)

                    def on_end(d, ri, order):
                        seq = order[ri][0] // 4
                        fw.dma("sp", o_cC[seq, l, d].rearrange("h k v -> k h v"), cn[:, d * 5:(d + 1) * 5, 0:128], reads=[B_cn], writes=[B_out])
                        fw.dma("sp", o_cn[seq, l, d], cn[:, d * 5:(d + 1) * 5, 128], reads=[B_cn], writes=[B_out], slow=True)
                    hc, B_hc, tok, B_tok, mcol, B_mcol, gm_ring, ti_ring, hn_ring, s3_ring = alloc_scan_bufs()
                    scan_outputs(runs, on_end, on_start)
                else:
                    runs = [[0, 1, 2, 3, 4, 5, 6, 7]]
                    for d in range(2):
                        zero_state(d)
                        r = chunk_order(d, runs)[0]
                        mchain(d, r, set_const(NEG))
                        for c in r:
                            for h in range(5):
                                state_update(d, h, c, need_bf=False)
                    with ExitStack() as sg:
                        cst_t, B_cst = P.sb(sg, "cst_t", [128, 20], F32)
                        A(lambda: nc.scalar.copy(out=cst_t[:, 0:10], in_=mcur[:].rearrange("p d h -> p (d h)")), [B_mcur], [B_cst])
                        A(lambda: nc.scalar.copy(out=cst_t[:, 10:20], in_=nfacc[:].rearrange("p d h -> p (d h)")), [B_nf], [B_cst])
                        fw.dma("sp", cs_in[:, 0:1290], cn[:].rearrange("p a b -> p (a b)"), reads=[B_cn], writes=[B_csi])
                        fw.dma("sp", cs_in[:, 1290:1310], cst_t[:], reads=[B_cst], writes=[B_csi])
                        fw.allgather(cs_in, cs_out, reads=[B_csi], writes=[B_cso])
                        gs, B_gs = P.sb(sg, "gs", [128, 4, CSF], F32)
                        fw.dma("sp", gs[:], cs_out.rearrange("(r p) f -> p r f", p=128), reads=[B_cso], writes=[B_gs])
                        c0t, B_c0 = P.sb(sg, "c0t", [128, 10, 129], F32)
                        m0t, B_m0 = P.sb(sg, "m0t", [128, 10], F32)
                        fw.dma("sp", c0t[:, :, 0:128], cC_d[l].rearrange("d h k v -> k (d h) v"), writes=[B_c0])
                        fw.dma("sp", c0t[:, :, 128], cn_d[l], writes=[B_c0], slow=True)
                        fw.dma("sp", m0t[:], bass.AP(cm_d.tensor, l * 10, [[0, 128], [1, 10]]), writes=[B_m0])
                        av, B_av = P.sb(sg, "av", [128, 2, 6, 5], F32)
                        wv5, B_wv5 = P.sb(sg, "wv5", [128, 2, 5, 5], F32)
                        for d in range(2):
                            for i in range(5):
                                src = m0t[:, d * 5:(d + 1) * 5] if i == 0 else gs[:, i - 1, 1290 + d * 5:1290 + d * 5 + 5]
                                V(lambda: nc.vector.tensor_tensor(out=av[:, d, i, :], in0=src, in1=vtab[:, d, i, :], op=ALU.add), [B_m0, B_gs, B_c], [B_av])
                                for r in range(4):
                                    V(lambda: nc.vector.tensor_tensor(out=t5[:, 0, :], in0=cftab[:, d, i, r, :], in1=gs[:, r, 1300 + d * 5:1305 + d * 5], op=ALU.mult),
                                      [B_c, B_gs], [B_t5])
                                    V(lambda: nc.vector.tensor_tensor(out=av[:, d, i, :], in0=av[:, d, i, :], in1=t5[:, 0, :], op=ALU.subtract), [B_av, B_t5], [B_av])
                            V(lambda: nc.vector.tensor_tensor(out=av[:, d, 5, :], in0=av[:, d, 0, :], in1=av[:, d, 1, :], op=ALU.max), [B_av], [B_av])
                            for i in range(2, 5):
                                V(lambda: nc.vector.tensor_tensor(out=av[:, d, 5, :], in0=av[:, d, 5, :], in1=av[:, d, i, :], op=ALU.max), [B_av], [B_av])
                            for i in range(5):
                                V(lambda: nc.vector.tensor_tensor(out=t5[:, 1, :], in0=av[:, d, i, :], in1=av[:, d, 5, :], op=ALU.subtract), [B_av], [B_t5])
                                A(lambda: nc.scalar.activation(out=wv5[:, d, i, :], in_=t5[:, 1, :], func=AF.Exp), [B_t5], [B_wv5])
                            for h in range(5):
                                dh = d * 5 + h
                                A(lambda: nc.scalar.activation(out=cn[:, dh, :], in_=c0t[:, dh, :], func=AF.Identity, scale=wv5[:, d, 0, h:h + 1]), [B_c0, B_wv5], [B_cn])
                                for r in range(4):
                                    V(lambda: nc.vector.scalar_tensor_tensor(out=cn[:, dh, :], in0=gs[:, r, dh * 129:(dh + 1) * 129], scalar=wv5[:, d, 1 + r, h:h + 1],
                                                                             in1=cn[:, dh, :], op0=ALU.mult, op1=ALU.add), [B_gs, B_wv5, B_cn], [B_cn])
                                A(lambda: nc.scalar.copy(out=cnb[:, dh, 0:129], in_=cn[:, dh, :]), [B_cn], [B_cnb])

                        def m_from_av(d):
                            def f(ap):
                                A(lambda: nc.scalar.copy(out=ap, in_=av[:, d, 5, :]), [B_av], [B_mcur])
                            return f
                        for d in range(2):
                            mchain(d, chunk_order(d, runs)[0], m_from_av(d))
                        fw.barrier()
                    hc, B_hc, tok, B_tok, mcol, B_mcol, gm_ring, ti_ring, hn_ring, s3_ring = alloc_scan_bufs()
                    scan_outputs(runs, lambda *a: None, lambda *a: None)

                kstop("c3")
                ss, B_ss = P.sb(sc, "ss", [128, 20], F32)
                junk, B_junk = P.sb(sc, "junk", [128, 128], F32)
                yct_ring = P.ring(sc, "yct", 3, [128, 128], BF16)
                ytmp_ring = P.ring(sc, "ytmp", 2, [128, 128], F32)
                G(lambda: nc.gpsimd.memset(ss[:], 0.0), [], [B_ss])
                for tt in range(4):
                    for h in range(5):
                        A(lambda: nc.scalar.activation(out=junk[:], in_=hc[:, tt, h * 128:(h + 1) * 128], func=AF.Square,
                                                       accum_out=ss[:, tt * 5 + h:tt * 5 + h + 1]), [B_hc], [B_junk, B_ss])
                A(lambda: nc.scalar.activation(out=ss[:], in_=ss[:], func=AF.Ln, scale=1.0 / 128, bias=RMS_EPS), [B_ss], [B_ss])
                A(lambda: nc.scalar.activation(out=ss[:], in_=ss[:], func=AF.Exp, scale=-0.5), [B_ss], [B_ss])
                kstop("ca")
                for h in range(5):
                    for tt in range(4):
                        yt, ytb = ytmp_ring()
                        A(lambda: nc.scalar.activation(out=yt[:], in_=hc[:, tt, h * 128:(h + 1) * 128], func=AF.Identity, scale=ss[:, tt * 5 + h:tt * 5 + h + 1]),
                          [B_hc, B_ss], [ytb])
                        V(lambda: nc.vector.tensor_tensor(out=yt[:], in0=yt[:], in1=cnorm[:, l, :], op=ALU.mult), [ytb, B_c], [ytb])
                        yc, ycb = yct_ring()
                        V(lambda: nc.vector.tensor_tensor(out=yc[:], in0=yt[:], in1=sigoc[:, tt, h * 128:(h + 1) * 128], op=ALU.mult), [ytb, B_so], [ycb])
                        if KSTOP == "cb":
                            continue
                        ps, pb = ps_short()
                        mm(ps[:, 0:128], yc[:], id_bf[:], True, True, [ycb, B_c], [pb])
                        if KSTOP == "cc":
                            continue
                        A(lambda: nc.scalar.copy(out=ycat[:, 11 + h, tt * 128:(tt + 1) * 128], in_=ps[:, 0:128]), [pb], [B_y])
                fw.barrier()

            kstop("c4")
            kstop("cb")
            kstop("cc")
            with ExitStack() as sa:
                qta, B_qta = P.sb(sa, "qta", [128, 6, T], BF16)
                q1p, B_q1p = P.sb(sa, "q1p", [128, 5, T], BF16)
                q2p, B_q2p = P.sb(sa, "q2p", [128, 5, T], BF16)
                G(lambda: nc.gpsimd.memset(q1p[:], 0.0), [], [B_q1p])
                G(lambda: nc.gpsimd.memset(q2p[:], 0.0), [], [B_q2p])
                if g == "S":
                    q1r, B_q1r = P.sb(sa, "q1r", [128, 5, T], BF16)
                    q2r, B_q2r = P.sb(sa, "q2r", [128, 5, T], BF16)
                    G(lambda: nc.gpsimd.memset(q1r[:], 0.0), [], [B_q1r])
                    G(lambda: nc.gpsimd.memset(q2r[:], 0.0), [], [B_q2r])
                    sip = ExitStack()
                    kst_ring = P.ring(sip, "kst", 3, [128, T], BF16)
                    vst, B_vst = P.sb(sip, "vst", [128, 4, 1408], BF16)
                    rp_ring = P.ring(sip, "rpx", 2, [128, T], F32)
                    rp2_ring = P.ring(sip, "rpy", 2, [128, T], F32)
                else:
                    kta, B_kta = P.sb(sa, "kta", [128, 6, T], BF16)
                    ktb, B_ktb = P.sb(sa, "ktb", [128, 5, T], BF16)
                    vab, B_vab = P.sb(sa, "vab", [128, 4, 1408], BF16)
                    stg_ring = P.ring(sa, "stg", 3, [128, T], F32)

                def rope_apply(ps, pb, outs):
                    xf, xb = rp_ring()
                    A(lambda: nc.scalar.copy(out=xf[:], in_=ps[:]), [pb], [xb])
                    p2, pb2 = ps_short()
                    mm(p2[:], CM(C_PERM), xf[:], True, True, [B_c, xb], [pb2])
                    x2, x2b = rp2_ring()
                    V(lambda: nc.vector.tensor_tensor(out=x2[:], in0=p2[:], in1=rope[:, 1, :], op=ALU.mult), [pb2, B_c], [x2b])
                    V(lambda: nc.vector.tensor_tensor(out=xf[:], in0=xf[:], in1=rope[:, 0, :], op=ALU.mult), [xb, B_c], [xb])
                    for (rs, dst, db_) in outs:
                        V(lambda: nc.vector.tensor_tensor(out=dst[rs, :], in0=xf[rs, :], in1=x2[rs, :], op=ALU.add), [xb, x2b], [db_])

                abtiles = [(i * 512, 512) for i in range(8)] + [(4096, 128)]
                if "t" in KSKIP:
                    abtiles = abtiles[:int(KSKIP[KSKIP.index("t") + 1])]
                with extra_slots(2):
                    for (c0, ncol) in abtiles:
                        wv, wb = wload(w_in[l, :, c0:c0 + ncol], 16, ncol, key=("w_in", l, c0))
                        for q in range(ncol // 128):
                            k = (c0 + q * 128) // 128
                            fm = (k <= 11) or (18 <= k <= 27)
                            if not fm:
                                continue
                            ps, pb = ps_short()
                            for c in range(16):
                                mm(ps[:], wv[:, c, q * 128:(q + 1) * 128], h_sb[:, c, :], c == 0, c == 15, [wb, B_h], [pb])
                            if k <= 5:
                                A(lambda: nc.scalar.copy(out=qta[:, k, :], in_=ps[:]), [pb], [B_qta])
                            elif k <= 11:
                                hh = k - 6
                                if g == "P":
                                    A(lambda: nc.scalar.copy(out=kta[:, hh, :], in_=ps[:]), [pb], [B_kta])
                                    sg, sgb = stg_ring()
                                    A(lambda: nc.scalar.copy(out=sg[:], in_=ps[:]), [pb], [sgb])
                                    if "k" not in KSKIP:
                                        fw.dma("sp", o_ak[:, l, hh].rearrange("s d t -> d s t"), sg[:].rearrange("p (s t) -> p s t", s=2), reads=[sgb], writes=[B_out])
                                else:
                                    ks, ksb = kst_ring()
                                    A(lambda: nc.scalar.copy(out=ks[:], in_=ps[:]), [pb], [ksb])
                                    kr, krb = bnc_k_rows(hh)
                                    fw.dma("sp", kr, ks[:], reads=[ksb], writes=[krb])
                            elif k <= 22:
                                hh = k - 18
                                A(lambda: nc.scalar.copy(out=q1p[0:64, hh, :], in_=ps[0:64, :]), [pb], [B_q1p])
                                A(lambda: nc.scalar.copy(out=q2p[64:128, hh, :], in_=ps[64:128, :]), [pb], [B_q2p])
                                if g == "S":
                                    rope_apply(ps, pb, [(slice(0, 64), q1r[:, hh, :], B_q1r), (slice(64, 128), q2r[:, hh, :], B_q2r)])
                            else:
                                hh = k - 23
                                if g == "P":
                                    A(lambda: nc.scalar.copy(out=ktb[:, hh, :], in_=ps[:]), [pb], [B_ktb])
                                    sg, sgb = stg_ring()
                                    A(lambda: nc.scalar.copy(out=sg[:], in_=ps[:]), [pb], [sgb])
                                    if "k" not in KSKIP:
                                        fw.dma("sp", o_bk[:, l, hh].rearrange("s d t -> d s t"), sg[:].rearrange("p (s t) -> p s t", s=2), reads=[sgb], writes=[B_out])
                                else:
                                    ks, ksb = kst_ring()
                                    rope_apply(ps, pb, [(slice(0, 128), ks, ksb)])
                                    kr, krb = bnc_k_rows(6 + hh)
                                    fw.dma("sp", kr, ks[:], reads=[ksb], writes=[krb])
                        for (name, lo, hi, o_base) in (("va", 1536, 2304, 0), ("vb", 3584, 4224, 768)):
                            a, b_ = max(lo, c0), min(hi, c0 + ncol)
                            if a >= b_:
                                continue
                            n = b_ - a
                            o0 = o_base + a - lo
                            for tt in range(4):
                                ps, pb = ps_short()
                                for c in range(16):
                                    mm(ps[:, 0:n], h_sb[:, c, tt * 128:(tt + 1) * 128], wv[:, c, a - c0:b_ - c0], c == 0, c == 15, [wb, B_h], [pb])
                                if g == "P":
                                    A(lambda: nc.scalar.copy(out=vab[:, tt, o0:o0 + n], in_=ps[:, 0:n]), [pb], [B_vab])
                                    sg, sgb = stg_ring()
                                    A(lambda: nc.scalar.copy(out=sg[:, 0:n], in_=ps[:, 0:n]), [pb], [sgb])
                                    seq, s0_ = tt // 2, (tt % 2) * 128
                                    h0 = (a - lo) // 128
                                    nh = n // 128
                                    dst = (o_av if name == "va" else o_bv)[seq, l, h0:h0 + nh, s0_:s0_ + 128, :].rearrange("h s d -> s h d")
                                    if "v" not in KSKIP:
                                        fw.dma("sp", dst, sg[:, 0:n].rearrange("p (h d) -> p h d", d=128), reads=[sgb], writes=[B_out])
                                else:
                                    A(lambda: nc.scalar.copy(out=vst[:, tt, o0:o0 + n], in_=ps[:, 0:n]), [pb], [B_vst])

                for tc_ in (0, 1):
                    prefetch(("o", l, tc_), w_out[l][:, tc_ * 512:(tc_ + 1) * 512], 16, 512)
                kstop("c5")

                def alloc_attn_rings():
                    return (P.ring(sa, "et", 5, [128, T], BF16), P.ring(sa, "bt", 4, [128, T], F32),
                            P.ring(sa, "rd", 2, [128, T], F32), P.ring(sa, "ybt", 3, [128, T], F32))

                def fin_A(h):
                    def f(yps, yb, dps, db):
                        rd, rdb = rd_ring()
                        A(lambda: nc.scalar.activation(out=rd[:], in_=dps[:], func=AF.Ln), [db], [rdb])
                        A(lambda: nc.scalar.activation(out=rd[:], in_=rd[:], func=AF.Exp, scale=-1.0), [rdb], [rdb])
                        V(lambda: nc.vector.tensor_tensor(out=ycat[:, h, :], in0=yps[:], in1=rd[:], op=ALU.mult), [yb, rdb], [B_y])
                    return f

                def fin_B(dst, dstb):
                    def f(yps, yb, dps, db):
                        rd, rdb = rd_ring()
                        A(lambda: nc.scalar.activation(out=rd[:], in_=dps[:], func=AF.Ln), [db], [rdb])
                        A(lambda: nc.scalar.activation(out=rd[:], in_=rd[:], func=AF.Exp, scale=-1.0), [rdb], [rdb])
                        V(lambda: nc.vector.tensor_tensor(out=dst[:], in0=yps[:], in1=rd[:], op=ALU.mult), [yb, rdb], [dstb])
                    return f

                def diff_finish(h, y1, y1b, y2, y2b):
                    V(lambda: nc.vector.scalar_tensor_tensor(out=y1[:], in0=y2[:], scalar=nlam[:, l:l + 1], in1=y1[:], op0=ALU.mult, op1=ALU.add),
                      [y2b, y1b, B_nlam], [y1b])
                    A(lambda: nc.scalar.activation(out=y2[:], in_=y1[:], func=AF.Square), [y1b], [y2b])
                    sp_, spb = ps_short()
                    mm(sp_[:], CM(C_ONE), y2[:], True, True, [B_c, y2b], [spb])
                    A(lambda: nc.scalar.activation(out=y2[:], in_=sp_[:], func=AF.Ln, scale=1.0 / 128, bias=RMS_EPS), [spb], [y2b])
                    A(lambda: nc.scalar.activation(out=y2[:], in_=y2[:], func=AF.Exp, scale=-0.5), [y2b], [y2b])
                    V(lambda: nc.vector.tensor_tensor(out=y1[:], in0=y1[:], in1=y2[:], op=ALU.mult), [y1b, y2b], [y1b])
                    A(lambda: nc.scalar.activation(out=ycat[:, 6 + h, :], in_=y1[:], func=AF.Identity, scale=sublns[:, l:l + 1]), [y1b, B_nlam], [B_y])

                if g == "P":
                    et_ring, bt_ring, rd_ring, yb_ring = alloc_attn_rings()
                    for h in range(6):
                        groups = []
                        for kb in range(2):
                            grp = []
                            for s in range(2):
                                t0 = s * 256 + kb * 128
                                grp.append(dict(k=kta[:, h, t0:t0 + 128], kb=B_kta, q=qta[:, h, s * 256:(s + 1) * 256], qb=B_qta,
                                                v=vab[:, 2 * s + kb, h * 128:(h + 1) * 128], vb=B_vab, c0=s * 256, n=256))
                            groups.append(grp)
                        attention(groups, 128.0 ** -0.5, et_ring, bt_ring, fin_A(h))
                    for h in range(5):
                        ys = []
                        for (qp, qpb) in ((q1p, B_q1p), (q2p, B_q2p)):
                            groups = []
                            for kb in range(2):
                                grp = []
                                for s in range(2):
                                    t0 = s * 256 + kb * 128
                                    grp.append(dict(k=ktb[:, h, t0:t0 + 128], kb=B_ktb, q=qp[:, h, s * 256:(s + 1) * 256], qb=qpb,
                                                    v=vab[:, 2 * s + kb, 768 + h * 128:768 + (h + 1) * 128], vb=B_vab, c0=s * 256, n=256))
                                groups.append(grp)
                            yt, ytb = yb_ring()
                            attention(groups, 64.0 ** -0.5, et_ring, bt_ring, fin_B(yt, ytb))
                            ys.append((yt, ytb))
                        diff_finish(h, ys[0][0], ys[0][1], ys[1][0], ys[1][1])
                else:
                    for hh in range(11):
                        vr, vrb = bnc_v_rows(hh)
                        fw.dma("sp", vr.rearrange("p (tt d) -> p tt d", d=128),
                               vst[:, :, hh * 128:(hh + 1) * 128], reads=[B_vst], writes=[vrb])
                    for ci in range(3):
                        fw.allgather(bnc_in[ci], bnc_out[ci], reads=[B_bi[ci]], writes=[B_bo[ci]])
                    fw.barrier()
                    sip.close()
                    et_ring, bt_ring, rd_ring, yb_ring = alloc_attn_rings()
                    kall_ring = P.ring(sa, "kall", 2, [128, 4, T], BF16)
                    vall_ring = P.ring(sa, "vall", 2, [128, 4, T], BF16)
                    kctx_ring = P.ring(sa, "kctx", 2, [128, 512], BF16)
                    vctx_ring = P.ring(sa, "vctx", 2, [128, 4, 128], BF16)
                    nb_ring = P.ring(sa, "nbias", 5, [128, T], F32)
                    gviews = [bo.rearrange("(r x) t -> x r t", r=4) for bo in bnc_out]
                    def load_head(hh):
                        isA = hh < 6
                        h = hh if isA else hh - 6
                        ka, kab = kall_ring()
                        va_, vab_ = vall_ring()
                        kc_, kcb_ = kctx_ring()
                        vc_, vcb_ = vctx_ring()
                        gch = HH_CH[hh]
                        krow = HH_IX[hh] * 128
                        vrow = (CH_NH[gch] + HH_IX[hh]) * 128
                        fw.dma("sp", ka[:], gviews[gch][krow:krow + 128], reads=[B_bo[gch]], writes=[kab])
                        fw.dma("sp", va_[:], gviews[gch][vrow:vrow + 128], reads=[B_bo[gch]], writes=[vab_])
                        if isA:
                            fw.dma("pool", kc_[:], cakT[l, h], writes=[kcb_])
                            fw.dma("pool", vc_[:], cav[l, h].rearrange("(b p) d -> p b d", p=128), writes=[vcb_])
                        else:
                            fw.dma("pool", kc_[:], cbkT[l, h], writes=[kcb_])
                            fw.dma("pool", vc_[:], cbv[l, h].rearrange("(b p) d -> p b d", p=128), writes=[vcb_])
                        return (ka, kab, va_, vab_, kc_, kcb_, vc_, vcb_)

                    nxt_head = load_head(0)
                    for hh in range(11):
                        isA = hh < 6
                        h = hh if isA else hh - 6
                        (ka, kab, va_, vab_, kc_, kcb_, vc_, vcb_) = nxt_head
                        if hh + 1 < 11:
                            nxt_head = load_head(hh + 1)

                        def mkgroups(q_lat, q_latb, q_ctx, q_ctxb):
                            groups = []
                            for kb in range(16):
                                r, t4 = kb // 4, kb % 4
                                sub = dict(k=ka[:, r, t4 * 128:(t4 + 1) * 128], kb=kab, q=q_lat, qb=q_latb,
                                           v=va_[:, r, t4 * 128:(t4 + 1) * 128], vb=vab_, c0=0, n=T)
                                if isA:
                                    def bias_fn(kb=kb, h=h):
                                        nbt, nbb = nb_ring()
                                        fw.dma("sp", nbt[:], nbias_d[l, h, kb], writes=[nbb])
                                        return nbt[:], nbb
                                    sub["bias_fn"] = bias_fn
                                groups.append([sub])
                            for kb in range(4):
                                groups.append([dict(k=kc_[:, kb * 128:(kb + 1) * 128], kb=kcb_, q=q_ctx, qb=q_ctxb,
                                                    v=vc_[:, kb, :], vb=vcb_, c0=0, n=T)])
                            return groups
                        if isA:
                            attention(mkgroups(qta[:, h, :], B_qta, qta[:, h, :], B_qta), 128.0 ** -0.5, et_ring, bt_ring, fin_A(h))
                        else:
                            ys = []
                            for (qr, qrb, qp, qpb) in ((q1r, B_q1r, q1p, B_q1p), (q2r, B_q2r, q2p, B_q2p)):
                                yt, ytb = yb_ring()
                                attention(mkgroups(qr[:, h, :], qrb, qp[:, h, :], qpb), 64.0 ** -0.5, et_ring, bt_ring, fin_B(yt, ytb))
                                ys.append((yt, ytb))
                            diff_finish(h, ys[0][0], ys[0][1], ys[1][0], ys[1][1])
                fw.barrier()

            kstop("c6")
            with ExitStack() as so:
                def pf_up():
                    prefetch(("up", l, 0), w_up[l, :, 0:512], 16, 512)
                    prefetch(("up", l, DFF), w_up[l, :, DFF:DFF + 512], 16, 512)
                residual_proj(l, w_out[l], 16, 512, lambda c: ycat[:, c, :], [B_y], 2, row, so, "o", after=pf_up)
                layernorm(l, 0, so)
                fw.barrier()
        kstop("c7")
        modulate(l, 3, 4, row)
        with ExitStack() as sf:
            actb, B_act = P.sb(sf, "actb", [128, 43, T], BF16)
            u_ring = P.ring(sf, "u1", 4, [128, T], F32)
            hal, B_hal = P.sb(sf, "hal", [128, 2, 2], F32)
            if g == "S":
                hbs, B_hbs = P.sb(sf, "hbs", [128, 16, 2], F32)
                hba, B_hba = P.sb(sf, "hba", [128, 4, 32], F32)
                hbf, B_hbf = P.sb(sf, "hbf", [128, 16, 2], F32)
                A(lambda: nc.scalar.copy(out=hbs[:, :, 0], in_=h_sb[:, :, 0]), [B_h], [B_hbs])
                A(lambda: nc.scalar.copy(out=hbs[:, :, 1], in_=h_sb[:, :, T - 1]), [B_h], [B_hbs])
                fw.dma("sp", hb_in, hbs[:].rearrange("p c k -> p (c k)"), reads=[B_hbs], writes=[B_hbi])
                fw.allgather(hb_in, hb_out, reads=[B_hbi], writes=[B_hbo])
                fw.dma("sp", hba[:], hb_out.rearrange("(r p) f -> p r f", p=128), reads=[B_hbo], writes=[B_hba])
                hv = hba[:].rearrange("p r (c k) -> p r c k", k=2)
                for (side, kk, so_) in ((0, 1, 0), (1, 0, 4)):
                    A(lambda: nc.scalar.activation(out=hbf[:, :, side], in_=hv[:, 0, :, kk], func=AF.Identity, scale=sel[:, so_:so_ + 1]), [B_hba, B_c], [B_hbf])
                    for r in range(1, 4):
                        V(lambda: nc.vector.scalar_tensor_tensor(out=hbf[:, :, side], in0=hv[:, r, :, kk], scalar=sel[:, so_ + r:so_ + r + 1],
                                                                 in1=hbf[:, :, side], op0=ALU.mult, op1=ALU.add), [B_hba, B_c, B_hbf], [B_hbf])
                A(lambda: nc.scalar.copy(out=hh_sb[:], in_=hbf[:]), [B_hbf], [B_hh])
            segs = [(0, 256), (256, 256)] if g == "P" else [(0, 512)]

            def conv_chunk(ps, pb, hps, hpb, ch):
                u, ub = u_ring()
                cp = convp[:, l, ch, :]
                A(lambda: nc.scalar.activation(out=u[:], in_=ps[:], func=AF.Identity, scale=cp[:, 1:2], bias=cp[:, 3:4]), [pb, B_c], [ub])
                for (s0_, n) in segs:
                    V(lambda: nc.vector.scalar_tensor_tensor(out=u[:, s0_ + 1:s0_ + n], in0=ps[:, s0_:s0_ + n - 1], scalar=cp[:, 0:1],
                                                             in1=u[:, s0_ + 1:s0_ + n], op0=ALU.mult, op1=ALU.add), [pb, B_c, ub], [ub])
                    V(lambda: nc.vector.scalar_tensor_tensor(out=u[:, s0_:s0_ + n - 1], in0=ps[:, s0_ + 1:s0_ + n], scalar=cp[:, 2:3],
                                                             in1=u[:, s0_:s0_ + n - 1], op0=ALU.mult, op1=ALU.add), [pb, B_c, ub], [ub])
                if g == "S":
                    V(lambda: nc.vector.scalar_tensor_tensor(out=u[:, 0:1], in0=hps[:, 0:1], scalar=cp[:, 0:1], in1=u[:, 0:1],
                                                             op0=ALU.mult, op1=ALU.add), [hpb, B_c, ub], [ub])
                    V(lambda: nc.vector.scalar_tensor_tensor(out=u[:, T - 1:T], in0=hps[:, 1:2], scalar=cp[:, 2:3], in1=u[:, T - 1:T],
                                                             op0=ALU.mult, op1=ALU.add), [hpb, B_c, ub], [ub])
                return u, ub

            with extra_slots(2):
                for ti in range(11):
                    ncol = 512 if ti < 10 else 384
                    wa, wab = wload(w_up[l, :, ti * 512:ti * 512 + ncol], 16, ncol, key=("up", l, ti * 512))
                    wg, wgb = wload(w_up[l, :, DFF + ti * 512:DFF + ti * 512 + ncol], 16, ncol, key=("up", l, DFF + ti * 512))
                    for q in range(ncol // 128):
                        j = ti * 4 + q
                        res = []
                        for (wv, wb, ch) in ((wa, wab, j), (wg, wgb, 43 + j)):
                            ps, pb = ps_short()
                            for c in range(16):
                                mm(ps[:], wv[:, c, q * 128:(q + 1) * 128], h_sb[:, c, :], c == 0, c == 15, [wb, B_h], [pb])
                            hps, hpb = None, None
                            if g == "S":
                                hps, hpb = ps_short()
                                for c in range(16):
                                    mm(hps[:, 0:2], wv[:, c, q * 128:(q + 1) * 128], hh_sb[:, c, :], c == 0, c == 15, [wb, B_hh], [hpb])
                            res.append(conv_chunk(ps, pb, hps, hpb, ch))
                        (ua, uab), (ug, ugb) = res
                        A(lambda: nc.scalar.activation(out=ug[:], in_=ug[:], func=AF.Silu), [ugb], [ugb])
                        V(lambda: nc.vector.tensor_tensor(out=actb[:, j, :], in0=ug[:], in1=ua[:], op=ALU.mult), [ugb, uab], [B_act])
            def pf_next():
                nl = l if g == "P" else l + 1
                if nl < L:
                    for c0_ in (4224, 4736):
                        prefetch(("w_in", nl, c0_), w_in[nl, :, c0_:c0_ + 512], 16, 512)
            residual_proj(l, w_down[l], 43, 128, lambda c: actb[:, c, :], [B_act], 5, row, sf, "d", after=pf_next)
            layernorm(l, 1, sf)
            fw.barrier()
        if l == L - 1:
            fw.dma("sp", yout[g].rearrange("(c p) t -> p c t", p=128), x_sb[:], reads=[B_x], writes=[B_out])
        else:
            fw.dma("sp", xspill[g].rearrange("(c p) t -> p c t", p=128), x_sb[:], reads=[B_x], writes=[B_spill[g]])

    nblk = 0
    try:
        for l in range(L):
            for g in ("P", "S"):
                if KSTOP.startswith("b") and nblk >= int(KSTOP[1]):
                    break
                if KSTOP.startswith("c") and nblk >= 1:
                    break
                block(l, g)
                nblk += 1
    except StopBuild:
        fw.barrier()
        fw.finish()
        return P
    fw.finish()
    P.es.close()
    fw.close()
    return P


def _consts():
    c = np.zeros((NCONST, 128, 128), np.float32)
    idx = np.arange(128)
    c[C_ID] = np.eye(128)
    c[C_ONE] = 1.0
    same = (idx[:, None] // 64) == (idx[None, :] // 64)
    c[C_TRIF] = (same & (idx[:, None] <= idx[None, :]))
    c[C_TRIB] = (same & (idx[:, None] >= idx[None, :]))
    for k, p in ((C_E0, 0), (C_E63, 63), (C_E64, 64), (C_E127, 127)):
        c[k][p, :] = 1.0
    dd = idx % 32
    partner = np.where(dd < 16, idx + 16, idx - 16)
    c[C_PERM][partner, idx] = 1.0
    negm = np.zeros((2, 128, 5, 128), np.float32)
    negm[0] = np.where(c[C_TRIF].T[:, None, :] > 0, 0.0, NEG)
    negm[1] = np.where(c[C_TRIB].T[:, None, :] > 0, 0.0, NEG)
    return (np.ascontiguousarray(c.transpose(1, 0, 2)).reshape(128, NCONST * 128),
            np.ascontiguousarray(negm.transpose(1, 0, 2, 3)).reshape(128, 2 * 640))


def _rope_tables(j):
    t = np.arange(T) + j * T
    rows = (t // 64).astype(np.float32)
    cols = (t % 64).astype(np.float32)
    freqs = (np.float32(10000.0) ** (-np.arange(0, 32, 2, dtype=np.float32) / np.float32(32))).astype(np.float32)
    p = np.arange(128)
    dd = p % 64
    idx = dd % 32
    f = idx % 16
    first = idx < 16
    pos = np.where((dd < 32)[:, None], rows[None, :], cols[None, :]).astype(np.float32)
    ang = (pos * freqs[f][:, None]).astype(np.float32)
    cos = np.cos(ang).astype(np.float32)
    sin = np.sin(ang).astype(np.float32)
    sins = np.where(first[:, None], -sin, sin).astype(np.float32)
    return np.concatenate([cos, sins], axis=1)


def _natten_bias(a_rpb, j):
    kt = np.arange(2048)
    krow, kcol = kt // 64, kt % 64
    qt = np.arange(T) + j * T
    qrow, qcol = qt // 64, qt % 64
    rs = np.clip(qrow - 4, 0, 24)
    cs = np.clip(qcol - 8, 0, 48)
    vr = (krow[:, None] >= rs[None, :]) & (krow[:, None] < rs[None, :] + 8)
    vcm = (kcol[:, None] >= cs[None, :]) & (kcol[:, None] < cs[None, :] + 16)
    valid = vr & vcm
    ri = np.clip(7 + krow[:, None] - qrow[None, :], 0, 14)
    ci = np.clip(kcol[:, None] - qcol[None, :] + 15, 0, 30)
    out = np.empty((L, 6, 2048, T), np.float32)
    for l in range(L):
        for h in range(6):
            out[l, h] = np.where(valid, a_rpb[l, h][ri, ci], np.float32(NEG))
    return out.reshape(L, 6, 16, 128, T)


def _combine_tables(j):
    cf = np.zeros((2, 5, 4), np.float32)
    vt = np.zeros((2, 5), np.float32)
    for r2 in range(4):
        if r2 < j:
            cf[0, 0, r2] = 1
        if r2 > j:
            cf[1, 0, r2] = 1
    for r in range(4):
        vt[0, 1 + r] = 0.0 if r < j else NEG
        vt[1, 1 + r] = 0.0 if r > j else NEG
        for r2 in range(4):
            if r < r2 < j:
                cf[0, 1 + r, r2] = 1
            if j < r2 < r:
                cf[1, 1 + r, r2] = 1
    cft = np.broadcast_to(cf[None, :, :, :, None], (128, 2, 5, 4, 5)).reshape(128, -1)
    vtt = np.broadcast_to(vt[None, :, :, None], (128, 2, 5, 5)).reshape(128, -1)
    sel = np.zeros((8,), np.float32)
    if j > 0:
        sel[j - 1] = 1
    if j < 3:
        sel[4 + j + 1] = 1
    return np.ascontiguousarray(cft), np.ascontiguousarray(vtt), np.ascontiguousarray(np.broadcast_to(sel[None], (128, 8)))


_PROG = None


def kernel(x_prompt, x_sample, cache_a_k, cache_a_v, cache_b_k, cache_b_v, state_c_C, state_c_n, state_c_m,
           c, c_ctx, w_mod, b_mod, w_in, c_gate_b, a_rpb, b_lambda, b_subln, c_norm, w_out,
           ln1_g, ln1_b, ln2_g, ln2_b, w_up, conv_w, conv_b, w_down):
    global _PROG
    f = lambda a: np.ascontiguousarray(np.asarray(a, dtype=np.float32))
    x_prompt, x_sample = f(x_prompt), f(x_sample)
    if _PROG is None:
        _PROG = build_program()
    P = _PROG
    consts, negm = _consts()
    lnp = np.stack([f(ln1_g), f(ln1_b), f(ln2_g), f(ln2_b)], 1).reshape(L, 4, 16, 128).transpose(3, 0, 1, 2).reshape(128, -1)
    cw = np.concatenate([f(conv_w), f(conv_b)[:, None, :]], 1)
    convp = cw.reshape(L, 4, 86, 128).transpose(3, 0, 2, 1).reshape(128, -1)
    shared = {
        "w_in": f(w_in), "w_out": f(w_out), "w_up": f(w_up), "w_down": f(w_down),
        "lnp": np.ascontiguousarray(lnp), "convp": np.ascontiguousarray(convp),
        "gateb": f(c_gate_b).reshape(-1), "blam": f(b_lambda).reshape(-1),
        "subln": np.ascontiguousarray(f(b_subln).T), "cnorm": f(c_norm).reshape(-1),
        "consts": consts, "negm": negm,
    }
    w_mod, b_mod = f(w_mod), f(b_mod)
    in_maps = []
    for i in range(8):
        b, j = i // 4, i % 4
        m = dict(shared)
        m["xpT"] = np.ascontiguousarray(x_prompt[2 * i:2 * i + 2].reshape(T, D).T)
        m["xsT"] = np.ascontiguousarray(x_sample[b, j * T:(j + 1) * T].T)
        m["wmod"] = np.ascontiguousarray(w_mod[:, :, j * 3072:(j + 1) * 3072])
        m["bmod"] = np.ascontiguousarray(b_mod[:, j * 3072:(j + 1) * 3072].reshape(L, 24, 128).transpose(2, 0, 1).reshape(128, -1))
        cond = np.stack([f(c_ctx), f(c)[b]], 1)
        m["cond2"] = np.ascontiguousarray(cond.reshape(16, 128, 2).transpose(1, 0, 2).reshape(128, 32))
        m["rope"] = _rope_tables(j)
        m["nbias"] = _natten_bias(f(a_rpb), j)
        m["cakT"] = np.ascontiguousarray(f(cache_a_k)[b].transpose(0, 1, 3, 2))
        m["cav"] = f(cache_a_v)[b]
        m["cbkT"] = np.ascontiguousarray(f(cache_b_k)[b].transpose(0, 1, 3, 2))
        m["cbv"] = f(cache_b_v)[b]
        m["cC"] = f(state_c_C)[b]
        m["cn"] = np.ascontiguousarray(f(state_c_n)[b].reshape(L, 10, 128).transpose(0, 2, 1))
        m["cm"] = f(state_c_m)[b].reshape(-1)
        m["cftab"], m["vtab"], m["sel"] = _combine_tables(j)
        in_maps.append(m)
    in_maps = [{k: v for k, v in m.items() if k in P.din} for m in in_maps]
    res = run_bass_kernel_spmd(P.nc, in_maps, core_ids=list(range(8))).results
    yp = np.stack([r["ypT"].T.reshape(2, 256, D) for r in res], 0).reshape(16, 256, D)
    ys = np.stack([r["ysT"].T for r in res], 0).reshape(2, 4 * T, D)
    cat = lambda k: np.concatenate([r[k] for r in res], 0)
    n_ak = np.ascontiguousarray(cat("o_ak").transpose(0, 1, 2, 4, 3))
    n_av = cat("o_av")
    n_bk = np.ascontiguousarray(cat("o_bk").transpose(0, 1, 2, 4, 3))
    n_bv = cat("o_bv")
    return (np.ascontiguousarray(yp, dtype=np.float32), np.ascontiguousarray(ys, dtype=np.float32), n_ak, n_av, n_bk, n_bv,
            cat("o_cC"), np.ascontiguousarray(cat("o_cn").transpose(0, 1, 2, 4, 3)), cat("o_cm"))
```

```python
import math
import os
KSTOP = os.environ.get('KSTOP', '')
KSKIP = os.environ.get('KSKIP', '')
SKIP_IN = set()
if KSTOP.startswith('c'):
    SKIP_IN = {"nbias", "cakT", "cav", "cbkT", "cbv", "cC", "cn", "cm", "xsT"}
    if not KSTOP[1].isdigit() or int(KSTOP[1]) < 8:
        SKIP_IN |= {"w_up", "w_down"}
    if not KSTOP[1].isdigit() or int(KSTOP[1]) < 7:
        SKIP_IN |= {"w_out"}


class StopBuild(Exception):
    pass


def kstop(tag):
    if KSTOP == tag:
        raise StopBuild(tag)
from contextlib import ExitStack
import numpy as np
import ml_dtypes
import concourse.bass as bass
import concourse.mybir as mybir
from concourse.bass_utils import run_bass_kernel_spmd

F32 = mybir.dt.float32
BF16 = mybir.dt.bfloat16
AF = mybir.ActivationFunctionType
ALU = mybir.AluOpType
AX = mybir.AxisListType

L = 2
D = 2048
T = 512
DFF = 5504
NEG = -30000.0
ALPHA = (2 * L) ** 0.25
LN_EPS = 1e-5
RMS_EPS = 1e-6
RG = [[0, 1, 2, 3], [4, 5, 6, 7]]
NCONST = 11
C_ID, C_ONE, C_TRIF, C_TRIB, C_E0, C_E63, C_E64, C_E127, C_PERM, C_HA, C_HB = range(NCONST)


class Buf:
    __slots__ = ("name", "w", "r")

    def __init__(self, name=""):
        self.name = name
        self.w = None
        self.r = []


class FW:
    NDMA = 48

    def __init__(self, nc):
        self.nc = nc
        self.eng = {"pe": nc.tensor, "act": nc.scalar, "dve": nc.vector, "pool": nc.gpsimd, "sp": nc.sync}
        self.sem, self.cnt, self._stack = {}, {}, []
        for e in self.eng:
            cm = nc.semaphore("s_" + e)
            self.sem[e] = cm.__enter__()
            self._stack.append(cm)
            self.cnt[e] = 0
        self.dsem, self.dcnt = [], []
        for i in range(self.NDMA):
            cm = nc.semaphore("d_%d" % i)
            self.dsem.append(cm.__enter__())
            self._stack.append(cm)
            self.dcnt.append(0)
        self.dnext = 0
        self.seen = {e: {} for e in self.eng}
        self.ninst = 0
        self.nwaits = 0

    def close(self):
        for cm in reversed(self._stack):
            cm.__exit__(None, None, None)

    def _semobj(self, key):
        return self.sem[key] if isinstance(key, str) else self.dsem[key]

    def _wait(self, e, tok):
        if tok is None:
            return
        key, val = tok
        if self.seen[e].get(key, 0) >= val:
            return
        self.eng[e].wait_ge(self._semobj(key), val)
        self.seen[e][key] = val
        self.nwaits += 1

    def _deps(self, e, reads, writes, is_dma=False):
        for b in reads:
            if b.w is not None:
                if b.w[0] == e and (e == "pe" or is_dma):
                    continue
                self._wait(e, b.w)
        skip_same = (e == "pe" or is_dma)
        for b in writes:
            if b.w is not None and not (skip_same and b.w[0] == e):
                self._wait(e, b.w)
            for t in b.r:
                if not (skip_same and t[0] == e):
                    self._wait(e, t)

    def _commit(self, tok, reads, writes):
        for b in reads:
            b.r.append(tok)
            if len(b.r) > 16:
                best = {}
                for k, v in b.r:
                    if best.get(k, 0) < v:
                        best[k] = v
                b.r = list(best.items())
        for b in writes:
            b.w = tok
            b.r = []

    def op(self, e, fn, reads=(), writes=(), inc=True):
        self._deps(e, reads, writes)
        ins = fn()
        self.ninst += 1
        if inc:
            self.cnt[e] += 1
            ins.then_inc(self.sem[e], 1)
            tok = (e, self.cnt[e])
        else:
            tok = (e, self.cnt[e] + 1)
        self._commit(tok, reads, writes)
        return tok

    def _next_dsem(self, q, kind=None):
        kind = kind or q
        lo, hi = {"sp": (0, 32), "pool": (32, 44), "cc": (44, 48)}[kind]
        if not hasattr(self, "dnx"):
            self.dnx = {}
        i = self.dnx.get(kind, lo)
        self.dnx[kind] = lo + (i + 1 - lo) % (hi - lo)
        if self.dcnt[i] > 0:
            self._wait(q, (i, self.dcnt[i]))
        return i

    def dma(self, q, out, in_, reads=(), writes=(), slow=False):
        self._deps(q, reads, writes, is_dma=True)
        i = self._next_dsem(q)
        if slow:
            ins = self.eng[q].dma_start(out=out, in_=in_, allow_slow_non_contiguous=True)
        else:
            ins = self.eng[q].dma_start(out=out, in_=in_)
        self.dcnt[i] += 16
        ins.then_inc(self.dsem[i], 16)
        self.ninst += 1
        tok = (i, self.dcnt[i])
        self._commit(tok, reads, writes)
        return tok

    def allgather(self, in_ap, out_ap, reads=(), writes=()):
        q = "pool"
        self._deps(q, reads, writes, is_dma=True)
        i = self._next_dsem(q, "cc")
        ins = self.nc.gpsimd.collective_compute("AllGather", ALU.bypass, replica_groups=RG, ins=[in_ap], outs=[out_ap])
        self.dcnt[i] += 1
        ins.then_inc(self.dsem[i], 1)
        self.ninst += 1
        tok = (i, self.dcnt[i])
        self._commit(tok, reads, writes)
        return tok

    def soft_barrier(self):
        for e in self.eng:
            if e != "pe" and self.cnt["pe"] > 0:
                self._wait(e, ("pe", self.cnt["pe"]))
            for i in range(32, 44):
                if self.dcnt[i] > 0:
                    self._wait(e, (i, self.dcnt[i]))

    def barrier(self):
        for e in self.eng:
            for f in self.eng:
                if f != e and self.cnt[f] > 0:
                    self._wait(e, (f, self.cnt[f]))
            for i in range(self.NDMA):
                if self.dcnt[i] > 0:
                    self._wait(e, (i, self.dcnt[i]))

    def finish(self):
        for i in range(self.NDMA):
            if self.dcnt[i] > 0:
                self._wait("sp", (i, self.dcnt[i]))


class Prog:
    def __init__(self):
        nc = bass.Bass("TRN2", target_bir_lowering=False)
        self.nc = nc
        self.fw = FW(nc)
        self.es = ExitStack()
        self.ring_idx = {}
        self.din = {}
        self.dout = {}

    def inp(self, name, shape, dt=F32):
        if name in SKIP_IN:
            return None
        t = self.nc.dram_tensor(name, list(shape), dt, kind="ExternalInput").ap()
        self.din[name] = t
        return t

    def outp(self, name, shape, dt=F32):
        t = self.nc.dram_tensor(name, list(shape), dt, kind="ExternalOutput").ap()
        self.dout[name] = t
        return t

    def scratch(self, name, shape, dt=F32):
        return self.nc.dram_tensor(name, list(shape), dt).ap()

    def sb(self, es, name, shape, dt=F32):
        self.uid = getattr(self, "uid", 0) + 1
        t = es.enter_context(self.nc.sbuf_tensor("%s_%d" % (name, self.uid), list(shape), dt))
        return t, Buf(name)

    def ring(self, es, name, n, shape, dt=F32):
        items = [self.sb(es, "%s%d" % (name, i), shape, dt) for i in range(n)]
        key = name
        self.ring_idx[key] = 0

        def nxt():
            i = self.ring_idx[key]
            self.ring_idx[key] = (i + 1) % n
            return items[i]
        return nxt

    def V(self, fn, reads=(), writes=()):
        return self.fw.op("dve", fn, reads, writes)

    def A(self, fn, reads=(), writes=()):
        return self.fw.op("act", fn, reads, writes)

    def G(self, fn, reads=(), writes=()):
        return self.fw.op("pool", fn, reads, writes)

    def PE(self, fn, reads=(), writes=(), inc=True):
        return self.fw.op("pe", fn, reads, writes, inc=inc)

    def mm(self, out, lhsT, rhs, start, stop, reads, writes, inc=None, sgc=False):
        nc = self.nc
        return self.fw.op("pe", lambda: nc.tensor.matmul(out, lhsT=lhsT, rhs=rhs, start=start, stop=stop, skip_group_check=sgc),
                          reads, writes, inc=(stop if inc is None else inc))


def build_program():
    P = Prog()
    nc, fw = P.nc, P.fw
    V, A, PE, mm, G = P.V, P.A, P.PE, P.mm, P.G

    xin = {"P": P.inp("xpT", [D, T]), "S": P.inp("xsT", [D, T])}
    w_in = P.inp("w_in", [L, D, 6804])
    w_out = P.inp("w_out", [L, D, D])
    w_up = P.inp("w_up", [L, D, 2 * DFF])
    w_down = P.inp("w_down", [L, DFF, D])
    wmod = P.inp("wmod", [L, D, 3072])
    bmod = P.inp("bmod", [128, L * 24])
    cond2 = P.inp("cond2", [128, 32])
    lnp_d = P.inp("lnp", [128, L * 4 * 16])
    convp_d = P.inp("convp", [128, L * 86 * 4])
    gateb_d = P.inp("gateb", [L * 20])
    blam_d = P.inp("blam", [L * 256])
    subln_d = P.inp("subln", [128, L])
    cnorm_d = P.inp("cnorm", [L * 128])
    consts_d = P.inp("consts", [128, NCONST * 128])
    negm_d = P.inp("negm", [128, 2 * 640])
    rope_d = P.inp("rope", [128, 2 * T])
    nbias_d = P.inp("nbias", [L, 6, 16, 128, T])
    cakT = P.inp("cakT", [L, 6, 128, 512])
    cav = P.inp("cav", [L, 6, 512, 128])
    cbkT = P.inp("cbkT", [L, 5, 128, 512])
    cbv = P.inp("cbv", [L, 5, 512, 128])
    cC_d = P.inp("cC", [L, 2, 5, 128, 128])
    cn_d = P.inp("cn", [L, 128, 10])
    cm_d = P.inp("cm", [L * 10])
    cftab_d = P.inp("cftab", [128, 2 * 5 * 4 * 5])
    vtab_d = P.inp("vtab", [128, 2 * 5 * 5])
    sel_d = P.inp("sel", [128, 8])

    yout = {"P": P.outp("ypT", [D, T]), "S": P.outp("ysT", [D, T])}
    o_ak = P.outp("o_ak", [2, L, 6, 128, 256])
    o_av = P.outp("o_av", [2, L, 6, 256, 128])
    o_bk = P.outp("o_bk", [2, L, 5, 128, 256])
    o_bv = P.outp("o_bv", [2, L, 5, 256, 128])
    o_cC = P.outp("o_cC", [2, L, 2, 5, 128, 128])
    o_cn = P.outp("o_cn", [2, L, 2, 128, 5])
    o_cm = P.outp("o_cm", [2, L, 2, 5])
    B_out = Buf("outputs")

    xspill = {"P": P.scratch("xspP", [D, T]), "S": P.scratch("xspS", [D, T])}
    B_spill = {"P": Buf(), "S": Buf()}
    mg_in = P.scratch("mg_in", [128, 96]); mg_out = P.scratch("mg_out", [512, 96])
    B_mgi, B_mgo = Buf(), Buf()
    CH_NH = [4, 4, 3]
    HH_CH = [0, 0, 0, 0, 1, 1, 1, 1, 2, 2, 2]
    HH_IX = [0, 1, 2, 3, 0, 1, 2, 3, 0, 1, 2]
    bnc_in = [P.scratch("bnc_in%d" % i, [2 * n * 128, T], BF16) for i, n in enumerate(CH_NH)]
    bnc_out = [P.scratch("bnc_out%d" % i, [4 * 2 * n * 128, T], BF16) for i, n in enumerate(CH_NH)]
    B_bi = [Buf() for _ in CH_NH]
    B_bo = [Buf() for _ in CH_NH]

    def bnc_k_rows(hh):
        i = HH_IX[hh]
        return bnc_in[HH_CH[hh]][i * 128:(i + 1) * 128, :], B_bi[HH_CH[hh]]

    def bnc_v_rows(hh):
        c = HH_CH[hh]
        i = CH_NH[c] + HH_IX[hh]
        return bnc_in[c][i * 128:(i + 1) * 128, :], B_bi[c]
    CSF = 1310
    cs_in = P.scratch("cs_in", [128, CSF]); cs_out = P.scratch("cs_out", [512, CSF])
    B_csi, B_cso = Buf(), Buf()
    hb_in = P.scratch("hb_in", [128, 32]); hb_out = P.scratch("hb_out", [512, 32])
    B_hbi, B_hbo = Buf(), Buf()

    es = P.es
    x_sb, B_x = P.sb(es, "x_sb", [128, 16, T], F32)
    h_sb, B_h = P.sb(es, "h_sb", [128, 16, T], BF16)
    hh_sb, B_hh = P.sb(es, "hh_sb", [128, 16, 2], BF16)
    WSL = 8704
    wslots = [P.sb(es, "wr%d" % i, [128, WSL], BF16) for i in range(2)]
    wstate = {"i": 0}

    def wring():
        i = wstate["i"] % len(wslots)
        wstate["i"] += 1
        return wslots[i]

    class extra_slots:
        def __init__(self, want, reserve=2048):
            self.want, self.reserve = want, reserve

        def __enter__(self):
            self.sx = ExitStack()
            self.n = 0
            while self.n < self.want and nc.sbuf_bytes_remaining >= WSL * 2 + self.reserve + 256:
                wslots.append(P.sb(self.sx, "wx", [128, WSL], BF16))
                self.n += 1
            return self

        def __exit__(self, *a):
            if a[0] is None:
                fw.soft_barrier()
                for _ in range(self.n):
                    wslots.pop()
                self.sx.close()
            return False
    cst, B_c = P.sb(es, "cst", [128, NCONST, 128], F32)
    negm, _ = P.sb(es, "negm", [128, 2, 640], F32)
    rope, _ = P.sb(es, "rope", [128, 2, T], F32)
    ones_bf, _ = P.sb(es, "ones_bf", [128, 128], BF16)
    id_bf, _ = P.sb(es, "id_bf", [128, 128], BF16)
    tri_bf, _ = P.sb(es, "tri_bf", [128, 2, 128], BF16)
    lnp, _ = P.sb(es, "lnp", [128, L, 4, 16], F32)
    convp, _ = P.sb(es, "convp", [128, L, 86, 4], F32)
    gateb, _ = P.sb(es, "gateb", [128, L, 20], F32)
    blam, _ = P.sb(es, "blam", [128, L, 4, 64], F32)
    subln, _ = P.sb(es, "subln", [128, L], F32)
    cnorm, _ = P.sb(es, "cnorm", [128, L, 128], F32)
    cftab, _ = P.sb(es, "cftab", [128, 2, 5, 4, 5], F32)
    vtab, _ = P.sb(es, "vtab", [128, 2, 5, 5], F32)
    sel, _ = P.sb(es, "sel", [128, 8], F32)
    modv, B_modv = P.sb(es, "modv", [128, L, 96, 2], F32)
    nlam, B_nlam = P.sb(es, "nlam", [128, L], F32)
    sublns, _ = P.sb(es, "sublns", [128, L], F32)

    def CM(i):
        return cst[:, i, :]

    pbanks = [es.enter_context(nc.psum_tensor("ps%d" % i, [128, 512], F32)) for i in range(8)]
    pbufs = [Buf("ps%d" % i) for i in range(8)]
    pidx = {"s": 0, "l": 0}

    def ps_short():
        i = pidx["s"]
        pidx["s"] = (i + 1) % 5
        return pbanks[i], pbufs[i]

    def ps_long():
        i = pidx["l"]
        pidx["l"] = (i + 1) % 3
        return pbanks[5 + i], pbufs[5 + i]

    def bcast(ap1d, n):
        return bass.AP(ap1d.tensor, 0, [[0, 128], [1, n]])

    fw.dma("sp", cst[:], consts_d.rearrange("p (k n) -> p k n", k=NCONST), writes=[B_c])
    fw.dma("sp", negm[:], negm_d.rearrange("p (k n) -> p k n", k=2), writes=[B_c])
    fw.dma("sp", rope[:], rope_d.rearrange("p (k n) -> p k n", k=2), writes=[B_c])
    fw.dma("sp", lnp[:], lnp_d.rearrange("p (l k c) -> p l k c", l=L, k=4), writes=[B_c])
    fw.dma("sp", convp[:], convp_d.rearrange("p (l c k) -> p l c k", l=L, k=4), writes=[B_c])
    fw.dma("sp", gateb[:], bcast(gateb_d, L * 20).rearrange("p (l k) -> p l k", l=L), writes=[B_c])
    fw.dma("sp", blam[:], bcast(blam_d, L * 256).rearrange("p (l k c) -> p l k c", l=L, k=4), writes=[B_c])
    fw.dma("sp", subln[:], subln_d, writes=[B_c])
    fw.dma("sp", cnorm[:], bcast(cnorm_d, L * 128).rearrange("p (l k) -> p l k", l=L), writes=[B_c])
    fw.dma("sp", cftab[:], cftab_d.rearrange("p (d i r h) -> p d i r h", d=2, i=5, r=4), writes=[B_c])
    fw.dma("sp", vtab[:], vtab_d.rearrange("p (d i h) -> p d i h", d=2, i=5), writes=[B_c])
    fw.dma("sp", sel[:], sel_d, writes=[B_c])
    A(lambda: nc.scalar.copy(out=ones_bf[:], in_=CM(C_ONE)), [B_c], [B_c])
    A(lambda: nc.scalar.copy(out=id_bf[:], in_=CM(C_ID)), [B_c], [B_c])
    A(lambda: nc.scalar.copy(out=tri_bf[:, 0, :], in_=CM(C_TRIF)), [B_c], [B_c])
    A(lambda: nc.scalar.copy(out=tri_bf[:, 1, :], in_=CM(C_TRIB)), [B_c], [B_c])

    pre = {}

    def wload(src2d, kc, ncols, key=None):
        if key is not None and key in pre:
            return pre.pop(key)
        return _wload(src2d, kc, ncols)

    def prefetch(key, src2d, kc, ncols):
        pre[key] = _wload(src2d, kc, ncols)

    def _wload(src2d, kc, ncols):
        t, b = wring()
        view = t[:, 0:kc * ncols].rearrange("p (c n) -> p c n", n=ncols)
        srcv = src2d.rearrange("(c p) n -> p c n", p=128)
        step = max(1, 2048 // 128 // 1 if ncols >= 256 else 8)
        step = 16 if ncols >= 256 else 22
        for c0 in range(0, kc, step):
            c1 = min(kc, c0 + step)
            fw.dma("pool", view[:, c0:c1, :], srcv[:, c0:c1, :], writes=[b])
        return view, b

    with ExitStack() as s0:
        c2, B_c2 = P.sb(s0, "c2", [128, 16, 2], F32)
        c2b, _ = P.sb(s0, "c2b", [128, 16, 2], BF16)
        bm, B_bm = P.sb(s0, "bm", [128, L, 24], F32)
        mloc, B_ml = P.sb(s0, "mloc", [128, L, 24, 2], F32)
        mall, B_ma = P.sb(s0, "mall", [128, 4, L, 24, 2], F32)
        fw.dma("sp", c2[:], cond2.rearrange("p (c r) -> p c r", r=2), writes=[B_c2])
        fw.dma("sp", bm[:], bmod.rearrange("p (l c) -> p l c", l=L), writes=[B_bm])
        A(lambda: nc.scalar.activation(out=c2b[:], in_=c2[:], func=AF.Silu), [B_c2], [B_c2])
        with extra_slots(3):
            for l in range(L):
                for t4 in range(6):
                    wv, wb = wload(wmod[l, :, t4 * 512:(t4 + 1) * 512], 16, 512)
                    for q in range(4):
                        cc = t4 * 4 + q
                        ps, pb = ps_short()
                        for c in range(16):
                            mm(ps[:, 0:2], wv[:, c, q * 128:(q + 1) * 128], c2b[:, c, :], c == 0, c == 15, [wb, B_c2], [pb])
                        A(lambda: nc.scalar.activation(out=mloc[:, l, cc, :], in_=ps[:, 0:2], func=AF.Identity,
                                                       bias=bm[:, l, cc:cc + 1], scale=1.0), [pb, B_bm], [B_ml])
        fw.dma("sp", mg_in, mloc[:].rearrange("p l c r -> p (l c r)"), reads=[B_ml], writes=[B_mgi])
        fw.allgather(mg_in, mg_out, reads=[B_mgi], writes=[B_mgo])
        fw.dma("sp", mall[:].rearrange("p r l c w -> p r (l c w)"), mg_out.rearrange("(r p) f -> p r f", p=128),
               reads=[B_mgo], writes=[B_ma])
        for r in range(4):
            for l in range(L):
                A(lambda: nc.scalar.copy(out=modv[:, l, r * 24:(r + 1) * 24, :], in_=mall[:, r, l, :, :]), [B_ma], [B_modv])
        for l in range(L):
            for v0 in (16, 64):
                A(lambda: nc.scalar.activation(out=modv[:, l, v0:v0 + 16, :], in_=modv[:, l, v0:v0 + 16, :], func=AF.Identity, bias=1.0, scale=1.0),
                  [B_modv], [B_modv])
        lt, B_lt = P.sb(s0, "lt", [128, 64], F32)
        ld, B_ld = P.sb(s0, "ld", [128, 4], F32)
        for l in range(L):
            lam_init = 0.8 - 0.6 * math.exp(-0.3 * l)
            for k in range(2):
                V(lambda: nc.vector.tensor_tensor(out=lt[:], in0=blam[:, l, 2 * k, :], in1=blam[:, l, 2 * k + 1, :], op=ALU.mult), [B_c], [B_lt])
                V(lambda: nc.vector.reduce_sum(out=ld[:, k:k + 1], in_=lt[:], axis=AX.X), [B_lt], [B_ld])
            A(lambda: nc.scalar.activation(out=ld[:, 2:4], in_=ld[:, 0:2], func=AF.Exp), [B_ld], [B_ld])
            V(lambda: nc.vector.tensor_tensor(out=nlam[:, l:l + 1], in0=ld[:, 3:4], in1=ld[:, 2:3], op=ALU.subtract), [B_ld], [B_nlam])
            A(lambda: nc.scalar.activation(out=nlam[:, l:l + 1], in_=nlam[:, l:l + 1], func=AF.Identity, bias=-lam_init, scale=1.0), [B_nlam], [B_nlam])
            A(lambda: nc.scalar.activation(out=sublns[:, l:l + 1], in_=subln[:, l:l + 1], func=AF.Identity, scale=1.0 - lam_init), [B_c], [B_nlam])
        fw.barrier()

    def modp(l, v, fc, row):
        return modv[:, l, v * 16 + fc, row:row + 1]

    def modulate(l, vsh, vsc, row):
        for c in range(16):
            A(lambda: nc.scalar.activation(out=h_sb[:, c, :], in_=x_sb[:, c, :], func=AF.Identity,
                                           bias=modp(l, vsh, c, row), scale=modp(l, vsc, c, row)), [B_x, B_modv], [B_h])

    def layernorm(l, k, scope):
        sq_ring = P.ring(scope, "lnsq%d" % k, 2, [128, T], F32)
        st, B_st = P.sb(scope, "lnst%d" % k, [128, 2, T], F32)
        p1, b1 = ps_long()
        p2, b2 = ps_long()
        for c in range(16):
            sq, bq = sq_ring()
            A(lambda: nc.scalar.activation(out=sq[:], in_=x_sb[:, c, :], func=AF.Square), [B_x], [bq])
            mm(p1[:], CM(C_ONE), x_sb[:, c, :], c == 0, c == 15, [B_c, B_x], [b1], inc=True)
            mm(p2[:], CM(C_ONE), sq[:], c == 0, c == 15, [B_c, bq], [b2], inc=True)
        mean, var = st[:, 0, :], st[:, 1, :]
        A(lambda: nc.scalar.activation(out=mean, in_=p1[:], func=AF.Identity, scale=1.0 / D), [b1], [B_st])
        V(lambda: nc.vector.tensor_tensor(out=var, in0=mean, in1=mean, op=ALU.mult), [B_st], [B_st])
        V(lambda: nc.vector.scalar_tensor_tensor(out=var, in0=p2[:], scalar=1.0 / D, in1=var, op0=ALU.mult, op1=ALU.subtract), [b2, B_st], [B_st])
        A(lambda: nc.scalar.activation(out=var, in_=var, func=AF.Ln, bias=LN_EPS, scale=1.0), [B_st], [B_st])
        A(lambda: nc.scalar.activation(out=var, in_=var, func=AF.Exp, scale=-0.5), [B_st], [B_st])
        for c in range(16):
            V(lambda: nc.vector.tensor_tensor(out=x_sb[:, c, :], in0=x_sb[:, c, :], in1=mean, op=ALU.subtract), [B_x, B_st], [B_x])
            V(lambda: nc.vector.tensor_tensor(out=x_sb[:, c, :], in0=x_sb[:, c, :], in1=var, op=ALU.mult), [B_x, B_st], [B_x])
            A(lambda: nc.scalar.activation(out=x_sb[:, c, :], in_=x_sb[:, c, :], func=AF.Identity,
                                           bias=lnp[:, l, 2 * k + 1, c:c + 1], scale=lnp[:, l, 2 * k, c:c + 1]), [B_x, B_c], [B_x])

    def residual_proj(l, wsrc, kc, ncols_tile, rhs_fn, rhs_bufs, vgate, row, scope, tag, after=None):
        tmp_ring = P.ring(scope, "rp" + tag, 2, [128, T], F32)
        per = ncols_tile // 128
        with extra_slots(5):
            for tcol in range(D // ncols_tile):
                wv, wb = wload(wsrc[:, tcol * ncols_tile:(tcol + 1) * ncols_tile], kc, ncols_tile, key=(tag, l, tcol))
                for q in range(per):
                    fc = tcol * per + q
                    ps, pb = ps_short()
                    for c in range(kc):
                        mm(ps[:], wv[:, c, q * 128:(q + 1) * 128], rhs_fn(c), c == 0, c == kc - 1, [wb] + rhs_bufs, [pb])
                    tmp, tb = tmp_ring()
                    A(lambda: nc.scalar.activation(out=tmp[:], in_=ps[:], func=AF.Identity, scale=modp(l, vgate, fc, row)), [pb, B_modv], [tb])
                    V(lambda: nc.vector.scalar_tensor_tensor(out=x_sb[:, fc, :], in0=x_sb[:, fc, :], scalar=ALPHA, in1=tmp[:],
                                                             op0=ALU.mult, op1=ALU.add), [B_x, tb], [B_x])
        if after is not None:
            after()

    def attention(groups, scale, et_ring, btmp_ring, finish):
        yps, yb = ps_long()
        dps, db = ps_long()
        ng = len(groups)
        for gi, grp in enumerate(groups):
            st, sb_ = ps_short()
            for si, s in enumerate(grp):
                mm(st[:, s["c0"]:s["c0"] + s["n"]], s["k"], s["q"], si == 0, True, [s["kb"], s["qb"]], [sb_], inc=(si == len(grp) - 1), sgc=True)
            et, eb = et_ring()
            bias_fn = grp[0].get("bias_fn")
            if bias_fn is not None:
                bias, bias_b = bias_fn()
                bt, btb = btmp_ring()
                V(lambda: nc.vector.scalar_tensor_tensor(out=bt[:], in0=st[:], scalar=scale, in1=bias, op0=ALU.mult, op1=ALU.add),
                  [sb_, bias_b], [btb])
                A(lambda: nc.scalar.activation(out=et[:], in_=bt[:], func=AF.Exp), [btb], [eb])
            else:
                A(lambda: nc.scalar.activation(out=et[:], in_=st[:], func=AF.Exp, scale=scale), [sb_], [eb])
            for si, s in enumerate(grp):
                mm(yps[:, s["c0"]:s["c0"] + s["n"]], s["v"], et[:, s["c0"]:s["c0"] + s["n"]],
                   gi == 0 and si == 0, gi == ng - 1, [s["vb"], eb], [yb], inc=False, sgc=True)
            mm(dps[:], ones_bf[:], et[:], gi == 0, gi == ng - 1, [B_c, eb], [db], inc=True)
        finish(yps, yb, dps, db)

    def block(l, g):
        row = 0 if g == "P" else 1
        lam_init = 0.8 - 0.6 * math.exp(-0.3 * l)
        xsrc = xin[g] if l == 0 else xspill[g]
        fw.dma("sp", x_sb[:], xsrc.rearrange("(c p) t -> p c t", p=128), reads=[B_spill[g]], writes=[B_x])
        modulate(l, 0, 1, row)
        with ExitStack() as sm:
            ycat, B_y = P.sb(sm, "ycat", [128, 16, T], BF16)
            with ExitStack() as sc:
                qtc, B_qtc = P.sb(sc, "qtc", [128, 5, T], BF16)
                ktc, B_ktc = P.sb(sc, "ktc", [128, 5, T], BF16)
                kcA, B_kcA = P.sb(sc, "kcA", [128, 4, 640], BF16)
                kcB, B_kcB = P.sb(sc, "kcB", [128, 4, 640], BF16)
                vc, B_vc = P.sb(sc, "vc", [128, 4, 640], BF16)
                sigoc, B_so = P.sb(sc, "sigoc", [128, 4, 640], F32)
                gat, B_gat = P.sb(sc, "gat", [128, 4, 20], F32)
                G(lambda: nc.gpsimd.memset(kcA[:], 0.0), [], [B_kcA])
                G(lambda: nc.gpsimd.memset(kcB[:], 0.0), [], [B_kcB])
                ctiles = [(4224, 512), (4736, 512), (5248, 512), (5760, 512), (6272, 532)]
                with extra_slots(2):
                    for (c0, ncol) in ctiles:
                        wv, wb = wload(w_in[l, :, c0:c0 + ncol], 16, ncol, key=("w_in", l, c0))
                        for q in range(min(4, ncol // 128)):
                            col = c0 + q * 128
                            k = col // 128
                            if 33 <= k <= 42:
                                ps, pb = ps_short()
                                for c in range(16):
                                    mm(ps[:], wv[:, c, q * 128:(q + 1) * 128], h_sb[:, c, :], c == 0, c == 15, [wb, B_h], [pb])
                                if k <= 37:
                                    A(lambda: nc.scalar.activation(out=qtc[:, k - 33, :], in_=ps[:], func=AF.Copy, scale=128.0 ** -0.5), [pb], [B_qtc])
                                else:
                                    A(lambda: nc.scalar.copy(out=ktc[:, k - 38, :], in_=ps[:]), [pb], [B_ktc])
                        segs = []
                        for (name, lo, hi) in (("kc", 4864, 5504), ("vc", 5504, 6144), ("oc", 6144, 6784), ("gc", 6784, 6804)):
                            a, b_ = max(lo, c0), min(hi, c0 + ncol)
                            if a < b_:
                                segs.append((name, a, b_, lo))
                        for (name, a, b_, lo) in segs:
                            for tt in range(4):
                                ps, pb = ps_short()
                                n = b_ - a
                                for c in range(16):
                                    mm(ps[:, 0:n], h_sb[:, c, tt * 128:(tt + 1) * 128], wv[:, c, a - c0:b_ - c0], c == 0, c == 15, [wb, B_h], [pb])
                                o0 = a - lo
                                if name == "kc":
                                    A(lambda: nc.scalar.copy(out=kcA[0:64, tt, o0:o0 + n], in_=ps[0:64, 0:n]), [pb], [B_kcA])
                                    A(lambda: nc.scalar.copy(out=kcB[64:128, tt, o0:o0 + n], in_=ps[64:128, 0:n]), [pb], [B_kcB])
                                elif name == "vc":
                                    A(lambda: nc.scalar.copy(out=vc[:, tt, o0:o0 + n], in_=ps[:, 0:n]), [pb], [B_vc])
                                elif name == "oc":
                                    A(lambda: nc.scalar.activation(out=sigoc[:, tt, o0:o0 + n], in_=ps[:, 0:n], func=AF.Sigmoid), [pb], [B_so])
                                else:
                                    V(lambda: nc.vector.tensor_tensor(out=gat[:, tt, :], in0=ps[:, 0:20], in1=gateb[:, l, :], op=ALU.add), [pb, B_c], [B_gat])
                for c0_ in (0, 512):
                    prefetch(("w_in", l, c0_), w_in[l, :, c0_:c0_ + 512], 16, 512)
                kstop("c1")
                bc, B_bc = P.sb(sc, "bc", [128, 2, 4, 10], F32)
                ea, B_ea = P.sb(sc, "ea", [128, 2, 4, 5], F32)
                bl, B_bl = P.sb(sc, "bl", [128, 2, 8, 10], F32)
                t5, B_t5 = P.sb(sc, "t5", [128, 4, 5], F32)
                vp, B_vp = P.sb(sc, "vp", [128, 2, 4, 5, 130], BF16)
                sgp = ExitStack()
                dg, B_dg = P.sb(sgp, "dg", [128, 5, 128], F32)
                mk, B_mk = P.sb(sgp, "mk", [128, 5, 128], F32)
                for d in range(2):
                    tri = CM(C_TRIF if d == 0 else C_TRIB)
                    for tt in range(4):
                        ig = gat[:, tt, d * 10:d * 10 + 5]
                        fg = gat[:, tt, d * 10 + 5:d * 10 + 10]
                        A(lambda: nc.scalar.activation(out=t5[:, 0, :], in_=fg, func=AF.Exp, scale=-1.0), [B_gat], [B_t5])
                        A(lambda: nc.scalar.activation(out=t5[:, 1, :], in_=t5[:, 0, :], func=AF.Ln, bias=1.0, scale=1.0), [B_t5], [B_t5])
                        ps, pb = ps_short()
                        mm(ps[:, 0:5], tri, t5[:, 1, :], True, True, [B_c, B_t5], [pb])
                        A(lambda: nc.scalar.copy(out=bc[:, d, tt, 0:5], in_=ps[:, 0:5]), [pb], [B_bc])
                        V(lambda: nc.vector.tensor_tensor(out=t5[:, 2, :], in0=ig, in1=bc[:, d, tt, 0:5], op=ALU.add), [B_gat, B_bc], [B_t5])
                        A(lambda: nc.scalar.activation(out=ea[:, d, tt, :], in_=t5[:, 2, :], func=AF.Exp), [B_t5], [B_ea])
                        for h in range(5):
                            A(lambda: nc.scalar.activation(out=dg[:, h, :], in_=CM(C_ID), func=AF.Identity, scale=t5[:, 2, h:h + 1]), [B_c, B_t5], [B_dg])
                        ps1, pb1 = ps_short()
                        mm(ps1[:, 0:384], CM(C_ONE), dg[:, 0:3, :].rearrange("p h s -> p (h s)"), True, True, [B_c, B_dg], [pb1])
                        ps2, pb2 = ps_short()
                        mm(ps2[:, 0:256], CM(C_ONE), dg[:, 3:5, :].rearrange("p h s -> p (h s)"), True, True, [B_c, B_dg], [pb2])
                        V(lambda: nc.vector.tensor_tensor(out=mk[:, 0:3, :].rearrange("p h s -> p (h s)"), in0=ps1[:, 0:384],
                                                          in1=negm[:, d, 0:384], op=ALU.add), [pb1, B_c], [B_mk])
                        V(lambda: nc.vector.tensor_tensor(out=mk[:, 3:5, :].rearrange("p h s -> p (h s)"), in0=ps2[:, 0:256],
                                                          in1=negm[:, d, 384:640], op=ALU.add), [pb2, B_c], [B_mk])
                        V(lambda: nc.vector.tensor_reduce(out=bc[:, d, tt, 5:10], in_=mk[:], axis=AX.X, op=ALU.max), [B_mk], [B_bc])
                        for X in range(2):
                            ep = (C_E63, C_E127)[X] if d == 0 else (C_E0, C_E64)[X]
                            ps, pb = ps_short()
                            mm(ps[:, 0:10], CM(ep), bc[:, d, tt, :], True, True, [B_c, B_bc], [pb])
                            A(lambda: nc.scalar.copy(out=bl[:, d, 2 * tt + X, :], in_=ps[:, 0:10]), [pb], [B_bl])
                        for h in range(5):
                            A(lambda: nc.scalar.activation(out=vp[:, d, tt, h, 0:128], in_=vc[:, tt, h * 128:(h + 1) * 128], func=AF.Identity,
                                                           scale=ea[:, d, tt, h:h + 1]), [B_vc, B_ea], [B_vp])
                        A(lambda: nc.scalar.copy(out=vp[:, d, tt, :, 128], in_=ea[:, d, tt, :]), [B_ea], [B_vp])

                fw.barrier()
                sgp.close()
                kstop("c2")
                mc, B_mc = P.sb(sc, "mc", [128, 2, 8, 5], F32)
                wold, B_wo = P.sb(sc, "wold", [128, 2, 8, 5], F32)
                snew, B_sn = P.sb(sc, "snew", [128, 2, 8, 5], F32)
                mcur, B_mcur = P.sb(sc, "mcur", [128, 2, 5], F32)
                mt, B_mt = P.sb(sc, "mt", [128, 2, 5], F32)
                nfacc, B_nf = P.sb(sc, "nfacc", [128, 2, 5], F32)
                cn, B_cn = P.sb(sc, "cn", [128, 10, 129], F32)
                cnb, B_cnb = P.sb(sc, "cnb", [128, 10, 130], BF16)
                tmpu_ring = P.ring(sc, "tmpu", 2, [128, 129], F32)

                def chunk_order(d, runs):
                    out = []
                    rr = runs if d == 0 else [list(reversed(r)) for r in reversed(runs)]
                    for r in rr:
                        out.append(r)
                    return out

                def mchain(d, run, m_init_fn):
                    m_init_fn(mcur[:, d, :])
                    for c in run:
                        A(lambda: nc.scalar.copy(out=mc[:, d, c, :], in_=mcur[:, d, :]), [B_mcur], [B_mc])
                        V(lambda: nc.vector.tensor_tensor(out=mt[:, 0, :], in0=mcur[:, d, :], in1=bl[:, d, c, 5:10], op=ALU.max), [B_mcur, B_bl], [B_mt])
                        V(lambda: nc.vector.tensor_tensor(out=mt[:, 1, :], in0=mcur[:, d, :], in1=mt[:, 0, :], op=ALU.subtract), [B_mcur, B_mt], [B_mt])
                        A(lambda: nc.scalar.activation(out=wold[:, d, c, :], in_=mt[:, 1, :], func=AF.Exp), [B_mt], [B_wo])
                        A(lambda: nc.scalar.activation(out=snew[:, d, c, :], in_=mt[:, 0, :], func=AF.Exp, scale=-1.0), [B_mt], [B_sn])
                        V(lambda: nc.vector.tensor_tensor(out=mcur[:, d, :], in0=mt[:, 0, :], in1=bl[:, d, c, 0:5], op=ALU.subtract), [B_mt, B_bl], [B_mcur])
                        V(lambda: nc.vector.tensor_tensor(out=nfacc[:, d, :], in0=nfacc[:, d, :], in1=bl[:, d, c, 0:5], op=ALU.add), [B_nf, B_bl], [B_nf])

                def state_update(d, h, c, need_bf=True):
                    tt, X = c // 2, c % 2
                    kk = kcA if X == 0 else kcB
                    kkb = B_kcA if X == 0 else B_kcB
                    ps, pb = ps_short()
                    mm(ps[:, 0:129], kk[:, tt, h * 128:(h + 1) * 128], vp[:, d, tt, h, 0:129], True, True, [kkb, B_vp], [pb])
                    tu, tub = tmpu_ring()
                    A(lambda: nc.scalar.activation(out=tu[:], in_=ps[:, 0:129], func=AF.Identity, scale=snew[:, d, c, h:h + 1]), [pb, B_sn], [tub])
                    V(lambda: nc.vector.scalar_tensor_tensor(out=cn[:, d * 5 + h, :], in0=cn[:, d * 5 + h, :], scalar=wold[:, d, c, h:h + 1],
                                                             in1=tu[:], op0=ALU.mult, op1=ALU.add), [B_cn, B_wo, tub], [B_cn])
                    if need_bf:
                        A(lambda: nc.scalar.copy(out=cnb[:, d * 5 + h, 0:129], in_=cn[:, d * 5 + h, :]), [B_cn], [B_cnb])

                def zero_state(d):
                    G(lambda: nc.gpsimd.memset(cn[:, d * 5:(d + 1) * 5, :], 0.0), [], [B_cn])
                    G(lambda: nc.gpsimd.memset(cnb[:, d * 5:(d + 1) * 5, :], 0.0), [], [B_cnb])

                def alloc_scan_bufs():
                    a_ = P.sb(sc, "hc", [128, 4, 640], F32)
                    b_ = P.sb(sc, "tok", [128, 2, 4, 15], F32)
                    c_ = P.sb(sc, "mcol", [128, 5], F32)
                    return (a_[0], a_[1], b_[0], b_[1], c_[0], c_[1], P.ring(sc, "gm", 3, [128, 128], BF16),
                            P.ring(sc, "ti", 3, [128, 129], F32), P.ring(sc, "hn", 3, [128, 129], F32), P.ring(sc, "s3", 3, [128, 3], F32))

                def token_scalars(d, tt):
                    A(lambda: nc.scalar.copy(out=mcol[0:64, :], in_=mc[0:64, d, 2 * tt, :]), [B_mc], [B_mcol])
                    A(lambda: nc.scalar.copy(out=mcol[64:128, :], in_=mc[64:128, d, 2 * tt + 1, :]), [B_mc], [B_mcol])
                    V(lambda: nc.vector.tensor_tensor(out=t5[:, 3, :], in0=bc[:, d, tt, 5:10], in1=mcol[:], op=ALU.max), [B_bc, B_mcol], [B_t5])
                    A(lambda: nc.scalar.activation(out=tok[:, d, tt, 0:5], in_=t5[:, 3, :], func=AF.Exp, scale=-1.0), [B_t5], [B_tok])
                    V(lambda: nc.vector.tensor_tensor(out=t5[:, 0, :], in0=mcol[:], in1=t5[:, 3, :], op=ALU.subtract), [B_mcol, B_t5], [B_t5])
                    A(lambda: nc.scalar.activation(out=tok[:, d, tt, 5:10], in_=t5[:, 0, :], func=AF.Exp), [B_t5], [B_tok])
                    V(lambda: nc.vector.tensor_tensor(out=t5[:, 1, :], in0=bc[:, d, tt, 0:5], in1=t5[:, 3, :], op=ALU.subtract), [B_bc, B_t5], [B_t5])
                    A(lambda: nc.scalar.activation(out=tok[:, d, tt, 10:15], in_=t5[:, 1, :], func=AF.Exp), [B_t5], [B_tok])

                def scan_outputs(runs, on_run_end, on_run_start):
                    G(lambda: nc.gpsimd.memset(hc[:], 0.0), [], [B_hc])
                    for d in range(2):
                        for tt in range(4):
                            token_scalars(d, tt)
                    order = {d: chunk_order(d, runs) for d in range(2)}
                    nsteps = sum(len(r) for r in runs) // 2
                    flat = {d: [c for r in order[d] for c in r] for d in range(2)}
                    run_start = {d: {r[0]: ri for ri, r in enumerate(order[d])} for d in range(2)}
                    run_end = {d: {r[-1]: ri for ri, r in enumerate(order[d])} for d in range(2)}
                    for step in range(nsteps):
                        for d in range(2):
                            c_pair = flat[d][2 * step:2 * step + 2]
                            tt = c_pair[0] // 2
                            if c_pair[0] in run_start[d]:
                                on_run_start(d, run_start[d][c_pair[0]], order[d])
                            for h in range(5):
                                gps, gpb = ps_short()
                                mm(gps[:, 0:128], ktc[:, h, tt * 128:(tt + 1) * 128], qtc[:, h, tt * 128:(tt + 1) * 128], True, True, [B_ktc, B_qtc], [gpb])
                                gm, gmb = gm_ring()
                                V(lambda: nc.vector.tensor_tensor(out=gm[:], in0=gps[:, 0:128], in1=CM(C_TRIF if d == 0 else C_TRIB), op=ALU.mult), [gpb, B_c], [gmb])
                                ips, ipb = ps_short()
                                mm(ips[:, 0:129], gm[:], vp[:, d, tt, h, 0:129], True, True, [gmb, B_vp], [ipb])
                                ti, tib = ti_ring()
                                A(lambda: nc.scalar.activation(out=ti[:], in_=ips[:, 0:129], func=AF.Identity, scale=tok[:, d, tt, h:h + 1]), [ipb, B_tok], [tib])
                                hn, hnb = hn_ring()
                                for c in c_pair:
                                    X = c % 2
                                    rs = slice(0, 64) if X == 0 else slice(64, 128)
                                    xps, xpb = ps_short()
                                    mm(xps[:, 0:129], qtc[:, h, tt * 128:(tt + 1) * 128], cnb[:, d * 5 + h, 0:129], True, True, [B_qtc, B_cnb], [xpb])
                                    V(lambda: nc.vector.scalar_tensor_tensor(out=hn[rs, :], in0=xps[rs, 0:129], scalar=tok[rs, d, tt, 5 + h:6 + h],
                                                                             in1=ti[rs, :], op0=ALU.mult, op1=ALU.add), [xpb, B_tok, tib], [hnb])
                                    state_update(d, h, c)
                                s3, s3b = s3_ring()
                                V(lambda: nc.vector.scalar_tensor_tensor(out=s3[:, 0:1], in0=hn[:, 128:129], scalar=-1.0, in1=hn[:, 128:129],
                                                                         op0=ALU.mult, op1=ALU.max), [hnb], [s3b])
                                V(lambda: nc.vector.tensor_tensor(out=s3[:, 1:2], in0=s3[:, 0:1], in1=tok[:, d, tt, 10 + h:11 + h], op=ALU.max), [s3b, B_tok], [s3b])
                                A(lambda: nc.scalar.activation(out=s3[:, 2:3], in_=s3[:, 1:2], func=AF.Ln), [s3b], [s3b])
                                A(lambda: nc.scalar.activation(out=s3[:, 2:3], in_=s3[:, 2:3], func=AF.Exp, scale=-1.0), [s3b], [s3b])
                                hsl = hc[:, tt, h * 128:(h + 1) * 128]
                                V(lambda: nc.vector.scalar_tensor_tensor(out=hsl, in0=hn[:, 0:128], scalar=s3[:, 2:3], in1=hsl,
                                                                         op0=ALU.mult, op1=ALU.add), [hnb, s3b, B_hc], [B_hc])
                            if c_pair[1] in run_end[d]:
                                on_run_end(d, run_end[d][c_pair[1]], order[d])

                def set_const(val):
                    def f(ap):
                        G(lambda: nc.gpsimd.memset(ap, val), [], [B_mcur])
                    return f

                G(lambda: nc.gpsimd.memset(nfacc[:], 0.0), [], [B_nf])
                if g == "P":
                    runs = [[0, 1, 2, 3], [4, 5, 6, 7]]
                    mfin, B_mfin = P.sb(sc, "mfin", [128, 2, 2, 5], F32)
                    for d in range(2):
                        for ri, r in enumerate(chunk_order(d, runs)):
                            mchain(d, r, set_const(0.0))
                            seq = r[0] // 4
                            A(lambda: nc.scalar.copy(out=mfin[:, seq, d, :], in_=mcur[:, d, :]), [B_mcur], [B_mfin])
                    for seq in range(2):
                        fw.dma("sp", o_cm[seq, l].rearrange("(o d) h -> o (d h)", o=1), mfin[0:1, seq, :, :].rearrange("p d h -> p (d h)"),
                               reads=[B_mfin], writes=[B_out])

                    def on_start(d, ri, order):
                        zero_state(d)

                    def on_end(d, ri, order):
                        seq = order[ri][0] // 4
                        fw.dma("sp", o_cC[seq, l, d].rearrange("h k v -> k h v"), cn[:, d * 5:(d + 1) * 5, 0:128], reads=[B_cn], writes=[B_out])
                        fw.dma("sp", o_cn[seq, l, d], cn[:, d * 5:(d + 1) * 5, 128], reads=[B_cn], writes=[B_out], slow=True)
                    hc, B_hc, tok, B_tok, mcol, B_mcol, gm_ring, ti_ring, hn_ring, s3_ring = alloc_scan_bufs()
                    scan_outputs(runs, on_end, on_start)
                else:
                    runs = [[0, 1, 2, 3, 4, 5, 6, 7]]
                    for d in range(2):
                        zero_state(d)
                        r = chunk_order(d, runs)[0]
                        mchain(d, r, set_const(NEG))
                        for c in r:
                            for h in range(5):
                                state_update(d, h, c, need_bf=False)
                    with ExitStack() as sg:
                        cst_t, B_cst = P.sb(sg, "cst_t", [128, 20], F32)
                        A(lambda: nc.scalar.copy(out=cst_t[:, 0:10], in_=mcur[:].rearrange("p d h -> p (d h)")), [B_mcur], [B_cst])
                        A(lambda: nc.scalar.copy(out=cst_t[:, 10:20], in_=nfacc[:].rearrange("p d h -> p (d h)")), [B_nf], [B_cst])
                        fw.dma("sp", cs_in[:, 0:1290], cn[:].rearrange("p a b -> p (a b)"), reads=[B_cn], writes=[B_csi])
                        fw.dma("sp", cs_in[:, 1290:1310], cst_t[:], reads=[B_cst], writes=[B_csi])
                        fw.allgather(cs_in, cs_out, reads=[B_csi], writes=[B_cso])
                        gs, B_gs = P.sb(sg, "gs", [128, 4, CSF], F32)
                        fw.dma("sp", gs[:], cs_out.rearrange("(r p) f -> p r f", p=128), reads=[B_cso], writes=[B_gs])
                        c0t, B_c0 = P.sb(sg, "c0t", [128, 10, 129], F32)
                        m0t, B_m0 = P.sb(sg, "m0t", [128, 10], F32)
                        fw.dma("sp", c0t[:, :, 0:128], cC_d[l].rearrange("d h k v -> k (d h) v"), writes=[B_c0])
                        fw.dma("sp", c0t[:, :, 128], cn_d[l], writes=[B_c0], slow=True)
                        fw.dma("sp", m0t[:], bass.AP(cm_d.tensor, l * 10, [[0, 128], [1, 10]]), writes=[B_m0])
                        av, B_av = P.sb(sg, "av", [128, 2, 6, 5], F32)
                        wv5, B_wv5 = P.sb(sg, "wv5", [128, 2, 5, 5], F32)
                        for d in range(2):
                            for i in range(5):
                                src = m0t[:, d * 5:(d + 1) * 5] if i == 0 else gs[:, i - 1, 1290 + d * 5:1290 + d * 5 + 5]
                                V(lambda: nc.vector.tensor_tensor(out=av[:, d, i, :], in0=src, in1=vtab[:, d, i, :], op=ALU.add), [B_m0, B_gs, B_c], [B_av])
                                for r in range(4):
                                    V(lambda: nc.vector.tensor_tensor(out=t5[:, 0, :], in0=cftab[:, d, i, r, :], in1=gs[:, r, 1300 + d * 5:1305 + d * 5], op=ALU.mult),
                                      [B_c, B_gs], [B_t5])
                                    V(lambda: nc.vector.tensor_tensor(out=av[:, d, i, :], in0=av[:, d, i, :], in1=t5[:, 0, :], op=ALU.subtract), [B_av, B_t5], [B_av])
                            V(lambda: nc.vector.tensor_tensor(out=av[:, d, 5, :], in0=av[:, d, 0, :], in1=av[:, d, 1, :], op=ALU.max), [B_av], [B_av])
                            for i in range(2, 5):
                                V(lambda: nc.vector.tensor_tensor(out=av[:, d, 5, :], in0=av[:, d, 5, :], in1=av[:, d, i, :], op=ALU.max), [B_av], [B_av])
                            for i in range(5):
                                V(lambda: nc.vector.tensor_tensor(out=t5[:, 1, :], in0=av[:, d, i, :], in1=av[:, d, 5, :], op=ALU.subtract), [B_av], [B_t5])
                                A(lambda: nc.scalar.activation(out=wv5[:, d, i, :], in_=t5[:, 1, :], func=AF.Exp), [B_t5], [B_wv5])
                            for h in range(5):
                                dh = d * 5 + h
                                A(lambda: nc.scalar.activation(out=cn[:, dh, :], in_=c0t[:, dh, :], func=AF.Identity, scale=wv5[:, d, 0, h:h + 1]), [B_c0, B_wv5], [B_cn])
                                for r in range(4):
                                    V(lambda: nc.vector.scalar_tensor_tensor(out=cn[:, dh, :], in0=gs[:, r, dh * 129:(dh + 1) * 129], scalar=wv5[:, d, 1 + r, h:h + 1],
                                                                             in1=cn[:, dh, :], op0=ALU.mult, op1=ALU.add), [B_gs, B_wv5, B_cn], [B_cn])
                                A(lambda: nc.scalar.copy(out=cnb[:, dh, 0:129], in_=cn[:, dh, :]), [B_cn], [B_cnb])

                        def m_from_av(d):
                            def f(ap):
                                A(lambda: nc.scalar.copy(out=ap, in_=av[:, d, 5, :]), [B_av], [B_mcur])
                            return f
                        for d in range(2):
                            mchain(d, chunk_order(d, runs)[0], m_from_av(d))
                        fw.barrier()
                    hc, B_hc, tok, B_tok, mcol, B_mcol, gm_ring, ti_ring, hn_ring, s3_ring = alloc_scan_bufs()
                    scan_outputs(runs, lambda *a: None, lambda *a: None)

                kstop("c3")
                ss, B_ss = P.sb(sc, "ss", [128, 20], F32)
                junk, B_junk = P.sb(sc, "junk", [128, 128], F32)
                yct_ring = P.ring(sc, "yct", 3, [128, 128], BF16)
                ytmp_ring = P.ring(sc, "ytmp", 2, [128, 128], F32)
                G(lambda: nc.gpsimd.memset(ss[:], 0.0), [], [B_ss])
                for tt in range(4):
                    for h in range(5):
                        A(lambda: nc.scalar.activation(out=junk[:], in_=hc[:, tt, h * 128:(h + 1) * 128], func=AF.Square,
                                                       accum_out=ss[:, tt * 5 + h:tt * 5 + h + 1]), [B_hc], [B_junk, B_ss])
                A(lambda: nc.scalar.activation(out=ss[:], in_=ss[:], func=AF.Ln, scale=1.0 / 128, bias=RMS_EPS), [B_ss], [B_ss])
                A(lambda: nc.scalar.activation(out=ss[:], in_=ss[:], func=AF.Exp, scale=-0.5), [B_ss], [B_ss])
                kstop("ca")
                for h in range(5):
                    for tt in range(4):
                        yt, ytb = ytmp_ring()
                        A(lambda: nc.scalar.activation(out=yt[:], in_=hc[:, tt, h * 128:(h + 1) * 128], func=AF.Identity, scale=ss[:, tt * 5 + h:tt * 5 + h + 1]),
                          [B_hc, B_ss], [ytb])
                        V(lambda: nc.vector.tensor_tensor(out=yt[:], in0=yt[:], in1=cnorm[:, l, :], op=ALU.mult), [ytb, B_c], [ytb])
                        yc, ycb = yct_ring()
                        V(lambda: nc.vector.tensor_tensor(out=yc[:], in0=yt[:], in1=sigoc[:, tt, h * 128:(h + 1) * 128], op=ALU.mult), [ytb, B_so], [ycb])
                        if KSTOP == "cb":
                            continue
                        ps, pb = ps_short()
                        mm(ps[:, 0:128], yc[:], id_bf[:], True, True, [ycb, B_c], [pb])
                        if KSTOP == "cc":
                            continue
                        A(lambda: nc.scalar.copy(out=ycat[:, 11 + h, tt * 128:(tt + 1) * 128], in_=ps[:, 0:128]), [pb], [B_y])
                fw.barrier()

            kstop("c4")
            kstop("cb")
            kstop("cc")
            with ExitStack() as sa:
                qta, B_qta = P.sb(sa, "qta", [128, 6, T], BF16)
                q1p, B_q1p = P.sb(sa, "q1p", [128, 5, T], BF16)
                q2p, B_q2p = P.sb(sa, "q2p", [128, 5, T], BF16)
                G(lambda: nc.gpsimd.memset(q1p[:], 0.0), [], [B_q1p])
                G(lambda: nc.gpsimd.memset(q2p[:], 0.0), [], [B_q2p])
                if g == "S":
                    q1r, B_q1r = P.sb(sa, "q1r", [128, 5, T], BF16)
                    q2r, B_q2r = P.sb(sa, "q2r", [128, 5, T], BF16)
                    G(lambda: nc.gpsimd.memset(q1r[:], 0.0), [], [B_q1r])
                    G(lambda: nc.gpsimd.memset(q2r[:], 0.0), [], [B_q2r])
                    sip = ExitStack()
                    kst_ring = P.ring(sip, "kst", 3, [128, T], BF16)
                    vst, B_vst = P.sb(sip, "vst", [128, 4, 1408], BF16)
                    rp_ring = P.ring(sip, "rpx", 2, [128, T], F32)
                    rp2_ring = P.ring(sip, "rpy", 2, [128, T], F32)
                else:
                    kta, B_kta = P.sb(sa, "kta", [128, 6, T], BF16)
                    ktb, B_ktb = P.sb(sa, "ktb", [128, 5, T], BF16)
                    vab, B_vab = P.sb(sa, "vab", [128, 4, 1408], BF16)
                    stg_ring = P.ring(sa, "stg", 3, [128, T], F32)

                def rope_apply(ps, pb, outs):
                    xf, xb = rp_ring()
                    A(lambda: nc.scalar.copy(out=xf[:], in_=ps[:]), [pb], [xb])
                    p2, pb2 = ps_short()
                    mm(p2[:], CM(C_PERM), xf[:], True, True, [B_c, xb], [pb2])
                    x2, x2b = rp2_ring()
                    V(lambda: nc.vector.tensor_tensor(out=x2[:], in0=p2[:], in1=rope[:, 1, :], op=ALU.mult), [pb2, B_c], [x2b])
                    V(lambda: nc.vector.tensor_tensor(out=xf[:], in0=xf[:], in1=rope[:, 0, :], op=ALU.mult), [xb, B_c], [xb])
                    for (rs, dst, db_) in outs:
                        V(lambda: nc.vector.tensor_tensor(out=dst[rs, :], in0=xf[rs, :], in1=x2[rs, :], op=ALU.add), [xb, x2b], [db_])

                abtiles = [(i * 512, 512) for i in range(8)] + [(4096, 128)]
                if "t" in KSKIP:
                    abtiles = abtiles[:int(KSKIP[KSKIP.index("t") + 1])]
                with extra_slots(2):
                    for (c0, ncol) in abtiles:
                        wv, wb = wload(w_in[l, :, c0:c0 + ncol], 16, ncol, key=("w_in", l, c0))
                        for q in range(ncol // 128):
                            k = (c0 + q * 128) // 128
                            fm = (k <= 11) or (18 <= k <= 27)
                            if not fm:
                                continue
                            ps, pb = ps_short()
                            for c in range(16):
                                mm(ps[:], wv[:, c, q * 128:(q + 1) * 128], h_sb[:, c, :], c == 0, c == 15, [wb, B_h], [pb])
                            if k <= 5:
                                A(lambda: nc.scalar.copy(out=qta[:, k, :], in_=ps[:]), [pb], [B_qta])
                            elif k <= 11:
                                hh = k - 6
                                if g == "P":
                                    A(lambda: nc.scalar.copy(out=kta[:, hh, :], in_=ps[:]), [pb], [B_kta])
                                    sg, sgb = stg_ring()
                                    A(lambda: nc.scalar.copy(out=sg[:], in_=ps[:]), [pb], [sgb])
                                    if "k" not in KSKIP:
                                        fw.dma("sp", o_ak[:, l, hh].rearrange("s d t -> d s t"), sg[:].rearrange("p (s t) -> p s t", s=2), reads=[sgb], writes=[B_out])
                                else:
                                    ks, ksb = kst_ring()
                                    A(lambda: nc.scalar.copy(out=ks[:], in_=ps[:]), [pb], [ksb])
                                    kr, krb = bnc_k_rows(hh)
                                    fw.dma("sp", kr, ks[:], reads=[ksb], writes=[krb])
                            elif k <= 22:
                                hh = k - 18
                                A(lambda: nc.scalar.copy(out=q1p[0:64, hh, :], in_=ps[0:64, :]), [pb], [B_q1p])
                                A(lambda: nc.scalar.copy(out=q2p[64:128, hh, :], in_=ps[64:128, :]), [pb], [B_q2p])
                                if g == "S":
                                    rope_apply(ps, pb, [(slice(0, 64), q1r[:, hh, :], B_q1r), (slice(64, 128), q2r[:, hh, :], B_q2r)])
                            else:
                                hh = k - 23
                                if g == "P":
                                    A(lambda: nc.scalar.copy(out=ktb[:, hh, :], in_=ps[:]), [pb], [B_ktb])
                                    sg, sgb = stg_ring()
                                    A(lambda: nc.scalar.copy(out=sg[:], in_=ps[:]), [pb], [sgb])
                                    if "k" not in KSKIP:
                                        fw.dma("sp", o_bk[:, l, hh].rearrange("s d t -> d s t"), sg[:].rearrange("p (s t) -> p s t", s=2), reads=[sgb], writes=[B_out])
                                else:
                                    ks, ksb = kst_ring()
                                    rope_apply(ps, pb, [(slice(0, 128), ks, ksb)])
                                    kr, krb = bnc_k_rows(6 + hh)
                                    fw.dma("sp", kr, ks[:], reads=[ksb], writes=[krb])
                        for (name, lo, hi, o_base) in (("va", 1536, 2304, 0), ("vb", 3584, 4224, 768)):
                            a, b_ = max(lo, c0), min(hi, c0 + ncol)
                            if a >= b_:
                                continue
                            n = b_ - a
                            o0 = o_base + a - lo
                            for tt in range(4):
                                ps, pb = ps_short()
                                for c in range(16):
                                    mm(ps[:, 0:n], h_sb[:, c, tt * 128:(tt + 1) * 128], wv[:, c, a - c0:b_ - c0], c == 0, c == 15, [wb, B_h], [pb])
                                if g == "P":
                                    A(lambda: nc.scalar.copy(out=vab[:, tt, o0:o0 + n], in_=ps[:, 0:n]), [pb], [B_vab])
                                    sg, sgb = stg_ring()
                                    A(lambda: nc.scalar.copy(out=sg[:, 0:n], in_=ps[:, 0:n]), [pb], [sgb])
                                    seq, s0_ = tt // 2, (tt % 2) * 128
                                    h0 = (a - lo) // 128
                                    nh = n // 128
                                    dst = (o_av if name == "va" else o_bv)[seq, l, h0:h0 + nh, s0_:s0_ + 128, :].rearrange("h s d -> s h d")
                                    if "v" not in KSKIP:
                                        fw.dma("sp", dst, sg[:, 0:n].rearrange("p (h d) -> p h d", d=128), reads=[sgb], writes=[B_out])
                                else:
                                    A(lambda: nc.scalar.copy(out=vst[:, tt, o0:o0 + n], in_=ps[:, 0:n]), [pb], [B_vst])

                for tc_ in (0, 1):
                    prefetch(("o", l, tc_), w_out[l][:, tc_ * 512:(tc_ + 1) * 512], 16, 512)
                kstop("c5")

                def alloc_attn_rings():
                    return (P.ring(sa, "et", 5, [128, T], BF16), P.ring(sa, "bt", 4, [128, T], F32),
                            P.ring(sa, "rd", 2, [128, T], F32), P.ring(sa, "ybt", 3, [128, T], F32))

                def fin_A(h):
                    def f(yps, yb, dps, db):
                        rd, rdb = rd_ring()
                        A(lambda: nc.scalar.activation(out=rd[:], in_=dps[:], func=AF.Ln), [db], [rdb])
                        A(lambda: nc.scalar.activation(out=rd[:], in_=rd[:], func=AF.Exp, scale=-1.0), [rdb], [rdb])
                        V(lambda: nc.vector.tensor_tensor(out=ycat[:, h, :], in0=yps[:], in1=rd[:], op=ALU.mult), [yb, rdb], [B_y])
                    return f

                def fin_B(dst, dstb):
                    def f(yps, yb, dps, db):
                        rd, rdb = rd_ring()
                        A(lambda: nc.scalar.activation(out=rd[:], in_=dps[:], func=AF.Ln), [db], [rdb])
                        A(lambda: nc.scalar.activation(out=rd[:], in_=rd[:], func=AF.Exp, scale=-1.0), [rdb], [rdb])
                        V(lambda: nc.vector.tensor_tensor(out=dst[:], in0=yps[:], in1=rd[:], op=ALU.mult), [yb, rdb], [dstb])
                    return f

                def diff_finish(h, y1, y1b, y2, y2b):
                    V(lambda: nc.vector.scalar_tensor_tensor(out=y1[:], in0=y2[:], scalar=nlam[:, l:l + 1], in1=y1[:], op0=ALU.mult, op1=ALU.add),
                      [y2b, y1b, B_nlam], [y1b])
                    A(lambda: nc.scalar.activation(out=y2[:], in_=y1[:], func=AF.Square), [y1b], [y2b])
                    sp_, spb = ps_short()
                    mm(sp_[:], CM(C_ONE), y2[:], True, True, [B_c, y2b], [spb])
                    A(lambda: nc.scalar.activation(out=y2[:], in_=sp_[:], func=AF.Ln, scale=1.0 / 128, bias=RMS_EPS), [spb], [y2b])
                    A(lambda: nc.scalar.activation(out=y2[:], in_=y2[:], func=AF.Exp, scale=-0.5), [y2b], [y2b])
                    V(lambda: nc.vector.tensor_tensor(out=y1[:], in0=y1[:], in1=y2[:], op=ALU.mult), [y1b, y2b], [y1b])
                    A(lambda: nc.scalar.activation(out=ycat[:, 6 + h, :], in_=y1[:], func=AF.Identity, scale=sublns[:, l:l + 1]), [y1b, B_nlam], [B_y])

                if g == "P":
                    et_ring, bt_ring, rd_ring, yb_ring = alloc_attn_rings()
                    for h in range(6):
                        groups = []
                        for kb in range(2):
                            grp = []
                            for s in range(2):
                                t0 = s * 256 + kb * 128
                                grp.append(dict(k=kta[:, h, t0:t0 + 128], kb=B_kta, q=qta[:, h, s * 256:(s + 1) * 256], qb=B_qta,
                                                v=vab[:, 2 * s + kb, h * 128:(h + 1) * 128], vb=B_vab, c0=s * 256, n=256))
                            groups.append(grp)
                        attention(groups, 128.0 ** -0.5, et_ring, bt_ring, fin_A(h))
                    for h in range(5):
                        ys = []
                        for (qp, qpb) in ((q1p, B_q1p), (q2p, B_q2p)):
                            groups = []
                            for kb in range(2):
                                grp = []
                                for s in range(2):
                                    t0 = s * 256 + kb * 128
                                    grp.append(dict(k=ktb[:, h, t0:t0 + 128], kb=B_ktb, q=qp[:, h, s * 256:(s + 1) * 256], qb=qpb,
                                                    v=vab[:, 2 * s + kb, 768 + h * 128:768 + (h + 1) * 128], vb=B_vab, c0=s * 256, n=256))
                                groups.append(grp)
                            yt, ytb = yb_ring()
                            attention(groups, 64.0 ** -0.5, et_ring, bt_ring, fin_B(yt, ytb))
                            ys.append((yt, ytb))
                        diff_finish(h, ys[0][0], ys[0][1], ys[1][0], ys[1][1])
                else:
                    for hh in range(11):
                        vr, vrb = bnc_v_rows(hh)
                        fw.dma("sp", vr.rearrange("p (tt d) -> p tt d", d=128),
                               vst[:, :, hh * 128:(hh + 1) * 128], reads=[B_vst], writes=[vrb])
                    for ci in range(3):
                        fw.allgather(bnc_in[ci], bnc_out[ci], reads=[B_bi[ci]], writes=[B_bo[ci]])
                    fw.barrier()
                    sip.close()
                    et_ring, bt_ring, rd_ring, yb_ring = alloc_attn_rings()
                    kall_ring = P.ring(sa, "kall", 2, [128, 4, T], BF16)
                    vall_ring = P.ring(sa, "vall", 2, [128, 4, T], BF16)
                    kctx_ring = P.ring(sa, "kctx", 2, [128, 512], BF16)
                    vctx_ring = P.ring(sa, "vctx", 2, [128, 4, 128], BF16)
                    nb_ring = P.ring(sa, "nbias", 5, [128, T], F32)
                    gviews = [bo.rearrange("(r x) t -> x r t", r=4) for bo in bnc_out]
                    for hh in range(11):
                        isA = hh < 6
                        h = hh if isA else hh - 6
                        ka, kab = kall_ring()
                        va_, vab_ = vall_ring()
                        kc_, kcb_ = kctx_ring()
                        vc_, vcb_ = vctx_ring()
                        gch = HH_CH[hh]
                        krow = HH_IX[hh] * 128
                        vrow = (CH_NH[gch] + HH_IX[hh]) * 128
                        fw.dma("sp", ka[:], gviews[gch][krow:krow + 128], reads=[B_bo[gch]], writes=[kab])
                        fw.dma("sp", va_[:], gviews[gch][vrow:vrow + 128], reads=[B_bo[gch]], writes=[vab_])
                        if isA:
                            fw.dma("pool", kc_[:], cakT[l, h], writes=[kcb_])
                            fw.dma("pool", vc_[:], cav[l, h].rearrange("(b p) d -> p b d", p=128), writes=[vcb_])
                        else:
                            fw.dma("pool", kc_[:], cbkT[l, h], writes=[kcb_])
                            fw.dma("pool", vc_[:], cbv[l, h].rearrange("(b p) d -> p b d", p=128), writes=[vcb_])

                        def mkgroups(q_lat, q_latb, q_ctx, q_ctxb):
                            groups = []
                            for kb in range(16):
                                r, t4 = kb // 4, kb % 4
                                sub = dict(k=ka[:, r, t4 * 128:(t4 + 1) * 128], kb=kab, q=q_lat, qb=q_latb,
                                           v=va_[:, r, t4 * 128:(t4 + 1) * 128], vb=vab_, c0=0, n=T)
                                if isA:
                                    def bias_fn(kb=kb, h=h):
                                        nbt, nbb = nb_ring()
                                        fw.dma("sp", nbt[:], nbias_d[l, h, kb], writes=[nbb])
                                        return nbt[:], nbb
                                    sub["bias_fn"] = bias_fn
                                groups.append([sub])
                            for kb in range(4):
                                groups.append([dict(k=kc_[:, kb * 128:(kb + 1) * 128], kb=kcb_, q=q_ctx, qb=q_ctxb,
                                                    v=vc_[:, kb, :], vb=vcb_, c0=0, n=T)])
                            return groups
                        if isA:
                            attention(mkgroups(qta[:, h, :], B_qta, qta[:, h, :], B_qta), 128.0 ** -0.5, et_ring, bt_ring, fin_A(h))
                        else:
                            ys = []
                            for (qr, qrb, qp, qpb) in ((q1r, B_q1r, q1p, B_q1p), (q2r, B_q2r, q2p, B_q2p)):
                                yt, ytb = yb_ring()
                                attention(mkgroups(qr[:, h, :], qrb, qp[:, h, :], qpb), 64.0 ** -0.5, et_ring, bt_ring, fin_B(yt, ytb))
                                ys.append((yt, ytb))
                            diff_finish(h, ys[0][0], ys[0][1], ys[1][0], ys[1][1])
                fw.barrier()

            kstop("c6")
            with ExitStack() as so:
                def pf_up():
                    prefetch(("up", l, 0), w_up[l, :, 0:512], 16, 512)
                    prefetch(("up", l, DFF), w_up[l, :, DFF:DFF + 512], 16, 512)
                residual_proj(l, w_out[l], 16, 512, lambda c: ycat[:, c, :], [B_y], 2, row, so, "o", after=pf_up)
                layernorm(l, 0, so)
                fw.barrier()
        kstop("c7")
        modulate(l, 3, 4, row)
        with ExitStack() as sf:
            actb, B_act = P.sb(sf, "actb", [128, 43, T], BF16)
            u_ring = P.ring(sf, "u1", 4, [128, T], F32)
            hal, B_hal = P.sb(sf, "hal", [128, 2, 2], F32)
            if g == "S":
                hbs, B_hbs = P.sb(sf, "hbs", [128, 16, 2], F32)
                hba, B_hba = P.sb(sf, "hba", [128, 4, 32], F32)
                hbf, B_hbf = P.sb(sf, "hbf", [128, 16, 2], F32)
                A(lambda: nc.scalar.copy(out=hbs[:, :, 0], in_=h_sb[:, :, 0]), [B_h], [B_hbs])
                A(lambda: nc.scalar.copy(out=hbs[:, :, 1], in_=h_sb[:, :, T - 1]), [B_h], [B_hbs])
                fw.dma("sp", hb_in, hbs[:].rearrange("p c k -> p (c k)"), reads=[B_hbs], writes=[B_hbi])
                fw.allgather(hb_in, hb_out, reads=[B_hbi], writes=[B_hbo])
                fw.dma("sp", hba[:], hb_out.rearrange("(r p) f -> p r f", p=128), reads=[B_hbo], writes=[B_hba])
                hv = hba[:].rearrange("p r (c k) -> p r c k", k=2)
                for (side, kk, so_) in ((0, 1, 0), (1, 0, 4)):
                    A(lambda: nc.scalar.activation(out=hbf[:, :, side], in_=hv[:, 0, :, kk], func=AF.Identity, scale=sel[:, so_:so_ + 1]), [B_hba, B_c], [B_hbf])
                    for r in range(1, 4):
                        V(lambda: nc.vector.scalar_tensor_tensor(out=hbf[:, :, side], in0=hv[:, r, :, kk], scalar=sel[:, so_ + r:so_ + r + 1],
                                                                 in1=hbf[:, :, side], op0=ALU.mult, op1=ALU.add), [B_hba, B_c, B_hbf], [B_hbf])
                A(lambda: nc.scalar.copy(out=hh_sb[:], in_=hbf[:]), [B_hbf], [B_hh])
            segs = [(0, 256), (256, 256)] if g == "P" else [(0, 512)]

            def conv_chunk(ps, pb, hps, hpb, ch):
                u, ub = u_ring()
                cp = convp[:, l, ch, :]
                A(lambda: nc.scalar.activation(out=u[:], in_=ps[:], func=AF.Identity, scale=cp[:, 1:2], bias=cp[:, 3:4]), [pb, B_c], [ub])
                for (s0_, n) in segs:
                    V(lambda: nc.vector.scalar_tensor_tensor(out=u[:, s0_ + 1:s0_ + n], in0=ps[:, s0_:s0_ + n - 1], scalar=cp[:, 0:1],
                                                             in1=u[:, s0_ + 1:s0_ + n], op0=ALU.mult, op1=ALU.add), [pb, B_c, ub], [ub])
                    V(lambda: nc.vector.scalar_tensor_tensor(out=u[:, s0_:s0_ + n - 1], in0=ps[:, s0_ + 1:s0_ + n], scalar=cp[:, 2:3],
                                                             in1=u[:, s0_:s0_ + n - 1], op0=ALU.mult, op1=ALU.add), [pb, B_c, ub], [ub])
                if g == "S":
                    V(lambda: nc.vector.scalar_tensor_tensor(out=u[:, 0:1], in0=hps[:, 0:1], scalar=cp[:, 0:1], in1=u[:, 0:1],
                                                             op0=ALU.mult, op1=ALU.add), [hpb, B_c, ub], [ub])
                    V(lambda: nc.vector.scalar_tensor_tensor(out=u[:, T - 1:T], in0=hps[:, 1:2], scalar=cp[:, 2:3], in1=u[:, T - 1:T],
                                                             op0=ALU.mult, op1=ALU.add), [hpb, B_c, ub], [ub])
                return u, ub

            with extra_slots(4):
                for ti in range(11):
                    ncol = 512 if ti < 10 else 384
                    wa, wab = wload(w_up[l, :, ti * 512:ti * 512 + ncol], 16, ncol, key=("up", l, ti * 512))
                    wg, wgb = wload(w_up[l, :, DFF + ti * 512:DFF + ti * 512 + ncol], 16, ncol, key=("up", l, DFF + ti * 512))
                    for q in range(ncol // 128):
                        j = ti * 4 + q
                        res = []
                        for (wv, wb, ch) in ((wa, wab, j), (wg, wgb, 43 + j)):
                            ps, pb = ps_short()
                            for c in range(16):
                                mm(ps[:], wv[:, c, q * 128:(q + 1) * 128], h_sb[:, c, :], c == 0, c == 15, [wb, B_h], [pb])
                            hps, hpb = None, None
                            if g == "S":
                                hps, hpb = ps_short()
                                for c in range(16):
                                    mm(hps[:, 0:2], wv[:, c, q * 128:(q + 1) * 128], hh_sb[:, c, :], c == 0, c == 15, [wb, B_hh], [hpb])
                            res.append(conv_chunk(ps, pb, hps, hpb, ch))
                        (ua, uab), (ug, ugb) = res
                        A(lambda: nc.scalar.activation(out=ug[:], in_=ug[:], func=AF.Silu), [ugb], [ugb])
                        V(lambda: nc.vector.tensor_tensor(out=actb[:, j, :], in0=ug[:], in1=ua[:], op=ALU.mult), [ugb, uab], [B_act])
            def pf_next():
                nl = l if g == "P" else l + 1
                if nl < L:
                    for c0_ in (4224, 4736):
                        prefetch(("w_in", nl, c0_), w_in[nl, :, c0_:c0_ + 512], 16, 512)
            residual_proj(l, w_down[l], 43, 128, lambda c: actb[:, c, :], [B_act], 5, row, sf, "d", after=pf_next)
            layernorm(l, 1, sf)
            fw.barrier()
        if l == L - 1:
            fw.dma("sp", yout[g].rearrange("(c p) t -> p c t", p=128), x_sb[:], reads=[B_x], writes=[B_out])
        else:
            fw.dma("sp", xspill[g].rearrange("(c p) t -> p c t", p=128), x_sb[:], reads=[B_x], writes=[B_spill[g]])

    nblk = 0
    try:
        for l in range(L):
            for g in ("P", "S"):
                if KSTOP.startswith("b") and nblk >= int(KSTOP[1]):
                    break
                if KSTOP.startswith("c") and nblk >= 1:
                    break
                block(l, g)
                nblk += 1
    except StopBuild:
        fw.barrier()
        fw.finish()
        return P
    fw.finish()
    P.es.close()
    fw.close()
    return P


def _consts():
    c = np.zeros((NCONST, 128, 128), np.float32)
    idx = np.arange(128)
    c[C_ID] = np.eye(128)
    c[C_ONE] = 1.0
    same = (idx[:, None] // 64) == (idx[None, :] // 64)
    c[C_TRIF] = (same & (idx[:, None] <= idx[None, :]))
    c[C_TRIB] = (same & (idx[:, None] >= idx[None, :]))
    for k, p in ((C_E0, 0), (C_E63, 63), (C_E64, 64), (C_E127, 127)):
        c[k][p, :] = 1.0
    dd = idx % 32
    partner = np.where(dd < 16, idx + 16, idx - 16)
    c[C_PERM][partner, idx] = 1.0
    negm = np.zeros((2, 128, 5, 128), np.float32)
    negm[0] = np.where(c[C_TRIF].T[:, None, :] > 0, 0.0, NEG)
    negm[1] = np.where(c[C_TRIB].T[:, None, :] > 0, 0.0, NEG)
    return (np.ascontiguousarray(c.transpose(1, 0, 2)).reshape(128, NCONST * 128),
            np.ascontiguousarray(negm.transpose(1, 0, 2, 3)).reshape(128, 2 * 640))


def _rope_tables(j):
    t = np.arange(T) + j * T
    rows = (t // 64).astype(np.float32)
    cols = (t % 64).astype(np.float32)
    freqs = (np.float32(10000.0) ** (-np.arange(0, 32, 2, dtype=np.float32) / np.float32(32))).astype(np.float32)
    p = np.arange(128)
    dd = p % 64
    idx = dd % 32
    f = idx % 16
    first = idx < 16
    pos = np.where((dd < 32)[:, None], rows[None, :], cols[None, :]).astype(np.float32)
    ang = (pos * freqs[f][:, None]).astype(np.float32)
    cos = np.cos(ang).astype(np.float32)
    sin = np.sin(ang).astype(np.float32)
    sins = np.where(first[:, None], -sin, sin).astype(np.float32)
    return np.concatenate([cos, sins], axis=1)


def _natten_bias(a_rpb, j):
    kt = np.arange(2048)
    krow, kcol = kt // 64, kt % 64
    qt = np.arange(T) + j * T
    qrow, qcol = qt // 64, qt % 64
    rs = np.clip(qrow - 4, 0, 24)
    cs = np.clip(qcol - 8, 0, 48)
    vr = (krow[:, None] >= rs[None, :]) & (krow[:, None] < rs[None, :] + 8)
    vcm = (kcol[:, None] >= cs[None, :]) & (kcol[:, None] < cs[None, :] + 16)
    valid = vr & vcm
    ri = np.clip(7 + krow[:, None] - qrow[None, :], 0, 14)
    ci = np.clip(kcol[:, None] - qcol[None, :] + 15, 0, 30)
    out = np.empty((L, 6, 2048, T), np.float32)
    for l in range(L):
        for h in range(6):
            out[l, h] = np.where(valid, a_rpb[l, h][ri, ci], np.float32(NEG))
    return out.reshape(L, 6, 16, 128, T)


def _combine_tables(j):
    cf = np.zeros((2, 5, 4), np.float32)
    vt = np.zeros((2, 5), np.float32)
    for r2 in range(4):
        if r2 < j:
            cf[0, 0, r2] = 1
        if r2 > j:
            cf[1, 0, r2] = 1
    for r in range(4):
        vt[0, 1 + r] = 0.0 if r < j else NEG
        vt[1, 1 + r] = 0.0 if r > j else NEG
        for r2 in range(4):
            if r < r2 < j:
                cf[0, 1 + r, r2] = 1
            if j < r2 < r:
                cf[1, 1 + r, r2] = 1
    cft = np.broadcast_to(cf[None, :, :, :, None], (128, 2, 5, 4, 5)).reshape(128, -1)
    vtt = np.broadcast_to(vt[None, :, :, None], (128, 2, 5, 5)).reshape(128, -1)
    sel = np.zeros((8,), np.float32)
    if j > 0:
        sel[j - 1] = 1
    if j < 3:
        sel[4 + j + 1] = 1
    return np.ascontiguousarray(cft), np.ascontiguousarray(vtt), np.ascontiguousarray(np.broadcast_to(sel[None], (128, 8)))


_PROG = None


def kernel(x_prompt, x_sample, cache_a_k, cache_a_v, cache_b_k, cache_b_v, state_c_C, state_c_n, state_c_m,
           c, c_ctx, w_mod, b_mod, w_in, c_gate_b, a_rpb, b_lambda, b_subln, c_norm, w_out,
           ln1_g, ln1_b, ln2_g, ln2_b, w_up, conv_w, conv_b, w_down):
    global _PROG
    f = lambda a: np.ascontiguousarray(np.asarray(a, dtype=np.float32))
    x_prompt, x_sample = f(x_prompt), f(x_sample)
    if _PROG is None:
        _PROG = build_program()
    P = _PROG
    consts, negm = _consts()
    lnp = np.stack([f(ln1_g), f(ln1_b), f(ln2_g), f(ln2_b)], 1).reshape(L, 4, 16, 128).transpose(3, 0, 1, 2).reshape(128, -1)
    cw = np.concatenate([f(conv_w), f(conv_b)[:, None, :]], 1)
    convp = cw.reshape(L, 4, 86, 128).transpose(3, 0, 2, 1).reshape(128, -1)
    shared = {
        "w_in": f(w_in), "w_out": f(w_out), "w_up": f(w_up), "w_down": f(w_down),
        "lnp": np.ascontiguousarray(lnp), "convp": np.ascontiguousarray(convp),
        "gateb": f(c_gate_b).reshape(-1), "blam": f(b_lambda).reshape(-1),
        "subln": np.ascontiguousarray(f(b_subln).T), "cnorm": f(c_norm).reshape(-1),
        "consts": consts, "negm": negm,
    }
    w_mod, b_mod = f(w_mod), f(b_mod)
    in_maps = []
    for i in range(8):
        b, j = i // 4, i % 4
        m = dict(shared)
        m["xpT"] = np.ascontiguousarray(x_prompt[2 * i:2 * i + 2].reshape(T, D).T)
        m["xsT"] = np.ascontiguousarray(x_sample[b, j * T:(j + 1) * T].T)
        m["wmod"] = np.ascontiguousarray(w_mod[:, :, j * 3072:(j + 1) * 3072])
        m["bmod"] = np.ascontiguousarray(b_mod[:, j * 3072:(j + 1) * 3072].reshape(L, 24, 128).transpose(2, 0, 1).reshape(128, -1))
        cond = np.stack([f(c_ctx), f(c)[b]], 1)
        m["cond2"] = np.ascontiguousarray(cond.reshape(16, 128, 2).transpose(1, 0, 2).reshape(128, 32))
        m["rope"] = _rope_tables(j)
        m["nbias"] = _natten_bias(f(a_rpb), j)
        m["cakT"] = np.ascontiguousarray(f(cache_a_k)[b].transpose(0, 1, 3, 2))
        m["cav"] = f(cache_a_v)[b]
        m["cbkT"] = np.ascontiguousarray(f(cache_b_k)[b].transpose(0, 1, 3, 2))
        m["cbv"] = f(cache_b_v)[b]
        m["cC"] = f(state_c_C)[b]
        m["cn"] = np.ascontiguousarray(f(state_c_n)[b].reshape(L, 10, 128).transpose(0, 2, 1))
        m["cm"] = f(state_c_m)[b].reshape(-1)
        m["cftab"], m["vtab"], m["sel"] = _combine_tables(j)
        in_maps.append(m)
    in_maps = [{k: v for k, v in m.items() if k in P.din} for m in in_maps]
    res = run_bass_kernel_spmd(P.nc, in_maps, core_ids=list(range(8))).results
    yp = np.stack([r["ypT"].T.reshape(2, 256, D) for r in res], 0).reshape(16, 256, D)
    ys = np.stack([r["ysT"].T for r in res], 0).reshape(2, 4 * T, D)
    cat = lambda k: np.concatenate([r[k] for r in res], 0)
    n_ak = np.ascontiguousarray(cat("o_ak").transpose(0, 1, 2, 4, 3))
    n_av = cat("o_av")
    n_bk = np.ascontiguousarray(cat("o_bk").transpose(0, 1, 2, 4, 3))
    n_bv = cat("o_bv")
    return (np.ascontiguousarray(yp, dtype=np.float32), np.ascontiguousarray(ys, dtype=np.float32), n_ak, n_av, n_bk, n_bv,
            cat("o_cC"), np.ascontiguousarray(cat("o_cn").transpose(0, 1, 2, 4, 3)), cat("o_cm"))
```
